# Optimizing a Trainium2 kernel written in Bass

```python
import math
import jax, jax.numpy as jnp
from jax import lax
import numpy as np

D_MODEL = 4096
BATCH = 4
SEQ = 4096
DEPTH = 1

CHUNK = 64
Q_BLOCK = 128
PLE_DIM = 256
NORM_EPS = 1e-6

RWKV_WIDTH = D_MODEL // 2
RWKV_HEAD_DIM = 64
RWKV_HEADS = RWKV_WIDTH // RWKV_HEAD_DIM
DECAY_LORA = max(32, int(round(1.8 * RWKV_WIDTH ** 0.5 / 32)) * 32)
AAA_LORA = max(32, int(round(2.5 * RWKV_WIDTH ** 0.5 / 32)) * 32)
GATE_LORA = max(32, int(round(0.6 * RWKV_WIDTH ** 0.8 / 32)) * 32)
RWKV_GN_EPS = 64e-5
RWKV_COLS = 3 * RWKV_WIDTH + DECAY_LORA + AAA_LORA + GATE_LORA
RWKV_SPLITS = [RWKV_WIDTH, 2 * RWKV_WIDTH, 3 * RWKV_WIDTH,
               3 * RWKV_WIDTH + DECAY_LORA, 3 * RWKV_WIDTH + DECAY_LORA + AAA_LORA]

DIFF_WIDTH = D_MODEL // 2
DIFF_HEAD_DIM = 64
DIFF_HEADS = DIFF_WIDTH // (2 * DIFF_HEAD_DIM)
DIFF_COLS = 3 * DIFF_WIDTH

GATE_COLS = 2 * D_MODEL
IN_COLS = RWKV_COLS + DIFF_COLS + GATE_COLS

D_FF = int(round(8 * D_MODEL / 3 / 256)) * 256
CONV_WIDTH = 3

kernel_name = 'hybrid_rwkv7_diffattn_convffn_block'


def rms_norm(x, g, eps=NORM_EPS):
    xf = x.astype(jnp.float32)
    y = xf * lax.rsqrt(jnp.mean(xf * xf, axis=-1, keepdims=True) + eps)
    return (y * g.astype(jnp.float32)).astype(x.dtype)


def token_shift(z):
    return jnp.pad(z, ((0, 0), (1, 0), (0, 0)))[:, :-1]


def causal_depthwise_conv(u, w, b):
    k_width, seq = w.shape[0], u.shape[1]
    up = jnp.pad(u, ((0, 0), (k_width - 1, 0), (0, 0)))
    out = b
    for j in range(k_width):
        out = out + up[:, j:j + seq] * w[j]
    return out


def rwkv7_time_mix(z, mu, w0, w2, a0, a2, g2, k_k, k_a, r_k, ln_w, ln_b):
    out_dtype = z.dtype
    f32 = jnp.float32
    bsz, seq, _ = z.shape
    z = z.astype(f32)
    z = z + (token_shift(z) - z) * mu.astype(f32)
    r, k, v, w_lo, a_lo, g_lo = jnp.split(z, RWKV_SPLITS, axis=-1)
    log_w = -jax.nn.softplus(-(w0.astype(f32) + jnp.tanh(w_lo) @ w2.astype(f32))) - 0.5
    decay = jnp.exp(-jnp.exp(log_w))
    a = jax.nn.sigmoid(a0.astype(f32) + a_lo @ a2.astype(f32))
    g = jax.nn.sigmoid(g_lo) @ g2.astype(f32)

    def heads(t):
        return t.reshape(bsz, seq, RWKV_HEADS, RWKV_HEAD_DIM)

    kk = heads(k * k_k.astype(f32))
    kk = kk / jnp.maximum(jnp.sqrt(jnp.sum(kk * kk, axis=-1, keepdims=True)), 1e-12)
    k = k * (1.0 + (a - 1.0) * k_a.astype(f32))
    rh, kh, vh, wh, ah = heads(r), heads(k), heads(v), heads(decay), heads(a)
    bh = kk * ah
    xs = tuple(jnp.moveaxis(t, 1, 0) for t in (rh, wh, kh, vh, kk, bh))

    def step(state, inp):
        r_t, w_t, k_t, v_t, kk_t, b_t = inp
        s_kk = jnp.einsum('bhvk,bhk->bhv', state, kk_t)
        state = (state * w_t[:, :, None, :]
                 - s_kk[..., None] * b_t[:, :, None, :]
                 + v_t[..., None] * k_t[:, :, None, :])
        return state, jnp.einsum('bhvk,bhk->bhv', state, r_t)

    s0 = jnp.zeros((bsz, RWKV_HEADS, RWKV_HEAD_DIM, RWKV_HEAD_DIM), f32)
    _, o = lax.scan(step, s0, xs)
    o = jnp.moveaxis(o, 0, 1)
    mean = jnp.mean(o, axis=-1, keepdims=True)
    var = jnp.mean(jnp.square(o - mean), axis=-1, keepdims=True)
    o = (o - mean) * lax.rsqrt(var + RWKV_GN_EPS)
    o = o.reshape(bsz, seq, RWKV_WIDTH) * ln_w.astype(f32) + ln_b.astype(f32)
    bonus = jnp.sum(rh * kh * r_k.astype(f32), axis=-1, keepdims=True) * vh
    o = o + bonus.reshape(bsz, seq, RWKV_WIDTH)
    return (o * g).astype(out_dtype)


def diff_attention(z, q_g, k_g, lam_q1, lam_k1, lam_q2, lam_k2, subln_g, lambda_init):
    out_dtype = z.dtype
    f32 = jnp.float32
    bsz, seq, _ = z.shape
    q, k, v = jnp.split(z.astype(f32), [DIFF_WIDTH, 2 * DIFF_WIDTH], axis=-1)
    q = rms_norm(q.reshape(bsz, seq, DIFF_HEADS, 2, DIFF_HEAD_DIM), q_g)
    k = rms_norm(k.reshape(bsz, seq, DIFF_HEADS, 2, DIFF_HEAD_DIM), k_g)
    v = v.reshape(bsz, seq, DIFF_HEADS, 2 * DIFF_HEAD_DIM)
    lam = (jnp.exp(jnp.sum(lam_q1.astype(f32) * lam_k1.astype(f32)))
           - jnp.exp(jnp.sum(lam_q2.astype(f32) * lam_k2.astype(f32))) + lambda_init)
    scale = DIFF_HEAD_DIM ** -0.5
    n_blocks = seq // Q_BLOCK
    q_blocks = jnp.moveaxis(q.reshape(bsz, n_blocks, Q_BLOCK, DIFF_HEADS, 2, DIFF_HEAD_DIM), 1, 0)
    key_chunk = jnp.arange(seq) // CHUNK

    def block(args):
        q_blk, bi = args
        q_chunk = (bi * Q_BLOCK + jnp.arange(Q_BLOCK)) // CHUNK
        mask = key_chunk[None, :] <= q_chunk[:, None]
        s = jnp.einsum('bqhcd,bkhcd->bhcqk', q_blk, k) * scale
        s = jnp.where(mask, s, -jnp.inf)
        pr = jax.nn.softmax(s, axis=-1)
        diff = pr[:, :, 0] - lam * pr[:, :, 1]
        return jnp.einsum('bhqk,bkhe->bqhe', diff, v)

    o = lax.map(block, (q_blocks, jnp.arange(n_blocks)))
    o = jnp.moveaxis(o, 0, 1).reshape(bsz, seq, DIFF_HEADS, 2 * DIFF_HEAD_DIM)
    o = rms_norm(o, subln_g) * (1.0 - lambda_init)
    return o.reshape(bsz, seq, DIFF_WIDTH).astype(out_dtype)


def conv_ffn(h, w_in, conv_w, conv_b, w_out):
    u = h @ w_in
    u = causal_depthwise_conv(u, conv_w, conv_b)
    gate, up = jnp.split(u, 2, axis=-1)
    return (jax.nn.silu(gate) * up) @ w_out


def setup_inputs(seed: int = 0) -> dict:
    key = jax.random.key(seed)
    ks = iter(jax.random.split(key, 40))
    f32 = jnp.float32
    L = DEPTH

    def nrm(shape, scale):
        return scale * jax.random.normal(next(ks), shape, f32)

    def uni(shape, lo, hi):
        return jax.random.uniform(next(ks), shape, f32, lo, hi)

    def gain(shape):
        return 1.0 + nrm(shape, 0.02)

    return {
        'x': nrm((BATCH, SEQ, D_MODEL), 1.0),
        'p': nrm((DEPTH, BATCH, SEQ, PLE_DIM), 1.0),
        'norm_mix_g': gain((L, D_MODEL)),
        'w_in': nrm((L, D_MODEL, IN_COLS), D_MODEL ** -0.5),
        'rwkv_mu': uni((L, RWKV_COLS), 0.0, 1.0),
        'rwkv_w0': uni((L, RWKV_WIDTH), -4.0, 0.0),
        'rwkv_w2': nrm((L, DECAY_LORA, RWKV_WIDTH), 0.5 * DECAY_LORA ** -0.5),
        'rwkv_a0': nrm((L, RWKV_WIDTH), 0.1),
        'rwkv_a2': nrm((L, AAA_LORA, RWKV_WIDTH), 0.5 * AAA_LORA ** -0.5),
        'rwkv_g2': nrm((L, GATE_LORA, RWKV_WIDTH), GATE_LORA ** -0.5),
        'rwkv_k_k': 0.85 + nrm((L, RWKV_WIDTH), 0.05),
        'rwkv_k_a': 1.0 + nrm((L, RWKV_WIDTH), 0.05),
        'rwkv_r_k': nrm((L, RWKV_HEADS, RWKV_HEAD_DIM), 0.1),
        'rwkv_ln_w': gain((L, RWKV_WIDTH)),
        'rwkv_ln_b': nrm((L, RWKV_WIDTH), 0.02),
        'q_norm_g': gain((L, DIFF_HEAD_DIM)),
        'k_norm_g': gain((L, DIFF_HEAD_DIM)),
        'lam_q1': nrm((L, DIFF_HEAD_DIM), 0.1),
        'lam_k1': nrm((L, DIFF_HEAD_DIM), 0.1),
        'lam_q2': nrm((L, DIFF_HEAD_DIM), 0.1),
        'lam_k2': nrm((L, DIFF_HEAD_DIM), 0.1),
        'subln_g': gain((L, 2 * DIFF_HEAD_DIM)),
        'w_branch_a': nrm((L, RWKV_WIDTH, D_MODEL), RWKV_WIDTH ** -0.5),
        'w_branch_b': nrm((L, DIFF_WIDTH, D_MODEL), DIFF_WIDTH ** -0.5),
        'w_out': nrm((L, D_MODEL, D_MODEL), D_MODEL ** -0.5),
        'norm_ffn_g': gain((L, D_MODEL)),
        'w_ffn_in': nrm((L, D_MODEL, 2 * D_FF), D_MODEL ** -0.5),
        'ffn_conv_w': nrm((L, CONV_WIDTH, 2 * D_FF), CONV_WIDTH ** -0.5),
        'ffn_conv_b': nrm((L, 2 * D_FF), 0.02),
        'w_ffn_out': nrm((L, D_FF, D_MODEL), D_FF ** -0.5),
        'norm_ple_g': gain((L, D_MODEL)),
        'w_ple_gate': nrm((L, D_MODEL, D_MODEL), D_MODEL ** -0.5),
        'w_ple_proj': nrm((L, PLE_DIM, D_MODEL), PLE_DIM ** -0.5),
    }


def reference(x, p, norm_mix_g, w_in, rwkv_mu, rwkv_w0, rwkv_w2, rwkv_a0, rwkv_a2, rwkv_g2,
              rwkv_k_k, rwkv_k_a, rwkv_r_k, rwkv_ln_w, rwkv_ln_b, q_norm_g, k_norm_g,
              lam_q1, lam_k1, lam_q2, lam_k2, subln_g, w_branch_a, w_branch_b, w_out,
              norm_ffn_g, w_ffn_in, ffn_conv_w, ffn_conv_b, w_ffn_out,
              norm_ple_g, w_ple_gate, w_ple_proj):
    for i in range(DEPTH):
        lambda_init = 0.8 - 0.6 * math.exp(-0.3 * i)
        h = rms_norm(x, norm_mix_g[i])
        z = h @ w_in[i]
        z_a, z_b, z_g = jnp.split(z, [RWKV_COLS, RWKV_COLS + DIFF_COLS], axis=-1)
        o_a = rwkv7_time_mix(z_a, rwkv_mu[i], rwkv_w0[i], rwkv_w2[i], rwkv_a0[i], rwkv_a2[i],
                             rwkv_g2[i], rwkv_k_k[i], rwkv_k_a[i], rwkv_r_k[i],
                             rwkv_ln_w[i], rwkv_ln_b[i])
        o_b = diff_attention(z_b, q_norm_g[i], k_norm_g[i], lam_q1[i], lam_k1[i],
                             lam_q2[i], lam_k2[i], subln_g[i], lambda_init)
        g_a, g_b = jnp.split(jax.nn.sigmoid(z_g), 2, axis=-1)
        merged = g_a * (o_a @ w_branch_a[i]) + g_b * (o_b @ w_branch_b[i])
        x = x + merged @ w_out[i]
        x = x + conv_ffn(rms_norm(x, norm_ffn_g[i]), w_ffn_in[i], ffn_conv_w[i],
                         ffn_conv_b[i], w_ffn_out[i])
        ple_gate = jax.nn.sigmoid(rms_norm(x, norm_ple_g[i]) @ w_ple_gate[i])
        x = x + ple_gate * (p[i] @ w_ple_proj[i])
    return x
```

```python
import contextlib
import math
import numpy as np
import concourse.bass as bass
import concourse.mybir as mybir
from concourse.bass_utils import run_bass_kernel_spmd

F32 = mybir.dt.float32
BF16 = mybir.dt.bfloat16
AF = mybir.ActivationFunctionType
ALU = mybir.AluOpType


class Buf:
    __slots__ = ("name", "w", "r", "g")

    def __init__(self, name):
        self.name = name
        self.w = {}
        self.r = {}
        self.g = None


class Tl:
    def __init__(self, h, name):
        self.h = h
        self.b = Buf(name)


class Rot:
    def __init__(self, tiles):
        self.t = tiles
        self.i = 0

    def get(self):
        t = self.t[self.i % len(self.t)]
        self.i += 1
        return t


class Sched:
    ENGS = ("pe", "act", "dve", "pool", "sp")
    import os
    NDS = int(os.environ.get('NDS', 6))

    def __init__(self, nc, es):
        self.nc = nc
        self.streams = {e: [] for e in self.ENGS}
        self.cnt = {e: 0 for e in self.ENGS}
        self.sems = {}
        for e in self.ENGS:
            self.sems[e] = es.enter_context(nc.semaphore("s_" + e))
        self.dq = {}
        self.dlast = {}
        for q in ("sp", "act", "pool"):
            self.dq[q] = 0
            for i in range(self.NDS):
                self.sems[("d", q, i)] = es.enter_context(nc.semaphore("d_%s_%d" % (q, i)))
        self.known = {}
        self.final = []
        self.rr = 0
        self.log = []

    def _wait(self, e, key, val):
        if key == "pe" and e == "pe":
            return
        if self.known.get((e, key), 0) >= val:
            return
        self.known[(e, key)] = val
        self.log.append((e, "wait", key, val))
        sem = self.sems[key]
        self.streams[e].append(lambda eng, sem=sem, val=val: eng.wait_ge(sem, val))

    def _deps(self, e, r, w, wa):
        for b in r:
            for k, v in b.w.items():
                self._wait(e, k, v)
        for b in w:
            for k, v in b.w.items():
                self._wait(e, k, v)
            for k, v in b.r.items():
                self._wait(e, k, v)
        for b in wa:
            if b.g is not None:
                self._wait(e, b.g[0], b.g[1])
            for k, v in b.r.items():
                self._wait(e, k, v)

    def _mark(self, tok, r, w, wa):
        k, v = tok
        for b in r:
            if b.r.get(k, 0) < v:
                b.r[k] = v
        for b in w:
            b.w = {k: v}
            b.r = {}
            b.g = tok
        for b in wa:
            if b.w.get(k, 0) < v:
                b.w[k] = v

    def op(self, e, fn, r=(), w=(), wa=(), inc=True):
        self._deps(e, r, w, wa)
        sem = self.sems[e]
        if inc:
            self.cnt[e] += 1
            tok = (e, self.cnt[e])
            self.streams[e].append(lambda eng, fn=fn, sem=sem: fn(eng).then_inc(sem, 1))
        else:
            tok = (e, self.cnt[e] + 1)
            self.streams[e].append(lambda eng, fn=fn: fn(eng))
        self._mark(tok, r, w, wa)
        self.log.append((e, "op", tok, inc))
        return tok

    def dma(self, q, out, in_, r=(), w=(), wa=(), final=False):
        j = self.dq[q]
        self.dq[q] += 1
        slot = j % self.NDS
        key = ("d", q, slot)
        val = 16 * (j // self.NDS + 1)
        if j >= self.NDS:
            self._wait(q, key, val - 16)
        self._deps(q, r, w, wa)
        sem = self.sems[key]
        self.streams[q].append(
            lambda eng, out=out, in_=in_, sem=sem: eng.dma_start(out=out, in_=in_).then_inc(sem, 16))
        tok = (key, val)
        self.log.append((q, "dma", tok))
        self.dlast[key] = val
        self._mark(tok, r, w, wa)
        if final:
            self.final.append(tok)
        return tok

    def act(self, out, in_, func, r=(), w=(), wa=(), **kw):
        return self.op("act", lambda e: e.activation(out=out, in_=in_, func=func, **kw), r, w, wa)

    def ts(self, eng, out, in0, s1, s2, op0, op1=None, r=(), w=(), wa=()):
        if op1 is None:
            return self.op(eng, lambda e: e.tensor_scalar(out=out, in0=in0, scalar1=s1, scalar2=None, op0=op0), r, w, wa)
        return self.op(eng, lambda e: e.tensor_scalar(out=out, in0=in0, scalar1=s1, scalar2=s2, op0=op0, op1=op1), r, w, wa)

    def tt(self, eng, out, in0, in1, op, r=(), w=(), wa=()):
        return self.op(eng, lambda e: e.tensor_tensor(out=out, in0=in0, in1=in1, op=op), r, w, wa)

    def stt(self, eng, out, in0, scalar, in1, op0, op1, r=(), w=(), wa=()):
        return self.op(eng, lambda e: e.scalar_tensor_tensor(out=out, in0=in0, scalar=scalar, in1=in1, op0=op0, op1=op1), r, w, wa)

    def copy(self, eng, out, in_, r=(), w=(), wa=()):
        if eng == "act":
            return self.act(out, in_, AF.Identity, r, w, wa)
        return self.op(eng, lambda e: e.tensor_copy(out=out, in_=in_), r, w, wa)

    def memset(self, eng, ap, val, w=(), wa=()):
        return self.op(eng, lambda e: e.memset(ap, val), (), w, wa)

    def recip(self, out, in_, r=(), w=(), wa=()):
        return self.op("dve", lambda e: e.reciprocal(out=out, in_=in_), r, w, wa)

    def scan(self, out, d0, d1, init, op0, op1, r=(), w=(), wa=()):
        return self.op("dve", lambda e: e.tensor_tensor_scan(out=out, data0=d0, data1=d1, initial=init, op0=op0, op1=op1), r, w, wa)

    def transpose(self, out, in_, ident, r=(), w=(), wa=()):
        return self.op("pe", lambda e: e.transpose(out, in_, ident), r, w, wa)

    def mms(self, items, r=(), w=(), wa=()):
        self._deps("pe", r, w, wa)
        n = len(items)
        sem = self.sems["pe"]
        self.cnt["pe"] += 1
        tok = ("pe", self.cnt["pe"])
        for i, (o, l, rh, st, sp) in enumerate(items):
            if i == n - 1:
                self.streams["pe"].append(
                    lambda eng, o=o, l=l, rh=rh, st=st, sp=sp, sem=sem: eng.matmul(o, l, rh, start=st, stop=sp).then_inc(sem, 1))
            else:
                self.streams["pe"].append(
                    lambda eng, o=o, l=l, rh=rh, st=st, sp=sp: eng.matmul(o, l, rh, start=st, stop=sp))
        self._mark(tok, r, w, wa)
        return tok

    def ev(self):
        self.rr += 1
        return "act" if self.rr % 2 else "dve"

    def emit(self, last=False):
        nc = self.nc
        for e in self.ENGS:
            for f in self.ENGS:
                if f != e and self.cnt[f] > 0:
                    self._wait(e, f, self.cnt[f])
            for key, val in self.dlast.items():
                self._wait(e, key, val)
        self.final = []
        streams = self.streams
        self.streams = {e: [] for e in self.ENGS}
        with nc.Block() as block:
            @block.tensor
            def _(eng):
                for f in streams["pe"]:
                    f(eng)

            @block.scalar
            def _(eng):
                for f in streams["act"]:
                    f(eng)

            @block.vector
            def _(eng):
                for f in streams["dve"]:
                    f(eng)

            @block.gpsimd
            def _(eng):
                for f in streams["pool"]:
                    f(eng)

            @block.sync
            def _(eng):
                for f in streams["sp"]:
                    f(eng)


class Cfg:
    def __init__(self, D=4096, SEQ=4096, B=4, PLE=256):
        self.D, self.SEQ, self.B, self.PLE = D, SEQ, B, PLE
        self.EPS = 1e-6
        self.RW = D // 2
        self.NH = self.RW // 64
        self.NHC = self.RW // 128
        self.DL = max(32, int(round(1.8 * self.RW ** 0.5 / 32)) * 32)
        self.AL = max(32, int(round(2.5 * self.RW ** 0.5 / 32)) * 32)
        self.GL = max(32, int(round(0.6 * self.RW ** 0.8 / 32)) * 32)
        self.GN_EPS = 64e-5
        self.RC = 3 * self.RW + self.DL + self.AL + self.GL
        self.DW = D // 2
        self.NDH = self.DW // 128
        self.DCOL = 3 * self.DW
        self.IC = self.RC + self.DCOL + 2 * D
        self.DFF = int(round(8 * D / 3 / 256)) * 256
        self.HALF = SEQ // 2
        self.CTX = SEQ
        self.HALO = 128
        self.OWN0 = self.HALF - self.HALO
        self.NT = min(512, self.HALF)
        self.NOH = self.HALF + self.HALO
        self.o_r, self.o_k, self.o_v = 0, self.RW, 2 * self.RW
        self.o_wl = 3 * self.RW
        self.o_al = self.o_wl + self.DL
        self.o_gl = self.o_al + self.AL
        self.o_q = self.RC
        self.o_dk = self.RC + self.DW
        self.o_dv = self.RC + 2 * self.DW
        self.o_ga = self.RC + self.DCOL
        self.o_gb = self.o_ga + D
        self.lambda_init = 0.8 - 0.6 * math.exp(-0.3 * 0)
        self.colmap = {}
        off = 0
        dc = D // 128
        hc = self.RW // 128
        fc = 2 * self.DFF // 128
        for name, n in [("g_mix", dc), ("g_ffn", dc), ("g_ple", dc),
                        ("mu_r", hc), ("mu_k", hc), ("mu_v", hc), ("mu_w", 1), ("mu_a", 1), ("mu_g", (self.GL + 127) // 128),
                        ("w0", hc), ("a0", hc), ("k_k", hc), ("k_a", hc), ("r_k", hc), ("ln_w", hc), ("ln_b", hc),
                        ("q_g", 1), ("k_g", 1), ("subln", 1),
                        ("cw0", fc), ("cw1", fc), ("cw2", fc), ("cb", fc)]:
            self.colmap[name] = (off, n)
            off += n
        self.NCOLS = off
        self.cm = {"ident": (0, 128), "blk": (128, 128), "ones": (256, 128), "mUs": (384, 64), "mUi": (448, 64),
                   "mLs": (512, 64), "scan": (576, 512), "pvalid": (1088, 128)}
        self.NCONST = 1216

    def tiles_oh(self):
        t = [(self.OWN0, self.HALO)]
        for i in range(self.HALF // self.NT):
            t.append((self.HALF + i * self.NT, self.NT))
        return t


class Ctx:
    pass


def sbt(es, nc, name, shape, dt):
    return Tl(es.enter_context(nc.sbuf_tensor(name, list(shape), dt)), name)


def sbrot(es, nc, name, shape, dt, n):
    return Rot([sbt(es, nc, "%s%d" % (name, i), shape, dt) for i in range(n)])


_PSN = [0]


def psrot(es, nc, n=8):
    _PSN[0] += 1
    return Rot([Tl(es.enter_context(nc.psum_tensor("ps%d_%d" % (_PSN[0], i), [128, 512], F32)), "ps%d" % i) for i in range(n)])


def chunks(total, size):
    return [(i, min(size, total - i)) for i in range(0, total, size)]


def load_cols(K, es, name, names):
    S, C, nc = K.S, K.C, K.nc
    lo = min(C.colmap[x][0] for x in names)
    hi = max(C.colmap[x][0] + C.colmap[x][1] for x in names)
    t = sbt(es, nc, name, [128, hi - lo], F32)
    if hi - lo == 1:
        with nc.allow_non_contiguous_dma(reason="single column"):
            pass
    S.dma("sp", t.h[:, :], K.d["cols"][:, lo:hi], w=[t.b])
    return t, {x: C.colmap[x][0] - lo for x in names}


def load_const(K, es, name, key, rows=128, dt=F32):
    S, C, nc = K.S, K.C, K.nc
    o, n = C.cm[key]
    t = sbt(es, nc, name, [rows, n], dt)
    q = "sp" if dt == F32 else "pool"
    S.dma(q, t.h[:, :], K.d["consts"][0:rows, o:o + n], w=[t.b])
    return t


def build_hT(K, R, src, t0, nt, gt, goff, hT):
    S, C = K.S, K.C
    DC = C.D // 128
    import os
    for s in range(min(nt // 128, int(os.environ.get("KSTOP", "99")))):
        xs = R.xs.get()
        S.dma("sp", xs.h[:, :], src[t0 + s * 128: t0 + (s + 1) * 128, :], w=[xs.b])
        st = R.st.get()
        S.memset("pool", st.h[:, 0:1], 0.0, w=[st.b])
        S.act(R.junk.h[:, :], xs.h[:, :], AF.Square, r=[xs.b], w=[R.junk.b, st.b], accum_out=st.h[:, 0:1])
        S.ts("dve", st.h[:, 1:2], st.h[:, 0:1], 1.0 / C.D, C.EPS, ALU.mult, ALU.add, r=[st.b], w=[st.b])
        S.act(st.h[:, 2:3], st.h[:, 1:2], AF.Ln, r=[st.b], w=[st.b])
        S.act(st.h[:, 3:4], st.h[:, 2:3], AF.Exp, r=[st.b], w=[st.b], scale=-0.5)
        S.ts("dve", xs.h[:, :], xs.h[:, :], st.h[:, 3:4], None, ALU.mult, r=[xs.b, st.b], w=[xs.b])
        import os
        if os.environ.get("SKIPT"):
            continue
        for c0 in range(0, DC, 4):
            ps = K.ps.get()
            n = min(4, DC - c0)
            for j in range(n):
                S.transpose(ps.h[:, j * 128:(j + 1) * 128], xs.h[:, (c0 + j) * 128:(c0 + j + 1) * 128], R.ident.h[:, :],
                            r=[xs.b, R.ident.b], w=[ps.b] if j == 0 else [], wa=[] if j == 0 else [ps.b])
            eng = S.ev()
            for j in range(n):
                c = c0 + j
                o = hT.h[:, c, s * 128:(s + 1) * 128]
                i = ps.h[:, j * 128:(j + 1) * 128]
                g = gt.h[:, goff + c:goff + c + 1]
                if eng == "act":
                    S.act(o, i, AF.Identity, r=[gt.b], w=[ps.b], wa=[hT.b], scale=g)
                else:
                    S.ts("dve", o, i, g, None, ALU.mult, r=[gt.b], w=[ps.b], wa=[hT.b])


def norm_res(K, es, pfx, nxs=3, junk=None):
    R = Ctx()
    nc, C = K.nc, K.C
    R.xs = sbrot(es, nc, pfx + "xs", [128, C.D], F32, nxs)
    R.junk = junk if junk is not None else sbt(es, nc, pfx + "junk", [128, C.D], BF16)
    R.st = sbrot(es, nc, pfx + "st", [128, 4], F32, 4)
    R.ident = load_const(K, es, pfx + "ident", "ident")
    return R


def load_w(K, wt, wap, c0, cn, col0, width, kstep=8):
    S = K.S
    v = wap.rearrange("(c p) n -> p c n", p=128)
    first = True
    for k0 in range(0, cn, kstep):
        kn = min(kstep, cn - k0)
        S.dma("pool", wt.h[:, k0:k0 + kn, 0:width], v[:, c0 + k0:c0 + k0 + kn, col0:col0 + width],
              w=[wt.b] if first else [], wa=[] if first else [wt.b])
        first = False


def z_store(K, col, w, zt, t0, nt):
    S, C = K.S, K.C
    regions = [("zR", 0, C.RC, 0), ("zQ", C.o_q, C.o_dk, C.OWN0), ("zKV", C.o_dk, C.o_ga, 0), ("zG", C.o_ga, C.IC, C.OWN0)]
    for (nm, c0, c1, tk0) in regions:
        a, b = max(col, c0), min(col + w, c1)
        if a >= b:
            continue
        ta = max(t0, tk0)
        if ta >= t0 + nt:
            continue
        S.dma("sp", K.d[nm][a - c0:b - c0, ta - tk0:t0 + nt - tk0], zt.h[a - col:b - col, ta - t0:nt], r=[zt.b], wa=[K.b[nm]])


def phase_A(K):
    S, C, nc = K.S, K.C, K.nc
    DC = C.D // 128
    with contextlib.ExitStack() as es:
        K.ps = psrot(es, nc)
        R = norm_res(K, es, "A")
        gt, gm = load_cols(K, es, "Ag", ["g_mix"])
        hT = sbt(es, nc, "AhT", [128, DC, C.NT], BF16)
        wpool = sbrot(es, nc, "Aw", [128, DC, 512], BF16, 2)
        zpool = sbrot(es, nc, "Az", [128, C.NT], F32, 3)
        ntile = C.CTX // C.NT
        first_full = C.OWN0 // C.NT
        import os
        lvl = int(os.environ.get("KDBG", "9"))
        for tt in range(ntile):
            t0 = tt * C.NT
            if lvl >= 2:
                build_hT(K, R, K.d["xc"], t0, C.NT, gt, gm["g_mix"], hT)
            if lvl < 3:
                continue
            if tt >= first_full:
                ranges = [(0, C.IC)]
            else:
                ranges = [(0, C.RC), (C.o_dk, 2 * C.DW)]
            for (rs, rn) in ranges:
                for (g0, gw) in chunks(rn, 512):
                    col0 = rs + g0
                    wt = wpool.get()
                    load_w(K, wt, K.d["w_in"], 0, DC, col0, gw)
                    for (j0, wj) in chunks(gw, 128):
                        ps = K.ps.get()
                        items = [(ps.h[0:wj, 0:C.NT], wt.h[:, k, j0:j0 + wj], hT.h[:, k, 0:C.NT], k == 0, k == DC - 1)
                                 for k in range(DC)]
                        S.mms(items, r=[wt.b, hT.b], w=[ps.b])
                        zt = zpool.get()
                        S.copy(S.ev(), zt.h[0:wj, :], ps.h[0:wj, 0:C.NT], w=[zt.b, ps.b])
                        z_store(K, col0 + j0, wj, zt, t0, C.NT)
        S.emit()


def phase_B(K):
    S, C, nc = K.S, K.C, K.nc
    NHC = C.NHC
    TB = min(256, C.HALF)
    NCH = TB // 64
    HG = min(4, NHC)
    NG = NHC // HG
    GW = HG * 128
    GLC = (C.GL + 127) // 128
    zT = K.d["zR"]
    with contextlib.ExitStack() as es:
        K.ps = psrot(es, nc)
        ident = load_const(K, es, "Bident", "ident")
        blk = load_const(K, es, "Bblk", "blk")
        mUs = load_const(K, es, "BmUs", "mUs", rows=64)
        mUi = load_const(K, es, "BmUi", "mUi", rows=64)
        mLs = load_const(K, es, "BmLs", "mLs", rows=64)
        scanm = load_const(K, es, "Bscan", "scan")
        identb = load_const(K, es, "Bidentb", "ident", dt=BF16)
        names = ["mu_r", "mu_k", "mu_v", "mu_w", "mu_a", "mu_g", "w0", "a0", "k_k", "k_a", "r_k", "ln_w", "ln_b"]
        ct, co = load_cols(K, es, "Bcols", names)
        nmu = 3 * NHC + 2 + GLC
        omu = sbt(es, nc, "Bomu", [128, nmu], F32)
        S.ts("dve", omu.h[:, :], ct.h[:, 0:nmu], -1.0, 1.0, ALU.mult, ALU.add, r=[ct.b], w=[omu.b])
        nw0 = sbt(es, nc, "Bnw0", [128, NHC], F32)
        S.ts("dve", nw0.h[:, :], ct.h[:, co["w0"]:co["w0"] + NHC], -1.0, None, ALU.mult, r=[ct.b], w=[nw0.b])
        omka = sbt(es, nc, "Bomka", [128, NHC], F32)
        S.ts("dve", omka.h[:, :], ct.h[:, co["k_a"]:co["k_a"] + NHC], -1.0, 1.0, ALU.mult, ALU.add, r=[ct.b], w=[omka.b])
        cm05 = sbt(es, nc, "Bcm05", [128, 1], F32)
        S.memset("pool", cm05.h[:, :], -0.5, w=[cm05.b])
        c1 = sbt(es, nc, "Bc1", [128, 1], F32)
        S.memset("pool", c1.h[:, :], 1.0, w=[c1.b])
        lw = sbrot(es, nc, "Blw", [128, 2 + GLC, 128], F32, 2)
        wl = sbt(es, nc, "Bwl", [128, TB + 1], F32)
        al = sbt(es, nc, "Bal", [128, TB + 1], F32)
        gl = sbt(es, nc, "Bgl", [128, GLC, TB + 1], F32)
        tw = sbt(es, nc, "Btw", [128, TB], F32)
        als = sbt(es, nc, "Bals", [128, TB], F32)
        sg = sbt(es, nc, "Bsg", [128, GLC, TB], F32)
        tp = sbrot(es, nc, "Btp", [128, TB + 1], F32, 28)
        Rt, KKt, Bt, Kt, Vt, Bct, Kct = [sbt(es, nc, "B" + n, [128, NHC, TB], BF16) for n in ("Rt", "KKt", "Bt", "Kt", "Vt", "Bct", "Kct")]
        gC = sbt(es, nc, "BgC", [128, NHC, NCH], F32)
        OT = sbt(es, nc, "BOT", [128, NHC, TB], F32)
        H = sbt(es, nc, "BH", [128, NHC, 128], F32)
        Hb = sbt(es, nc, "BHb", [128, NHC, 128], BF16)
        bdp = [[sbt(es, nc, "Bbd%d_%d" % (i, j), [128, NHC, 128], BF16) for j in range(3)] for i in range(1)]
        for i in range(1):
            for j in range(3):
                S.memset("pool", bdp[i][j].h[:, :, :], 0.0, w=[bdp[i][j].b])
        Hbufs = [Buf("H%d" % g) for g in range(NG)]
        S.memset("pool", H.h[:, :, :], 0.0, w=[H.b] + Hbufs)
        S.memset("pool", Hb.h[:, :, :], 0.0, w=[Hb.b], wa=Hbufs)
        SLN = ("A0", "A1", "B0", "B1", "X0", "X1", "nM3", "M2", "M4", "Vm", "Bcm", "Kcm")
        slots = [{n: sbt(es, nc, "Bs%d%s" % (g, n), [64, GW], BF16) for n in SLN} for g in range(NG)]
        fp = sbrot(es, nc, "Bfp", [64, GW], F32, 2)
        oabp = sbrot(es, nc, "Boab", [128, TB], BF16, 2)

        def bc64(t):
            return t.h[0:64, 0:64].unsqueeze(1).broadcast_to([64, 2 * HG, 64])

        def v3(ap):
            return ap.rearrange("p (h t) -> p h t", t=64)

        def load_shift(dst_ap_fn, rows, row0, t0, buf, q="sp"):
            if t0 == 0:
                S.memset("pool", dst_ap_fn(0, 1), 0.0, w=[buf])
                S.dma(q, dst_ap_fn(1, TB + 1), zT[row0:row0 + rows, 0:TB], r=[K.b["zR"]], wa=[buf])
            else:
                S.dma(q, dst_ap_fn(0, TB + 1), zT[row0:row0 + rows, t0 - 1:t0 + TB], r=[K.b["zR"]], w=[buf])

        def lerp(eng, out_ap, zt_prev, zt_cur, mu_ap, omu_ap, rbufs, wbuf):
            t = tp.get()
            n = out_ap.shape[0]
            S.ts(eng, t.h[0:n, 0:TB], zt_prev, mu_ap, None, ALU.mult, r=rbufs, w=[t.b])
            S.stt("dve", out_ap, zt_cur, omu_ap, t.h[0:n, 0:TB], ALU.mult, ALU.add, r=rbufs + [t.b], w=[wbuf])

        ntile = C.CTX // TB
        for tt in range(ntile):
            t0 = tt * TB
            need3 = (t0 + TB > C.OWN0)
            lo = max(C.OWN0 - t0, 0)
            load_shift(lambda a, b: wl.h[0:C.DL, a:b], C.DL, C.o_wl, t0, wl.b)
            load_shift(lambda a, b: al.h[0:C.AL, a:b], C.AL, C.o_al, t0, al.b)
            for c in range(GLC):
                n = min(128, C.GL - c * 128)
                load_shift(lambda a, b, c=c, n=n: gl.h[0:n, c, a:b], n, C.o_gl + c * 128, t0, gl.b)
            cw, ca, cg = co["mu_w"], co["mu_a"], co["mu_g"]
            lerp("dve", tw.h[0:C.DL, :], wl.h[0:C.DL, 0:TB], wl.h[0:C.DL, 1:TB + 1], ct.h[0:C.DL, cw:cw + 1], omu.h[0:C.DL, cw:cw + 1], [wl.b, ct.b, omu.b], tw.b)
            S.act(tw.h[0:C.DL, :], tw.h[0:C.DL, :], AF.Tanh, w=[tw.b])
            lerp("dve", als.h[0:C.AL, :], al.h[0:C.AL, 0:TB], al.h[0:C.AL, 1:TB + 1], ct.h[0:C.AL, ca:ca + 1], omu.h[0:C.AL, ca:ca + 1], [al.b, ct.b, omu.b], als.b)
            for c in range(GLC):
                n = min(128, C.GL - c * 128)
                lerp("dve", sg.h[0:n, c, :], gl.h[0:n, c, 0:TB], gl.h[0:n, c, 1:TB + 1], ct.h[0:n, cg + c:cg + c + 1], omu.h[0:n, cg + c:cg + c + 1], [gl.b, ct.b, omu.b], sg.b)
                S.act(sg.h[0:n, c, :], sg.h[0:n, c, :], AF.Sigmoid, w=[sg.b])
            for hc in range(NHC):
                cs = slice(hc * 128, (hc + 1) * 128)
                col = lambda nm: ct.h[:, co[nm] + hc:co[nm] + hc + 1]
                zs = []
                for (o_, nm) in ((C.o_r, "mu_r"), (C.o_k, "mu_k"), (C.o_v, "mu_v")):
                    zt_ = tp.get()
                    load_shift(lambda a, b, zt_=zt_: zt_.h[:, a:b], 128, o_ + hc * 128, t0, zt_.b)
                    out = tp.get()
                    mo = co[nm] + hc
                    lerp("pool", out.h[:, 0:TB], zt_.h[:, 0:TB], zt_.h[:, 1:TB + 1], ct.h[:, mo:mo + 1], omu.h[:, mo:mo + 1], [zt_.b, ct.b, omu.b], out.b)
                    zs.append(out)
                r_s, k_s, v_s = zs
                X = slice(0, TB)
                lwt = lw.get()
                S.dma("sp", lwt.h[0:C.DL, 0, :], K.d["w2"][:, cs], w=[lwt.b])
                S.dma("sp", lwt.h[0:C.AL, 1, :], K.d["a2"][:, cs], wa=[lwt.b])
                for c in range(GLC):
                    n = min(128, C.GL - c * 128)
                    S.dma("sp", lwt.h[0:n, 2 + c, :], K.d["g2"][c * 128:c * 128 + n, cs], wa=[lwt.b])
                ps = K.ps.get()
                S.mms([(ps.h[:, 0:TB], lwt.h[0:C.DL, 0, :], tw.h[0:C.DL, :], True, True)], r=[lwt.b, tw.b], w=[ps.b])
                e1 = tp.get()
                S.act(e1.h[:, X], ps.h[:, 0:TB], AF.Exp, r=[nw0.b], w=[e1.b, ps.b], scale=-1.0, bias=nw0.h[:, hc:hc + 1])
                S.act(e1.h[:, X], e1.h[:, X], AF.Ln, r=[c1.b], w=[e1.b], bias=c1.h[:, 0:1])
                elw = tp.get()
                S.act(elw.h[:, X], e1.h[:, X], AF.Exp, r=[e1.b, cm05.b], w=[elw.b], scale=-1.0, bias=cm05.h[:, 0:1])
                ps = K.ps.get()
                S.mms([(ps.h[:, 0:TB], lwt.h[0:C.AL, 1, :], als.h[0:C.AL, :], True, True)], r=[lwt.b, als.b], w=[ps.b])
                a_ = tp.get()
                S.act(a_.h[:, X], ps.h[:, 0:TB], AF.Sigmoid, r=[ct.b], w=[a_.b, ps.b], bias=col("a0"))
                if need3:
                    ps = K.ps.get()
                    items = []
                    for c in range(GLC):
                        n = min(128, C.GL - c * 128)
                        items.append((ps.h[:, 0:TB], lwt.h[0:n, 2 + c, :], sg.h[0:n, c, :], c == 0, c == GLC - 1))
                    S.mms(items, r=[lwt.b, sg.b], w=[ps.b])
                    gt_ = tp.get()
                    S.copy("act", gt_.h[:, X], ps.h[:, 0:TB], w=[gt_.b, ps.b])
                    S.dma("sp", K.d["gT"][cs, t0:t0 + TB], gt_.h[:, X], r=[gt_.b], wa=[K.b["gT"]])
                kk = tp.get()
                S.ts("pool", kk.h[:, X], k_s.h[:, X], col("k_k"), None, ALU.mult, r=[k_s.b, ct.b], w=[kk.b])
                kk2 = tp.get()
                S.tt("pool", kk2.h[:, X], kk.h[:, X], kk.h[:, X], ALU.mult, r=[kk.b], w=[kk2.b])
                ps = K.ps.get()
                S.mms([(ps.h[:, 0:TB], blk.h[:, :], kk2.h[:, X], True, True)], r=[blk.b, kk2.b], w=[ps.b])
                rn = tp.get()
                S.ts("dve", rn.h[:, X], ps.h[:, 0:TB], 1e-24, None, ALU.max, w=[rn.b, ps.b])
                S.act(rn.h[:, X], rn.h[:, X], AF.Ln, w=[rn.b])
                S.act(rn.h[:, X], rn.h[:, X], AF.Exp, w=[rn.b], scale=-0.5)
                kkn = tp.get()
                S.tt("dve", kkn.h[:, X], kk.h[:, X], rn.h[:, X], ALU.mult, r=[kk.b, rn.b], w=[kkn.b])
                t1 = tp.get()
                S.ts("dve", t1.h[:, X], a_.h[:, X], col("k_a"), omka.h[:, hc:hc + 1], ALU.mult, ALU.add, r=[a_.b, ct.b, omka.b], w=[t1.b])
                kmod = tp.get()
                S.tt("pool", kmod.h[:, X], k_s.h[:, X], t1.h[:, X], ALU.mult, r=[k_s.b, t1.b], w=[kmod.b])
                b_ = tp.get()
                S.tt("pool", b_.h[:, X], kkn.h[:, X], a_.h[:, X], ALU.mult, r=[kkn.b, a_.b], w=[b_.b])
                if need3:
                    rkr = tp.get()
                    S.stt("dve", rkr.h[:, X], r_s.h[:, X], col("r_k"), kmod.h[:, X], ALU.mult, ALU.mult, r=[r_s.b, ct.b, kmod.b], w=[rkr.b])
                    ps = K.ps.get()
                    S.mms([(ps.h[:, 0:TB], blk.h[:, :], rkr.h[:, X], True, True)], r=[blk.b, rkr.b], w=[ps.b])
                    bon = tp.get()
                    S.tt("dve", bon.h[:, X], ps.h[:, 0:TB], v_s.h[:, X], ALU.mult, r=[v_s.b], w=[bon.b, ps.b])
                    S.dma("sp", K.d["bonT"][cs, t0:t0 + TB], bon.h[:, X], r=[bon.b], wa=[K.b["bonT"]])
                cum = tp.get()
                S.scan(cum.h[:, X], scanm.h[:, 0:TB], elw.h[:, X], 0.0, ALU.mult, ALU.add, r=[scanm.b, elw.b], w=[cum.b])
                gi = tp.get()
                S.act(gi.h[:, X], cum.h[:, X], AF.Exp, r=[cum.b], w=[gi.b], scale=-1.0)
                ge = tp.get()
                S.act(ge.h[:, X], cum.h[:, X], AF.Exp, r=[cum.b], w=[ge.b])
                gx = tp.get()
                S.tt("pool", gx.h[:, X], cum.h[:, X], elw.h[:, X], ALU.subtract, r=[cum.b, elw.b], w=[gx.b])
                S.act(gx.h[:, X], gx.h[:, X], AF.Exp, w=[gx.b], scale=-1.0)
                S.tt("dve", Rt.h[:, hc, :], r_s.h[:, X], gi.h[:, X], ALU.mult, r=[r_s.b, gi.b], wa=[Rt.b])
                S.tt("pool", KKt.h[:, hc, :], kkn.h[:, X], gx.h[:, X], ALU.mult, r=[kkn.b, gx.b], wa=[KKt.b])
                tb_ = tp.get()
                S.tt("dve", tb_.h[:, X], b_.h[:, X], ge.h[:, X], ALU.mult, r=[b_.b, ge.b], w=[tb_.b])
                tk_ = tp.get()
                S.tt("pool", tk_.h[:, X], kmod.h[:, X], ge.h[:, X], ALU.mult, r=[kmod.b, ge.b], w=[tk_.b])
                S.copy("act", Bt.h[:, hc, :], tb_.h[:, X], r=[tb_.b], wa=[Bt.b])
                S.copy("act", Kt.h[:, hc, :], tk_.h[:, X], r=[tk_.b], wa=[Kt.b])
                S.copy("pool", Vt.h[:, hc, :], v_s.h[:, X], r=[v_s.b], wa=[Vt.b])
                S.copy("dve", gC.h[:, hc, :], gi.h[:, X].rearrange("p (c t) -> p c t", t=64)[:, :, 63], r=[gi.b], wa=[gC.b])
                gcb = gC.h[:, hc, :].unsqueeze(2).broadcast_to([128, NCH, 64])
                S.stt("dve", Bct.h[:, hc, :].rearrange("p (c t) -> p c t", t=64), tb_.h[:, X].rearrange("p (c t) -> p c t", t=64), -1.0, gcb,
                      ALU.mult, ALU.mult, r=[tb_.b, gC.b], wa=[Bct.b])
                S.tt("pool", Kct.h[:, hc, :].rearrange("p (c t) -> p c t", t=64), tk_.h[:, X].rearrange("p (c t) -> p c t", t=64), gcb,
                     ALU.mult, r=[tk_.b, gC.b], wa=[Kct.b])
            import os
            for ci in range(NCH if not os.environ.get("BSKIP2") else 0):
                cc = slice(ci * 64, (ci + 1) * 64)
                hd = lambda g, hh: (g * HG + hh // 2, (hh % 2) * 64)
                NHG = 2 * HG
                st = [dict() for _ in range(NG)]
                bd = bdp[0]
                for j, src in enumerate((KKt, Rt, Bt)):
                    eng = ("pool", "act", "pool")[j]
                    S.copy(eng, bd[j].h[0:64, :, 0:64], src.h[0:64, :, cc], r=[src.b], w=[bd[j].b])
                    S.copy(eng, bd[j].h[64:128, :, 64:128], src.h[64:128, :, cc], r=[src.b], wa=[bd[j].b])
                KKbd, Rbd, Bbd = bd
                for g in range(NG):
                    d = st[g]
                    def prod(lT, rbd):
                        ps = K.ps.get()
                        items = []
                        for hl in range(HG):
                            hc = g * HG + hl
                            items.append((ps.h[0:64, hl * 128:(hl + 1) * 128], lT.h[:, hc, cc], rbd.h[:, hc, :], True, True))
                        S.mms(items, r=[lT.b, rbd.b], w=[ps.b])
                        return ps
                    ps = prod(Bt, KKbd)
                    sl_ = slots[g]
                    d["A"] = sl_["A0"]
                    S.tt("dve", v3(d["A"].h[:, :]), v3(ps.h[0:64, 0:GW]), bc64(mUs), ALU.mult, r=[mUs.b], w=[d["A"].b, ps.b])
                    bprep = int(os.environ.get("BPREP", "99"))
                    if bprep <= 1:
                        continue
                    d["X"] = sl_["X0"]
                    S.tt("pool", v3(d["X"].h[:, :]), bc64(ident), v3(d["A"].h[:, :]), ALU.subtract, r=[ident.b, d["A"].b], w=[d["X"].b])
                    if bprep <= 2:
                        continue
                    ps = prod(KKt, Bbd)
                    d["Bq"] = sl_["B0"]
                    S.tt("dve", v3(d["Bq"].h[:, :]), v3(ps.h[0:64, 0:GW]), bc64(mLs), ALU.mult, r=[mLs.b], w=[d["Bq"].b, ps.b])
                    ps = prod(Bt, Rbd)
                    d["nM3"] = sl_["nM3"]
                    S.stt("dve", v3(d["nM3"].h[:, :]), v3(ps.h[0:64, 0:GW]), -1.0, bc64(mUi), ALU.mult, ALU.mult, r=[mUi.b], w=[d["nM3"].b, ps.b])
                    ps = prod(Kt, KKbd)
                    d["M2"] = sl_["M2"]
                    S.tt("dve", v3(d["M2"].h[:, :]), v3(ps.h[0:64, 0:GW]), bc64(mUs), ALU.mult, r=[mUs.b], w=[d["M2"].b, ps.b])
                    ps = prod(Kt, Rbd)
                    d["M4"] = sl_["M4"]
                    S.tt("dve", v3(d["M4"].h[:, :]), v3(ps.h[0:64, 0:GW]), bc64(mUi), ALU.mult, r=[mUi.b], w=[d["M4"].b, ps.b])
                    if bprep <= 3:
                        continue
                    for nm, src in (("Vm", Vt), ("Bcm", Bct), ("Kcm", Kct)):
                        ps = K.ps.get()
                        items = []
                        for hl in range(HG):
                            hc = g * HG + hl
                            items.append((ps.h[0:64, hl * 128:(hl + 1) * 128], src.h[:, hc, cc], identb.h[:, :], True, True))
                        S.mms(items, r=[src.b, identb.b], w=[ps.b])
                        d[nm] = sl_[nm]
                        S.copy("act", d[nm].h[:, :], ps.h[0:64, 0:GW], w=[d[nm].b, ps.b])
                bstop = int(os.environ.get("BSTOP", "99"))
                if bstop <= 1:
                    continue
                for lvl in range(1, 6):
                    for g in range(NG):
                        d = st[g]
                        def hmm(l, r_):
                            ps = K.ps.get()
                            items = [(ps.h[0:64, hh * 64:(hh + 1) * 64], l.h[0:64, hh * 64:(hh + 1) * 64], r_.h[0:64, hh * 64:(hh + 1) * 64], True, True)
                                     for hh in range(NHG)]
                            S.mms(items, r=[l.b, r_.b], w=[ps.b])
                            return ps
                        psB = hmm(d["A"], d["Bq"])
                        nB = slots[g]["B%d" % (lvl % 2)]
                        S.copy("act", nB.h[:, :], psB.h[0:64, 0:GW], w=[nB.b, psB.b])
                        if lvl < 5:
                            psA = hmm(d["Bq"], d["A"])
                            nA = slots[g]["A%d" % (lvl % 2)]
                            S.copy("dve", nA.h[:, :], psA.h[0:64, 0:GW], w=[nA.b, psA.b])
                            d["A"] = nA
                        d["Bq"] = nB
                    for g in range(NG):
                        d = st[g]
                        ps = K.ps.get()
                        items = [(ps.h[0:64, hh * 64:(hh + 1) * 64], d["Bq"].h[0:64, hh * 64:(hh + 1) * 64], d["X"].h[0:64, hh * 64:(hh + 1) * 64], True, True)
                                 for hh in range(NHG)]
                        S.mms(items, r=[d["Bq"].b, d["X"].b], w=[ps.b])
                        nX = slots[g]["X%d" % (lvl % 2)]
                        S.tt("dve", nX.h[:, :], ps.h[0:64, 0:GW], d["X"].h[:, :], ALU.add, r=[d["X"].b], w=[nX.b, ps.b])
                        d["X"] = nX
                if bstop <= 2:
                    continue
                for g in range(NG):
                    d = st[g]
                    ps = K.ps.get()
                    items = []
                    for hl in range(HG):
                        hc = g * HG + hl
                        items.append((ps.h[0:64, hl * 128:(hl + 1) * 128], KKt.h[:, hc, cc], Hb.h[:, hc, :], True, False))
                        for e2 in range(2):
                            sl = slice(hl * 128 + e2 * 64, hl * 128 + e2 * 64 + 64)
                            items.append((ps.h[0:64, sl], d["M2"].h[0:64, sl], d["Vm"].h[0:64, sl], False, e2 == 1))
                    S.mms(items, r=[KKt.b, Hbufs[g], d["M2"].b, d["Vm"].b], w=[ps.b])
                    d["W"] = slots[g]["A1"]
                    S.copy("act", d["W"].h[:, :], ps.h[0:64, 0:GW], w=[d["W"].b, ps.b])
                for g in range(NG):
                    d = st[g]
                    ps = K.ps.get()
                    items = [(ps.h[0:64, hh * 64:(hh + 1) * 64], d["X"].h[0:64, hh * 64:(hh + 1) * 64], d["W"].h[0:64, hh * 64:(hh + 1) * 64], True, True)
                             for hh in range(NHG)]
                    S.mms(items, r=[d["X"].b, d["W"].b], w=[ps.b])
                    d["U"] = slots[g]["B0"]
                    S.copy("dve", d["U"].h[:, :], ps.h[0:64, 0:GW], w=[d["U"].b, ps.b])
                if bstop <= 3:
                    continue
                for g in range(NG):
                    d = st[g]
                    ps = K.ps.get()
                    items = []
                    for hl in range(HG):
                        hc = g * HG + hl
                        items.append((ps.h[0:64, hl * 128:(hl + 1) * 128], Rt.h[:, hc, cc], Hb.h[:, hc, :], True, False))
                        for e2 in range(2):
                            sl = slice(hl * 128 + e2 * 64, hl * 128 + e2 * 64 + 64)
                            items.append((ps.h[0:64, sl], d["nM3"].h[0:64, sl], d["U"].h[0:64, sl], False, False))
                            items.append((ps.h[0:64, sl], d["M4"].h[0:64, sl], d["Vm"].h[0:64, sl], False, e2 == 1))
                    S.mms(items, r=[Rt.b, Hbufs[g], d["nM3"].b, d["U"].b, d["M4"].b, d["Vm"].b], w=[ps.b])
                    if need3:
                        otm = fp.get()
                        S.copy("act", otm.h[:, :], ps.h[0:64, 0:GW], w=[otm.b, ps.b])
                        ps2 = K.ps.get()
                        for hl in range(HG):
                            S.transpose(ps2.h[:, hl * 64:(hl + 1) * 64], otm.h[0:64, hl * 128:(hl + 1) * 128], ident.h[0:64, 0:64],
                                        r=[otm.b, ident.b], w=[ps2.b] if hl == 0 else [], wa=[] if hl == 0 else [ps2.b])
                        S.copy("dve", OT.h[:, g * HG:(g + 1) * HG, cc], ps2.h[:, 0:HG * 64].rearrange("p (h t) -> p h t", t=64), w=[ps2.b], wa=[OT.b])
                    ps = K.ps.get()
                    items = []
                    for hl in range(HG):
                        sl = slice(hl * 128, (hl + 1) * 128)
                        o = ps.h[:, sl]
                        items.append((o, d["Bcm"].h[0:64, sl], d["U"].h[0:64, sl], True, False))
                        items.append((o, d["Kcm"].h[0:64, sl], d["Vm"].h[0:64, sl], False, True))
                    S.mms(items, r=[d["Bcm"].b, d["U"].b, d["Kcm"].b, d["Vm"].b], w=[ps.b])
                    for hh in range(NHG):
                        hc, pb = hd(g, hh)
                        hl = hh // 2
                        S.stt("dve", H.h[pb:pb + 64, hc, pb:pb + 64], H.h[pb:pb + 64, hc, pb:pb + 64], gC.h[pb:pb + 64, hc, ci:ci + 1],
                              ps.h[pb:pb + 64, hl * 128 + pb: hl * 128 + pb + 64], ALU.mult, ALU.add,
                              r=[gC.b], w=[Hbufs[g], ps.b])
                    S.copy("pool", Hb.h[:, g * HG:(g + 1) * HG, :], H.h[:, g * HG:(g + 1) * HG, :], w=[Hbufs[g]])
            if need3 and not os.environ.get("BSKIP3"):
                for hc in range(NHC):
                    cs = slice(hc * 128, (hc + 1) * 128)
                    col = lambda nm: ct.h[:, co[nm] + hc:co[nm] + hc + 1]
                    X = slice(0, TB)
                    gt_ = tp.get()
                    S.dma("sp", gt_.h[:, X], K.d["gT"][cs, t0:t0 + TB], r=[K.b["gT"]], w=[gt_.b])
                    bon = tp.get()
                    S.dma("sp", bon.h[:, X], K.d["bonT"][cs, t0:t0 + TB], r=[K.b["bonT"]], w=[bon.b])
                    ps = K.ps.get()
                    S.mms([(ps.h[:, 0:TB], blk.h[:, :], OT.h[:, hc, :], True, True)], r=[blk.b, OT.b], w=[ps.b])
                    cen = tp.get()
                    S.stt("dve", cen.h[:, X], ps.h[:, 0:TB], -1.0 / 64, OT.h[:, hc, :], ALU.mult, ALU.add, r=[OT.b], w=[cen.b, ps.b])
                    sq = tp.get()
                    S.act(sq.h[:, X], cen.h[:, X], AF.Square, r=[cen.b], w=[sq.b])
                    ps = K.ps.get()
                    S.mms([(ps.h[:, 0:TB], blk.h[:, :], sq.h[:, X], True, True)], r=[blk.b, sq.b], w=[ps.b])
                    rs = tp.get()
                    S.ts("dve", rs.h[:, X], ps.h[:, 0:TB], 1.0 / 64, C.GN_EPS, ALU.mult, ALU.add, w=[rs.b, ps.b])
                    S.act(rs.h[:, X], rs.h[:, X], AF.Ln, w=[rs.b])
                    S.act(rs.h[:, X], rs.h[:, X], AF.Exp, w=[rs.b], scale=-0.5)
                    y = tp.get()
                    S.tt("pool", y.h[:, X], cen.h[:, X], rs.h[:, X], ALU.mult, r=[cen.b, rs.b], w=[y.b])
                    S.ts("pool", y.h[:, X], y.h[:, X], col("ln_w"), col("ln_b"), ALU.mult, ALU.add, r=[ct.b], w=[y.b])
                    S.tt("pool", y.h[:, X], y.h[:, X], bon.h[:, X], ALU.add, r=[bon.b], w=[y.b])
                    S.tt("dve", y.h[:, X], y.h[:, X], gt_.h[:, X], ALU.mult, r=[gt_.b], w=[y.b])
                    obf = oabp.get()
                    S.copy("act", obf.h[:, :], y.h[:, X], r=[y.b], w=[obf.b])
                    S.dma("sp", K.d["oaT"][cs, t0 + lo - C.OWN0:t0 + TB - C.OWN0], obf.h[:, lo:TB], r=[obf.b], wa=[K.b["oaT"]])
        S.emit()


def phase_C(K):
    S, C, nc = K.S, K.C, K.nc
    NKB = C.CTX // 128
    with contextlib.ExitStack() as es:
        allps = psrot(es, nc)
        K.ps = Rot(allps.t[0:4])
        acc = Rot(allps.t[4:8])
        ident = load_const(K, es, "Cident", "ident")
        blk = load_const(K, es, "Cblk", "blk")
        ones = load_const(K, es, "Cones", "ones")
        onesb = load_const(K, es, "Conesb", "ones", dt=BF16)
        pvb = load_const(K, es, "Cpvb", "pvalid", dt=BF16)
        ct, co = load_cols(K, es, "Ccols", ["q_g", "k_g", "subln"])
        lt = sbt(es, nc, "Clamb", [128, 256], F32)
        S.dma("sp", lt.h[:, :], K.d["lamb"][:, :], w=[lt.b])
        sc = sbt(es, nc, "Csc", [128, 12], F32)
        S.memset("pool", sc.h[:, :], 0.0, w=[sc.b])
        tmp = sbt(es, nc, "Cltmp", [128, 128], F32)
        S.tt("dve", tmp.h[:, 0:64], lt.h[:, 0:64], lt.h[:, 64:128], ALU.mult, r=[lt.b], w=[tmp.b])
        S.tt("dve", tmp.h[:, 64:128], lt.h[:, 128:192], lt.h[:, 192:256], ALU.mult, r=[lt.b], w=[tmp.b])
        S.act(tmp.h[:, 0:64], tmp.h[:, 0:64], AF.Identity, w=[tmp.b, sc.b], accum_out=sc.h[:, 0:1])
        S.act(tmp.h[:, 64:128], tmp.h[:, 64:128], AF.Identity, w=[tmp.b, sc.b], accum_out=sc.h[:, 1:2])
        S.act(sc.h[:, 2:4], sc.h[:, 0:2], AF.Exp, w=[sc.b])
        S.tt("dve", sc.h[:, 4:5], sc.h[:, 3:4], sc.h[:, 2:3], ALU.subtract, w=[sc.b])
        S.ts("dve", sc.h[:, 5:6], sc.h[:, 4:5], -C.lambda_init, None, ALU.add, w=[sc.b])
        neglam = sc.h[:, 5:6]
        S.ts("dve", sc.h[:, 6:7], ct.h[:, co["q_g"]:co["q_g"] + 1], 0.125, None, ALU.mult, r=[ct.b], w=[sc.b])
        S.ts("dve", sc.h[:, 7:8], ct.h[:, co["subln"]:co["subln"] + 1], 1.0 - C.lambda_init, None, ALU.mult, r=[ct.b], w=[sc.b])
        qg8, slg = sc.h[:, 6:7], sc.h[:, 7:8]
        kg = ct.h[:, co["k_g"]:co["k_g"] + 1]
        zq = sbrot(es, nc, "Czq", [128, C.NOH], F32, 2)
        zk = sbrot(es, nc, "Czk", [128, C.CTX], F32, 2)
        zv = sbrot(es, nc, "Czv", [128, C.CTX], F32, 2)
        qn = sbt(es, nc, "Cqn", [128, C.NOH], BF16)
        kn = [sbt(es, nc, "Ckn%d" % c, [128, C.CTX], BF16) for c in range(2)]
        S.memset("pool", kn[0].h[:, :], 0.0, w=[kn[0].b])
        S.memset("pool", kn[1].h[:, :], 0.0, w=[kn[1].b])
        Vtm = sbt(es, nc, "CVtm", [128, NKB, 128], BF16)
        tq = sbrot(es, nc, "Ctq", [128, 512], F32, 6)
        ptp = sbrot(es, nc, "Cpt", [128, 512], BF16, 4)
        ocp = sbrot(es, nc, "Coc", [128, 512], F32, 4)
        obp = sbrot(es, nc, "Cob", [128, 512], BF16, 2)

        def rmsn(src, n_tot, gcol, outs):
            for (c0, cn) in chunks(n_tot, 512):
                sq = tq.get()
                S.act(sq.h[:, 0:cn], src.h[:, c0:c0 + cn], AF.Square, r=[src.b], w=[sq.b])
                ps = K.ps.get()
                S.mms([(ps.h[:, 0:cn], blk.h[:, :], sq.h[:, 0:cn], True, True)], r=[blk.b, sq.b], w=[ps.b])
                rs = tq.get()
                S.ts("dve", rs.h[:, 0:cn], ps.h[:, 0:cn], 1.0 / 64, C.EPS, ALU.mult, ALU.add, w=[rs.b, ps.b])
                S.act(rs.h[:, 0:cn], rs.h[:, 0:cn], AF.Ln, w=[rs.b])
                S.act(rs.h[:, 0:cn], rs.h[:, 0:cn], AF.Exp, w=[rs.b], scale=-0.5)
                for (r0, r1, dst) in outs:
                    S.stt("dve", dst.h[r0:r1, c0:c0 + cn], src.h[r0:r1, c0:c0 + cn], gcol[r0:r1, :], rs.h[r0:r1, 0:cn], ALU.mult, ALU.mult,
                          r=[src.b, rs.b, sc.b, ct.b], wa=[dst.b])

        for h in range(C.NDH):
            rows = slice(h * 128, (h + 1) * 128)
            q_ = zq.get()
            S.dma("sp", q_.h[:, :], K.d["zQ"][h * 128:(h + 1) * 128, 0:C.NOH], r=[K.b["zQ"]], w=[q_.b])
            k_ = zk.get()
            S.dma("sp", k_.h[:, :], K.d["zKV"][h * 128:(h + 1) * 128, 0:C.CTX], r=[K.b["zKV"]], w=[k_.b])
            v_ = zv.get()
            S.dma("sp", v_.h[:, :], K.d["zKV"][C.DW + h * 128:C.DW + (h + 1) * 128, 0:C.CTX], r=[K.b["zKV"]], w=[v_.b])
            S.memset("pool", qn.h[:, 0:1], 0.0, w=[qn.b])
            S.memset("pool", kn[0].h[0:64, 0:1], 0.0, w=[kn[0].b])
            S.memset("pool", kn[1].h[64:128, 0:1], 0.0, w=[kn[1].b])
            rmsn(q_, C.NOH, qg8, [(0, 128, qn)])
            rmsn(k_, C.CTX, kg, [(0, 64, kn[0]), (64, 128, kn[1])])
            first = True
            for kb0 in range(0, NKB, 4):
                ps = K.ps.get()
                n = min(4, NKB - kb0)
                for j in range(n):
                    S.transpose(ps.h[:, j * 128:(j + 1) * 128], v_.h[:, (kb0 + j) * 128:(kb0 + j + 1) * 128], ident.h[:, :],
                                r=[v_.b, ident.b], w=[ps.b] if j == 0 else [], wa=[] if j == 0 else [ps.b])
                S.copy(S.ev(), Vtm.h[:, kb0:kb0 + n, :], ps.h[:, 0:n * 128].rearrange("p (k e) -> p k e", e=128),
                       w=[ps.b] + ([Vtm.b] if first else []), wa=[] if first else [Vtm.b])
                first = False
            for (q0, NQ) in C.tiles_oh():
                jq = q0 - C.OWN0
                nkb = (q0 + NQ) // 128
                ocs = []
                for c in range(2):
                    psO = acc.get()
                    psL = acc.get()

                    def smm(kb):
                        j0 = max(kb * 128 - q0, 0)
                        n = NQ - j0
                        ps = K.ps.get()
                        S.mms([(ps.h[:, 0:n], kn[c].h[:, kb * 128:(kb + 1) * 128], qn.h[:, jq + j0:jq + NQ], True, True)],
                              r=[kn[c].b, qn.b], w=[ps.b])
                        return ps, j0, n
                    nxt = smm(0)
                    for kb in range(nkb):
                        ps, j0, n = nxt
                        if kb + 1 < nkb:
                            nxt = smm(kb + 1)
                        pt = ptp.get()
                        S.act(pt.h[:, 0:n], ps.h[:, 0:n], AF.Exp, w=[pt.b, ps.b])
                        if kb * 128 >= q0:
                            S.memset("pool", pt.h[64:128, 0:64], 0.0, w=[pt.b])
                        lv = pvb if kb * 128 < C.HALF else onesb
                        S.mms([(psO.h[:, j0:NQ], Vtm.h[:, kb, :], pt.h[:, 0:n], kb == 0, kb == nkb - 1)], r=[Vtm.b, pt.b],
                              w=[psO.b] if kb == 0 else [], wa=[] if kb == 0 else [psO.b])
                        S.mms([(psL.h[:, j0:NQ], lv.h[:, :], pt.h[:, 0:n], kb == 0, kb == nkb - 1)], r=[lv.b, pt.b],
                              w=[psL.b] if kb == 0 else [], wa=[] if kb == 0 else [psL.b])
                    rl = tq.get()
                    S.ts("dve", rl.h[:, 0:NQ], psL.h[:, 0:NQ], 1e-30, None, ALU.max, w=[rl.b, psL.b])
                    S.recip(rl.h[:, 0:NQ], rl.h[:, 0:NQ], w=[rl.b])
                    oc = ocp.get()
                    S.tt("dve", oc.h[:, 0:NQ], psO.h[:, 0:NQ], rl.h[:, 0:NQ], ALU.mult, r=[rl.b], w=[oc.b, psO.b])
                    ocs.append(oc)
                df = tq.get()
                S.stt("dve", df.h[:, 0:NQ], ocs[1].h[:, 0:NQ], neglam, ocs[0].h[:, 0:NQ], ALU.mult, ALU.add, r=[ocs[0].b, ocs[1].b, sc.b], w=[df.b])
                sq = tq.get()
                S.act(sq.h[:, 0:NQ], df.h[:, 0:NQ], AF.Square, r=[df.b], w=[sq.b])
                ps = K.ps.get()
                S.mms([(ps.h[:, 0:NQ], ones.h[:, :], sq.h[:, 0:NQ], True, True)], r=[ones.b, sq.b], w=[ps.b])
                rs = tq.get()
                S.ts("dve", rs.h[:, 0:NQ], ps.h[:, 0:NQ], 1.0 / 128, C.EPS, ALU.mult, ALU.add, w=[rs.b, ps.b])
                S.act(rs.h[:, 0:NQ], rs.h[:, 0:NQ], AF.Ln, w=[rs.b])
                S.act(rs.h[:, 0:NQ], rs.h[:, 0:NQ], AF.Exp, w=[rs.b], scale=-0.5)
                ob = obp.get()
                S.stt("dve", ob.h[:, 0:NQ], df.h[:, 0:NQ], slg, rs.h[:, 0:NQ], ALU.mult, ALU.mult, r=[df.b, rs.b, sc.b], w=[ob.b])
                S.dma("sp", K.d["obT"][rows, jq:jq + NQ], ob.h[:, 0:NQ], r=[ob.b], wa=[K.b["obT"]])
        S.emit()


def phase_DE(K):
    S, C, nc = K.S, K.C, K.nc
    DC = C.D // 128
    RCH = C.RW // 128
    with contextlib.ExitStack() as es:
        K.ps = psrot(es, nc)
        oat = sbt(es, nc, "Doat", [128, RCH, C.NT], BF16)
        obt = sbt(es, nc, "Dobt", [128, RCH, C.NT], BF16)
        mT = sbt(es, nc, "DmT", [128, DC, C.NT], BF16)
        wap = sbrot(es, nc, "Dwa", [128, RCH, 256], BF16, 2)
        wbp = sbrot(es, nc, "Dwb", [128, RCH, 256], BF16, 2)
        wop = sbrot(es, nc, "Dwo", [128, DC, 512], BF16, 2)
        gp = sbrot(es, nc, "Dg", [128, C.NT], F32, 4)
        mp_ = sbrot(es, nc, "Dm", [128, C.NT], F32, 4)
        xp = sbrot(es, nc, "Dx", [128, 512], F32, 4)
        oav = K.d["oaT"].rearrange("(c p) t -> p c t", p=128)
        obv = K.d["obT"].rearrange("(c p) t -> p c t", p=128)
        for (t0, nt) in C.tiles_oh():
            jq = t0 - C.OWN0
            S.dma("sp", oat.h[:, :, 0:nt], oav[:, :, jq:jq + nt], r=[K.b["oaT"]], w=[oat.b])
            S.dma("sp", obt.h[:, :, 0:nt], obv[:, :, jq:jq + nt], r=[K.b["obT"]], w=[obt.b])
            firstm = True
            for (g0, gw) in chunks(C.D, 256):
                wa_ = wap.get()
                load_w(K, wa_, K.d["w_a"], 0, RCH, g0, gw)
                wb_ = wbp.get()
                load_w(K, wb_, K.d["w_b"], 0, RCH, g0, gw)
                for (j0, wj) in chunks(gw, 128):
                    col = g0 + j0
                    psA = K.ps.get()
                    S.mms([(psA.h[0:wj, 0:nt], wa_.h[:, k, j0:j0 + wj], oat.h[:, k, 0:nt], k == 0, k == RCH - 1) for k in range(RCH)],
                          r=[wa_.b, oat.b], w=[psA.b])
                    psB = K.ps.get()
                    S.mms([(psB.h[0:wj, 0:nt], wb_.h[:, k, j0:j0 + wj], obt.h[:, k, 0:nt], k == 0, k == RCH - 1) for k in range(RCH)],
                          r=[wb_.b, obt.b], w=[psB.b])
                    ga = gp.get()
                    S.dma("sp", ga.h[:, 0:nt], K.d["zG"][col:col + 128, jq:jq + nt], r=[K.b["zG"]], w=[ga.b])
                    gb = gp.get()
                    S.dma("sp", gb.h[:, 0:nt], K.d["zG"][C.D + col:C.D + col + 128, jq:jq + nt], r=[K.b["zG"]], w=[gb.b])
                    S.act(ga.h[:, 0:nt], ga.h[:, 0:nt], AF.Sigmoid, w=[ga.b])
                    S.act(gb.h[:, 0:nt], gb.h[:, 0:nt], AF.Sigmoid, w=[gb.b])
                    m1 = mp_.get()
                    S.tt("dve", m1.h[:, 0:nt], psA.h[:, 0:nt], ga.h[:, 0:nt], ALU.mult, r=[ga.b], w=[m1.b, psA.b])
                    m2 = mp_.get()
                    S.tt("dve", m2.h[:, 0:nt], psB.h[:, 0:nt], gb.h[:, 0:nt], ALU.mult, r=[gb.b], w=[m2.b, psB.b])
                    S.tt("pool", mT.h[:, col // 128, 0:nt], m1.h[:, 0:nt], m2.h[:, 0:nt], ALU.add, r=[m1.b, m2.b],
                         w=[mT.b] if firstm else [], wa=[] if firstm else [mT.b])
                    firstm = False
            for (g0, gw) in chunks(C.D, 512):
                wo_ = wop.get()
                load_w(K, wo_, K.d["w_o"], 0, DC, g0, gw)
                for s_ in range(nt // 128):
                    ps = K.ps.get()
                    S.mms([(ps.h[:, 0:gw], mT.h[:, k, s_ * 128:(s_ + 1) * 128], wo_.h[:, k, 0:gw], k == 0, k == DC - 1) for k in range(DC)],
                          r=[mT.b, wo_.b], w=[ps.b])
                    xt = xp.get()
                    S.dma("sp", xt.h[:, 0:gw], K.d["xc"][t0 + s_ * 128:t0 + (s_ + 1) * 128, g0:g0 + gw], w=[xt.b])
                    S.tt("dve", xt.h[:, 0:gw], ps.h[:, 0:gw], xt.h[:, 0:gw], ALU.add, w=[xt.b, ps.b])
                    S.dma("sp", K.d["x1"][jq + s_ * 128:jq + (s_ + 1) * 128, g0:g0 + gw], xt.h[:, 0:gw], r=[xt.b], wa=[K.b["x1"]])
        S.emit()


def phase_F(K):
    S, C, nc = K.S, K.C, K.nc
    DC = C.D // 128
    FB = C.DFF // 128
    KH = FB // 2
    SEG = chunks(KH, 22)
    with contextlib.ExitStack() as es:
        allps = psrot(es, nc)
        K.ps = Rot(allps.t[0:4])
        acc = allps.t[4:8]
        aT = sbt(es, nc, "FaT", [128, KH, C.NT], BF16)
        junk = Ctx()
        junk.h = aT.h[:, 0:C.D // C.NT, :].rearrange("p a b -> p (a b)")
        junk.b = aT.b
        R = norm_res(K, es, "F", nxs=2, junk=junk)
        ct, co = load_cols(K, es, "Fcols", ["g_ffn", "cw0", "cw1", "cw2", "cb"])
        pv = load_const(K, es, "Fpv", "pvalid")
        hT = sbt(es, nc, "FhT", [128, DC, C.NT], BF16)
        wgp = sbrot(es, nc, "Fwg", [128, DC, 256], BF16, 2)
        wfp = sbrot(es, nc, "Fwf", [128, max(n for _, n in SEG), 256], BF16, 2)
        halo = sbt(es, nc, "Fhalo", [128, 2 * FB, 2], F32)
        S.memset("pool", halo.h[:, :, :], 0.0, w=[halo.b])
        ubp = sbrot(es, nc, "Fub", [128, C.NT + 2], F32, 4)
        cvp = sbrot(es, nc, "Fcv", [128, C.NT], F32, 4)
        xp = sbrot(es, nc, "Fx", [128, 256], F32, 4)
        wv = K.d["w_fi"].rearrange("(c p) n -> p c n", p=128)
        for ti, (t0, nt) in enumerate(C.tiles_oh()):
            jq = t0 - C.OWN0
            is_halo = (ti == 0)
            row0 = t0 - C.HALF
            build_hT(K, R, K.d["x1"], jq, nt, ct, co["g_ffn"], hT)
            for half2 in range(2):
                firsta = True
                for jb in range(half2 * KH, (half2 + 1) * KH):
                    wg_ = wgp.get()
                    load_w(K, wg_, K.d["w_fi"], 0, DC, jb * 128, 128)
                    for k0 in range(0, DC, 8):
                        kn_ = min(8, DC - k0)
                        S.dma("pool", wg_.h[:, k0:k0 + kn_, 128:256], wv[:, k0:k0 + kn_, C.DFF + jb * 128:C.DFF + (jb + 1) * 128], wa=[wg_.b])
                    cvs = []
                    for half in range(2):
                        blk_i = jb + half * FB
                        ps = K.ps.get()
                        S.mms([(ps.h[:, 0:nt], wg_.h[:, k, half * 128:(half + 1) * 128], hT.h[:, k, 0:nt], k == 0, k == DC - 1) for k in range(DC)],
                              r=[wg_.b, hT.b], w=[ps.b])
                        ub = ubp.get()
                        S.copy("act", ub.h[:, 2:nt + 2], ps.h[:, 0:nt], w=[ub.b, ps.b])
                        S.copy("pool", ub.h[:, 0:2], halo.h[:, blk_i, :], r=[halo.b], wa=[ub.b])
                        if is_halo:
                            S.ts("pool", halo.h[:, blk_i, :], ub.h[:, nt:nt + 2], pv.h[:, 0:1], None, ALU.mult, r=[ub.b, pv.b], w=[halo.b])
                            continue
                        S.copy("pool", halo.h[:, blk_i, :], ub.h[:, nt:nt + 2], r=[ub.b], w=[halo.b])
                        cv = cvp.get()
                        cc = lambda nm: ct.h[:, co[nm] + blk_i:co[nm] + blk_i + 1]
                        S.act(cv.h[:, 0:nt], ub.h[:, 2:nt + 2], AF.Identity, r=[ub.b, ct.b], w=[cv.b], scale=cc("cw2"), bias=cc("cb"))
                        S.stt("dve", cv.h[:, 0:nt], ub.h[:, 1:nt + 1], cc("cw1"), cv.h[:, 0:nt], ALU.mult, ALU.add, r=[ub.b, ct.b], w=[cv.b])
                        S.stt("dve", cv.h[:, 0:nt], ub.h[:, 0:nt], cc("cw0"), cv.h[:, 0:nt], ALU.mult, ALU.add, r=[ub.b, ct.b], w=[cv.b])
                        cvs.append(cv)
                    if is_halo:
                        continue
                    sgt = cvp.get()
                    S.act(sgt.h[:, 0:nt], cvs[0].h[:, 0:nt], AF.Silu, r=[cvs[0].b], w=[sgt.b])
                    S.tt("pool", aT.h[:, jb - half2 * KH, 0:nt], sgt.h[:, 0:nt], cvs[1].h[:, 0:nt], ALU.mult, r=[sgt.b, cvs[1].b],
                         w=[aT.b] if firsta else [], wa=[] if firsta else [aT.b])
                    firsta = False
                if is_halo:
                    continue
                nsub = nt // 128
                for (g0, gw) in chunks(C.D, 256):
                    for si, (k0, kn_) in enumerate(SEG):
                        wf_ = wfp.get()
                        load_w(K, wf_, K.d["w_fo"], half2 * KH + k0, kn_, g0, gw)
                        for s_ in range(nsub):
                            ps = acc[s_]
                            items = [(ps.h[:, 0:gw], aT.h[:, k0 + k, s_ * 128:(s_ + 1) * 128], wf_.h[:, k, 0:gw],
                                      si == 0 and k == 0, si == len(SEG) - 1 and k == kn_ - 1) for k in range(kn_)]
                            S.mms(items, r=[aT.b, wf_.b], w=[ps.b] if si == 0 else [], wa=[] if si == 0 else [ps.b])
                    for s_ in range(nsub):
                        ps = acc[s_]
                        xt = xp.get()
                        if half2 == 0:
                            S.dma("sp", xt.h[:, 0:gw], K.d["x1"][jq + s_ * 128:jq + (s_ + 1) * 128, g0:g0 + gw], r=[K.b["x1"]], w=[xt.b])
                        else:
                            S.dma("sp", xt.h[:, 0:gw], K.d["x2"][row0 + s_ * 128:row0 + (s_ + 1) * 128, g0:g0 + gw], r=[K.b["x2"]], w=[xt.b])
                        S.tt("dve", xt.h[:, 0:gw], ps.h[:, 0:gw], xt.h[:, 0:gw], ALU.add, w=[xt.b, ps.b])
                        S.dma("sp", K.d["x2"][row0 + s_ * 128:row0 + (s_ + 1) * 128, g0:g0 + gw], xt.h[:, 0:gw], r=[xt.b], wa=[K.b["x2"]])
        S.emit()


def phase_G(K):
    S, C, nc = K.S, K.C, K.nc
    DC = C.D // 128
    PC = C.PLE // 128
    with contextlib.ExitStack() as es:
        K.ps = psrot(es, nc)
        R = norm_res(K, es, "G")
        ct, co = load_cols(K, es, "Gcols", ["g_ple"])
        hT = sbt(es, nc, "GhT", [128, DC, C.NT], BF16)
        pT = sbt(es, nc, "GpT", [128, PC, C.NT], BF16)
        pp = sbrot(es, nc, "Gp", [128, C.PLE], F32, 2)
        wgp = sbrot(es, nc, "Gwg", [128, DC, 512], BF16, 2)
        wpp = sbrot(es, nc, "Gwp", [128, PC, 512], BF16, 2)
        sgp = sbrot(es, nc, "Gsg", [128, 512], F32, 3)
        xp = sbrot(es, nc, "Gx", [128, 512], F32, 3)
        for (t0, nt) in C.tiles_oh()[1:]:
            row0 = t0 - C.HALF
            build_hT(K, R, K.d["x2"], row0, nt, ct, co["g_ple"], hT)
            firstp = True
            for s_ in range(nt // 128):
                pt = pp.get()
                S.dma("sp", pt.h[:, :], K.d["pc"][row0 + s_ * 128:row0 + (s_ + 1) * 128, :], w=[pt.b])
                ps = K.ps.get()
                for c in range(PC):
                    S.transpose(ps.h[:, c * 128:(c + 1) * 128], pt.h[:, c * 128:(c + 1) * 128], R.ident.h[:, :], r=[pt.b, R.ident.b],
                                w=[ps.b] if c == 0 else [], wa=[] if c == 0 else [ps.b])
                S.copy("act", pT.h[:, :, s_ * 128:(s_ + 1) * 128], ps.h[:, 0:PC * 128].rearrange("p (c t) -> p c t", t=128),
                       w=[ps.b] + ([pT.b] if firstp else []), wa=[] if firstp else [pT.b])
                firstp = False
            for (g0, gw) in chunks(C.D, 512):
                wg_ = wgp.get()
                load_w(K, wg_, K.d["w_pg"], 0, DC, g0, gw)
                wp_ = wpp.get()
                load_w(K, wp_, K.d["w_pp"], 0, PC, g0, gw)
                for s_ in range(nt // 128):
                    psG = K.ps.get()
                    S.mms([(psG.h[:, 0:gw], hT.h[:, k, s_ * 128:(s_ + 1) * 128], wg_.h[:, k, 0:gw], k == 0, k == DC - 1) for k in range(DC)],
                          r=[hT.b, wg_.b], w=[psG.b])
                    psP = K.ps.get()
                    S.mms([(psP.h[:, 0:gw], pT.h[:, k, s_ * 128:(s_ + 1) * 128], wp_.h[:, k, 0:gw], k == 0, k == PC - 1) for k in range(PC)],
                          r=[pT.b, wp_.b], w=[psP.b])
                    sg = sgp.get()
                    S.act(sg.h[:, 0:gw], psG.h[:, 0:gw], AF.Sigmoid, w=[sg.b, psG.b])
                    S.tt("dve", sg.h[:, 0:gw], psP.h[:, 0:gw], sg.h[:, 0:gw], ALU.mult, w=[sg.b, psP.b])
                    xt = xp.get()
                    S.dma("sp", xt.h[:, 0:gw], K.d["x2"][row0 + s_ * 128:row0 + (s_ + 1) * 128, g0:g0 + gw], r=[K.b["x2"]], w=[xt.b])
                    S.tt("pool", xt.h[:, 0:gw], xt.h[:, 0:gw], sg.h[:, 0:gw], ALU.add, r=[sg.b], w=[xt.b])
                    S.dma("sp", K.d["out"][row0 + s_ * 128:row0 + (s_ + 1) * 128, g0:g0 + gw], xt.h[:, 0:gw], r=[xt.b], wa=[K.b["out"]])
        S.emit()

def build_program(C, upto="A", debug=()):
    nc = bass.Bass("TRN2", target_bir_lowering=False)
    K = Ctx()
    K.nc, K.C = nc, C
    K.d, K.b = {}, {}

    def din(name, shape):
        K.d[name] = nc.dram_tensor(name, list(shape), F32, kind="ExternalInput").ap()

    def dscr(name, shape, dt=F32, out=False):
        kind = "ExternalOutput" if out else "Internal"
        K.d[name] = nc.dram_tensor(name, list(shape), dt, kind=kind).ap()
        K.b[name] = Buf(name)

    din("xc", [C.CTX, C.D])
    din("pc", [C.HALF, C.PLE])
    din("cols", [128, C.NCOLS])
    din("consts", [128, C.NCONST])
    din("lamb", [128, 256])
    din("w_in", [C.D, C.IC])
    din("w2", [C.DL, C.RW])
    din("a2", [C.AL, C.RW])
    din("g2", [C.GL, C.RW])
    din("w_a", [C.RW, C.D])
    din("w_b", [C.DW, C.D])
    din("w_o", [C.D, C.D])
    din("w_fi", [C.D, 2 * C.DFF])
    din("w_fo", [C.DFF, C.D])
    din("w_pg", [C.D, C.D])
    din("w_pp", [C.PLE, C.D])
    dscr("out", [C.HALF, C.D], out=True)
    dscr("zR", [C.RC, C.CTX], out=("zT" in debug))
    dscr("zQ", [C.DW, C.NOH], out=("zT" in debug))
    dscr("zKV", [2 * C.DW, C.CTX], out=("zT" in debug))
    dscr("zG", [2 * C.D, C.NOH], out=("zT" in debug))
    dscr("gT", [C.RW, C.CTX], out=("gT" in debug))
    dscr("bonT", [C.RW, C.CTX], out=("gT" in debug))
    dscr("oaT", [C.RW, C.NOH], BF16, out=("oaT" in debug))
    dscr("obT", [C.DW, C.NOH], BF16, out=("obT" in debug))
    dscr("x1", [C.NOH, C.D], out=("x1" in debug))
    dscr("x2", [C.HALF, C.D], out=("x2" in debug))
    with contextlib.ExitStack() as es0:
        K.S = Sched(nc, es0)
        phase_A(K)
        if upto == "A":
            return nc, K
        phase_B(K)
        if upto == "B":
            return nc, K
        phase_C(K)
        if upto == "C":
            return nc, K
        phase_DE(K)
        if upto == "DE":
            return nc, K
        phase_F(K)
        if upto == "F":
            return nc, K
        phase_G(K)
    return nc, K


def colpack(v):
    v = np.asarray(v, np.float32).reshape(-1)
    n = (v.size + 127) // 128
    p = np.zeros(n * 128, np.float32)
    p[:v.size] = v
    return np.ascontiguousarray(p.reshape(n, 128).T)


def host_inputs(C, inp):
    g = lambda k: np.asarray(inp[k], np.float32)[0]
    mu = g("rwkv_mu")
    cw = g("ffn_conv_w")
    parts = {
        "g_mix": g("norm_mix_g"), "g_ffn": g("norm_ffn_g"), "g_ple": g("norm_ple_g"),
        "mu_r": mu[C.o_r:C.o_r + C.RW], "mu_k": mu[C.o_k:C.o_k + C.RW], "mu_v": mu[C.o_v:C.o_v + C.RW],
        "mu_w": mu[C.o_wl:C.o_wl + C.DL], "mu_a": mu[C.o_al:C.o_al + C.AL], "mu_g": mu[C.o_gl:C.o_gl + C.GL],
        "w0": g("rwkv_w0"), "a0": g("rwkv_a0"), "k_k": g("rwkv_k_k"), "k_a": g("rwkv_k_a"),
        "r_k": g("rwkv_r_k").reshape(-1), "ln_w": g("rwkv_ln_w"), "ln_b": g("rwkv_ln_b"),
        "q_g": np.tile(g("q_norm_g"), 2), "k_g": np.tile(g("k_norm_g"), 2), "subln": g("subln_g"),
        "cw0": cw[0], "cw1": cw[1], "cw2": cw[2], "cb": g("ffn_conv_b"),
    }
    cols = np.zeros((128, C.NCOLS), np.float32)
    for name, (off, n) in C.colmap.items():
        cp = colpack(parts[name])
        assert cp.shape[1] == n, (name, cp.shape, n)
        cols[:, off:off + n] = cp
    consts = np.zeros((128, C.NCONST), np.float32)
    consts[:, 0:128] = np.eye(128, dtype=np.float32)
    consts[0:64, 128:192] = 1.0
    consts[64:128, 192:256] = 1.0
    consts[:, 256:384] = 1.0
    s = np.arange(64)[:, None]
    t = np.arange(64)[None, :]
    consts[0:64, 384:448] = (s < t)
    consts[0:64, 448:512] = (s <= t)
    consts[0:64, 512:576] = (s > t)
    sm = np.ones(512, np.float32)
    sm[0::64] = 0.0
    consts[:, 576:1088] = sm[None, :]
    lamb = np.concatenate([g("lam_q1"), g("lam_k1"), g("lam_q2"), g("lam_k2")])[None, :].repeat(128, 0)
    x = np.asarray(inp["x"], np.float32)
    p = np.asarray(inp["p"], np.float32)[0]
    shared = {
        "cols": cols, "lamb": np.ascontiguousarray(lamb),
        "w_in": g("w_in"), "w2": g("rwkv_w2"), "a2": g("rwkv_a2"), "g2": g("rwkv_g2"),
        "w_a": g("w_branch_a"), "w_b": g("w_branch_b"), "w_o": g("w_out"),
        "w_fi": g("w_ffn_in"), "w_fo": g("w_ffn_out"), "w_pg": g("w_ple_gate"), "w_pp": g("w_ple_proj"),
    }
    maps = []
    for core in range(2 * C.B):
        b, hf = core // 2, core % 2
        m = dict(shared)
        if hf == 1:
            m["xc"] = np.ascontiguousarray(x[b])
        else:
            xc = np.zeros((C.CTX, C.D), np.float32)
            xc[C.HALF:] = x[b, 0:C.HALF]
            m["xc"] = xc
        m["pc"] = np.ascontiguousarray(p[b, hf * C.HALF:(hf + 1) * C.HALF])
        cc = consts.copy()
        cc[:, 1088:1216] = float(hf)
        m["consts"] = cc
        maps.append(m)
    return maps


_PROG = {}


def kernel(**inputs):
    C = Cfg()
    if "full" not in _PROG:
        _PROG["full"] = build_program(C, upto="ALL")
    nc, K = _PROG["full"]
    maps = host_inputs(C, inputs)
    res = run_bass_kernel_spmd(nc, maps, core_ids=list(range(2 * C.B)))
    out = np.zeros((C.B, C.SEQ, C.D), np.float32)
    for core in range(2 * C.B):
        b, hf = core // 2, core % 2
        out[b, hf * C.HALF:(hf + 1) * C.HALF] = res.results[core]["out"]
    return out
```

```python
import contextlib
import math
import numpy as np
import concourse.bass as bass
import concourse.mybir as mybir
from concourse.bass_utils import run_bass_kernel_spmd

F32 = mybir.dt.float32
BF16 = mybir.dt.bfloat16
AF = mybir.ActivationFunctionType
ALU = mybir.AluOpType


class Buf:
    __slots__ = ("name", "w", "r", "g")

    def __init__(self, name):
        self.name = name
        self.w = {}
        self.r = {}
        self.g = None


class Tl:
    def __init__(self, h, name):
        self.h = h
        self.b = Buf(name)


class Rot:
    def __init__(self, tiles):
        self.t = tiles
        self.i = 0

    def get(self):
        t = self.t[self.i % len(self.t)]
        self.i += 1
        return t


class Sched:
    ENGS = ("pe", "act", "dve", "pool", "sp")
    import os
    NDS = int(os.environ.get('NDS', 6))

    def __init__(self, nc, es):
        self.nc = nc
        self.streams = {e: [] for e in self.ENGS}
        self.cnt = {e: 0 for e in self.ENGS}
        self.sems = {}
        for e in self.ENGS:
            self.sems[e] = es.enter_context(nc.semaphore("s_" + e))
        self.dq = {}
        self.dlast = {}
        for q in ("sp", "act", "pool"):
            self.dq[q] = 0
            for i in range(self.NDS):
                self.sems[("d", q, i)] = es.enter_context(nc.semaphore("d_%s_%d" % (q, i)))
        self.known = {}
        self.final = []
        self.rr = 0
        self.log = []

    def _wait(self, e, key, val):
        if key == "pe" and e == "pe":
            return
        if self.known.get((e, key), 0) >= val:
            return
        self.known[(e, key)] = val
        self.log.append((e, "wait", key, val))
        sem = self.sems[key]
        self.streams[e].append(lambda eng, sem=sem, val=val: eng.wait_ge(sem, val))

    def _deps(self, e, r, w, wa):
        for b in r:
            for k, v in b.w.items():
                self._wait(e, k, v)
        for b in w:
            for k, v in b.w.items():
                self._wait(e, k, v)
            for k, v in b.r.items():
                self._wait(e, k, v)
        for b in wa:
            if b.g is not None:
                self._wait(e, b.g[0], b.g[1])
            for k, v in b.r.items():
                self._wait(e, k, v)

    def _mark(self, tok, r, w, wa):
        k, v = tok
        for b in r:
            if b.r.get(k, 0) < v:
                b.r[k] = v
        for b in w:
            b.w = {k: v}
            b.r = {}
            b.g = tok
        for b in wa:
            if b.w.get(k, 0) < v:
                b.w[k] = v

    def op(self, e, fn, r=(), w=(), wa=(), inc=True):
        self._deps(e, r, w, wa)
        sem = self.sems[e]
        if inc:
            self.cnt[e] += 1
            tok = (e, self.cnt[e])
            self.streams[e].append(lambda eng, fn=fn, sem=sem: fn(eng).then_inc(sem, 1))
        else:
            tok = (e, self.cnt[e] + 1)
            self.streams[e].append(lambda eng, fn=fn: fn(eng))
        self._mark(tok, r, w, wa)
        self.log.append((e, "op", tok, inc))
        return tok

    def dma(self, q, out, in_, r=(), w=(), wa=(), final=False):
        j = self.dq[q]
        self.dq[q] += 1
        slot = j % self.NDS
        key = ("d", q, slot)
        val = 16 * (j // self.NDS + 1)
        if j >= self.NDS:
            self._wait(q, key, val - 16)
        self._deps(q, r, w, wa)
        sem = self.sems[key]
        self.streams[q].append(
            lambda eng, out=out, in_=in_, sem=sem: eng.dma_start(out=out, in_=in_).then_inc(sem, 16))
        tok = (key, val)
        self.log.append((q, "dma", tok))
        self.dlast[key] = val
        self._mark(tok, r, w, wa)
        if final:
            self.final.append(tok)
        return tok

    def act(self, out, in_, func, r=(), w=(), wa=(), **kw):
        return self.op("act", lambda e: e.activation(out=out, in_=in_, func=func, **kw), r, w, wa)

    def ts(self, eng, out, in0, s1, s2, op0, op1=None, r=(), w=(), wa=()):
        if op1 is None:
            return self.op(eng, lambda e: e.tensor_scalar(out=out, in0=in0, scalar1=s1, scalar2=None, op0=op0), r, w, wa)
        return self.op(eng, lambda e: e.tensor_scalar(out=out, in0=in0, scalar1=s1, scalar2=s2, op0=op0, op1=op1), r, w, wa)

    def tt(self, eng, out, in0, in1, op, r=(), w=(), wa=()):
        return self.op(eng, lambda e: e.tensor_tensor(out=out, in0=in0, in1=in1, op=op), r, w, wa)

    def stt(self, eng, out, in0, scalar, in1, op0, op1, r=(), w=(), wa=()):
        return self.op(eng, lambda e: e.scalar_tensor_tensor(out=out, in0=in0, scalar=scalar, in1=in1, op0=op0, op1=op1), r, w, wa)

    def copy(self, eng, out, in_, r=(), w=(), wa=()):
        if eng == "act":
            return self.act(out, in_, AF.Identity, r, w, wa)
        return self.op(eng, lambda e: e.tensor_copy(out=out, in_=in_), r, w, wa)

    def memset(self, eng, ap, val, w=(), wa=()):
        return self.op(eng, lambda e: e.memset(ap, val), (), w, wa)

    def recip(self, out, in_, r=(), w=(), wa=()):
        return self.op("dve", lambda e: e.reciprocal(out=out, in_=in_), r, w, wa)

    def scan(self, out, d0, d1, init, op0, op1, r=(), w=(), wa=()):
        return self.op("dve", lambda e: e.tensor_tensor_scan(out=out, data0=d0, data1=d1, initial=init, op0=op0, op1=op1), r, w, wa)

    def transpose(self, out, in_, ident, r=(), w=(), wa=()):
        return self.op("pe", lambda e: e.transpose(out, in_, ident), r, w, wa)

    def mms(self, items, r=(), w=(), wa=()):
        self._deps("pe", r, w, wa)
        n = len(items)
        sem = self.sems["pe"]
        self.cnt["pe"] += 1
        tok = ("pe", self.cnt["pe"])
        for i, (o, l, rh, st, sp) in enumerate(items):
            if i == n - 1:
                self.streams["pe"].append(
                    lambda eng, o=o, l=l, rh=rh, st=st, sp=sp, sem=sem: eng.matmul(o, l, rh, start=st, stop=sp).then_inc(sem, 1))
            else:
                self.streams["pe"].append(
                    lambda eng, o=o, l=l, rh=rh, st=st, sp=sp: eng.matmul(o, l, rh, start=st, stop=sp))
        self._mark(tok, r, w, wa)
        return tok

    def ev(self):
        self.rr += 1
        return "act" if self.rr % 2 else "dve"

    def emit(self, last=False):
        nc = self.nc
        for e in self.ENGS:
            for f in self.ENGS:
                if f != e and self.cnt[f] > 0:
                    self._wait(e, f, self.cnt[f])
            for key, val in self.dlast.items():
                self._wait(e, key, val)
        self.final = []
        streams = self.streams
        self.streams = {e: [] for e in self.ENGS}
        with nc.Block() as block:
            @block.tensor
            def _(eng):
                for f in streams["pe"]:
                    f(eng)

            @block.scalar
            def _(eng):
                for f in streams["act"]:
                    f(eng)

            @block.vector
            def _(eng):
                for f in streams["dve"]:
                    f(eng)

            @block.gpsimd
            def _(eng):
                for f in streams["pool"]:
                    f(eng)

            @block.sync
            def _(eng):
                for f in streams["sp"]:
                    f(eng)


class Cfg:
    def __init__(self, D=4096, SEQ=4096, B=4, PLE=256):
        self.D, self.SEQ, self.B, self.PLE = D, SEQ, B, PLE
        self.EPS = 1e-6
        self.RW = D // 2
        self.NH = self.RW // 64
        self.NHC = self.RW // 128
        self.DL = max(32, int(round(1.8 * self.RW ** 0.5 / 32)) * 32)
        self.AL = max(32, int(round(2.5 * self.RW ** 0.5 / 32)) * 32)
        self.GL = max(32, int(round(0.6 * self.RW ** 0.8 / 32)) * 32)
        self.GN_EPS = 64e-5
        self.RC = 3 * self.RW + self.DL + self.AL + self.GL
        self.DW = D // 2
        self.NDH = self.DW // 128
        self.DCOL = 3 * self.DW
        self.IC = self.RC + self.DCOL + 2 * D
        self.DFF = int(round(8 * D / 3 / 256)) * 256
        self.HALF = SEQ // 2
        self.CTX = SEQ
        self.HALO = 128
        self.OWN0 = self.HALF - self.HALO
        self.NT = min(512, self.HALF)
        self.NOH = self.HALF + self.HALO
        self.o_r, self.o_k, self.o_v = 0, self.RW, 2 * self.RW
        self.o_wl = 3 * self.RW
        self.o_al = self.o_wl + self.DL
        self.o_gl = self.o_al + self.AL
        self.o_q = self.RC
        self.o_dk = self.RC + self.DW
        self.o_dv = self.RC + 2 * self.DW
        self.o_ga = self.RC + self.DCOL
        self.o_gb = self.o_ga + D
        self.lambda_init = 0.8 - 0.6 * math.exp(-0.3 * 0)
        self.colmap = {}
        off = 0
        dc = D // 128
        hc = self.RW // 128
        fc = 2 * self.DFF // 128
        for name, n in [("g_mix", dc), ("g_ffn", dc), ("g_ple", dc),
                        ("mu_r", hc), ("mu_k", hc), ("mu_v", hc), ("mu_w", 1), ("mu_a", 1), ("mu_g", (self.GL + 127) // 128),
                        ("w0", hc), ("a0", hc), ("k_k", hc), ("k_a", hc), ("r_k", hc), ("ln_w", hc), ("ln_b", hc),
                        ("q_g", 1), ("k_g", 1), ("subln", 1),
                        ("cw0", fc), ("cw1", fc), ("cw2", fc), ("cb", fc)]:
            self.colmap[name] = (off, n)
            off += n
        self.NCOLS = off
        self.cm = {"ident": (0, 128), "blk": (128, 128), "ones": (256, 128), "mUs": (384, 64), "mUi": (448, 64),
                   "mLs": (512, 64), "scan": (576, 512), "pvalid": (1088, 128)}
        self.NCONST = 1216

    def tiles_oh(self):
        t = [(self.OWN0, self.HALO)]
        for i in range(self.HALF // self.NT):
            t.append((self.HALF + i * self.NT, self.NT))
        return t


class Ctx:
    pass


def sbt(es, nc, name, shape, dt):
    return Tl(es.enter_context(nc.sbuf_tensor(name, list(shape), dt)), name)


def sbrot(es, nc, name, shape, dt, n):
    return Rot([sbt(es, nc, "%s%d" % (name, i), shape, dt) for i in range(n)])


_PSN = [0]


def psrot(es, nc, n=8):
    _PSN[0] += 1
    return Rot([Tl(es.enter_context(nc.psum_tensor("ps%d_%d" % (_PSN[0], i), [128, 512], F32)), "ps%d" % i) for i in range(n)])


def chunks(total, size):
    return [(i, min(size, total - i)) for i in range(0, total, size)]


def load_cols(K, es, name, names):
    S, C, nc = K.S, K.C, K.nc
    lo = min(C.colmap[x][0] for x in names)
    hi = max(C.colmap[x][0] + C.colmap[x][1] for x in names)
    t = sbt(es, nc, name, [128, hi - lo], F32)
    if hi - lo == 1:
        with nc.allow_non_contiguous_dma(reason="single column"):
            pass
    S.dma("sp", t.h[:, :], K.d["cols"][:, lo:hi], w=[t.b])
    return t, {x: C.colmap[x][0] - lo for x in names}


def load_const(K, es, name, key, rows=128, dt=F32):
    S, C, nc = K.S, K.C, K.nc
    o, n = C.cm[key]
    t = sbt(es, nc, name, [rows, n], dt)
    q = "sp" if dt == F32 else "pool"
    S.dma(q, t.h[:, :], K.d["consts"][0:rows, o:o + n], w=[t.b])
    return t


def build_hT(K, R, src, t0, nt, gt, goff, hT):
    S, C = K.S, K.C
    DC = C.D // 128
    import os
    for s in range(min(nt // 128, int(os.environ.get("KSTOP", "99")))):
        xs = R.xs.get()
        S.dma("sp", xs.h[:, :], src[t0 + s * 128: t0 + (s + 1) * 128, :], w=[xs.b])
        st = R.st.get()
        S.memset("pool", st.h[:, 0:1], 0.0, w=[st.b])
        S.act(R.junk.h[:, :], xs.h[:, :], AF.Square, r=[xs.b], w=[R.junk.b, st.b], accum_out=st.h[:, 0:1])
        S.ts("dve", st.h[:, 1:2], st.h[:, 0:1], 1.0 / C.D, C.EPS, ALU.mult, ALU.add, r=[st.b], w=[st.b])
        S.act(st.h[:, 2:3], st.h[:, 1:2], AF.Ln, r=[st.b], w=[st.b])
        S.act(st.h[:, 3:4], st.h[:, 2:3], AF.Exp, r=[st.b], w=[st.b], scale=-0.5)
        S.ts("dve", xs.h[:, :], xs.h[:, :], st.h[:, 3:4], None, ALU.mult, r=[xs.b, st.b], w=[xs.b])
        import os
        if os.environ.get("SKIPT"):
            continue
        for c0 in range(0, DC, 4):
            ps = K.ps.get()
            n = min(4, DC - c0)
            for j in range(n):
                S.transpose(ps.h[:, j * 128:(j + 1) * 128], xs.h[:, (c0 + j) * 128:(c0 + j + 1) * 128], R.ident.h[:, :],
                            r=[xs.b, R.ident.b], w=[ps.b] if j == 0 else [], wa=[] if j == 0 else [ps.b])
            eng = S.ev()
            for j in range(n):
                c = c0 + j
                o = hT.h[:, c, s * 128:(s + 1) * 128]
                i = ps.h[:, j * 128:(j + 1) * 128]
                g = gt.h[:, goff + c:goff + c + 1]
                if eng == "act":
                    S.act(o, i, AF.Identity, r=[gt.b], w=[ps.b], wa=[hT.b], scale=g)
                else:
                    S.ts("dve", o, i, g, None, ALU.mult, r=[gt.b], w=[ps.b], wa=[hT.b])


def norm_res(K, es, pfx, nxs=3, junk=None):
    R = Ctx()
    nc, C = K.nc, K.C
    R.xs = sbrot(es, nc, pfx + "xs", [128, C.D], F32, nxs)
    R.junk = junk if junk is not None else sbt(es, nc, pfx + "junk", [128, C.D], BF16)
    R.st = sbrot(es, nc, pfx + "st", [128, 4], F32, 4)
    R.ident = load_const(K, es, pfx + "ident", "ident")
    return R


def load_w(K, wt, w4, g, c0, cn, width, kstep=8):
    S = K.S
    v = w4[g]
    first = True
    for k0 in range(0, cn, kstep):
        kn = min(kstep, cn - k0)
        S.dma("pool", wt.h[:, k0:k0 + kn, 0:width], v[:, c0 + k0:c0 + k0 + kn, 0:width],
              w=[wt.b] if first else [], wa=[] if first else [wt.b])
        first = False


def z_store(K, col, w, zt, t0, nt):
    S, C = K.S, K.C
    regions = [("zR", 0, C.RC, 0), ("zQ", C.o_q, C.o_dk, C.OWN0), ("zKV", C.o_dk, C.o_ga, 0), ("zG", C.o_ga, C.IC, C.OWN0)]
    for (nm, c0, c1, tk0) in regions:
        a, b = max(col, c0), min(col + w, c1)
        if a >= b:
            continue
        ta = max(t0, tk0)
        if ta >= t0 + nt:
            continue
        S.dma("sp", K.d[nm][a - c0:b - c0, ta - tk0:t0 + nt - tk0], zt.h[a - col:b - col, ta - t0:nt], r=[zt.b], wa=[K.b[nm]])


def phase_A(K):
    S, C, nc = K.S, K.C, K.nc
    DC = C.D // 128
    with contextlib.ExitStack() as es:
        K.ps = psrot(es, nc)
        R = norm_res(K, es, "A")
        gt, gm = load_cols(K, es, "Ag", ["g_mix"])
        hT = sbt(es, nc, "AhT", [128, DC, C.NT], BF16)
        wpool = sbrot(es, nc, "Aw", [128, DC, 512], BF16, 2)
        zpool = sbrot(es, nc, "Az", [128, C.NT], F32, 3)
        ntile = C.CTX // C.NT
        first_full = C.OWN0 // C.NT
        NGA = (C.IC + 511) // 512
        for tt in range(ntile):
            t0 = tt * C.NT
            build_hT(K, R, K.d["xc"], t0, C.NT, gt, gm["g_mix"], hT)
            if tt >= first_full:
                groups = list(range(NGA))
            else:
                groups = [g for g in range(NGA) if (g * 512 < C.RC) or (g * 512 + 512 > C.o_dk and g * 512 < C.o_ga)]
            for g in groups:
                col0 = g * 512
                gw = min(512, C.IC - col0)
                wt = wpool.get()
                load_w(K, wt, K.d["w_in"], g, 0, DC, 512)
                for (j0, wj) in chunks(gw, 128):
                    ps = K.ps.get()
                    items = [(ps.h[0:wj, 0:C.NT], wt.h[:, k, j0:j0 + wj], hT.h[:, k, 0:C.NT], k == 0, k == DC - 1)
                             for k in range(DC)]
                    S.mms(items, r=[wt.b, hT.b], w=[ps.b])
                    zt = zpool.get()
                    S.copy(S.ev(), zt.h[0:wj, :], ps.h[0:wj, 0:C.NT], w=[zt.b, ps.b])
                    z_store(K, col0 + j0, wj, zt, t0, C.NT)
        S.emit()


def phase_B(K):
    S, C, nc = K.S, K.C, K.nc
    NHC = C.NHC
    TB = min(256, C.HALF)
    NCH = TB // 64
    HG = min(4, NHC)
    NG = NHC // HG
    GW = HG * 128
    GLC = (C.GL + 127) // 128
    zT = K.d["zR"]
    with contextlib.ExitStack() as es:
        K.ps = psrot(es, nc)
        ident = load_const(K, es, "Bident", "ident")
        blk = load_const(K, es, "Bblk", "blk")
        mUs = load_const(K, es, "BmUs", "mUs", rows=64)
        mUi = load_const(K, es, "BmUi", "mUi", rows=64)
        mLs = load_const(K, es, "BmLs", "mLs", rows=64)
        scanm = load_const(K, es, "Bscan", "scan")
        identb = load_const(K, es, "Bidentb", "ident", dt=BF16)
        names = ["mu_r", "mu_k", "mu_v", "mu_w", "mu_a", "mu_g", "w0", "a0", "k_k", "k_a", "r_k", "ln_w", "ln_b"]
        ct, co = load_cols(K, es, "Bcols", names)
        nmu = 3 * NHC + 2 + GLC
        omu = sbt(es, nc, "Bomu", [128, nmu], F32)
        S.ts("dve", omu.h[:, :], ct.h[:, 0:nmu], -1.0, 1.0, ALU.mult, ALU.add, r=[ct.b], w=[omu.b])
        nw0 = sbt(es, nc, "Bnw0", [128, NHC], F32)
        S.ts("dve", nw0.h[:, :], ct.h[:, co["w0"]:co["w0"] + NHC], -1.0, None, ALU.mult, r=[ct.b], w=[nw0.b])
        omka = sbt(es, nc, "Bomka", [128, NHC], F32)
        S.ts("dve", omka.h[:, :], ct.h[:, co["k_a"]:co["k_a"] + NHC], -1.0, 1.0, ALU.mult, ALU.add, r=[ct.b], w=[omka.b])
        cm05 = sbt(es, nc, "Bcm05", [128, 1], F32)
        S.memset("pool", cm05.h[:, :], -0.5, w=[cm05.b])
        c1 = sbt(es, nc, "Bc1", [128, 1], F32)
        S.memset("pool", c1.h[:, :], 1.0, w=[c1.b])
        lw = sbrot(es, nc, "Blw", [128, 2 + GLC, 128], F32, 2)
        wl = sbt(es, nc, "Bwl", [128, TB + 1], F32)
        al = sbt(es, nc, "Bal", [128, TB + 1], F32)
        gl = sbt(es, nc, "Bgl", [128, GLC, TB + 1], F32)
        tw = sbt(es, nc, "Btw", [128, TB], F32)
        als = sbt(es, nc, "Bals", [128, TB], F32)
        sg = sbt(es, nc, "Bsg", [128, GLC, TB], F32)
        tp = sbrot(es, nc, "Btp", [128, TB + 1], F32, 28)
        Rt, KKt, Bt, Kt, Vt, Bct, Kct = [sbt(es, nc, "B" + n, [128, NHC, TB], BF16) for n in ("Rt", "KKt", "Bt", "Kt", "Vt", "Bct", "Kct")]
        gC = sbt(es, nc, "BgC", [128, NHC, NCH], F32)
        OT = sbt(es, nc, "BOT", [128, NHC, TB], F32)
        H = sbt(es, nc, "BH", [128, NHC, 128], F32)
        Hb = sbt(es, nc, "BHb", [128, NHC, 128], BF16)
        bdp = [[sbt(es, nc, "Bbd%d_%d" % (i, j), [128, NHC, 128], BF16) for j in range(3)] for i in range(1)]
        for i in range(1):
            for j in range(3):
                S.memset("pool", bdp[i][j].h[:, :, :], 0.0, w=[bdp[i][j].b])
        Hbufs = [Buf("H%d" % g) for g in range(NG)]
        S.memset("pool", H.h[:, :, :], 0.0, w=[H.b] + Hbufs)
        S.memset("pool", Hb.h[:, :, :], 0.0, w=[Hb.b], wa=Hbufs)
        SLN = ("A0", "A1", "B0", "B1", "X0", "X1", "nM3", "M2", "M4", "Vm", "Bcm", "Kcm")
        slots = [{n: sbt(es, nc, "Bs%d%s" % (g, n), [64, GW], BF16) for n in SLN} for g in range(NG)]
        fp = sbrot(es, nc, "Bfp", [64, GW], F32, 2)
        oabp = sbrot(es, nc, "Boab", [128, TB], BF16, 2)

        def bc64(t):
            return t.h[0:64, 0:64].unsqueeze(1).broadcast_to([64, 2 * HG, 64])

        def v3(ap):
            return ap.rearrange("p (h t) -> p h t", t=64)

        def load_shift(dst_ap_fn, rows, row0, t0, buf, q="sp"):
            if t0 == 0:
                S.memset("pool", dst_ap_fn(0, 1), 0.0, w=[buf])
                S.dma(q, dst_ap_fn(1, TB + 1), zT[row0:row0 + rows, 0:TB], r=[K.b["zR"]], wa=[buf])
            else:
                S.dma(q, dst_ap_fn(0, TB + 1), zT[row0:row0 + rows, t0 - 1:t0 + TB], r=[K.b["zR"]], w=[buf])

        def lerp(eng, out_ap, zt_prev, zt_cur, mu_ap, omu_ap, rbufs, wbuf):
            t = tp.get()
            n = out_ap.shape[0]
            if eng == "act":
                S.act(t.h[0:n, 0:TB], zt_prev, AF.Identity, r=rbufs, w=[t.b], scale=mu_ap)
            else:
                S.ts(eng, t.h[0:n, 0:TB], zt_prev, mu_ap, None, ALU.mult, r=rbufs, w=[t.b])
            S.stt("dve", out_ap, zt_cur, omu_ap, t.h[0:n, 0:TB], ALU.mult, ALU.add, r=rbufs + [t.b], w=[wbuf])

        ntile = C.CTX // TB
        for tt in range(ntile):
            t0 = tt * TB
            need3 = (t0 + TB > C.OWN0)
            lo = max(C.OWN0 - t0, 0)
            load_shift(lambda a, b: wl.h[0:C.DL, a:b], C.DL, C.o_wl, t0, wl.b)
            load_shift(lambda a, b: al.h[0:C.AL, a:b], C.AL, C.o_al, t0, al.b)
            for c in range(GLC):
                n = min(128, C.GL - c * 128)
                load_shift(lambda a, b, c=c, n=n: gl.h[0:n, c, a:b], n, C.o_gl + c * 128, t0, gl.b)
            cw, ca, cg = co["mu_w"], co["mu_a"], co["mu_g"]
            lerp("dve", tw.h[0:C.DL, :], wl.h[0:C.DL, 0:TB], wl.h[0:C.DL, 1:TB + 1], ct.h[0:C.DL, cw:cw + 1], omu.h[0:C.DL, cw:cw + 1], [wl.b, ct.b, omu.b], tw.b)
            S.act(tw.h[0:C.DL, :], tw.h[0:C.DL, :], AF.Tanh, w=[tw.b])
            lerp("dve", als.h[0:C.AL, :], al.h[0:C.AL, 0:TB], al.h[0:C.AL, 1:TB + 1], ct.h[0:C.AL, ca:ca + 1], omu.h[0:C.AL, ca:ca + 1], [al.b, ct.b, omu.b], als.b)
            for c in range(GLC):
                n = min(128, C.GL - c * 128)
                lerp("dve", sg.h[0:n, c, :], gl.h[0:n, c, 0:TB], gl.h[0:n, c, 1:TB + 1], ct.h[0:n, cg + c:cg + c + 1], omu.h[0:n, cg + c:cg + c + 1], [gl.b, ct.b, omu.b], sg.b)
                S.act(sg.h[0:n, c, :], sg.h[0:n, c, :], AF.Sigmoid, w=[sg.b])
            for hc in range(NHC):
                cs = slice(hc * 128, (hc + 1) * 128)
                col = lambda nm: ct.h[:, co[nm] + hc:co[nm] + hc + 1]
                zs = []
                for (o_, nm) in ((C.o_r, "mu_r"), (C.o_k, "mu_k"), (C.o_v, "mu_v")):
                    zt_ = tp.get()
                    load_shift(lambda a, b, zt_=zt_: zt_.h[:, a:b], 128, o_ + hc * 128, t0, zt_.b)
                    out = tp.get()
                    mo = co[nm] + hc
                    lerp("act", out.h[:, 0:TB], zt_.h[:, 0:TB], zt_.h[:, 1:TB + 1], ct.h[:, mo:mo + 1], omu.h[:, mo:mo + 1], [zt_.b, ct.b, omu.b], out.b)
                    zs.append(out)
                r_s, k_s, v_s = zs
                X = slice(0, TB)
                lwt = lw.get()
                S.dma("sp", lwt.h[0:C.DL, 0, :], K.d["w2"][:, cs], w=[lwt.b])
                S.dma("sp", lwt.h[0:C.AL, 1, :], K.d["a2"][:, cs], wa=[lwt.b])
                for c in range(GLC):
                    n = min(128, C.GL - c * 128)
                    S.dma("sp", lwt.h[0:n, 2 + c, :], K.d["g2"][c * 128:c * 128 + n, cs], wa=[lwt.b])
                ps = K.ps.get()
                S.mms([(ps.h[:, 0:TB], lwt.h[0:C.DL, 0, :], tw.h[0:C.DL, :], True, True)], r=[lwt.b, tw.b], w=[ps.b])
                e1 = tp.get()
                S.act(e1.h[:, X], ps.h[:, 0:TB], AF.Exp, r=[nw0.b], w=[e1.b, ps.b], scale=-1.0, bias=nw0.h[:, hc:hc + 1])
                S.act(e1.h[:, X], e1.h[:, X], AF.Ln, r=[c1.b], w=[e1.b], bias=c1.h[:, 0:1])
                elw = tp.get()
                S.act(elw.h[:, X], e1.h[:, X], AF.Exp, r=[e1.b, cm05.b], w=[elw.b], scale=-1.0, bias=cm05.h[:, 0:1])
                ps = K.ps.get()
                S.mms([(ps.h[:, 0:TB], lwt.h[0:C.AL, 1, :], als.h[0:C.AL, :], True, True)], r=[lwt.b, als.b], w=[ps.b])
                a_ = tp.get()
                S.act(a_.h[:, X], ps.h[:, 0:TB], AF.Sigmoid, r=[ct.b], w=[a_.b, ps.b], bias=col("a0"))
                if need3:
                    ps = K.ps.get()
                    items = []
                    for c in range(GLC):
                        n = min(128, C.GL - c * 128)
                        items.append((ps.h[:, 0:TB], lwt.h[0:n, 2 + c, :], sg.h[0:n, c, :], c == 0, c == GLC - 1))
                    S.mms(items, r=[lwt.b, sg.b], w=[ps.b])
                    gt_ = tp.get()
                    S.copy("act", gt_.h[:, X], ps.h[:, 0:TB], w=[gt_.b, ps.b])
                    S.dma("sp", K.d["gT"][cs, t0:t0 + TB], gt_.h[:, X], r=[gt_.b], wa=[K.b["gT"]])
                kk = tp.get()
                S.act(kk.h[:, X], k_s.h[:, X], AF.Identity, r=[k_s.b, ct.b], w=[kk.b], scale=col("k_k"))
                kk2 = tp.get()
                S.act(kk2.h[:, X], kk.h[:, X], AF.Square, r=[kk.b], w=[kk2.b])
                ps = K.ps.get()
                S.mms([(ps.h[:, 0:TB], blk.h[:, :], kk2.h[:, X], True, True)], r=[blk.b, kk2.b], w=[ps.b])
                rn = tp.get()
                S.ts("dve", rn.h[:, X], ps.h[:, 0:TB], 1e-24, None, ALU.max, w=[rn.b, ps.b])
                S.act(rn.h[:, X], rn.h[:, X], AF.Ln, w=[rn.b])
                S.act(rn.h[:, X], rn.h[:, X], AF.Exp, w=[rn.b], scale=-0.5)
                kkn = tp.get()
                S.tt("dve", kkn.h[:, X], kk.h[:, X], rn.h[:, X], ALU.mult, r=[kk.b, rn.b], w=[kkn.b])
                t1 = tp.get()
                S.ts("dve", t1.h[:, X], a_.h[:, X], col("k_a"), omka.h[:, hc:hc + 1], ALU.mult, ALU.add, r=[a_.b, ct.b, omka.b], w=[t1.b])
                kmod = tp.get()
                S.tt("pool", kmod.h[:, X], k_s.h[:, X], t1.h[:, X], ALU.mult, r=[k_s.b, t1.b], w=[kmod.b])
                b_ = tp.get()
                S.tt("pool", b_.h[:, X], kkn.h[:, X], a_.h[:, X], ALU.mult, r=[kkn.b, a_.b], w=[b_.b])
                if need3:
                    rkr = tp.get()
                    S.stt("dve", rkr.h[:, X], r_s.h[:, X], col("r_k"), kmod.h[:, X], ALU.mult, ALU.mult, r=[r_s.b, ct.b, kmod.b], w=[rkr.b])
                    ps = K.ps.get()
                    S.mms([(ps.h[:, 0:TB], blk.h[:, :], rkr.h[:, X], True, True)], r=[blk.b, rkr.b], w=[ps.b])
                    bon = tp.get()
                    S.tt("dve", bon.h[:, X], ps.h[:, 0:TB], v_s.h[:, X], ALU.mult, r=[v_s.b], w=[bon.b, ps.b])
                    S.dma("sp", K.d["bonT"][cs, t0:t0 + TB], bon.h[:, X], r=[bon.b], wa=[K.b["bonT"]])
                cum = tp.get()
                S.scan(cum.h[:, X], scanm.h[:, 0:TB], elw.h[:, X], 0.0, ALU.mult, ALU.add, r=[scanm.b, elw.b], w=[cum.b])
                gi = tp.get()
                S.act(gi.h[:, X], cum.h[:, X], AF.Exp, r=[cum.b], w=[gi.b], scale=-1.0)
                ge = tp.get()
                S.act(ge.h[:, X], cum.h[:, X], AF.Exp, r=[cum.b], w=[ge.b])
                gx = tp.get()
                S.tt("pool", gx.h[:, X], cum.h[:, X], elw.h[:, X], ALU.subtract, r=[cum.b, elw.b], w=[gx.b])
                S.act(gx.h[:, X], gx.h[:, X], AF.Exp, w=[gx.b], scale=-1.0)
                S.tt("dve", Rt.h[:, hc, :], r_s.h[:, X], gi.h[:, X], ALU.mult, r=[r_s.b, gi.b], wa=[Rt.b])
                S.tt("pool", KKt.h[:, hc, :], kkn.h[:, X], gx.h[:, X], ALU.mult, r=[kkn.b, gx.b], wa=[KKt.b])
                tb_ = tp.get()
                S.tt("dve", tb_.h[:, X], b_.h[:, X], ge.h[:, X], ALU.mult, r=[b_.b, ge.b], w=[tb_.b])
                tk_ = tp.get()
                S.tt("pool", tk_.h[:, X], kmod.h[:, X], ge.h[:, X], ALU.mult, r=[kmod.b, ge.b], w=[tk_.b])
                S.copy("act", Bt.h[:, hc, :], tb_.h[:, X], r=[tb_.b], wa=[Bt.b])
                S.copy("act", Kt.h[:, hc, :], tk_.h[:, X], r=[tk_.b], wa=[Kt.b])
                S.copy("act", Vt.h[:, hc, :], v_s.h[:, X], r=[v_s.b], wa=[Vt.b])
                S.copy("dve", gC.h[:, hc, :], gi.h[:, X].rearrange("p (c t) -> p c t", t=64)[:, :, 63], r=[gi.b], wa=[gC.b])
                gcb = gC.h[:, hc, :].unsqueeze(2).broadcast_to([128, NCH, 64])
                S.stt("dve", Bct.h[:, hc, :].rearrange("p (c t) -> p c t", t=64), tb_.h[:, X].rearrange("p (c t) -> p c t", t=64), -1.0, gcb,
                      ALU.mult, ALU.mult, r=[tb_.b, gC.b], wa=[Bct.b])
                S.tt("pool", Kct.h[:, hc, :].rearrange("p (c t) -> p c t", t=64), tk_.h[:, X].rearrange("p (c t) -> p c t", t=64), gcb,
                     ALU.mult, r=[tk_.b, gC.b], wa=[Kct.b])
            import os
            for ci in range(NCH if not os.environ.get("BSKIP2") else 0):
                cc = slice(ci * 64, (ci + 1) * 64)
                hd = lambda g, hh: (g * HG + hh // 2, (hh % 2) * 64)
                NHG = 2 * HG
                st = [dict() for _ in range(NG)]
                bd = bdp[0]
                for j, src in enumerate((KKt, Rt, Bt)):
                    eng = ("pool", "act", "pool")[j]
                    S.copy(eng, bd[j].h[0:64, :, 0:64], src.h[0:64, :, cc], r=[src.b], w=[bd[j].b])
                    S.copy(eng, bd[j].h[64:128, :, 64:128], src.h[64:128, :, cc], r=[src.b], wa=[bd[j].b])
                KKbd, Rbd, Bbd = bd
                for g in range(NG):
                    d = st[g]
                    def prod(lT, rbd):
                        ps = K.ps.get()
                        items = []
                        for hl in range(HG):
                            hc = g * HG + hl
                            items.append((ps.h[0:64, hl * 128:(hl + 1) * 128], lT.h[:, hc, cc], rbd.h[:, hc, :], True, True))
                        S.mms(items, r=[lT.b, rbd.b], w=[ps.b])
                        return ps
                    ps = prod(Bt, KKbd)
                    sl_ = slots[g]
                    d["A"] = sl_["A0"]
                    S.tt("dve", v3(d["A"].h[:, :]), v3(ps.h[0:64, 0:GW]), bc64(mUs), ALU.mult, r=[mUs.b], w=[d["A"].b, ps.b])
                    bprep = int(os.environ.get("BPREP", "99"))
                    if bprep <= 1:
                        continue
                    d["X"] = sl_["X0"]
                    S.tt("pool", v3(d["X"].h[:, :]), bc64(ident), v3(d["A"].h[:, :]), ALU.subtract, r=[ident.b, d["A"].b], w=[d["X"].b])
                    if bprep <= 2:
                        continue
                    ps = prod(KKt, Bbd)
                    d["Bq"] = sl_["B0"]
                    S.tt("dve", v3(d["Bq"].h[:, :]), v3(ps.h[0:64, 0:GW]), bc64(mLs), ALU.mult, r=[mLs.b], w=[d["Bq"].b, ps.b])
                    ps = prod(Bt, Rbd)
                    d["nM3"] = sl_["nM3"]
                    S.stt("dve", v3(d["nM3"].h[:, :]), v3(ps.h[0:64, 0:GW]), -1.0, bc64(mUi), ALU.mult, ALU.mult, r=[mUi.b], w=[d["nM3"].b, ps.b])
                    ps = prod(Kt, KKbd)
                    d["M2"] = sl_["M2"]
                    S.tt("dve", v3(d["M2"].h[:, :]), v3(ps.h[0:64, 0:GW]), bc64(mUs), ALU.mult, r=[mUs.b], w=[d["M2"].b, ps.b])
                    ps = prod(Kt, Rbd)
                    d["M4"] = sl_["M4"]
                    S.tt("dve", v3(d["M4"].h[:, :]), v3(ps.h[0:64, 0:GW]), bc64(mUi), ALU.mult, r=[mUi.b], w=[d["M4"].b, ps.b])
                    if bprep <= 3:
                        continue
                    for nm, src in (("Vm", Vt), ("Bcm", Bct), ("Kcm", Kct)):
                        ps = K.ps.get()
                        items = []
                        for hl in range(HG):
                            hc = g * HG + hl
                            items.append((ps.h[0:64, hl * 128:(hl + 1) * 128], src.h[:, hc, cc], identb.h[:, :], True, True))
                        S.mms(items, r=[src.b, identb.b], w=[ps.b])
                        d[nm] = sl_[nm]
                        S.copy("act", d[nm].h[:, :], ps.h[0:64, 0:GW], w=[d[nm].b, ps.b])
                bstop = int(os.environ.get("BSTOP", "99"))
                if bstop <= 1:
                    continue
                for lvl in range(1, 6):
                    for g in range(NG):
                        d = st[g]
                        def hmm(l, r_):
                            ps = K.ps.get()
                            items = [(ps.h[0:64, hh * 64:(hh + 1) * 64], l.h[0:64, hh * 64:(hh + 1) * 64], r_.h[0:64, hh * 64:(hh + 1) * 64], True, True)
                                     for hh in range(NHG)]
                            S.mms(items, r=[l.b, r_.b], w=[ps.b])
                            return ps
                        psB = hmm(d["A"], d["Bq"])
                        nB = slots[g]["B%d" % (lvl % 2)]
                        S.copy("act", nB.h[:, :], psB.h[0:64, 0:GW], w=[nB.b, psB.b])
                        if lvl < 5:
                            psA = hmm(d["Bq"], d["A"])
                            nA = slots[g]["A%d" % (lvl % 2)]
                            S.copy("dve", nA.h[:, :], psA.h[0:64, 0:GW], w=[nA.b, psA.b])
                            d["A"] = nA
                        d["Bq"] = nB
                    for g in range(NG):
                        d = st[g]
                        ps = K.ps.get()
                        items = [(ps.h[0:64, hh * 64:(hh + 1) * 64], d["Bq"].h[0:64, hh * 64:(hh + 1) * 64], d["X"].h[0:64, hh * 64:(hh + 1) * 64], True, True)
                                 for hh in range(NHG)]
                        S.mms(items, r=[d["Bq"].b, d["X"].b], w=[ps.b])
                        nX = slots[g]["X%d" % (lvl % 2)]
                        S.tt("dve", nX.h[:, :], ps.h[0:64, 0:GW], d["X"].h[:, :], ALU.add, r=[d["X"].b], w=[nX.b, ps.b])
                        d["X"] = nX
                if bstop <= 2:
                    continue
                for g in range(NG):
                    d = st[g]
                    ps = K.ps.get()
                    items = []
                    for hl in range(HG):
                        hc = g * HG + hl
                        items.append((ps.h[0:64, hl * 128:(hl + 1) * 128], KKt.h[:, hc, cc], Hb.h[:, hc, :], True, False))
                        for e2 in range(2):
                            sl = slice(hl * 128 + e2 * 64, hl * 128 + e2 * 64 + 64)
                            items.append((ps.h[0:64, sl], d["M2"].h[0:64, sl], d["Vm"].h[0:64, sl], False, e2 == 1))
                    S.mms(items, r=[KKt.b, Hbufs[g], d["M2"].b, d["Vm"].b], w=[ps.b])
                    d["W"] = slots[g]["A1"]
                    S.copy("act", d["W"].h[:, :], ps.h[0:64, 0:GW], w=[d["W"].b, ps.b])
                for g in range(NG):
                    d = st[g]
                    ps = K.ps.get()
                    items = [(ps.h[0:64, hh * 64:(hh + 1) * 64], d["X"].h[0:64, hh * 64:(hh + 1) * 64], d["W"].h[0:64, hh * 64:(hh + 1) * 64], True, True)
                             for hh in range(NHG)]
                    S.mms(items, r=[d["X"].b, d["W"].b], w=[ps.b])
                    d["U"] = slots[g]["B0"]
                    S.copy("dve", d["U"].h[:, :], ps.h[0:64, 0:GW], w=[d["U"].b, ps.b])
                if bstop <= 3:
                    continue
                for g in range(NG):
                    d = st[g]
                    ps = K.ps.get()
                    items = []
                    for hl in range(HG):
                        hc = g * HG + hl
                        items.append((ps.h[0:64, hl * 128:(hl + 1) * 128], Rt.h[:, hc, cc], Hb.h[:, hc, :], True, False))
                        for e2 in range(2):
                            sl = slice(hl * 128 + e2 * 64, hl * 128 + e2 * 64 + 64)
                            items.append((ps.h[0:64, sl], d["nM3"].h[0:64, sl], d["U"].h[0:64, sl], False, False))
                            items.append((ps.h[0:64, sl], d["M4"].h[0:64, sl], d["Vm"].h[0:64, sl], False, e2 == 1))
                    S.mms(items, r=[Rt.b, Hbufs[g], d["nM3"].b, d["U"].b, d["M4"].b, d["Vm"].b], w=[ps.b])
                    if need3:
                        otm = fp.get()
                        S.copy("act", otm.h[:, :], ps.h[0:64, 0:GW], w=[otm.b, ps.b])
                        ps2 = K.ps.get()
                        for hl in range(HG):
                            S.transpose(ps2.h[:, hl * 64:(hl + 1) * 64], otm.h[0:64, hl * 128:(hl + 1) * 128], ident.h[0:64, 0:64],
                                        r=[otm.b, ident.b], w=[ps2.b] if hl == 0 else [], wa=[] if hl == 0 else [ps2.b])
                        S.copy("dve", OT.h[:, g * HG:(g + 1) * HG, cc], ps2.h[:, 0:HG * 64].rearrange("p (h t) -> p h t", t=64), w=[ps2.b], wa=[OT.b])
                    ps = K.ps.get()
                    items = []
                    for hl in range(HG):
                        sl = slice(hl * 128, (hl + 1) * 128)
                        o = ps.h[:, sl]
                        items.append((o, d["Bcm"].h[0:64, sl], d["U"].h[0:64, sl], True, False))
                        items.append((o, d["Kcm"].h[0:64, sl], d["Vm"].h[0:64, sl], False, True))
                    S.mms(items, r=[d["Bcm"].b, d["U"].b, d["Kcm"].b, d["Vm"].b], w=[ps.b])
                    for hh in range(NHG):
                        hc, pb = hd(g, hh)
                        hl = hh // 2
                        S.stt("dve", H.h[pb:pb + 64, hc, pb:pb + 64], H.h[pb:pb + 64, hc, pb:pb + 64], gC.h[pb:pb + 64, hc, ci:ci + 1],
                              ps.h[pb:pb + 64, hl * 128 + pb: hl * 128 + pb + 64], ALU.mult, ALU.add,
                              r=[gC.b], w=[Hbufs[g], ps.b])
                    S.copy("pool", Hb.h[:, g * HG:(g + 1) * HG, :], H.h[:, g * HG:(g + 1) * HG, :], w=[Hbufs[g]])
            if need3 and not os.environ.get("BSKIP3"):
                for hc in range(NHC):
                    cs = slice(hc * 128, (hc + 1) * 128)
                    col = lambda nm: ct.h[:, co[nm] + hc:co[nm] + hc + 1]
                    X = slice(0, TB)
                    gt_ = tp.get()
                    S.dma("sp", gt_.h[:, X], K.d["gT"][cs, t0:t0 + TB], r=[K.b["gT"]], w=[gt_.b])
                    bon = tp.get()
                    S.dma("sp", bon.h[:, X], K.d["bonT"][cs, t0:t0 + TB], r=[K.b["bonT"]], w=[bon.b])
                    ps = K.ps.get()
                    S.mms([(ps.h[:, 0:TB], blk.h[:, :], OT.h[:, hc, :], True, True)], r=[blk.b, OT.b], w=[ps.b])
                    cen = tp.get()
                    S.stt("dve", cen.h[:, X], ps.h[:, 0:TB], -1.0 / 64, OT.h[:, hc, :], ALU.mult, ALU.add, r=[OT.b], w=[cen.b, ps.b])
                    sq = tp.get()
                    S.act(sq.h[:, X], cen.h[:, X], AF.Square, r=[cen.b], w=[sq.b])
                    ps = K.ps.get()
                    S.mms([(ps.h[:, 0:TB], blk.h[:, :], sq.h[:, X], True, True)], r=[blk.b, sq.b], w=[ps.b])
                    rs = tp.get()
                    S.ts("dve", rs.h[:, X], ps.h[:, 0:TB], 1.0 / 64, C.GN_EPS, ALU.mult, ALU.add, w=[rs.b, ps.b])
                    S.act(rs.h[:, X], rs.h[:, X], AF.Ln, w=[rs.b])
                    S.act(rs.h[:, X], rs.h[:, X], AF.Exp, w=[rs.b], scale=-0.5)
                    y = tp.get()
                    S.tt("pool", y.h[:, X], cen.h[:, X], rs.h[:, X], ALU.mult, r=[cen.b, rs.b], w=[y.b])
                    S.ts("pool", y.h[:, X], y.h[:, X], col("ln_w"), col("ln_b"), ALU.mult, ALU.add, r=[ct.b], w=[y.b])
                    S.tt("pool", y.h[:, X], y.h[:, X], bon.h[:, X], ALU.add, r=[bon.b], w=[y.b])
                    S.tt("dve", y.h[:, X], y.h[:, X], gt_.h[:, X], ALU.mult, r=[gt_.b], w=[y.b])
                    obf = oabp.get()
                    S.copy("act", obf.h[:, :], y.h[:, X], r=[y.b], w=[obf.b])
                    S.dma("sp", K.d["oaT"][cs, t0 + lo - C.OWN0:t0 + TB - C.OWN0], obf.h[:, lo:TB], r=[obf.b], wa=[K.b["oaT"]])
        S.emit()


def phase_C(K):
    S, C, nc = K.S, K.C, K.nc
    NKB = C.CTX // 128
    with contextlib.ExitStack() as es:
        allps = psrot(es, nc)
        K.ps = Rot(allps.t[0:4])
        acc = Rot(allps.t[4:8])
        ident = load_const(K, es, "Cident", "ident")
        blk = load_const(K, es, "Cblk", "blk")
        ones = load_const(K, es, "Cones", "ones")
        onesb = load_const(K, es, "Conesb", "ones", dt=BF16)
        pvb = load_const(K, es, "Cpvb", "pvalid", dt=BF16)
        ct, co = load_cols(K, es, "Ccols", ["q_g", "k_g", "subln"])
        lt = sbt(es, nc, "Clamb", [128, 256], F32)
        S.dma("sp", lt.h[:, :], K.d["lamb"][:, :], w=[lt.b])
        sc = sbt(es, nc, "Csc", [128, 12], F32)
        S.memset("pool", sc.h[:, :], 0.0, w=[sc.b])
        tmp = sbt(es, nc, "Cltmp", [128, 128], F32)
        S.tt("dve", tmp.h[:, 0:64], lt.h[:, 0:64], lt.h[:, 64:128], ALU.mult, r=[lt.b], w=[tmp.b])
        S.tt("dve", tmp.h[:, 64:128], lt.h[:, 128:192], lt.h[:, 192:256], ALU.mult, r=[lt.b], w=[tmp.b])
        S.act(tmp.h[:, 0:64], tmp.h[:, 0:64], AF.Identity, w=[tmp.b, sc.b], accum_out=sc.h[:, 0:1])
        S.act(tmp.h[:, 64:128], tmp.h[:, 64:128], AF.Identity, w=[tmp.b, sc.b], accum_out=sc.h[:, 1:2])
        S.act(sc.h[:, 2:4], sc.h[:, 0:2], AF.Exp, w=[sc.b])
        S.tt("dve", sc.h[:, 4:5], sc.h[:, 3:4], sc.h[:, 2:3], ALU.subtract, w=[sc.b])
        S.ts("dve", sc.h[:, 5:6], sc.h[:, 4:5], -C.lambda_init, None, ALU.add, w=[sc.b])
        neglam = sc.h[:, 5:6]
        S.ts("dve", sc.h[:, 6:7], ct.h[:, co["q_g"]:co["q_g"] + 1], 0.125, None, ALU.mult, r=[ct.b], w=[sc.b])
        S.ts("dve", sc.h[:, 7:8], ct.h[:, co["subln"]:co["subln"] + 1], 1.0 - C.lambda_init, None, ALU.mult, r=[ct.b], w=[sc.b])
        qg8, slg = sc.h[:, 6:7], sc.h[:, 7:8]
        kg = ct.h[:, co["k_g"]:co["k_g"] + 1]
        zq = sbrot(es, nc, "Czq", [128, C.NOH], F32, 2)
        zk = sbrot(es, nc, "Czk", [128, C.CTX], F32, 2)
        zv = sbrot(es, nc, "Czv", [128, C.CTX], F32, 2)
        qn = sbt(es, nc, "Cqn", [128, C.NOH], BF16)
        kn = [sbt(es, nc, "Ckn%d" % c, [128, C.CTX], BF16) for c in range(2)]
        S.memset("pool", kn[0].h[:, :], 0.0, w=[kn[0].b])
        S.memset("pool", kn[1].h[:, :], 0.0, w=[kn[1].b])
        Vtm = sbt(es, nc, "CVtm", [128, NKB, 128], BF16)
        tq = sbrot(es, nc, "Ctq", [128, 512], F32, 6)
        ptp = sbrot(es, nc, "Cpt", [128, 512], BF16, 4)
        ocp = sbrot(es, nc, "Coc", [128, 512], F32, 4)
        obp = sbrot(es, nc, "Cob", [128, 512], BF16, 2)

        def rmsn(src, n_tot, gcol, outs):
            for (c0, cn) in chunks(n_tot, 512):
                sq = tq.get()
                S.act(sq.h[:, 0:cn], src.h[:, c0:c0 + cn], AF.Square, r=[src.b], w=[sq.b])
                ps = K.ps.get()
                S.mms([(ps.h[:, 0:cn], blk.h[:, :], sq.h[:, 0:cn], True, True)], r=[blk.b, sq.b], w=[ps.b])
                rs = tq.get()
                S.ts("dve", rs.h[:, 0:cn], ps.h[:, 0:cn], 1.0 / 64, C.EPS, ALU.mult, ALU.add, w=[rs.b, ps.b])
                S.act(rs.h[:, 0:cn], rs.h[:, 0:cn], AF.Ln, w=[rs.b])
                S.act(rs.h[:, 0:cn], rs.h[:, 0:cn], AF.Exp, w=[rs.b], scale=-0.5)
                for (r0, r1, dst) in outs:
                    S.stt("dve", dst.h[r0:r1, c0:c0 + cn], src.h[r0:r1, c0:c0 + cn], gcol[r0:r1, :], rs.h[r0:r1, 0:cn], ALU.mult, ALU.mult,
                          r=[src.b, rs.b, sc.b, ct.b], wa=[dst.b])

        for h in range(C.NDH):
            rows = slice(h * 128, (h + 1) * 128)
            q_ = zq.get()
            S.dma("sp", q_.h[:, :], K.d["zQ"][h * 128:(h + 1) * 128, 0:C.NOH], r=[K.b["zQ"]], w=[q_.b])
            k_ = zk.get()
            S.dma("sp", k_.h[:, :], K.d["zKV"][h * 128:(h + 1) * 128, 0:C.CTX], r=[K.b["zKV"]], w=[k_.b])
            v_ = zv.get()
            S.dma("sp", v_.h[:, :], K.d["zKV"][C.DW + h * 128:C.DW + (h + 1) * 128, 0:C.CTX], r=[K.b["zKV"]], w=[v_.b])
            S.memset("pool", qn.h[:, 0:1], 0.0, w=[qn.b])
            S.memset("pool", kn[0].h[0:64, 0:1], 0.0, w=[kn[0].b])
            S.memset("pool", kn[1].h[64:128, 0:1], 0.0, w=[kn[1].b])
            rmsn(q_, C.NOH, qg8, [(0, 128, qn)])
            rmsn(k_, C.CTX, kg, [(0, 64, kn[0]), (64, 128, kn[1])])
            first = True
            for kb0 in range(0, NKB, 4):
                ps = K.ps.get()
                n = min(4, NKB - kb0)
                for j in range(n):
                    S.transpose(ps.h[:, j * 128:(j + 1) * 128], v_.h[:, (kb0 + j) * 128:(kb0 + j + 1) * 128], ident.h[:, :],
                                r=[v_.b, ident.b], w=[ps.b] if j == 0 else [], wa=[] if j == 0 else [ps.b])
                S.copy(S.ev(), Vtm.h[:, kb0:kb0 + n, :], ps.h[:, 0:n * 128].rearrange("p (k e) -> p k e", e=128),
                       w=[ps.b] + ([Vtm.b] if first else []), wa=[] if first else [Vtm.b])
                first = False
            for (q0, NQ) in C.tiles_oh():
                jq = q0 - C.OWN0
                nkb = (q0 + NQ) // 128
                ocs = []
                for c in range(2):
                    psO = acc.get()
                    psL = acc.get()

                    def smm(kb):
                        j0 = max(kb * 128 - q0, 0)
                        n = NQ - j0
                        ps = K.ps.get()
                        S.mms([(ps.h[:, 0:n], kn[c].h[:, kb * 128:(kb + 1) * 128], qn.h[:, jq + j0:jq + NQ], True, True)],
                              r=[kn[c].b, qn.b], w=[ps.b])
                        return ps, j0, n
                    nxt = smm(0)
                    for kb in range(nkb):
                        ps, j0, n = nxt
                        if kb + 1 < nkb:
                            nxt = smm(kb + 1)
                        pt = ptp.get()
                        S.act(pt.h[:, 0:n], ps.h[:, 0:n], AF.Exp, w=[pt.b, ps.b])
                        if kb * 128 >= q0:
                            S.memset("pool", pt.h[64:128, 0:64], 0.0, w=[pt.b])
                        lv = pvb if kb * 128 < C.HALF else onesb
                        S.mms([(psO.h[:, j0:NQ], Vtm.h[:, kb, :], pt.h[:, 0:n], kb == 0, kb == nkb - 1)], r=[Vtm.b, pt.b],
                              w=[psO.b] if kb == 0 else [], wa=[] if kb == 0 else [psO.b])
                        S.mms([(psL.h[:, j0:NQ], lv.h[:, :], pt.h[:, 0:n], kb == 0, kb == nkb - 1)], r=[lv.b, pt.b],
                              w=[psL.b] if kb == 0 else [], wa=[] if kb == 0 else [psL.b])
                    rl = tq.get()
                    S.ts("dve", rl.h[:, 0:NQ], psL.h[:, 0:NQ], 1e-30, None, ALU.max, w=[rl.b, psL.b])
                    S.recip(rl.h[:, 0:NQ], rl.h[:, 0:NQ], w=[rl.b])
                    oc = ocp.get()
                    S.tt("dve", oc.h[:, 0:NQ], psO.h[:, 0:NQ], rl.h[:, 0:NQ], ALU.mult, r=[rl.b], w=[oc.b, psO.b])
                    ocs.append(oc)
                df = tq.get()
                S.stt("dve", df.h[:, 0:NQ], ocs[1].h[:, 0:NQ], neglam, ocs[0].h[:, 0:NQ], ALU.mult, ALU.add, r=[ocs[0].b, ocs[1].b, sc.b], w=[df.b])
                sq = tq.get()
                S.act(sq.h[:, 0:NQ], df.h[:, 0:NQ], AF.Square, r=[df.b], w=[sq.b])
                ps = K.ps.get()
                S.mms([(ps.h[:, 0:NQ], ones.h[:, :], sq.h[:, 0:NQ], True, True)], r=[ones.b, sq.b], w=[ps.b])
                rs = tq.get()
                S.ts("dve", rs.h[:, 0:NQ], ps.h[:, 0:NQ], 1.0 / 128, C.EPS, ALU.mult, ALU.add, w=[rs.b, ps.b])
                S.act(rs.h[:, 0:NQ], rs.h[:, 0:NQ], AF.Ln, w=[rs.b])
                S.act(rs.h[:, 0:NQ], rs.h[:, 0:NQ], AF.Exp, w=[rs.b], scale=-0.5)
                ob = obp.get()
                S.stt("dve", ob.h[:, 0:NQ], df.h[:, 0:NQ], slg, rs.h[:, 0:NQ], ALU.mult, ALU.mult, r=[df.b, rs.b, sc.b], w=[ob.b])
                S.dma("sp", K.d["obT"][rows, jq:jq + NQ], ob.h[:, 0:NQ], r=[ob.b], wa=[K.b["obT"]])
        S.emit()


def phase_DE(K):
    S, C, nc = K.S, K.C, K.nc
    DC = C.D // 128
    RCH = C.RW // 128
    with contextlib.ExitStack() as es:
        K.ps = psrot(es, nc)
        oat = sbt(es, nc, "Doat", [128, RCH, C.NT], BF16)
        obt = sbt(es, nc, "Dobt", [128, RCH, C.NT], BF16)
        mT = sbt(es, nc, "DmT", [128, DC, C.NT], BF16)
        wap = sbrot(es, nc, "Dwa", [128, RCH, 256], BF16, 2)
        wbp = sbrot(es, nc, "Dwb", [128, RCH, 256], BF16, 2)
        wop = sbrot(es, nc, "Dwo", [128, DC, 512], BF16, 2)
        gp = sbrot(es, nc, "Dg", [128, C.NT], F32, 4)
        mp_ = sbrot(es, nc, "Dm", [128, C.NT], F32, 4)
        xp = sbrot(es, nc, "Dx", [128, 512], F32, 4)
        oav = K.d["oaT"].rearrange("(c p) t -> p c t", p=128)
        obv = K.d["obT"].rearrange("(c p) t -> p c t", p=128)
        for (t0, nt) in C.tiles_oh():
            jq = t0 - C.OWN0
            S.dma("sp", oat.h[:, :, 0:nt], oav[:, :, jq:jq + nt], r=[K.b["oaT"]], w=[oat.b])
            S.dma("sp", obt.h[:, :, 0:nt], obv[:, :, jq:jq + nt], r=[K.b["obT"]], w=[obt.b])
            firstm = True
            for (g0, gw) in chunks(C.D, 256):
                wa_ = wap.get()
                load_w(K, wa_, K.d["w_a"], g0 // 256, 0, RCH, gw)
                wb_ = wbp.get()
                load_w(K, wb_, K.d["w_b"], g0 // 256, 0, RCH, gw)
                for (j0, wj) in chunks(gw, 128):
                    col = g0 + j0
                    psA = K.ps.get()
                    S.mms([(psA.h[0:wj, 0:nt], wa_.h[:, k, j0:j0 + wj], oat.h[:, k, 0:nt], k == 0, k == RCH - 1) for k in range(RCH)],
                          r=[wa_.b, oat.b], w=[psA.b])
                    psB = K.ps.get()
                    S.mms([(psB.h[0:wj, 0:nt], wb_.h[:, k, j0:j0 + wj], obt.h[:, k, 0:nt], k == 0, k == RCH - 1) for k in range(RCH)],
                          r=[wb_.b, obt.b], w=[psB.b])
                    ga = gp.get()
                    S.dma("sp", ga.h[:, 0:nt], K.d["zG"][col:col + 128, jq:jq + nt], r=[K.b["zG"]], w=[ga.b])
                    gb = gp.get()
                    S.dma("sp", gb.h[:, 0:nt], K.d["zG"][C.D + col:C.D + col + 128, jq:jq + nt], r=[K.b["zG"]], w=[gb.b])
                    S.act(ga.h[:, 0:nt], ga.h[:, 0:nt], AF.Sigmoid, w=[ga.b])
                    S.act(gb.h[:, 0:nt], gb.h[:, 0:nt], AF.Sigmoid, w=[gb.b])
                    m1 = mp_.get()
                    S.tt("dve", m1.h[:, 0:nt], psA.h[:, 0:nt], ga.h[:, 0:nt], ALU.mult, r=[ga.b], w=[m1.b, psA.b])
                    m2 = mp_.get()
                    S.tt("dve", m2.h[:, 0:nt], psB.h[:, 0:nt], gb.h[:, 0:nt], ALU.mult, r=[gb.b], w=[m2.b, psB.b])
                    S.tt("pool", mT.h[:, col // 128, 0:nt], m1.h[:, 0:nt], m2.h[:, 0:nt], ALU.add, r=[m1.b, m2.b],
                         w=[mT.b] if firstm else [], wa=[] if firstm else [mT.b])
                    firstm = False
            for (g0, gw) in chunks(C.D, 512):
                wo_ = wop.get()
                load_w(K, wo_, K.d["w_o"], g0 // 512, 0, DC, gw)
                for s_ in range(nt // 128):
                    ps = K.ps.get()
                    S.mms([(ps.h[:, 0:gw], mT.h[:, k, s_ * 128:(s_ + 1) * 128], wo_.h[:, k, 0:gw], k == 0, k == DC - 1) for k in range(DC)],
                          r=[mT.b, wo_.b], w=[ps.b])
                    xt = xp.get()
                    S.dma("sp", xt.h[:, 0:gw], K.d["xc"][t0 + s_ * 128:t0 + (s_ + 1) * 128, g0:g0 + gw], w=[xt.b])
                    S.tt("dve", xt.h[:, 0:gw], ps.h[:, 0:gw], xt.h[:, 0:gw], ALU.add, w=[xt.b, ps.b])
                    S.dma("sp", K.d["x1"][jq + s_ * 128:jq + (s_ + 1) * 128, g0:g0 + gw], xt.h[:, 0:gw], r=[xt.b], wa=[K.b["x1"]])
        S.emit()


def phase_F(K):
    S, C, nc = K.S, K.C, K.nc
    DC = C.D // 128
    FB = C.DFF // 128
    KH = FB // 2
    SEG = chunks(KH, 22)
    with contextlib.ExitStack() as es:
        allps = psrot(es, nc)
        K.ps = Rot(allps.t[0:4])
        acc = allps.t[4:8]
        aT = sbt(es, nc, "FaT", [128, KH, C.NT], BF16)
        junk = Ctx()
        junk.h = aT.h[:, 0:C.D // C.NT, :].rearrange("p a b -> p (a b)")
        junk.b = aT.b
        R = norm_res(K, es, "F", nxs=2, junk=junk)
        ct, co = load_cols(K, es, "Fcols", ["g_ffn", "cw0", "cw1", "cw2", "cb"])
        pv = load_const(K, es, "Fpv", "pvalid")
        hT = sbt(es, nc, "FhT", [128, DC, C.NT], BF16)
        wgp = sbrot(es, nc, "Fwg", [128, DC, 256], BF16, 2)
        wfp = sbrot(es, nc, "Fwf", [128, max(n for _, n in SEG), 256], BF16, 2)
        halo = sbt(es, nc, "Fhalo", [128, 2 * FB, 2], F32)
        S.memset("pool", halo.h[:, :, :], 0.0, w=[halo.b])
        ubp = sbrot(es, nc, "Fub", [128, C.NT + 2], F32, 4)
        cvp = sbrot(es, nc, "Fcv", [128, C.NT], F32, 4)
        xp = sbrot(es, nc, "Fx", [128, 256], F32, 4)
        for ti, (t0, nt) in enumerate(C.tiles_oh()):
            jq = t0 - C.OWN0
            is_halo = (ti == 0)
            row0 = t0 - C.HALF
            build_hT(K, R, K.d["x1"], jq, nt, ct, co["g_ffn"], hT)
            for half2 in range(2):
                firsta = True
                for jb in range(half2 * KH, (half2 + 1) * KH):
                    wg_ = wgp.get()
                    load_w(K, wg_, K.d["w_fi"], jb, 0, DC, 256)
                    cvs = []
                    for half in range(2):
                        blk_i = jb + half * FB
                        ps = K.ps.get()
                        S.mms([(ps.h[:, 0:nt], wg_.h[:, k, half * 128:(half + 1) * 128], hT.h[:, k, 0:nt], k == 0, k == DC - 1) for k in range(DC)],
                              r=[wg_.b, hT.b], w=[ps.b])
                        ub = ubp.get()
                        S.copy("act", ub.h[:, 2:nt + 2], ps.h[:, 0:nt], w=[ub.b, ps.b])
                        S.copy("dve", ub.h[:, 0:2], halo.h[:, blk_i, :], r=[halo.b], wa=[ub.b])
                        if is_halo:
                            S.ts("dve", halo.h[:, blk_i, :], ub.h[:, nt:nt + 2], pv.h[:, 0:1], None, ALU.mult, r=[ub.b, pv.b], w=[halo.b])
                            continue
                        S.copy("dve", halo.h[:, blk_i, :], ub.h[:, nt:nt + 2], r=[ub.b], w=[halo.b])
                        cv = cvp.get()
                        cc = lambda nm: ct.h[:, co[nm] + blk_i:co[nm] + blk_i + 1]
                        S.act(cv.h[:, 0:nt], ub.h[:, 2:nt + 2], AF.Identity, r=[ub.b, ct.b], w=[cv.b], scale=cc("cw2"), bias=cc("cb"))
                        S.stt("dve", cv.h[:, 0:nt], ub.h[:, 1:nt + 1], cc("cw1"), cv.h[:, 0:nt], ALU.mult, ALU.add, r=[ub.b, ct.b], w=[cv.b])
                        S.stt("dve", cv.h[:, 0:nt], ub.h[:, 0:nt], cc("cw0"), cv.h[:, 0:nt], ALU.mult, ALU.add, r=[ub.b, ct.b], w=[cv.b])
                        cvs.append(cv)
                    if is_halo:
                        continue
                    sgt = cvp.get()
                    S.act(sgt.h[:, 0:nt], cvs[0].h[:, 0:nt], AF.Silu, r=[cvs[0].b], w=[sgt.b])
                    S.tt("dve", aT.h[:, jb - half2 * KH, 0:nt], sgt.h[:, 0:nt], cvs[1].h[:, 0:nt], ALU.mult, r=[sgt.b, cvs[1].b],
                         w=[aT.b] if firsta else [], wa=[] if firsta else [aT.b])
                    firsta = False
                if is_halo:
                    continue
                nsub = nt // 128
                for (g0, gw) in chunks(C.D, 256):
                    for si, (k0, kn_) in enumerate(SEG):
                        wf_ = wfp.get()
                        load_w(K, wf_, K.d["w_fo"], g0 // 256, half2 * KH + k0, kn_, gw, kstep=11)
                        for s_ in range(nsub):
                            ps = acc[s_]
                            items = [(ps.h[:, 0:gw], aT.h[:, k0 + k, s_ * 128:(s_ + 1) * 128], wf_.h[:, k, 0:gw],
                                      si == 0 and k == 0, si == len(SEG) - 1 and k == kn_ - 1) for k in range(kn_)]
                            S.mms(items, r=[aT.b, wf_.b], w=[ps.b] if si == 0 else [], wa=[] if si == 0 else [ps.b])
                    for s_ in range(nsub):
                        ps = acc[s_]
                        xt = xp.get()
                        if half2 == 0:
                            S.dma("sp", xt.h[:, 0:gw], K.d["x1"][jq + s_ * 128:jq + (s_ + 1) * 128, g0:g0 + gw], r=[K.b["x1"]], w=[xt.b])
                        else:
                            S.dma("sp", xt.h[:, 0:gw], K.d["x2"][row0 + s_ * 128:row0 + (s_ + 1) * 128, g0:g0 + gw], r=[K.b["x2"]], w=[xt.b])
                        S.tt("dve", xt.h[:, 0:gw], ps.h[:, 0:gw], xt.h[:, 0:gw], ALU.add, w=[xt.b, ps.b])
                        S.dma("sp", K.d["x2"][row0 + s_ * 128:row0 + (s_ + 1) * 128, g0:g0 + gw], xt.h[:, 0:gw], r=[xt.b], wa=[K.b["x2"]])
        S.emit()


def phase_G(K):
    S, C, nc = K.S, K.C, K.nc
    DC = C.D // 128
    PC = C.PLE // 128
    with contextlib.ExitStack() as es:
        K.ps = psrot(es, nc)
        R = norm_res(K, es, "G")
        ct, co = load_cols(K, es, "Gcols", ["g_ple"])
        hT = sbt(es, nc, "GhT", [128, DC, C.NT], BF16)
        pT = sbt(es, nc, "GpT", [128, PC, C.NT], BF16)
        pp = sbrot(es, nc, "Gp", [128, C.PLE], F32, 2)
        wgp = sbrot(es, nc, "Gwg", [128, DC, 512], BF16, 2)
        wpp = sbrot(es, nc, "Gwp", [128, PC, 512], BF16, 2)
        sgp = sbrot(es, nc, "Gsg", [128, 512], F32, 3)
        xp = sbrot(es, nc, "Gx", [128, 512], F32, 3)
        for (t0, nt) in C.tiles_oh()[1:]:
            row0 = t0 - C.HALF
            build_hT(K, R, K.d["x2"], row0, nt, ct, co["g_ple"], hT)
            firstp = True
            for s_ in range(nt // 128):
                pt = pp.get()
                S.dma("sp", pt.h[:, :], K.d["pc"][row0 + s_ * 128:row0 + (s_ + 1) * 128, :], w=[pt.b])
                ps = K.ps.get()
                for c in range(PC):
                    S.transpose(ps.h[:, c * 128:(c + 1) * 128], pt.h[:, c * 128:(c + 1) * 128], R.ident.h[:, :], r=[pt.b, R.ident.b],
                                w=[ps.b] if c == 0 else [], wa=[] if c == 0 else [ps.b])
                S.copy("act", pT.h[:, :, s_ * 128:(s_ + 1) * 128], ps.h[:, 0:PC * 128].rearrange("p (c t) -> p c t", t=128),
                       w=[ps.b] + ([pT.b] if firstp else []), wa=[] if firstp else [pT.b])
                firstp = False
            for (g0, gw) in chunks(C.D, 512):
                wg_ = wgp.get()
                load_w(K, wg_, K.d["w_pg"], g0 // 512, 0, DC, gw)
                wp_ = wpp.get()
                load_w(K, wp_, K.d["w_pp"], g0 // 512, 0, PC, gw)
                for s_ in range(nt // 128):
                    psG = K.ps.get()
                    S.mms([(psG.h[:, 0:gw], hT.h[:, k, s_ * 128:(s_ + 1) * 128], wg_.h[:, k, 0:gw], k == 0, k == DC - 1) for k in range(DC)],
                          r=[hT.b, wg_.b], w=[psG.b])
                    psP = K.ps.get()
                    S.mms([(psP.h[:, 0:gw], pT.h[:, k, s_ * 128:(s_ + 1) * 128], wp_.h[:, k, 0:gw], k == 0, k == PC - 1) for k in range(PC)],
                          r=[pT.b, wp_.b], w=[psP.b])
                    sg = sgp.get()
                    S.act(sg.h[:, 0:gw], psG.h[:, 0:gw], AF.Sigmoid, w=[sg.b, psG.b])
                    S.tt("dve", sg.h[:, 0:gw], psP.h[:, 0:gw], sg.h[:, 0:gw], ALU.mult, w=[sg.b, psP.b])
                    xt = xp.get()
                    S.dma("sp", xt.h[:, 0:gw], K.d["x2"][row0 + s_ * 128:row0 + (s_ + 1) * 128, g0:g0 + gw], r=[K.b["x2"]], w=[xt.b])
                    S.tt("pool", xt.h[:, 0:gw], xt.h[:, 0:gw], sg.h[:, 0:gw], ALU.add, r=[sg.b], w=[xt.b])
                    S.dma("sp", K.d["out"][row0 + s_ * 128:row0 + (s_ + 1) * 128, g0:g0 + gw], xt.h[:, 0:gw], r=[xt.b], wa=[K.b["out"]])
        S.emit()

def build_program(C, upto="A", debug=()):
    nc = bass.Bass("TRN2", target_bir_lowering=False)
    K = Ctx()
    K.nc, K.C = nc, C
    K.d, K.b = {}, {}

    def din(name, shape):
        K.d[name] = nc.dram_tensor(name, list(shape), F32, kind="ExternalInput").ap()

    def dscr(name, shape, dt=F32, out=False):
        kind = "ExternalOutput" if out else "Internal"
        K.d[name] = nc.dram_tensor(name, list(shape), dt, kind=kind).ap()
        K.b[name] = Buf(name)

    din("xc", [C.CTX, C.D])
    din("pc", [C.HALF, C.PLE])
    din("cols", [128, C.NCOLS])
    din("consts", [128, C.NCONST])
    din("lamb", [128, 256])
    din("w_in", [(C.IC + 511) // 512, 128, C.D // 128, 512])
    din("w2", [C.DL, C.RW])
    din("a2", [C.AL, C.RW])
    din("g2", [C.GL, C.RW])
    din("w_a", [C.D // 256, 128, C.RW // 128, 256])
    din("w_b", [C.D // 256, 128, C.DW // 128, 256])
    din("w_o", [C.D // 512, 128, C.D // 128, 512])
    din("w_fi", [C.DFF // 128, 128, C.D // 128, 256])
    din("w_fo", [C.D // 256, 128, C.DFF // 128, 256])
    din("w_pg", [C.D // 512, 128, C.D // 128, 512])
    din("w_pp", [C.D // 512, 128, C.PLE // 128, 512])
    dscr("out", [C.HALF, C.D], out=True)
    dscr("zR", [C.RC, C.CTX], out=("zT" in debug))
    dscr("zQ", [C.DW, C.NOH], out=("zT" in debug))
    dscr("zKV", [2 * C.DW, C.CTX], out=("zT" in debug))
    dscr("zG", [2 * C.D, C.NOH], out=("zT" in debug))
    dscr("gT", [C.RW, C.CTX], out=("gT" in debug))
    dscr("bonT", [C.RW, C.CTX], out=("gT" in debug))
    dscr("oaT", [C.RW, C.NOH], BF16, out=("oaT" in debug))
    dscr("obT", [C.DW, C.NOH], BF16, out=("obT" in debug))
    dscr("x1", [C.NOH, C.D], out=("x1" in debug))
    dscr("x2", [C.HALF, C.D], out=("x2" in debug))
    with contextlib.ExitStack() as es0:
        K.S = Sched(nc, es0)
        phase_A(K)
        if upto == "A":
            return nc, K
        phase_B(K)
        if upto == "B":
            return nc, K
        phase_C(K)
        if upto == "C":
            return nc, K
        phase_DE(K)
        if upto == "DE":
            return nc, K
        phase_F(K)
        if upto == "F":
            return nc, K
        phase_G(K)
    return nc, K


def colpack(v):
    v = np.asarray(v, np.float32).reshape(-1)
    n = (v.size + 127) // 128
    p = np.zeros(n * 128, np.float32)
    p[:v.size] = v
    return np.ascontiguousarray(p.reshape(n, 128).T)


def tile_w(w, W):
    w = np.asarray(w, np.float32)
    Kd, N = w.shape
    NG = (N + W - 1) // W
    if NG * W != N:
        w = np.concatenate([w, np.zeros((Kd, NG * W - N), np.float32)], axis=1)
    return np.ascontiguousarray(w.reshape(Kd // 128, 128, NG, W).transpose(2, 1, 0, 3))


def host_inputs(C, inp):
    g = lambda k: np.asarray(inp[k], np.float32)[0]
    mu = g("rwkv_mu")
    cw = g("ffn_conv_w")
    parts = {
        "g_mix": g("norm_mix_g"), "g_ffn": g("norm_ffn_g"), "g_ple": g("norm_ple_g"),
        "mu_r": mu[C.o_r:C.o_r + C.RW], "mu_k": mu[C.o_k:C.o_k + C.RW], "mu_v": mu[C.o_v:C.o_v + C.RW],
        "mu_w": mu[C.o_wl:C.o_wl + C.DL], "mu_a": mu[C.o_al:C.o_al + C.AL], "mu_g": mu[C.o_gl:C.o_gl + C.GL],
        "w0": g("rwkv_w0"), "a0": g("rwkv_a0"), "k_k": g("rwkv_k_k"), "k_a": g("rwkv_k_a"),
        "r_k": g("rwkv_r_k").reshape(-1), "ln_w": g("rwkv_ln_w"), "ln_b": g("rwkv_ln_b"),
        "q_g": np.tile(g("q_norm_g"), 2), "k_g": np.tile(g("k_norm_g"), 2), "subln": g("subln_g"),
        "cw0": cw[0], "cw1": cw[1], "cw2": cw[2], "cb": g("ffn_conv_b"),
    }
    cols = np.zeros((128, C.NCOLS), np.float32)
    for name, (off, n) in C.colmap.items():
        cp = colpack(parts[name])
        assert cp.shape[1] == n, (name, cp.shape, n)
        cols[:, off:off + n] = cp
    consts = np.zeros((128, C.NCONST), np.float32)
    consts[:, 0:128] = np.eye(128, dtype=np.float32)
    consts[0:64, 128:192] = 1.0
    consts[64:128, 192:256] = 1.0
    consts[:, 256:384] = 1.0
    s = np.arange(64)[:, None]
    t = np.arange(64)[None, :]
    consts[0:64, 384:448] = (s < t)
    consts[0:64, 448:512] = (s <= t)
    consts[0:64, 512:576] = (s > t)
    sm = np.ones(512, np.float32)
    sm[0::64] = 0.0
    consts[:, 576:1088] = sm[None, :]
    lamb = np.concatenate([g("lam_q1"), g("lam_k1"), g("lam_q2"), g("lam_k2")])[None, :].repeat(128, 0)
    x = np.asarray(inp["x"], np.float32)
    p = np.asarray(inp["p"], np.float32)[0]
    shared = {
        "cols": cols, "lamb": np.ascontiguousarray(lamb),
        "w_in": tile_w(g("w_in"), 512), "w2": g("rwkv_w2"), "a2": g("rwkv_a2"), "g2": g("rwkv_g2"),
        "w_a": tile_w(g("w_branch_a"), 256), "w_b": tile_w(g("w_branch_b"), 256), "w_o": tile_w(g("w_out"), 512),
        "w_fi": np.ascontiguousarray(np.concatenate([tile_w(g("w_ffn_in")[:, :C.DFF], 128), tile_w(g("w_ffn_in")[:, C.DFF:], 128)], axis=3)),
        "w_fo": tile_w(g("w_ffn_out"), 256), "w_pg": tile_w(g("w_ple_gate"), 512), "w_pp": tile_w(g("w_ple_proj"), 512),
    }
    maps = []
    for core in range(2 * C.B):
        b, hf = core // 2, core % 2
        m = dict(shared)
        if hf == 1:
            m["xc"] = np.ascontiguousarray(x[b])
        else:
            xc = np.zeros((C.CTX, C.D), np.float32)
            xc[C.HALF:] = x[b, 0:C.HALF]
            m["xc"] = xc
        m["pc"] = np.ascontiguousarray(p[b, hf * C.HALF:(hf + 1) * C.HALF])
        cc = consts.copy()
        cc[:, 1088:1216] = float(hf)
        m["consts"] = cc
        maps.append(m)
    return maps


_PROG = {}


def kernel(**inputs):
    C = Cfg()
    if "full" not in _PROG:
        _PROG["full"] = build_program(C, upto="ALL")
    nc, K = _PROG["full"]
    maps = host_inputs(C, inputs)
    res = run_bass_kernel_spmd(nc, maps, core_ids=list(range(2 * C.B)))
    out = np.zeros((C.B, C.SEQ, C.D), np.float32)
    for core in range(2 * C.B):
        b, hf = core // 2, core % 2
        out[b, hf * C.HALF:(hf + 1) * C.HALF] = res.results[core]["out"]
    return out
```

```python
import contextlib
import math
import numpy as np
import concourse.bass as bass
import concourse.mybir as mybir
from concourse.bass_utils import run_bass_kernel_spmd

F32 = mybir.dt.float32
BF16 = mybir.dt.bfloat16
AF = mybir.ActivationFunctionType
ALU = mybir.AluOpType


class Buf:
    __slots__ = ("name", "w", "r", "g")

    def __init__(self, name):
        self.name = name
        self.w = {}
        self.r = {}
        self.g = None


class Tl:
    def __init__(self, h, name):
        self.h = h
        self.b = Buf(name)


class Rot:
    def __init__(self, tiles):
        self.t = tiles
        self.i = 0

    def get(self):
        t = self.t[self.i % len(self.t)]
        self.i += 1
        return t


class Sched:
    ENGS = ("pe", "act", "dve", "pool", "sp")
    import os
    NDS = int(os.environ.get('NDS', 6))

    def __init__(self, nc, es):
        self.nc = nc
        self.streams = {e: [] for e in self.ENGS}
        self.cnt = {e: 0 for e in self.ENGS}
        self.sems = {}
        for e in self.ENGS:
            self.sems[e] = es.enter_context(nc.semaphore("s_" + e))
        self.dq = {}
        self.dlast = {}
        for q in ("sp", "act", "pool"):
            self.dq[q] = 0
            for i in range(self.NDS):
                self.sems[("d", q, i)] = es.enter_context(nc.semaphore("d_%s_%d" % (q, i)))
        self.known = {}
        self.final = []
        self.rr = 0
        self.log = []

    def _wait(self, e, key, val):
        if key == "pe" and e == "pe":
            return
        if self.known.get((e, key), 0) >= val:
            return
        self.known[(e, key)] = val
        self.log.append((e, "wait", key, val))
        sem = self.sems[key]
        self.streams[e].append(lambda eng, sem=sem, val=val: eng.wait_ge(sem, val))

    def _deps(self, e, r, w, wa):
        for b in r:
            for k, v in b.w.items():
                self._wait(e, k, v)
        for b in w:
            for k, v in b.w.items():
                self._wait(e, k, v)
            for k, v in b.r.items():
                self._wait(e, k, v)
        for b in wa:
            if b.g is not None:
                self._wait(e, b.g[0], b.g[1])
            for k, v in b.r.items():
                self._wait(e, k, v)

    def _mark(self, tok, r, w, wa):
        k, v = tok
        for b in r:
            if b.r.get(k, 0) < v:
                b.r[k] = v
        for b in w:
            b.w = {k: v}
            b.r = {}
            b.g = tok
        for b in wa:
            if b.w.get(k, 0) < v:
                b.w[k] = v

    def op(self, e, fn, r=(), w=(), wa=(), inc=True):
        self._deps(e, r, w, wa)
        sem = self.sems[e]
        if inc:
            self.cnt[e] += 1
            tok = (e, self.cnt[e])
            self.streams[e].append(lambda eng, fn=fn, sem=sem: fn(eng).then_inc(sem, 1))
        else:
            tok = (e, self.cnt[e] + 1)
            self.streams[e].append(lambda eng, fn=fn: fn(eng))
        self._mark(tok, r, w, wa)
        self.log.append((e, "op", tok, inc))
        return tok

    def dma(self, q, out, in_, r=(), w=(), wa=(), final=False):
        j = self.dq[q]
        self.dq[q] += 1
        slot = j % self.NDS
        key = ("d", q, slot)
        val = 16 * (j // self.NDS + 1)
        if j >= self.NDS:
            self._wait(q, key, val - 16)
        self._deps(q, r, w, wa)
        sem = self.sems[key]
        self.streams[q].append(
            lambda eng, out=out, in_=in_, sem=sem: eng.dma_start(out=out, in_=in_).then_inc(sem, 16))
        tok = (key, val)
        self.log.append((q, "dma", tok))
        self.dlast[key] = val
        self._mark(tok, r, w, wa)
        if final:
            self.final.append(tok)
        return tok

    def act(self, out, in_, func, r=(), w=(), wa=(), **kw):
        return self.op("act", lambda e: e.activation(out=out, in_=in_, func=func, **kw), r, w, wa)

    def ts(self, eng, out, in0, s1, s2, op0, op1=None, r=(), w=(), wa=()):
        if op1 is None:
            return self.op(eng, lambda e: e.tensor_scalar(out=out, in0=in0, scalar1=s1, scalar2=None, op0=op0), r, w, wa)
        return self.op(eng, lambda e: e.tensor_scalar(out=out, in0=in0, scalar1=s1, scalar2=s2, op0=op0, op1=op1), r, w, wa)

    def tt(self, eng, out, in0, in1, op, r=(), w=(), wa=()):
        return self.op(eng, lambda e: e.tensor_tensor(out=out, in0=in0, in1=in1, op=op), r, w, wa)

    def stt(self, eng, out, in0, scalar, in1, op0, op1, r=(), w=(), wa=()):
        return self.op(eng, lambda e: e.scalar_tensor_tensor(out=out, in0=in0, scalar=scalar, in1=in1, op0=op0, op1=op1), r, w, wa)

    def copy(self, eng, out, in_, r=(), w=(), wa=()):
        if eng == "act":
            return self.act(out, in_, AF.Identity, r, w, wa)
        return self.op(eng, lambda e: e.tensor_copy(out=out, in_=in_), r, w, wa)

    def memset(self, eng, ap, val, w=(), wa=()):
        return self.op(eng, lambda e: e.memset(ap, val), (), w, wa)

    def recip(self, out, in_, r=(), w=(), wa=()):
        return self.op("dve", lambda e: e.reciprocal(out=out, in_=in_), r, w, wa)

    def scan(self, out, d0, d1, init, op0, op1, r=(), w=(), wa=()):
        return self.op("dve", lambda e: e.tensor_tensor_scan(out=out, data0=d0, data1=d1, initial=init, op0=op0, op1=op1), r, w, wa)

    def transpose(self, out, in_, ident, r=(), w=(), wa=()):
        return self.op("pe", lambda e: e.transpose(out, in_, ident), r, w, wa)

    def mms(self, items, r=(), w=(), wa=()):
        self._deps("pe", r, w, wa)
        n = len(items)
        sem = self.sems["pe"]
        self.cnt["pe"] += 1
        tok = ("pe", self.cnt["pe"])
        for i, (o, l, rh, st, sp) in enumerate(items):
            if i == n - 1:
                self.streams["pe"].append(
                    lambda eng, o=o, l=l, rh=rh, st=st, sp=sp, sem=sem: eng.matmul(o, l, rh, start=st, stop=sp).then_inc(sem, 1))
            else:
                self.streams["pe"].append(
                    lambda eng, o=o, l=l, rh=rh, st=st, sp=sp: eng.matmul(o, l, rh, start=st, stop=sp))
        self._mark(tok, r, w, wa)
        return tok

    def ev(self):
        self.rr += 1
        return "act" if self.rr % 2 else "dve"

    def emit(self, last=False):
        nc = self.nc
        for e in self.ENGS:
            for f in self.ENGS:
                if f != e and self.cnt[f] > 0:
                    self._wait(e, f, self.cnt[f])
            for key, val in self.dlast.items():
                self._wait(e, key, val)
        self.final = []
        streams = self.streams
        self.streams = {e: [] for e in self.ENGS}
        with nc.Block() as block:
            @block.tensor
            def _(eng):
                for f in streams["pe"]:
                    f(eng)

            @block.scalar
            def _(eng):
                for f in streams["act"]:
                    f(eng)

            @block.vector
            def _(eng):
                for f in streams["dve"]:
                    f(eng)

            @block.gpsimd
            def _(eng):
                for f in streams["pool"]:
                    f(eng)

            @block.sync
            def _(eng):
                for f in streams["sp"]:
                    f(eng)


class Cfg:
    def __init__(self, D=4096, SEQ=4096, B=4, PLE=256):
        self.D, self.SEQ, self.B, self.PLE = D, SEQ, B, PLE
        self.EPS = 1e-6
        self.RW = D // 2
        self.NH = self.RW // 64
        self.NHC = self.RW // 128
        self.DL = max(32, int(round(1.8 * self.RW ** 0.5 / 32)) * 32)
        self.AL = max(32, int(round(2.5 * self.RW ** 0.5 / 32)) * 32)
        self.GL = max(32, int(round(0.6 * self.RW ** 0.8 / 32)) * 32)
        self.GN_EPS = 64e-5
        self.RC = 3 * self.RW + self.DL + self.AL + self.GL
        self.DW = D // 2
        self.NDH = self.DW // 128
        self.DCOL = 3 * self.DW
        self.IC = self.RC + self.DCOL + 2 * D
        self.DFF = int(round(8 * D / 3 / 256)) * 256
        self.HALF = SEQ // 2
        self.CTX = SEQ
        self.HALO = 128
        self.OWN0 = self.HALF - self.HALO
        self.NT = min(512, self.HALF)
        self.NOH = self.HALF + self.HALO
        self.o_r, self.o_k, self.o_v = 0, self.RW, 2 * self.RW
        self.o_wl = 3 * self.RW
        self.o_al = self.o_wl + self.DL
        self.o_gl = self.o_al + self.AL
        self.o_q = self.RC
        self.o_dk = self.RC + self.DW
        self.o_dv = self.RC + 2 * self.DW
        self.o_ga = self.RC + self.DCOL
        self.o_gb = self.o_ga + D
        self.lambda_init = 0.8 - 0.6 * math.exp(-0.3 * 0)
        self.colmap = {}
        off = 0
        dc = D // 128
        hc = self.RW // 128
        fc = 2 * self.DFF // 128
        for name, n in [("g_mix", dc), ("g_ffn", dc), ("g_ple", dc),
                        ("mu_r", hc), ("mu_k", hc), ("mu_v", hc), ("mu_w", 1), ("mu_a", 1), ("mu_g", (self.GL + 127) // 128),
                        ("w0", hc), ("a0", hc), ("k_k", hc), ("k_a", hc), ("r_k", hc), ("ln_w", hc), ("ln_b", hc),
                        ("q_g", 1), ("k_g", 1), ("subln", 1),
                        ("cw0", fc), ("cw1", fc), ("cw2", fc), ("cb", fc)]:
            self.colmap[name] = (off, n)
            off += n
        self.NCOLS = off
        self.cm = {"ident": (0, 128), "blk": (128, 128), "ones": (256, 128), "mUs": (384, 64), "mUi": (448, 64),
                   "mLs": (512, 64), "scan": (576, 512), "pvalid": (1088, 128)}
        self.NCONST = 1216

    def tiles_oh(self):
        t = [(self.OWN0, self.HALO)]
        for i in range(self.HALF // self.NT):
            t.append((self.HALF + i * self.NT, self.NT))
        return t


class Ctx:
    pass


def sbt(es, nc, name, shape, dt):
    return Tl(es.enter_context(nc.sbuf_tensor(name, list(shape), dt)), name)


def sbrot(es, nc, name, shape, dt, n):
    return Rot([sbt(es, nc, "%s%d" % (name, i), shape, dt) for i in range(n)])


_PSN = [0]


def psrot(es, nc, n=8):
    _PSN[0] += 1
    return Rot([Tl(es.enter_context(nc.psum_tensor("ps%d_%d" % (_PSN[0], i), [128, 512], F32)), "ps%d" % i) for i in range(n)])


def chunks(total, size):
    return [(i, min(size, total - i)) for i in range(0, total, size)]


def load_cols(K, es, name, names):
    S, C, nc = K.S, K.C, K.nc
    lo = min(C.colmap[x][0] for x in names)
    hi = max(C.colmap[x][0] + C.colmap[x][1] for x in names)
    t = sbt(es, nc, name, [128, hi - lo], F32)
    if hi - lo == 1:
        with nc.allow_non_contiguous_dma(reason="single column"):
            pass
    S.dma("sp", t.h[:, :], K.d["cols"][:, lo:hi], w=[t.b])
    return t, {x: C.colmap[x][0] - lo for x in names}


def load_const(K, es, name, key, rows=128, dt=F32):
    S, C, nc = K.S, K.C, K.nc
    o, n = C.cm[key]
    t = sbt(es, nc, name, [rows, n], dt)
    q = "sp" if dt == F32 else "pool"
    S.dma(q, t.h[:, :], K.d["consts"][0:rows, o:o + n], w=[t.b])
    return t


def build_hT(K, R, src, t0, nt, gt, goff, hT):
    S, C = K.S, K.C
    DC = C.D // 128
    import os
    for s in range(min(nt // 128, int(os.environ.get("KSTOP", "99")))):
        xs = R.xs.get()
        S.dma("sp", xs.h[:, :], src[t0 + s * 128: t0 + (s + 1) * 128, :], w=[xs.b])
        st = R.st.get()
        S.memset("pool", st.h[:, 0:1], 0.0, w=[st.b])
        S.act(R.junk.h[:, :], xs.h[:, :], AF.Square, r=[xs.b], w=[R.junk.b, st.b], accum_out=st.h[:, 0:1])
        S.ts("dve", st.h[:, 1:2], st.h[:, 0:1], 1.0 / C.D, C.EPS, ALU.mult, ALU.add, r=[st.b], w=[st.b])
        S.act(st.h[:, 2:3], st.h[:, 1:2], AF.Ln, r=[st.b], w=[st.b])
        S.act(st.h[:, 3:4], st.h[:, 2:3], AF.Exp, r=[st.b], w=[st.b], scale=-0.5)
        S.ts("dve", xs.h[:, :], xs.h[:, :], st.h[:, 3:4], None, ALU.mult, r=[xs.b, st.b], w=[xs.b])
        import os
        if os.environ.get("SKIPT"):
            continue
        for c0 in range(0, DC, 4):
            ps = K.ps.get()
            n = min(4, DC - c0)
            for j in range(n):
                S.transpose(ps.h[:, j * 128:(j + 1) * 128], xs.h[:, (c0 + j) * 128:(c0 + j + 1) * 128], R.ident.h[:, :],
                            r=[xs.b, R.ident.b], w=[ps.b] if j == 0 else [], wa=[] if j == 0 else [ps.b])
            eng = S.ev()
            for j in range(n):
                c = c0 + j
                o = hT.h[:, c, s * 128:(s + 1) * 128]
                i = ps.h[:, j * 128:(j + 1) * 128]
                g = gt.h[:, goff + c:goff + c + 1]
                if eng == "act":
                    S.act(o, i, AF.Identity, r=[gt.b], w=[ps.b], wa=[hT.b], scale=g)
                else:
                    S.ts("dve", o, i, g, None, ALU.mult, r=[gt.b], w=[ps.b], wa=[hT.b])


def norm_res(K, es, pfx, nxs=3, junk=None):
    R = Ctx()
    nc, C = K.nc, K.C
    R.xs = sbrot(es, nc, pfx + "xs", [128, C.D], F32, nxs)
    R.junk = junk if junk is not None else sbt(es, nc, pfx + "junk", [128, C.D], BF16)
    R.st = sbrot(es, nc, pfx + "st", [128, 4], F32, 4)
    R.ident = load_const(K, es, pfx + "ident", "ident")
    return R


def precast_list(K):
    out = []
    for nm in ("w_a", "w_b", "w_o", "w_fi", "w_fo", "w_pg"):
        src, dst = K.d[nm], K.d["bf_" + nm]
        for g in range(src.shape[0]):
            out.append((dst[g], src[g], K.b["bf_" + nm]))
    return out


def precast_step(K, n=1):
    for _ in range(n):
        if K.pc:
            dst, src, buf = K.pc.pop(0)
            K.S.dma("pool", dst, src, wa=[buf])


def load_w(K, wt, w4, g, c0, cn, width, kstep=8, rbuf=None):
    if rbuf is not None:
        kstep = 32
    S = K.S
    v = w4[g]
    first = True
    for k0 in range(0, cn, kstep):
        kn = min(kstep, cn - k0)
        S.dma("pool", wt.h[:, k0:k0 + kn, 0:width], v[:, c0 + k0:c0 + k0 + kn, 0:width], r=[rbuf] if rbuf is not None else [],
              w=[wt.b] if first else [], wa=[] if first else [wt.b])
        first = False


def z_store(K, col, w, zt, t0, nt):
    S, C = K.S, K.C
    regions = [("zR", 0, C.RC, 0), ("zQ", C.o_q, C.o_dk, C.OWN0), ("zKV", C.o_dk, C.o_ga, 0), ("zG", C.o_ga, C.IC, C.OWN0)]
    for (nm, c0, c1, tk0) in regions:
        a, b = max(col, c0), min(col + w, c1)
        if a >= b:
            continue
        ta = max(t0, tk0)
        if ta >= t0 + nt:
            continue
        S.dma("sp", K.d[nm][a - c0:b - c0, ta - tk0:t0 + nt - tk0], zt.h[a - col:b - col, ta - t0:nt], r=[zt.b], wa=[K.b[nm]])


def phase_A(K):
    S, C, nc = K.S, K.C, K.nc
    DC = C.D // 128
    with contextlib.ExitStack() as es:
        K.ps = psrot(es, nc)
        R = norm_res(K, es, "A")
        gt, gm = load_cols(K, es, "Ag", ["g_mix"])
        hT = sbt(es, nc, "AhT", [128, DC, C.NT], BF16)
        wpool = sbrot(es, nc, "Aw", [128, DC, 512], BF16, 2)
        zpool = sbrot(es, nc, "Az", [128, C.NT], F32, 3)
        ntile = C.CTX // C.NT
        first_full = C.OWN0 // C.NT
        NGA = (C.IC + 511) // 512
        for tt in range(ntile):
            t0 = tt * C.NT
            build_hT(K, R, K.d["xc"], t0, C.NT, gt, gm["g_mix"], hT)
            if tt >= first_full:
                groups = list(range(NGA))
            else:
                groups = [g for g in range(NGA) if (g * 512 < C.RC) or (g * 512 + 512 > C.o_dk and g * 512 < C.o_ga)]
            for g in groups:
                col0 = g * 512
                gw = min(512, C.IC - col0)
                wt = wpool.get()
                load_w(K, wt, K.d["w_in"], g, 0, DC, 512)
                for (j0, wj) in chunks(gw, 128):
                    ps = K.ps.get()
                    items = [(ps.h[0:wj, 0:C.NT], wt.h[:, k, j0:j0 + wj], hT.h[:, k, 0:C.NT], k == 0, k == DC - 1)
                             for k in range(DC)]
                    S.mms(items, r=[wt.b, hT.b], w=[ps.b])
                    zt = zpool.get()
                    S.copy(S.ev(), zt.h[0:wj, :], ps.h[0:wj, 0:C.NT], w=[zt.b, ps.b])
                    z_store(K, col0 + j0, wj, zt, t0, C.NT)
        K.rem.append(nc.sbuf_bytes_remaining)
        S.emit()


def phase_B(K):
    S, C, nc = K.S, K.C, K.nc
    NHC = C.NHC
    TB = min(256, C.HALF)
    NCH = TB // 64
    HG = min(4, NHC)
    NG = NHC // HG
    GW = HG * 128
    GLC = (C.GL + 127) // 128
    zT = K.d["zR"]
    with contextlib.ExitStack() as es:
        K.ps = psrot(es, nc)
        ident = load_const(K, es, "Bident", "ident")
        blk = load_const(K, es, "Bblk", "blk")
        mUs = load_const(K, es, "BmUs", "mUs", rows=64)
        mUi = load_const(K, es, "BmUi", "mUi", rows=64)
        mLs = load_const(K, es, "BmLs", "mLs", rows=64)
        scanm = load_const(K, es, "Bscan", "scan")
        identb = load_const(K, es, "Bidentb", "ident", dt=BF16)
        names = ["mu_r", "mu_k", "mu_v", "mu_w", "mu_a", "mu_g", "w0", "a0", "k_k", "k_a", "r_k", "ln_w", "ln_b"]
        ct, co = load_cols(K, es, "Bcols", names)
        nmu = 3 * NHC + 2 + GLC
        omu = sbt(es, nc, "Bomu", [128, nmu], F32)
        S.ts("dve", omu.h[:, :], ct.h[:, 0:nmu], -1.0, 1.0, ALU.mult, ALU.add, r=[ct.b], w=[omu.b])
        nw0 = sbt(es, nc, "Bnw0", [128, NHC], F32)
        S.ts("dve", nw0.h[:, :], ct.h[:, co["w0"]:co["w0"] + NHC], -1.0, None, ALU.mult, r=[ct.b], w=[nw0.b])
        na0 = sbt(es, nc, "Bna0", [128, NHC], F32)
        S.ts("dve", na0.h[:, :], ct.h[:, co["a0"]:co["a0"] + NHC], -1.0, None, ALU.mult, r=[ct.b], w=[na0.b])
        omka = sbt(es, nc, "Bomka", [128, NHC], F32)
        S.ts("dve", omka.h[:, :], ct.h[:, co["k_a"]:co["k_a"] + NHC], -1.0, 1.0, ALU.mult, ALU.add, r=[ct.b], w=[omka.b])
        cm05 = sbt(es, nc, "Bcm05", [128, 1], F32)
        S.memset("pool", cm05.h[:, :], -0.5, w=[cm05.b])
        c1 = sbt(es, nc, "Bc1", [128, 1], F32)
        S.memset("pool", c1.h[:, :], 1.0, w=[c1.b])
        lw = sbrot(es, nc, "Blw", [128, 2 + GLC, 128], F32, 2)
        wl = sbt(es, nc, "Bwl", [128, TB + 1], F32)
        al = sbt(es, nc, "Bal", [128, TB + 1], F32)
        gl = sbt(es, nc, "Bgl", [128, GLC, TB + 1], F32)
        tw = sbt(es, nc, "Btw", [128, TB], F32)
        als = sbt(es, nc, "Bals", [128, TB], F32)
        sg = sbt(es, nc, "Bsg", [128, GLC, TB], F32)
        tp = sbrot(es, nc, "Btp", [128, TB + 1], F32, 28)
        Rt, KKt, Bt, Kt, Vt, Bct, Kct = [sbt(es, nc, "B" + n, [128, NHC, TB], BF16) for n in ("Rt", "KKt", "Bt", "Kt", "Vt", "Bct", "Kct")]
        gC = sbt(es, nc, "BgC", [128, NHC, NCH], F32)
        OT = sbt(es, nc, "BOT", [128, NHC, TB], F32)
        H = sbt(es, nc, "BH", [128, NHC, 128], F32)
        Hb = sbt(es, nc, "BHb", [128, NHC, 128], BF16)
        bdp = [[sbt(es, nc, "Bbd%d_%d" % (i, j), [128, NHC, 128], BF16) for j in range(3)] for i in range(1)]
        for i in range(1):
            for j in range(3):
                S.memset("pool", bdp[i][j].h[:, :, :], 0.0, w=[bdp[i][j].b])
        Hbufs = [Buf("H%d" % g) for g in range(NG)]
        S.memset("pool", H.h[:, :, :], 0.0, w=[H.b] + Hbufs)
        S.memset("pool", Hb.h[:, :, :], 0.0, w=[Hb.b], wa=Hbufs)
        SLN = ("A0", "A1", "B0", "B1", "X0", "X1", "nM3", "M2", "M4", "Vm", "Bcm", "Kcm")
        slots = [{n: sbt(es, nc, "Bs%d%s" % (g, n), [64, GW], BF16) for n in SLN} for g in range(NG)]
        fp = sbrot(es, nc, "Bfp", [64, GW], F32, 2)
        oabp = sbrot(es, nc, "Boab", [128, TB], BF16, 2)

        def bc64(t):
            return t.h[0:64, 0:64].unsqueeze(1).broadcast_to([64, 2 * HG, 64])

        def v3(ap):
            return ap.rearrange("p (h t) -> p h t", t=64)

        def load_shift(dst_ap_fn, rows, row0, t0, buf, q="sp"):
            if t0 == 0:
                S.memset("pool", dst_ap_fn(0, 1), 0.0, w=[buf])
                S.dma(q, dst_ap_fn(1, TB + 1), zT[row0:row0 + rows, 0:TB], r=[K.b["zR"]], wa=[buf])
            else:
                S.dma(q, dst_ap_fn(0, TB + 1), zT[row0:row0 + rows, t0 - 1:t0 + TB], r=[K.b["zR"]], w=[buf])

        def lerp(eng, out_ap, zt_prev, zt_cur, mu_ap, omu_ap, rbufs, wbuf):
            t = tp.get()
            n = out_ap.shape[0]
            if eng == "act":
                S.act(t.h[0:n, 0:TB], zt_prev, AF.Identity, r=rbufs, w=[t.b], scale=mu_ap)
            else:
                S.ts(eng, t.h[0:n, 0:TB], zt_prev, mu_ap, None, ALU.mult, r=rbufs, w=[t.b])
            S.stt("dve", out_ap, zt_cur, omu_ap, t.h[0:n, 0:TB], ALU.mult, ALU.add, r=rbufs + [t.b], w=[wbuf])

        ntile = C.CTX // TB
        for tt in range(ntile):
            t0 = tt * TB
            need3 = (t0 + TB > C.OWN0)
            lo = max(C.OWN0 - t0, 0)
            load_shift(lambda a, b: wl.h[0:C.DL, a:b], C.DL, C.o_wl, t0, wl.b)
            load_shift(lambda a, b: al.h[0:C.AL, a:b], C.AL, C.o_al, t0, al.b)
            for c in range(GLC):
                n = min(128, C.GL - c * 128)
                load_shift(lambda a, b, c=c, n=n: gl.h[0:n, c, a:b], n, C.o_gl + c * 128, t0, gl.b)
            cw, ca, cg = co["mu_w"], co["mu_a"], co["mu_g"]
            lerp("dve", tw.h[0:C.DL, :], wl.h[0:C.DL, 0:TB], wl.h[0:C.DL, 1:TB + 1], ct.h[0:C.DL, cw:cw + 1], omu.h[0:C.DL, cw:cw + 1], [wl.b, ct.b, omu.b], tw.b)
            S.act(tw.h[0:C.DL, :], tw.h[0:C.DL, :], AF.Tanh, w=[tw.b])
            lerp("dve", als.h[0:C.AL, :], al.h[0:C.AL, 0:TB], al.h[0:C.AL, 1:TB + 1], ct.h[0:C.AL, ca:ca + 1], omu.h[0:C.AL, ca:ca + 1], [al.b, ct.b, omu.b], als.b)
            for c in range(GLC):
                n = min(128, C.GL - c * 128)
                lerp("dve", sg.h[0:n, c, :], gl.h[0:n, c, 0:TB], gl.h[0:n, c, 1:TB + 1], ct.h[0:n, cg + c:cg + c + 1], omu.h[0:n, cg + c:cg + c + 1], [gl.b, ct.b, omu.b], sg.b)
                S.act(sg.h[0:n, c, :], sg.h[0:n, c, :], AF.Sigmoid, w=[sg.b])
            for hc in range(NHC):
                cs = slice(hc * 128, (hc + 1) * 128)
                col = lambda nm: ct.h[:, co[nm] + hc:co[nm] + hc + 1]
                zs = []
                for (o_, nm) in ((C.o_r, "mu_r"), (C.o_k, "mu_k"), (C.o_v, "mu_v")):
                    zt_ = tp.get()
                    load_shift(lambda a, b, zt_=zt_: zt_.h[:, a:b], 128, o_ + hc * 128, t0, zt_.b)
                    out = tp.get()
                    mo = co[nm] + hc
                    lerp("act", out.h[:, 0:TB], zt_.h[:, 0:TB], zt_.h[:, 1:TB + 1], ct.h[:, mo:mo + 1], omu.h[:, mo:mo + 1], [zt_.b, ct.b, omu.b], out.b)
                    zs.append(out)
                r_s, k_s, v_s = zs
                X = slice(0, TB)
                precast_step(K, 1)
                lwt = lw.get()
                S.dma("sp", lwt.h[0:C.DL, 0, :], K.d["w2"][:, cs], w=[lwt.b])
                S.dma("sp", lwt.h[0:C.AL, 1, :], K.d["a2"][:, cs], wa=[lwt.b])
                for c in range(GLC):
                    n = min(128, C.GL - c * 128)
                    S.dma("sp", lwt.h[0:n, 2 + c, :], K.d["g2"][c * 128:c * 128 + n, cs], wa=[lwt.b])
                ps = K.ps.get()
                S.mms([(ps.h[:, 0:TB], lwt.h[0:C.DL, 0, :], tw.h[0:C.DL, :], True, True)], r=[lwt.b, tw.b], w=[ps.b])
                e1 = tp.get()
                S.act(e1.h[:, X], ps.h[:, 0:TB], AF.Exp, r=[nw0.b], w=[e1.b, ps.b], scale=-1.0, bias=nw0.h[:, hc:hc + 1])
                S.act(e1.h[:, X], e1.h[:, X], AF.Ln, r=[c1.b], w=[e1.b], bias=c1.h[:, 0:1])
                elw = tp.get()
                S.act(elw.h[:, X], e1.h[:, X], AF.Exp, r=[e1.b, cm05.b], w=[elw.b], scale=-1.0, bias=cm05.h[:, 0:1])
                ps = K.ps.get()
                S.mms([(ps.h[:, 0:TB], lwt.h[0:C.AL, 1, :], als.h[0:C.AL, :], True, True)], r=[lwt.b, als.b], w=[ps.b])
                a_ = tp.get()
                S.act(a_.h[:, X], ps.h[:, 0:TB], AF.Exp, r=[na0.b], w=[a_.b, ps.b], scale=-1.0, bias=na0.h[:, hc:hc + 1])
                S.ts("dve", a_.h[:, X], a_.h[:, X], 1.0, None, ALU.add, w=[a_.b])
                S.recip(a_.h[:, X], a_.h[:, X], w=[a_.b])
                if need3:
                    ps = K.ps.get()
                    items = []
                    for c in range(GLC):
                        n = min(128, C.GL - c * 128)
                        items.append((ps.h[:, 0:TB], lwt.h[0:n, 2 + c, :], sg.h[0:n, c, :], c == 0, c == GLC - 1))
                    S.mms(items, r=[lwt.b, sg.b], w=[ps.b])
                    gt_ = tp.get()
                    S.copy("act", gt_.h[:, X], ps.h[:, 0:TB], w=[gt_.b, ps.b])
                    S.dma("sp", K.d["gT"][cs, t0:t0 + TB], gt_.h[:, X], r=[gt_.b], wa=[K.b["gT"]])
                kk = tp.get()
                S.act(kk.h[:, X], k_s.h[:, X], AF.Identity, r=[k_s.b, ct.b], w=[kk.b], scale=col("k_k"))
                kk2 = tp.get()
                S.act(kk2.h[:, X], kk.h[:, X], AF.Square, r=[kk.b], w=[kk2.b])
                ps = K.ps.get()
                S.mms([(ps.h[:, 0:TB], blk.h[:, :], kk2.h[:, X], True, True)], r=[blk.b, kk2.b], w=[ps.b])
                rn = tp.get()
                S.ts("dve", rn.h[:, X], ps.h[:, 0:TB], 1e-24, None, ALU.max, w=[rn.b, ps.b])
                S.act(rn.h[:, X], rn.h[:, X], AF.Ln, w=[rn.b])
                S.act(rn.h[:, X], rn.h[:, X], AF.Exp, w=[rn.b], scale=-0.5)
                kkn = tp.get()
                S.tt("dve", kkn.h[:, X], kk.h[:, X], rn.h[:, X], ALU.mult, r=[kk.b, rn.b], w=[kkn.b])
                t1 = tp.get()
                S.ts("dve", t1.h[:, X], a_.h[:, X], col("k_a"), omka.h[:, hc:hc + 1], ALU.mult, ALU.add, r=[a_.b, ct.b, omka.b], w=[t1.b])
                kmod = tp.get()
                S.tt("pool", kmod.h[:, X], k_s.h[:, X], t1.h[:, X], ALU.mult, r=[k_s.b, t1.b], w=[kmod.b])
                b_ = tp.get()
                S.tt("pool", b_.h[:, X], kkn.h[:, X], a_.h[:, X], ALU.mult, r=[kkn.b, a_.b], w=[b_.b])
                if need3:
                    rkr = tp.get()
                    S.stt("dve", rkr.h[:, X], r_s.h[:, X], col("r_k"), kmod.h[:, X], ALU.mult, ALU.mult, r=[r_s.b, ct.b, kmod.b], w=[rkr.b])
                    ps = K.ps.get()
                    S.mms([(ps.h[:, 0:TB], blk.h[:, :], rkr.h[:, X], True, True)], r=[blk.b, rkr.b], w=[ps.b])
                    bon = tp.get()
                    S.tt("dve", bon.h[:, X], ps.h[:, 0:TB], v_s.h[:, X], ALU.mult, r=[v_s.b], w=[bon.b, ps.b])
                    S.dma("sp", K.d["bonT"][cs, t0:t0 + TB], bon.h[:, X], r=[bon.b], wa=[K.b["bonT"]])
                cum = tp.get()
                S.scan(cum.h[:, X], scanm.h[:, 0:TB], elw.h[:, X], 0.0, ALU.mult, ALU.add, r=[scanm.b, elw.b], w=[cum.b])
                gi = tp.get()
                S.act(gi.h[:, X], cum.h[:, X], AF.Exp, r=[cum.b], w=[gi.b], scale=-1.0)
                ge = tp.get()
                S.act(ge.h[:, X], cum.h[:, X], AF.Exp, r=[cum.b], w=[ge.b])
                gx = tp.get()
                S.tt("pool", gx.h[:, X], cum.h[:, X], elw.h[:, X], ALU.subtract, r=[cum.b, elw.b], w=[gx.b])
                S.act(gx.h[:, X], gx.h[:, X], AF.Exp, w=[gx.b], scale=-1.0)
                S.tt("dve", Rt.h[:, hc, :], r_s.h[:, X], gi.h[:, X], ALU.mult, r=[r_s.b, gi.b], wa=[Rt.b])
                S.tt("pool", KKt.h[:, hc, :], kkn.h[:, X], gx.h[:, X], ALU.mult, r=[kkn.b, gx.b], wa=[KKt.b])
                tb_ = tp.get()
                S.tt("dve", tb_.h[:, X], b_.h[:, X], ge.h[:, X], ALU.mult, r=[b_.b, ge.b], w=[tb_.b])
                tk_ = tp.get()
                S.tt("pool", tk_.h[:, X], kmod.h[:, X], ge.h[:, X], ALU.mult, r=[kmod.b, ge.b], w=[tk_.b])
                S.copy("act", Bt.h[:, hc, :], tb_.h[:, X], r=[tb_.b], wa=[Bt.b])
                S.copy("act", Kt.h[:, hc, :], tk_.h[:, X], r=[tk_.b], wa=[Kt.b])
                S.copy("act", Vt.h[:, hc, :], v_s.h[:, X], r=[v_s.b], wa=[Vt.b])
                S.copy("dve", gC.h[:, hc, :], gi.h[:, X].rearrange("p (c t) -> p c t", t=64)[:, :, 63], r=[gi.b], wa=[gC.b])
                gcb = gC.h[:, hc, :].unsqueeze(2).broadcast_to([128, NCH, 64])
                S.stt("dve", Bct.h[:, hc, :].rearrange("p (c t) -> p c t", t=64), tb_.h[:, X].rearrange("p (c t) -> p c t", t=64), -1.0, gcb,
                      ALU.mult, ALU.mult, r=[tb_.b, gC.b], wa=[Bct.b])
                S.tt("pool", Kct.h[:, hc, :].rearrange("p (c t) -> p c t", t=64), tk_.h[:, X].rearrange("p (c t) -> p c t", t=64), gcb,
                     ALU.mult, r=[tk_.b, gC.b], wa=[Kct.b])
            import os
            for ci in range(NCH if not os.environ.get("BSKIP2") else 0):
                cc = slice(ci * 64, (ci + 1) * 64)
                hd = lambda g, hh: (g * HG + hh // 2, (hh % 2) * 64)
                NHG = 2 * HG
                st = [dict() for _ in range(NG)]
                bd = bdp[0]
                for j, src in enumerate((KKt, Rt, Bt)):
                    eng = ("pool", "act", "pool")[j]
                    S.copy(eng, bd[j].h[0:64, :, 0:64], src.h[0:64, :, cc], r=[src.b], w=[bd[j].b])
                    S.copy(eng, bd[j].h[64:128, :, 64:128], src.h[64:128, :, cc], r=[src.b], wa=[bd[j].b])
                KKbd, Rbd, Bbd = bd
                for g in range(NG):
                    d = st[g]
                    def prod(lT, rbd):
                        ps = K.ps.get()
                        items = []
                        for hl in range(HG):
                            hc = g * HG + hl
                            items.append((ps.h[0:64, hl * 128:(hl + 1) * 128], lT.h[:, hc, cc], rbd.h[:, hc, :], True, True))
                        S.mms(items, r=[lT.b, rbd.b], w=[ps.b])
                        return ps
                    ps = prod(Bt, KKbd)
                    sl_ = slots[g]
                    d["A"] = sl_["A0"]
                    S.tt("dve", v3(d["A"].h[:, :]), v3(ps.h[0:64, 0:GW]), bc64(mUs), ALU.mult, r=[mUs.b], w=[d["A"].b, ps.b])
                    bprep = int(os.environ.get("BPREP", "99"))
                    if bprep <= 1:
                        continue
                    d["X"] = sl_["X0"]
                    S.tt("pool", v3(d["X"].h[:, :]), bc64(ident), v3(d["A"].h[:, :]), ALU.subtract, r=[ident.b, d["A"].b], w=[d["X"].b])
                    if bprep <= 2:
                        continue
                    ps = prod(KKt, Bbd)
                    d["Bq"] = sl_["B0"]
                    S.tt("dve", v3(d["Bq"].h[:, :]), v3(ps.h[0:64, 0:GW]), bc64(mLs), ALU.mult, r=[mLs.b], w=[d["Bq"].b, ps.b])
                    ps = prod(Bt, Rbd)
                    d["nM3"] = sl_["nM3"]
                    S.stt("dve", v3(d["nM3"].h[:, :]), v3(ps.h[0:64, 0:GW]), -1.0, bc64(mUi), ALU.mult, ALU.mult, r=[mUi.b], w=[d["nM3"].b, ps.b])
                    ps = prod(Kt, KKbd)
                    d["M2"] = sl_["M2"]
                    S.tt("dve", v3(d["M2"].h[:, :]), v3(ps.h[0:64, 0:GW]), bc64(mUs), ALU.mult, r=[mUs.b], w=[d["M2"].b, ps.b])
                    ps = prod(Kt, Rbd)
                    d["M4"] = sl_["M4"]
                    S.tt("dve", v3(d["M4"].h[:, :]), v3(ps.h[0:64, 0:GW]), bc64(mUi), ALU.mult, r=[mUi.b], w=[d["M4"].b, ps.b])
                    if bprep <= 3:
                        continue
                    for nm, src in (("Vm", Vt), ("Bcm", Bct), ("Kcm", Kct)):
                        ps = K.ps.get()
                        items = []
                        for hl in range(HG):
                            hc = g * HG + hl
                            items.append((ps.h[0:64, hl * 128:(hl + 1) * 128], src.h[:, hc, cc], identb.h[:, :], True, True))
                        S.mms(items, r=[src.b, identb.b], w=[ps.b])
                        d[nm] = sl_[nm]
                        S.copy("act", d[nm].h[:, :], ps.h[0:64, 0:GW], w=[d[nm].b, ps.b])
                bstop = int(os.environ.get("BSTOP", "99"))
                if bstop <= 1:
                    continue
                for lvl in range(1, 6):
                    for g in range(NG):
                        d = st[g]
                        def hmm(l, r_):
                            ps = K.ps.get()
                            items = [(ps.h[0:64, hh * 64:(hh + 1) * 64], l.h[0:64, hh * 64:(hh + 1) * 64], r_.h[0:64, hh * 64:(hh + 1) * 64], True, True)
                                     for hh in range(NHG)]
                            S.mms(items, r=[l.b, r_.b], w=[ps.b])
                            return ps
                        psB = hmm(d["A"], d["Bq"])
                        nB = slots[g]["B%d" % (lvl % 2)]
                        S.copy("act", nB.h[:, :], psB.h[0:64, 0:GW], w=[nB.b, psB.b])
                        if lvl < 5:
                            psA = hmm(d["Bq"], d["A"])
                            nA = slots[g]["A%d" % (lvl % 2)]
                            S.copy("dve", nA.h[:, :], psA.h[0:64, 0:GW], w=[nA.b, psA.b])
                            d["A"] = nA
                        d["Bq"] = nB
                    for g in range(NG):
                        d = st[g]
                        ps = K.ps.get()
                        items = [(ps.h[0:64, hh * 64:(hh + 1) * 64], d["Bq"].h[0:64, hh * 64:(hh + 1) * 64], d["X"].h[0:64, hh * 64:(hh + 1) * 64], True, True)
                                 for hh in range(NHG)]
                        S.mms(items, r=[d["Bq"].b, d["X"].b], w=[ps.b])
                        nX = slots[g]["X%d" % (lvl % 2)]
                        S.tt("dve", nX.h[:, :], ps.h[0:64, 0:GW], d["X"].h[:, :], ALU.add, r=[d["X"].b], w=[nX.b, ps.b])
                        d["X"] = nX
                if bstop <= 2:
                    continue
                for g in range(NG):
                    d = st[g]
                    ps = K.ps.get()
                    items = []
                    for hl in range(HG):
                        hc = g * HG + hl
                        items.append((ps.h[0:64, hl * 128:(hl + 1) * 128], KKt.h[:, hc, cc], Hb.h[:, hc, :], True, False))
                        for e2 in range(2):
                            sl = slice(hl * 128 + e2 * 64, hl * 128 + e2 * 64 + 64)
                            items.append((ps.h[0:64, sl], d["M2"].h[0:64, sl], d["Vm"].h[0:64, sl], False, e2 == 1))
                    S.mms(items, r=[KKt.b, Hbufs[g], d["M2"].b, d["Vm"].b], w=[ps.b])
                    d["W"] = slots[g]["A1"]
                    S.copy("act", d["W"].h[:, :], ps.h[0:64, 0:GW], w=[d["W"].b, ps.b])
                for g in range(NG):
                    d = st[g]
                    ps = K.ps.get()
                    items = [(ps.h[0:64, hh * 64:(hh + 1) * 64], d["X"].h[0:64, hh * 64:(hh + 1) * 64], d["W"].h[0:64, hh * 64:(hh + 1) * 64], True, True)
                             for hh in range(NHG)]
                    S.mms(items, r=[d["X"].b, d["W"].b], w=[ps.b])
                    d["U"] = slots[g]["B0"]
                    S.copy("dve", d["U"].h[:, :], ps.h[0:64, 0:GW], w=[d["U"].b, ps.b])
                if bstop <= 3:
                    continue
                for g in range(NG):
                    d = st[g]
                    ps = K.ps.get()
                    items = []
                    for hl in range(HG):
                        hc = g * HG + hl
                        items.append((ps.h[0:64, hl * 128:(hl + 1) * 128], Rt.h[:, hc, cc], Hb.h[:, hc, :], True, False))
                        for e2 in range(2):
                            sl = slice(hl * 128 + e2 * 64, hl * 128 + e2 * 64 + 64)
                            items.append((ps.h[0:64, sl], d["nM3"].h[0:64, sl], d["U"].h[0:64, sl], False, False))
                            items.append((ps.h[0:64, sl], d["M4"].h[0:64, sl], d["Vm"].h[0:64, sl], False, e2 == 1))
                    S.mms(items, r=[Rt.b, Hbufs[g], d["nM3"].b, d["U"].b, d["M4"].b, d["Vm"].b], w=[ps.b])
                    if need3:
                        otm = fp.get()
                        S.copy("act", otm.h[:, :], ps.h[0:64, 0:GW], w=[otm.b, ps.b])
                        ps2 = K.ps.get()
                        for hl in range(HG):
                            S.transpose(ps2.h[:, hl * 64:(hl + 1) * 64], otm.h[0:64, hl * 128:(hl + 1) * 128], ident.h[0:64, 0:64],
                                        r=[otm.b, ident.b], w=[ps2.b] if hl == 0 else [], wa=[] if hl == 0 else [ps2.b])
                        S.copy("dve", OT.h[:, g * HG:(g + 1) * HG, cc], ps2.h[:, 0:HG * 64].rearrange("p (h t) -> p h t", t=64), w=[ps2.b], wa=[OT.b])
                    ps = K.ps.get()
                    items = []
                    for hl in range(HG):
                        sl = slice(hl * 128, (hl + 1) * 128)
                        o = ps.h[:, sl]
                        items.append((o, d["Bcm"].h[0:64, sl], d["U"].h[0:64, sl], True, False))
                        items.append((o, d["Kcm"].h[0:64, sl], d["Vm"].h[0:64, sl], False, True))
                    S.mms(items, r=[d["Bcm"].b, d["U"].b, d["Kcm"].b, d["Vm"].b], w=[ps.b])
                    for hh in range(NHG):
                        hc, pb = hd(g, hh)
                        hl = hh // 2
                        S.stt("dve", H.h[pb:pb + 64, hc, pb:pb + 64], H.h[pb:pb + 64, hc, pb:pb + 64], gC.h[pb:pb + 64, hc, ci:ci + 1],
                              ps.h[pb:pb + 64, hl * 128 + pb: hl * 128 + pb + 64], ALU.mult, ALU.add,
                              r=[gC.b], w=[Hbufs[g], ps.b])
                    S.copy("pool", Hb.h[:, g * HG:(g + 1) * HG, :], H.h[:, g * HG:(g + 1) * HG, :], w=[Hbufs[g]])
            if need3 and not os.environ.get("BSKIP3"):
                for hc in range(NHC):
                    cs = slice(hc * 128, (hc + 1) * 128)
                    col = lambda nm: ct.h[:, co[nm] + hc:co[nm] + hc + 1]
                    X = slice(0, TB)
                    gt_ = tp.get()
                    S.dma("sp", gt_.h[:, X], K.d["gT"][cs, t0:t0 + TB], r=[K.b["gT"]], w=[gt_.b])
                    bon = tp.get()
                    S.dma("sp", bon.h[:, X], K.d["bonT"][cs, t0:t0 + TB], r=[K.b["bonT"]], w=[bon.b])
                    ps = K.ps.get()
                    S.mms([(ps.h[:, 0:TB], blk.h[:, :], OT.h[:, hc, :], True, True)], r=[blk.b, OT.b], w=[ps.b])
                    cen = tp.get()
                    S.stt("dve", cen.h[:, X], ps.h[:, 0:TB], -1.0 / 64, OT.h[:, hc, :], ALU.mult, ALU.add, r=[OT.b], w=[cen.b, ps.b])
                    sq = tp.get()
                    S.act(sq.h[:, X], cen.h[:, X], AF.Square, r=[cen.b], w=[sq.b])
                    ps = K.ps.get()
                    S.mms([(ps.h[:, 0:TB], blk.h[:, :], sq.h[:, X], True, True)], r=[blk.b, sq.b], w=[ps.b])
                    rs = tp.get()
                    S.ts("dve", rs.h[:, X], ps.h[:, 0:TB], 1.0 / 64, C.GN_EPS, ALU.mult, ALU.add, w=[rs.b, ps.b])
                    S.act(rs.h[:, X], rs.h[:, X], AF.Ln, w=[rs.b])
                    S.act(rs.h[:, X], rs.h[:, X], AF.Exp, w=[rs.b], scale=-0.5)
                    y = tp.get()
                    S.tt("pool", y.h[:, X], cen.h[:, X], rs.h[:, X], ALU.mult, r=[cen.b, rs.b], w=[y.b])
                    S.ts("pool", y.h[:, X], y.h[:, X], col("ln_w"), col("ln_b"), ALU.mult, ALU.add, r=[ct.b], w=[y.b])
                    S.tt("pool", y.h[:, X], y.h[:, X], bon.h[:, X], ALU.add, r=[bon.b], w=[y.b])
                    S.tt("dve", y.h[:, X], y.h[:, X], gt_.h[:, X], ALU.mult, r=[gt_.b], w=[y.b])
                    obf = oabp.get()
                    S.copy("act", obf.h[:, :], y.h[:, X], r=[y.b], w=[obf.b])
                    S.dma("sp", K.d["oaT"][cs, t0 + lo - C.OWN0:t0 + TB - C.OWN0], obf.h[:, lo:TB], r=[obf.b], wa=[K.b["oaT"]])
        K.rem.append(nc.sbuf_bytes_remaining)
        S.emit()


def phase_C(K):
    S, C, nc = K.S, K.C, K.nc
    NKB = C.CTX // 128
    with contextlib.ExitStack() as es:
        allps = psrot(es, nc)
        K.ps = Rot(allps.t[0:4])
        acc = Rot(allps.t[4:8])
        ident = load_const(K, es, "Cident", "ident")
        blk = load_const(K, es, "Cblk", "blk")
        ones = load_const(K, es, "Cones", "ones")
        onesb = load_const(K, es, "Conesb", "ones", dt=BF16)
        pvb = load_const(K, es, "Cpvb", "pvalid", dt=BF16)
        ct, co = load_cols(K, es, "Ccols", ["q_g", "k_g", "subln"])
        lt = sbt(es, nc, "Clamb", [128, 256], F32)
        S.dma("sp", lt.h[:, :], K.d["lamb"][:, :], w=[lt.b])
        sc = sbt(es, nc, "Csc", [128, 12], F32)
        S.memset("pool", sc.h[:, :], 0.0, w=[sc.b])
        tmp = sbt(es, nc, "Cltmp", [128, 128], F32)
        S.tt("dve", tmp.h[:, 0:64], lt.h[:, 0:64], lt.h[:, 64:128], ALU.mult, r=[lt.b], w=[tmp.b])
        S.tt("dve", tmp.h[:, 64:128], lt.h[:, 128:192], lt.h[:, 192:256], ALU.mult, r=[lt.b], w=[tmp.b])
        S.act(tmp.h[:, 0:64], tmp.h[:, 0:64], AF.Identity, w=[tmp.b, sc.b], accum_out=sc.h[:, 0:1])
        S.act(tmp.h[:, 64:128], tmp.h[:, 64:128], AF.Identity, w=[tmp.b, sc.b], accum_out=sc.h[:, 1:2])
        S.act(sc.h[:, 2:4], sc.h[:, 0:2], AF.Exp, w=[sc.b])
        S.tt("dve", sc.h[:, 4:5], sc.h[:, 3:4], sc.h[:, 2:3], ALU.subtract, w=[sc.b])
        S.ts("dve", sc.h[:, 5:6], sc.h[:, 4:5], -C.lambda_init, None, ALU.add, w=[sc.b])
        neglam = sc.h[:, 5:6]
        S.ts("dve", sc.h[:, 6:7], ct.h[:, co["q_g"]:co["q_g"] + 1], 0.125, None, ALU.mult, r=[ct.b], w=[sc.b])
        S.ts("dve", sc.h[:, 7:8], ct.h[:, co["subln"]:co["subln"] + 1], 1.0 - C.lambda_init, None, ALU.mult, r=[ct.b], w=[sc.b])
        qg8, slg = sc.h[:, 6:7], sc.h[:, 7:8]
        kg = ct.h[:, co["k_g"]:co["k_g"] + 1]
        zq = sbrot(es, nc, "Czq", [128, C.NOH], F32, 2)
        zk = sbrot(es, nc, "Czk", [128, C.CTX], F32, 2)
        zv = sbrot(es, nc, "Czv", [128, C.CTX], F32, 2)
        qn = sbt(es, nc, "Cqn", [128, C.NOH], BF16)
        kn = [sbt(es, nc, "Ckn%d" % c, [128, C.CTX], BF16) for c in range(2)]
        S.memset("pool", kn[0].h[:, :], 0.0, w=[kn[0].b])
        S.memset("pool", kn[1].h[:, :], 0.0, w=[kn[1].b])
        Vtm = sbt(es, nc, "CVtm", [128, NKB, 128], BF16)
        tq = sbrot(es, nc, "Ctq", [128, 512], F32, 6)
        ptp = sbrot(es, nc, "Cpt", [128, 512], BF16, 4)
        ocp = sbrot(es, nc, "Coc", [128, 512], F32, 4)
        obp = sbrot(es, nc, "Cob", [128, 512], BF16, 2)

        def rmsn(src, n_tot, gcol, outs):
            for (c0, cn) in chunks(n_tot, 512):
                sq = tq.get()
                S.act(sq.h[:, 0:cn], src.h[:, c0:c0 + cn], AF.Square, r=[src.b], w=[sq.b])
                ps = K.ps.get()
                S.mms([(ps.h[:, 0:cn], blk.h[:, :], sq.h[:, 0:cn], True, True)], r=[blk.b, sq.b], w=[ps.b])
                rs = tq.get()
                S.ts("dve", rs.h[:, 0:cn], ps.h[:, 0:cn], 1.0 / 64, C.EPS, ALU.mult, ALU.add, w=[rs.b, ps.b])
                S.act(rs.h[:, 0:cn], rs.h[:, 0:cn], AF.Ln, w=[rs.b])
                S.act(rs.h[:, 0:cn], rs.h[:, 0:cn], AF.Exp, w=[rs.b], scale=-0.5)
                for (r0, r1, dst) in outs:
                    S.stt("dve", dst.h[r0:r1, c0:c0 + cn], src.h[r0:r1, c0:c0 + cn], gcol[r0:r1, :], rs.h[r0:r1, 0:cn], ALU.mult, ALU.mult,
                          r=[src.b, rs.b, sc.b, ct.b], wa=[dst.b])

        for h in range(C.NDH):
            rows = slice(h * 128, (h + 1) * 128)
            precast_step(K, (len(K.pc) + C.NDH - 1 - h) // (C.NDH - h) if K.pc else 0)
            q_ = zq.get()
            S.dma("sp", q_.h[:, :], K.d["zQ"][h * 128:(h + 1) * 128, 0:C.NOH], r=[K.b["zQ"]], w=[q_.b])
            k_ = zk.get()
            S.dma("sp", k_.h[:, :], K.d["zKV"][h * 128:(h + 1) * 128, 0:C.CTX], r=[K.b["zKV"]], w=[k_.b])
            v_ = zv.get()
            S.dma("sp", v_.h[:, :], K.d["zKV"][C.DW + h * 128:C.DW + (h + 1) * 128, 0:C.CTX], r=[K.b["zKV"]], w=[v_.b])
            S.memset("pool", qn.h[:, 0:1], 0.0, w=[qn.b])
            S.memset("pool", kn[0].h[0:64, 0:1], 0.0, w=[kn[0].b])
            S.memset("pool", kn[1].h[64:128, 0:1], 0.0, w=[kn[1].b])
            rmsn(q_, C.NOH, qg8, [(0, 128, qn)])
            rmsn(k_, C.CTX, kg, [(0, 64, kn[0]), (64, 128, kn[1])])
            first = True
            for kb0 in range(0, NKB, 4):
                ps = K.ps.get()
                n = min(4, NKB - kb0)
                for j in range(n):
                    S.transpose(ps.h[:, j * 128:(j + 1) * 128], v_.h[:, (kb0 + j) * 128:(kb0 + j + 1) * 128], ident.h[:, :],
                                r=[v_.b, ident.b], w=[ps.b] if j == 0 else [], wa=[] if j == 0 else [ps.b])
                S.copy(S.ev(), Vtm.h[:, kb0:kb0 + n, :], ps.h[:, 0:n * 128].rearrange("p (k e) -> p k e", e=128),
                       w=[ps.b] + ([Vtm.b] if first else []), wa=[] if first else [Vtm.b])
                first = False
            for (q0, NQ) in C.tiles_oh():
                jq = q0 - C.OWN0
                nkb = (q0 + NQ) // 128
                ocs = []
                for c in range(2):
                    psO = acc.get()
                    psL = acc.get()

                    def smm(kb):
                        j0 = max(kb * 128 - q0, 0)
                        n = NQ - j0
                        ps = K.ps.get()
                        S.mms([(ps.h[:, 0:n], kn[c].h[:, kb * 128:(kb + 1) * 128], qn.h[:, jq + j0:jq + NQ], True, True)],
                              r=[kn[c].b, qn.b], w=[ps.b])
                        return ps, j0, n
                    nxt = smm(0)
                    for kb in range(nkb):
                        ps, j0, n = nxt
                        if kb + 1 < nkb:
                            nxt = smm(kb + 1)
                        pt = ptp.get()
                        S.act(pt.h[:, 0:n], ps.h[:, 0:n], AF.Exp, w=[pt.b, ps.b])
                        if kb * 128 >= q0:
                            S.memset("pool", pt.h[64:128, 0:64], 0.0, w=[pt.b])
                        lv = pvb if kb * 128 < C.HALF else onesb
                        S.mms([(psO.h[:, j0:NQ], Vtm.h[:, kb, :], pt.h[:, 0:n], kb == 0, kb == nkb - 1)], r=[Vtm.b, pt.b],
                              w=[psO.b] if kb == 0 else [], wa=[] if kb == 0 else [psO.b])
                        S.mms([(psL.h[:, j0:NQ], lv.h[:, :], pt.h[:, 0:n], kb == 0, kb == nkb - 1)], r=[lv.b, pt.b],
                              w=[psL.b] if kb == 0 else [], wa=[] if kb == 0 else [psL.b])
                    rl = tq.get()
                    S.ts("dve", rl.h[:, 0:NQ], psL.h[:, 0:NQ], 1e-30, None, ALU.max, w=[rl.b, psL.b])
                    S.recip(rl.h[:, 0:NQ], rl.h[:, 0:NQ], w=[rl.b])
                    oc = ocp.get()
                    S.tt("dve", oc.h[:, 0:NQ], psO.h[:, 0:NQ], rl.h[:, 0:NQ], ALU.mult, r=[rl.b], w=[oc.b, psO.b])
                    ocs.append(oc)
                df = tq.get()
                S.stt("dve", df.h[:, 0:NQ], ocs[1].h[:, 0:NQ], neglam, ocs[0].h[:, 0:NQ], ALU.mult, ALU.add, r=[ocs[0].b, ocs[1].b, sc.b], w=[df.b])
                sq = tq.get()
                S.act(sq.h[:, 0:NQ], df.h[:, 0:NQ], AF.Square, r=[df.b], w=[sq.b])
                ps = K.ps.get()
                S.mms([(ps.h[:, 0:NQ], ones.h[:, :], sq.h[:, 0:NQ], True, True)], r=[ones.b, sq.b], w=[ps.b])
                rs = tq.get()
                S.ts("dve", rs.h[:, 0:NQ], ps.h[:, 0:NQ], 1.0 / 128, C.EPS, ALU.mult, ALU.add, w=[rs.b, ps.b])
                S.act(rs.h[:, 0:NQ], rs.h[:, 0:NQ], AF.Ln, w=[rs.b])
                S.act(rs.h[:, 0:NQ], rs.h[:, 0:NQ], AF.Exp, w=[rs.b], scale=-0.5)
                ob = obp.get()
                S.stt("dve", ob.h[:, 0:NQ], df.h[:, 0:NQ], slg, rs.h[:, 0:NQ], ALU.mult, ALU.mult, r=[df.b, rs.b, sc.b], w=[ob.b])
                S.dma("sp", K.d["obT"][rows, jq:jq + NQ], ob.h[:, 0:NQ], r=[ob.b], wa=[K.b["obT"]])
        K.rem.append(nc.sbuf_bytes_remaining)
        S.emit()


def phase_DE(K):
    S, C, nc = K.S, K.C, K.nc
    DC = C.D // 128
    RCH = C.RW // 128
    with contextlib.ExitStack() as es:
        K.ps = psrot(es, nc)
        oat = sbt(es, nc, "Doat", [128, RCH, C.NT], BF16)
        obt = sbt(es, nc, "Dobt", [128, RCH, C.NT], BF16)
        mT = sbt(es, nc, "DmT", [128, DC, C.NT], BF16)
        wap = sbrot(es, nc, "Dwa", [128, RCH, 256], BF16, 2)
        wbp = sbrot(es, nc, "Dwb", [128, RCH, 256], BF16, 2)
        wop = sbrot(es, nc, "Dwo", [128, DC, 512], BF16, 2)
        gp = sbrot(es, nc, "Dg", [128, C.NT], F32, 4)
        mp_ = sbrot(es, nc, "Dm", [128, C.NT], F32, 4)
        xp = sbrot(es, nc, "Dx", [128, 512], F32, 4)
        oav = K.d["oaT"].rearrange("(c p) t -> p c t", p=128)
        obv = K.d["obT"].rearrange("(c p) t -> p c t", p=128)
        for (t0, nt) in C.tiles_oh():
            jq = t0 - C.OWN0
            S.dma("sp", oat.h[:, :, 0:nt], oav[:, :, jq:jq + nt], r=[K.b["oaT"]], w=[oat.b])
            S.dma("sp", obt.h[:, :, 0:nt], obv[:, :, jq:jq + nt], r=[K.b["obT"]], w=[obt.b])
            firstm = True
            for (g0, gw) in chunks(C.D, 256):
                wa_ = wap.get()
                load_w(K, wa_, K.d["bf_w_a"], g0 // 256, 0, RCH, gw, rbuf=K.b["bf_w_a"])
                wb_ = wbp.get()
                load_w(K, wb_, K.d["bf_w_b"], g0 // 256, 0, RCH, gw, rbuf=K.b["bf_w_b"])
                for (j0, wj) in chunks(gw, 128):
                    col = g0 + j0
                    psA = K.ps.get()
                    S.mms([(psA.h[0:wj, 0:nt], wa_.h[:, k, j0:j0 + wj], oat.h[:, k, 0:nt], k == 0, k == RCH - 1) for k in range(RCH)],
                          r=[wa_.b, oat.b], w=[psA.b])
                    psB = K.ps.get()
                    S.mms([(psB.h[0:wj, 0:nt], wb_.h[:, k, j0:j0 + wj], obt.h[:, k, 0:nt], k == 0, k == RCH - 1) for k in range(RCH)],
                          r=[wb_.b, obt.b], w=[psB.b])
                    ga = gp.get()
                    S.dma("sp", ga.h[:, 0:nt], K.d["zG"][col:col + 128, jq:jq + nt], r=[K.b["zG"]], w=[ga.b])
                    gb = gp.get()
                    S.dma("sp", gb.h[:, 0:nt], K.d["zG"][C.D + col:C.D + col + 128, jq:jq + nt], r=[K.b["zG"]], w=[gb.b])
                    S.act(ga.h[:, 0:nt], ga.h[:, 0:nt], AF.Sigmoid, w=[ga.b])
                    S.act(gb.h[:, 0:nt], gb.h[:, 0:nt], AF.Sigmoid, w=[gb.b])
                    m1 = mp_.get()
                    S.tt("dve", m1.h[:, 0:nt], psA.h[:, 0:nt], ga.h[:, 0:nt], ALU.mult, r=[ga.b], w=[m1.b, psA.b])
                    m2 = mp_.get()
                    S.tt("dve", m2.h[:, 0:nt], psB.h[:, 0:nt], gb.h[:, 0:nt], ALU.mult, r=[gb.b], w=[m2.b, psB.b])
                    S.tt("pool", mT.h[:, col // 128, 0:nt], m1.h[:, 0:nt], m2.h[:, 0:nt], ALU.add, r=[m1.b, m2.b],
                         w=[mT.b] if firstm else [], wa=[] if firstm else [mT.b])
                    firstm = False
            for (g0, gw) in chunks(C.D, 512):
                wo_ = wop.get()
                load_w(K, wo_, K.d["bf_w_o"], g0 // 512, 0, DC, gw, rbuf=K.b["bf_w_o"])
                for s_ in range(nt // 128):
                    ps = K.ps.get()
                    S.mms([(ps.h[:, 0:gw], mT.h[:, k, s_ * 128:(s_ + 1) * 128], wo_.h[:, k, 0:gw], k == 0, k == DC - 1) for k in range(DC)],
                          r=[mT.b, wo_.b], w=[ps.b])
                    xt = xp.get()
                    S.dma("sp", xt.h[:, 0:gw], K.d["xc"][t0 + s_ * 128:t0 + (s_ + 1) * 128, g0:g0 + gw], w=[xt.b])
                    S.tt("dve", xt.h[:, 0:gw], ps.h[:, 0:gw], xt.h[:, 0:gw], ALU.add, w=[xt.b, ps.b])
                    S.dma("sp", K.d["x1"][jq + s_ * 128:jq + (s_ + 1) * 128, g0:g0 + gw], xt.h[:, 0:gw], r=[xt.b], wa=[K.b["x1"]])
        K.rem.append(nc.sbuf_bytes_remaining)
        S.emit()


def phase_F(K):
    S, C, nc = K.S, K.C, K.nc
    DC = C.D // 128
    FB = C.DFF // 128
    KH = FB // 2
    SEG = chunks(KH, 22)
    with contextlib.ExitStack() as es:
        allps = psrot(es, nc)
        K.ps = Rot(allps.t[0:4])
        acc = allps.t[4:8]
        aT = sbt(es, nc, "FaT", [128, KH, C.NT], BF16)
        junk = Ctx()
        junk.h = aT.h[:, 0:C.D // C.NT, :].rearrange("p a b -> p (a b)")
        junk.b = aT.b
        R = norm_res(K, es, "F", nxs=2, junk=junk)
        ct, co = load_cols(K, es, "Fcols", ["g_ffn", "cw0", "cw1", "cw2", "cb"])
        pv = load_const(K, es, "Fpv", "pvalid")
        hT = sbt(es, nc, "FhT", [128, DC, C.NT], BF16)
        wgp = sbrot(es, nc, "Fwg", [128, DC, 256], BF16, 2)
        wfp = sbrot(es, nc, "Fwf", [128, max(n for _, n in SEG), 256], BF16, 2)
        halo = sbt(es, nc, "Fhalo", [128, 2 * FB, 2], F32)
        S.memset("pool", halo.h[:, :, :], 0.0, w=[halo.b])
        ubp = sbrot(es, nc, "Fub", [128, C.NT + 2], F32, 4)
        cvp = sbrot(es, nc, "Fcv", [128, C.NT], F32, 4)
        xp = sbrot(es, nc, "Fx", [128, 256], F32, 4)
        for ti, (t0, nt) in enumerate(C.tiles_oh()):
            jq = t0 - C.OWN0
            is_halo = (ti == 0)
            row0 = t0 - C.HALF
            build_hT(K, R, K.d["x1"], jq, nt, ct, co["g_ffn"], hT)
            for half2 in range(2):
                firsta = True
                for jb in range(half2 * KH, (half2 + 1) * KH):
                    wg_ = wgp.get()
                    load_w(K, wg_, K.d["bf_w_fi"], jb, 0, DC, 256, rbuf=K.b["bf_w_fi"])
                    cvs = []
                    for half in range(2):
                        blk_i = jb + half * FB
                        ps = K.ps.get()
                        S.mms([(ps.h[:, 0:nt], wg_.h[:, k, half * 128:(half + 1) * 128], hT.h[:, k, 0:nt], k == 0, k == DC - 1) for k in range(DC)],
                              r=[wg_.b, hT.b], w=[ps.b])
                        ub = ubp.get()
                        S.copy("act", ub.h[:, 2:nt + 2], ps.h[:, 0:nt], w=[ub.b, ps.b])
                        S.copy("dve", ub.h[:, 0:2], halo.h[:, blk_i, :], r=[halo.b], wa=[ub.b])
                        if is_halo:
                            S.ts("dve", halo.h[:, blk_i, :], ub.h[:, nt:nt + 2], pv.h[:, 0:1], None, ALU.mult, r=[ub.b, pv.b], w=[halo.b])
                            continue
                        S.copy("dve", halo.h[:, blk_i, :], ub.h[:, nt:nt + 2], r=[ub.b], w=[halo.b])
                        cv = cvp.get()
                        cc = lambda nm: ct.h[:, co[nm] + blk_i:co[nm] + blk_i + 1]
                        S.act(cv.h[:, 0:nt], ub.h[:, 2:nt + 2], AF.Identity, r=[ub.b, ct.b], w=[cv.b], scale=cc("cw2"), bias=cc("cb"))
                        S.stt("dve", cv.h[:, 0:nt], ub.h[:, 1:nt + 1], cc("cw1"), cv.h[:, 0:nt], ALU.mult, ALU.add, r=[ub.b, ct.b], w=[cv.b])
                        S.stt("dve", cv.h[:, 0:nt], ub.h[:, 0:nt], cc("cw0"), cv.h[:, 0:nt], ALU.mult, ALU.add, r=[ub.b, ct.b], w=[cv.b])
                        cvs.append(cv)
                    if is_halo:
                        continue
                    sgt = cvp.get()
                    S.act(sgt.h[:, 0:nt], cvs[0].h[:, 0:nt], AF.Silu, r=[cvs[0].b], w=[sgt.b])
                    S.tt("dve", aT.h[:, jb - half2 * KH, 0:nt], sgt.h[:, 0:nt], cvs[1].h[:, 0:nt], ALU.mult, r=[sgt.b, cvs[1].b],
                         w=[aT.b] if firsta else [], wa=[] if firsta else [aT.b])
                    firsta = False
                if is_halo:
                    continue
                nsub = nt // 128
                for (g0, gw) in chunks(C.D, 256):
                    for si, (k0, kn_) in enumerate(SEG):
                        wf_ = wfp.get()
                        load_w(K, wf_, K.d["bf_w_fo"], g0 // 256, half2 * KH + k0, kn_, gw, rbuf=K.b["bf_w_fo"])
                        for s_ in range(nsub):
                            ps = acc[s_]
                            items = [(ps.h[:, 0:gw], aT.h[:, k0 + k, s_ * 128:(s_ + 1) * 128], wf_.h[:, k, 0:gw],
                                      si == 0 and k == 0, si == len(SEG) - 1 and k == kn_ - 1) for k in range(kn_)]
                            S.mms(items, r=[aT.b, wf_.b], w=[ps.b] if si == 0 else [], wa=[] if si == 0 else [ps.b])
                    for s_ in range(nsub):
                        ps = acc[s_]
                        xt = xp.get()
                        if half2 == 0:
                            S.dma("sp", xt.h[:, 0:gw], K.d["x1"][jq + s_ * 128:jq + (s_ + 1) * 128, g0:g0 + gw], r=[K.b["x1"]], w=[xt.b])
                        else:
                            S.dma("sp", xt.h[:, 0:gw], K.d["x2"][row0 + s_ * 128:row0 + (s_ + 1) * 128, g0:g0 + gw], r=[K.b["x2"]], w=[xt.b])
                        S.tt("dve", xt.h[:, 0:gw], ps.h[:, 0:gw], xt.h[:, 0:gw], ALU.add, w=[xt.b, ps.b])
                        S.dma("sp", K.d["x2"][row0 + s_ * 128:row0 + (s_ + 1) * 128, g0:g0 + gw], xt.h[:, 0:gw], r=[xt.b], wa=[K.b["x2"]])
        K.rem.append(nc.sbuf_bytes_remaining)
        S.emit()


def phase_G(K):
    S, C, nc = K.S, K.C, K.nc
    DC = C.D // 128
    PC = C.PLE // 128
    with contextlib.ExitStack() as es:
        K.ps = psrot(es, nc)
        R = norm_res(K, es, "G")
        ct, co = load_cols(K, es, "Gcols", ["g_ple"])
        hT = sbt(es, nc, "GhT", [128, DC, C.NT], BF16)
        pT = sbt(es, nc, "GpT", [128, PC, C.NT], BF16)
        pp = sbrot(es, nc, "Gp", [128, C.PLE], F32, 2)
        wgp = sbrot(es, nc, "Gwg", [128, DC, 512], BF16, 2)
        wpp = sbrot(es, nc, "Gwp", [128, PC, 512], BF16, 2)
        sgp = sbrot(es, nc, "Gsg", [128, 512], F32, 3)
        xp = sbrot(es, nc, "Gx", [128, 512], F32, 3)
        for (t0, nt) in C.tiles_oh()[1:]:
            row0 = t0 - C.HALF
            build_hT(K, R, K.d["x2"], row0, nt, ct, co["g_ple"], hT)
            firstp = True
            for s_ in range(nt // 128):
                pt = pp.get()
                S.dma("sp", pt.h[:, :], K.d["pc"][row0 + s_ * 128:row0 + (s_ + 1) * 128, :], w=[pt.b])
                ps = K.ps.get()
                for c in range(PC):
                    S.transpose(ps.h[:, c * 128:(c + 1) * 128], pt.h[:, c * 128:(c + 1) * 128], R.ident.h[:, :], r=[pt.b, R.ident.b],
                                w=[ps.b] if c == 0 else [], wa=[] if c == 0 else [ps.b])
                S.copy("act", pT.h[:, :, s_ * 128:(s_ + 1) * 128], ps.h[:, 0:PC * 128].rearrange("p (c t) -> p c t", t=128),
                       w=[ps.b] + ([pT.b] if firstp else []), wa=[] if firstp else [pT.b])
                firstp = False
            for (g0, gw) in chunks(C.D, 512):
                wg_ = wgp.get()
                load_w(K, wg_, K.d["bf_w_pg"], g0 // 512, 0, DC, gw, rbuf=K.b["bf_w_pg"])
                wp_ = wpp.get()
                load_w(K, wp_, K.d["w_pp"], g0 // 512, 0, PC, gw)
                for s_ in range(nt // 128):
                    psG = K.ps.get()
                    S.mms([(psG.h[:, 0:gw], hT.h[:, k, s_ * 128:(s_ + 1) * 128], wg_.h[:, k, 0:gw], k == 0, k == DC - 1) for k in range(DC)],
                          r=[hT.b, wg_.b], w=[psG.b])
                    psP = K.ps.get()
                    S.mms([(psP.h[:, 0:gw], pT.h[:, k, s_ * 128:(s_ + 1) * 128], wp_.h[:, k, 0:gw], k == 0, k == PC - 1) for k in range(PC)],
                          r=[pT.b, wp_.b], w=[psP.b])
                    sg = sgp.get()
                    S.act(sg.h[:, 0:gw], psG.h[:, 0:gw], AF.Sigmoid, w=[sg.b, psG.b])
                    S.tt("dve", sg.h[:, 0:gw], psP.h[:, 0:gw], sg.h[:, 0:gw], ALU.mult, w=[sg.b, psP.b])
                    xt = xp.get()
                    S.dma("sp", xt.h[:, 0:gw], K.d["x2"][row0 + s_ * 128:row0 + (s_ + 1) * 128, g0:g0 + gw], r=[K.b["x2"]], w=[xt.b])
                    S.tt("pool", xt.h[:, 0:gw], xt.h[:, 0:gw], sg.h[:, 0:gw], ALU.add, r=[sg.b], w=[xt.b])
                    S.dma("sp", K.d["out"][row0 + s_ * 128:row0 + (s_ + 1) * 128, g0:g0 + gw], xt.h[:, 0:gw], r=[xt.b], wa=[K.b["out"]])
        K.rem.append(nc.sbuf_bytes_remaining)
        S.emit()

def build_program(C, upto="A", debug=()):
    nc = bass.Bass("TRN2", target_bir_lowering=False)
    K = Ctx()
    K.nc, K.C = nc, C
    K.d, K.b = {}, {}
    K.rem = []

    def din(name, shape):
        K.d[name] = nc.dram_tensor(name, list(shape), F32, kind="ExternalInput").ap()

    def dscr(name, shape, dt=F32, out=False):
        kind = "ExternalOutput" if out else "Internal"
        K.d[name] = nc.dram_tensor(name, list(shape), dt, kind=kind).ap()
        K.b[name] = Buf(name)

    din("xc", [C.CTX, C.D])
    din("pc", [C.HALF, C.PLE])
    din("cols", [128, C.NCOLS])
    din("consts", [128, C.NCONST])
    din("lamb", [128, 256])
    din("w_in", [(C.IC + 511) // 512, 128, C.D // 128, 512])
    din("w2", [C.DL, C.RW])
    din("a2", [C.AL, C.RW])
    din("g2", [C.GL, C.RW])
    din("w_a", [C.D // 256, 128, C.RW // 128, 256])
    din("w_b", [C.D // 256, 128, C.DW // 128, 256])
    din("w_o", [C.D // 512, 128, C.D // 128, 512])
    din("w_fi", [C.DFF // 128, 128, C.D // 128, 256])
    din("w_fo", [C.D // 256, 128, C.DFF // 128, 256])
    din("w_pg", [C.D // 512, 128, C.D // 128, 512])
    din("w_pp", [C.D // 512, 128, C.PLE // 128, 512])
    dscr("out", [C.HALF, C.D], out=True)
    dscr("zR", [C.RC, C.CTX], out=("zT" in debug))
    dscr("zQ", [C.DW, C.NOH], out=("zT" in debug))
    dscr("zKV", [2 * C.DW, C.CTX], out=("zT" in debug))
    dscr("zG", [2 * C.D, C.NOH], out=("zT" in debug))
    for nm_ in ("w_a", "w_b", "w_o", "w_fi", "w_fo", "w_pg"):
        dscr("bf_" + nm_, list(K.d[nm_].shape), BF16)
    dscr("gT", [C.RW, C.CTX], out=("gT" in debug))
    dscr("bonT", [C.RW, C.CTX], out=("gT" in debug))
    dscr("oaT", [C.RW, C.NOH], BF16, out=("oaT" in debug))
    dscr("obT", [C.DW, C.NOH], BF16, out=("obT" in debug))
    dscr("x1", [C.NOH, C.D], out=("x1" in debug))
    dscr("x2", [C.HALF, C.D], out=("x2" in debug))
    with contextlib.ExitStack() as es0:
        K.S = Sched(nc, es0)
        phase_A(K)
        if upto == "A":
            return nc, K
        K.pc = precast_list(K)
        phase_B(K)
        if upto == "B":
            return nc, K
        phase_C(K)
        if upto == "C":
            return nc, K
        phase_DE(K)
        if upto == "DE":
            return nc, K
        phase_F(K)
        if upto == "F":
            return nc, K
        phase_G(K)
    return nc, K


def colpack(v):
    v = np.asarray(v, np.float32).reshape(-1)
    n = (v.size + 127) // 128
    p = np.zeros(n * 128, np.float32)
    p[:v.size] = v
    return np.ascontiguousarray(p.reshape(n, 128).T)


def tile_w(w, W):
    w = np.asarray(w, np.float32)
    Kd, N = w.shape
    NG = (N + W - 1) // W
    if NG * W != N:
        w = np.concatenate([w, np.zeros((Kd, NG * W - N), np.float32)], axis=1)
    return np.ascontiguousarray(w.reshape(Kd // 128, 128, NG, W).transpose(2, 1, 0, 3))


def host_inputs(C, inp):
    g = lambda k: np.asarray(inp[k], np.float32)[0]
    mu = g("rwkv_mu")
    cw = g("ffn_conv_w")
    parts = {
        "g_mix": g("norm_mix_g"), "g_ffn": g("norm_ffn_g"), "g_ple": g("norm_ple_g"),
        "mu_r": mu[C.o_r:C.o_r + C.RW], "mu_k": mu[C.o_k:C.o_k + C.RW], "mu_v": mu[C.o_v:C.o_v + C.RW],
        "mu_w": mu[C.o_wl:C.o_wl + C.DL], "mu_a": mu[C.o_al:C.o_al + C.AL], "mu_g": mu[C.o_gl:C.o_gl + C.GL],
        "w0": g("rwkv_w0"), "a0": g("rwkv_a0"), "k_k": g("rwkv_k_k"), "k_a": g("rwkv_k_a"),
        "r_k": g("rwkv_r_k").reshape(-1), "ln_w": g("rwkv_ln_w"), "ln_b": g("rwkv_ln_b"),
        "q_g": np.tile(g("q_norm_g"), 2), "k_g": np.tile(g("k_norm_g"), 2), "subln": g("subln_g"),
        "cw0": cw[0], "cw1": cw[1], "cw2": cw[2], "cb": g("ffn_conv_b"),
    }
    cols = np.zeros((128, C.NCOLS), np.float32)
    for name, (off, n) in C.colmap.items():
        cp = colpack(parts[name])
        assert cp.shape[1] == n, (name, cp.shape, n)
        cols[:, off:off + n] = cp
    consts = np.zeros((128, C.NCONST), np.float32)
    consts[:, 0:128] = np.eye(128, dtype=np.float32)
    consts[0:64, 128:192] = 1.0
    consts[64:128, 192:256] = 1.0
    consts[:, 256:384] = 1.0
    s = np.arange(64)[:, None]
    t = np.arange(64)[None, :]
    consts[0:64, 384:448] = (s < t)
    consts[0:64, 448:512] = (s <= t)
    consts[0:64, 512:576] = (s > t)
    sm = np.ones(512, np.float32)
    sm[0::64] = 0.0
    consts[:, 576:1088] = sm[None, :]
    lamb = np.concatenate([g("lam_q1"), g("lam_k1"), g("lam_q2"), g("lam_k2")])[None, :].repeat(128, 0)
    x = np.asarray(inp["x"], np.float32)
    p = np.asarray(inp["p"], np.float32)[0]
    shared = {
        "cols": cols, "lamb": np.ascontiguousarray(lamb),
        "w_in": tile_w(g("w_in"), 512), "w2": g("rwkv_w2"), "a2": g("rwkv_a2"), "g2": g("rwkv_g2"),
        "w_a": tile_w(g("w_branch_a"), 256), "w_b": tile_w(g("w_branch_b"), 256), "w_o": tile_w(g("w_out"), 512),
        "w_fi": np.ascontiguousarray(np.concatenate([tile_w(g("w_ffn_in")[:, :C.DFF], 128), tile_w(g("w_ffn_in")[:, C.DFF:], 128)], axis=3)),
        "w_fo": tile_w(g("w_ffn_out"), 256), "w_pg": tile_w(g("w_ple_gate"), 512), "w_pp": tile_w(g("w_ple_proj"), 512),
    }
    maps = []
    for core in range(2 * C.B):
        b, hf = core // 2, core % 2
        m = dict(shared)
        if hf == 1:
            m["xc"] = np.ascontiguousarray(x[b])
        else:
            xc = np.zeros((C.CTX, C.D), np.float32)
            xc[C.HALF:] = x[b, 0:C.HALF]
            m["xc"] = xc
        m["pc"] = np.ascontiguousarray(p[b, hf * C.HALF:(hf + 1) * C.HALF])
        cc = consts.copy()
        cc[:, 1088:1216] = float(hf)
        m["consts"] = cc
        maps.append(m)
    return maps


_PROG = {}


def kernel(**inputs):
    C = Cfg()
    if "full" not in _PROG:
        _PROG["full"] = build_program(C, upto="ALL")
    nc, K = _PROG["full"]
    maps = host_inputs(C, inputs)
    res = run_bass_kernel_spmd(nc, maps, core_ids=list(range(2 * C.B)))
    out = np.zeros((C.B, C.SEQ, C.D), np.float32)
    for core in range(2 * C.B):
        b, hf = core // 2, core % 2
        out[b, hf * C.HALF:(hf + 1) * C.HALF] = res.results[core]["out"]
    return out
```

```python
import contextlib
import math
import numpy as np
import concourse.bass as bass
import concourse.mybir as mybir
from concourse.bass_utils import run_bass_kernel_spmd

F32 = mybir.dt.float32
BF16 = mybir.dt.bfloat16
AF = mybir.ActivationFunctionType
ALU = mybir.AluOpType


class Buf:
    __slots__ = ("name", "w", "r", "g")

    def __init__(self, name):
        self.name = name
        self.w = {}
        self.r = {}
        self.g = None


class Tl:
    def __init__(self, h, name):
        self.h = h
        self.b = Buf(name)


class Rot:
    def __init__(self, tiles):
        self.t = tiles
        self.i = 0

    def get(self):
        t = self.t[self.i % len(self.t)]
        self.i += 1
        return t


class Sched:
    ENGS = ("pe", "act", "dve", "pool", "sp")
    import os
    NDS = int(os.environ.get('NDS', 6))

    def __init__(self, nc, es):
        self.nc = nc
        self.streams = {e: [] for e in self.ENGS}
        self.cnt = {e: 0 for e in self.ENGS}
        self.sems = {}
        for e in self.ENGS:
            self.sems[e] = es.enter_context(nc.semaphore("s_" + e))
        self.dq = {}
        self.dlast = {}
        for q in ("sp", "act", "pool"):
            self.dq[q] = 0
            for i in range(self.NDS):
                self.sems[("d", q, i)] = es.enter_context(nc.semaphore("d_%s_%d" % (q, i)))
        self.known = {}
        self.final = []
        self.rr = 0
        self.log = []

    def _wait(self, e, key, val):
        if key == "pe" and e == "pe":
            return
        if self.known.get((e, key), 0) >= val:
            return
        self.known[(e, key)] = val
        self.log.append((e, "wait", key, val))
        sem = self.sems[key]
        self.streams[e].append(lambda eng, sem=sem, val=val: eng.wait_ge(sem, val))

    def _deps(self, e, r, w, wa):
        for b in r:
            for k, v in b.w.items():
                self._wait(e, k, v)
        for b in w:
            for k, v in b.w.items():
                self._wait(e, k, v)
            for k, v in b.r.items():
                self._wait(e, k, v)
        for b in wa:
            if b.g is not None:
                self._wait(e, b.g[0], b.g[1])
            for k, v in b.r.items():
                self._wait(e, k, v)

    def _mark(self, tok, r, w, wa):
        k, v = tok
        for b in r:
            if b.r.get(k, 0) < v:
                b.r[k] = v
        for b in w:
            b.w = {k: v}
            b.r = {}
            b.g = tok
        for b in wa:
            if b.w.get(k, 0) < v:
                b.w[k] = v

    def op(self, e, fn, r=(), w=(), wa=(), inc=True):
        self._deps(e, r, w, wa)
        sem = self.sems[e]
        if inc:
            self.cnt[e] += 1
            tok = (e, self.cnt[e])
            self.streams[e].append(lambda eng, fn=fn, sem=sem: fn(eng).then_inc(sem, 1))
        else:
            tok = (e, self.cnt[e] + 1)
            self.streams[e].append(lambda eng, fn=fn: fn(eng))
        self._mark(tok, r, w, wa)
        self.log.append((e, "op", tok, inc))
        return tok

    def dma(self, q, out, in_, r=(), w=(), wa=(), final=False):
        j = self.dq[q]
        self.dq[q] += 1
        slot = j % self.NDS
        key = ("d", q, slot)
        val = 16 * (j // self.NDS + 1)
        if j >= self.NDS:
            self._wait(q, key, val - 16)
        self._deps(q, r, w, wa)
        sem = self.sems[key]
        self.streams[q].append(
            lambda eng, out=out, in_=in_, sem=sem: eng.dma_start(out=out, in_=in_).then_inc(sem, 16))
        tok = (key, val)
        self.log.append((q, "dma", tok))
        self.dlast[key] = val
        self._mark(tok, r, w, wa)
        if final:
            self.final.append(tok)
        return tok

    def act(self, out, in_, func, r=(), w=(), wa=(), **kw):
        return self.op("act", lambda e: e.activation(out=out, in_=in_, func=func, **kw), r, w, wa)

    def ts(self, eng, out, in0, s1, s2, op0, op1=None, r=(), w=(), wa=()):
        if op1 is None:
            return self.op(eng, lambda e: e.tensor_scalar(out=out, in0=in0, scalar1=s1, scalar2=None, op0=op0), r, w, wa)
        return self.op(eng, lambda e: e.tensor_scalar(out=out, in0=in0, scalar1=s1, scalar2=s2, op0=op0, op1=op1), r, w, wa)

    def tt(self, eng, out, in0, in1, op, r=(), w=(), wa=()):
        return self.op(eng, lambda e: e.tensor_tensor(out=out, in0=in0, in1=in1, op=op), r, w, wa)

    def stt(self, eng, out, in0, scalar, in1, op0, op1, r=(), w=(), wa=()):
        return self.op(eng, lambda e: e.scalar_tensor_tensor(out=out, in0=in0, scalar=scalar, in1=in1, op0=op0, op1=op1), r, w, wa)

    def copy(self, eng, out, in_, r=(), w=(), wa=()):
        if eng == "act":
            return self.act(out, in_, AF.Identity, r, w, wa)
        return self.op(eng, lambda e: e.tensor_copy(out=out, in_=in_), r, w, wa)

    def memset(self, eng, ap, val, w=(), wa=()):
        return self.op(eng, lambda e: e.memset(ap, val), (), w, wa)

    def recip(self, out, in_, r=(), w=(), wa=()):
        return self.op("dve", lambda e: e.reciprocal(out=out, in_=in_), r, w, wa)

    def scan(self, out, d0, d1, init, op0, op1, r=(), w=(), wa=()):
        return self.op("dve", lambda e: e.tensor_tensor_scan(out=out, data0=d0, data1=d1, initial=init, op0=op0, op1=op1), r, w, wa)

    def transpose(self, out, in_, ident, r=(), w=(), wa=()):
        return self.op("pe", lambda e: e.transpose(out, in_, ident), r, w, wa)

    def mms(self, items, r=(), w=(), wa=()):
        self._deps("pe", r, w, wa)
        n = len(items)
        sem = self.sems["pe"]
        self.cnt["pe"] += 1
        tok = ("pe", self.cnt["pe"])
        for i, (o, l, rh, st, sp) in enumerate(items):
            if i == n - 1:
                self.streams["pe"].append(
                    lambda eng, o=o, l=l, rh=rh, st=st, sp=sp, sem=sem: eng.matmul(o, l, rh, start=st, stop=sp).then_inc(sem, 1))
            else:
                self.streams["pe"].append(
                    lambda eng, o=o, l=l, rh=rh, st=st, sp=sp: eng.matmul(o, l, rh, start=st, stop=sp))
        self._mark(tok, r, w, wa)
        return tok

    def ev(self):
        self.rr += 1
        return "act" if self.rr % 2 else "dve"

    def emit(self, last=False):
        nc = self.nc
        for e in self.ENGS:
            for f in self.ENGS:
                if f != e and self.cnt[f] > 0:
                    self._wait(e, f, self.cnt[f])
            for key, val in self.dlast.items():
                self._wait(e, key, val)
        self.final = []
        streams = self.streams
        self.streams = {e: [] for e in self.ENGS}
        with nc.Block() as block:
            @block.tensor
            def _(eng):
                for f in streams["pe"]:
                    f(eng)

            @block.scalar
            def _(eng):
                for f in streams["act"]:
                    f(eng)

            @block.vector
            def _(eng):
                for f in streams["dve"]:
                    f(eng)

            @block.gpsimd
            def _(eng):
                for f in streams["pool"]:
                    f(eng)

            @block.sync
            def _(eng):
                for f in streams["sp"]:
                    f(eng)


class Cfg:
    def __init__(self, D=4096, SEQ=4096, B=4, PLE=256):
        self.D, self.SEQ, self.B, self.PLE = D, SEQ, B, PLE
        self.EPS = 1e-6
        self.RW = D // 2
        self.NH = self.RW // 64
        self.NHC = self.RW // 128
        self.DL = max(32, int(round(1.8 * self.RW ** 0.5 / 32)) * 32)
        self.AL = max(32, int(round(2.5 * self.RW ** 0.5 / 32)) * 32)
        self.GL = max(32, int(round(0.6 * self.RW ** 0.8 / 32)) * 32)
        self.GN_EPS = 64e-5
        self.RC = 3 * self.RW + self.DL + self.AL + self.GL
        self.DW = D // 2
        self.NDH = self.DW // 128
        self.DCOL = 3 * self.DW
        self.IC = self.RC + self.DCOL + 2 * D
        self.DFF = int(round(8 * D / 3 / 256)) * 256
        self.HALF = SEQ // 2
        self.CTX = SEQ
        self.HALO = 128
        self.OWN0 = self.HALF - self.HALO
        self.NT = min(512, self.HALF)
        self.NOH = self.HALF + self.HALO
        self.o_r, self.o_k, self.o_v = 0, self.RW, 2 * self.RW
        self.o_wl = 3 * self.RW
        self.o_al = self.o_wl + self.DL
        self.o_gl = self.o_al + self.AL
        self.o_q = self.RC
        self.o_dk = self.RC + self.DW
        self.o_dv = self.RC + 2 * self.DW
        self.o_ga = self.RC + self.DCOL
        self.o_gb = self.o_ga + D
        self.lambda_init = 0.8 - 0.6 * math.exp(-0.3 * 0)
        self.colmap = {}
        off = 0
        dc = D // 128
        hc = self.RW // 128
        fc = 2 * self.DFF // 128
        for name, n in [("g_mix", dc), ("g_ffn", dc), ("g_ple", dc),
                        ("mu_r", hc), ("mu_k", hc), ("mu_v", hc), ("mu_w", 1), ("mu_a", 1), ("mu_g", (self.GL + 127) // 128),
                        ("w0", hc), ("a0", hc), ("k_k", hc), ("k_a", hc), ("r_k", hc), ("ln_w", hc), ("ln_b", hc),
                        ("q_g", 1), ("k_g", 1), ("subln", 1),
                        ("cw0", fc), ("cw1", fc), ("cw2", fc), ("cb", fc)]:
            self.colmap[name] = (off, n)
            off += n
        self.NCOLS = off
        self.cm = {"ident": (0, 128), "blk": (128, 128), "ones": (256, 128), "mUs": (384, 64), "mUi": (448, 64),
                   "mLs": (512, 64), "scan": (576, 512), "pvalid": (1088, 128)}
        self.NCONST = 1216

    def tiles_oh(self):
        t = [(self.OWN0, self.HALO)]
        for i in range(self.HALF // self.NT):
            t.append((self.HALF + i * self.NT, self.NT))
        return t


class Ctx:
    pass


def sbt(es, nc, name, shape, dt):
    return Tl(es.enter_context(nc.sbuf_tensor(name, list(shape), dt)), name)


def sbrot(es, nc, name, shape, dt, n):
    return Rot([sbt(es, nc, "%s%d" % (name, i), shape, dt) for i in range(n)])


_PSN = [0]


def psrot(es, nc, n=8):
    _PSN[0] += 1
    return Rot([Tl(es.enter_context(nc.psum_tensor("ps%d_%d" % (_PSN[0], i), [128, 512], F32)), "ps%d" % i) for i in range(n)])


def chunks(total, size):
    return [(i, min(size, total - i)) for i in range(0, total, size)]


def load_cols(K, es, name, names):
    S, C, nc = K.S, K.C, K.nc
    lo = min(C.colmap[x][0] for x in names)
    hi = max(C.colmap[x][0] + C.colmap[x][1] for x in names)
    t = sbt(es, nc, name, [128, hi - lo], F32)
    if hi - lo == 1:
        with nc.allow_non_contiguous_dma(reason="single column"):
            pass
    S.dma("sp", t.h[:, :], K.d["cols"][:, lo:hi], w=[t.b])
    return t, {x: C.colmap[x][0] - lo for x in names}


def load_const(K, es, name, key, rows=128, dt=F32):
    S, C, nc = K.S, K.C, K.nc
    o, n = C.cm[key]
    t = sbt(es, nc, name, [rows, n], dt)
    q = "sp" if dt == F32 else "pool"
    S.dma(q, t.h[:, :], K.d["consts"][0:rows, o:o + n], w=[t.b])
    return t


def build_hT(K, R, src, t0, nt, gt, goff, hT):
    S, C = K.S, K.C
    DC = C.D // 128
    import os
    for s in range(min(nt // 128, int(os.environ.get("KSTOP", "99")))):
        xs = R.xs.get()
        S.dma("sp", xs.h[:, :], src[t0 + s * 128: t0 + (s + 1) * 128, :], w=[xs.b])
        st = R.st.get()
        S.memset("pool", st.h[:, 0:1], 0.0, w=[st.b])
        S.act(R.junk.h[:, :], xs.h[:, :], AF.Square, r=[xs.b], w=[R.junk.b, st.b], accum_out=st.h[:, 0:1])
        S.ts("dve", st.h[:, 1:2], st.h[:, 0:1], 1.0 / C.D, C.EPS, ALU.mult, ALU.add, r=[st.b], w=[st.b])
        S.act(st.h[:, 2:3], st.h[:, 1:2], AF.Ln, r=[st.b], w=[st.b])
        S.act(st.h[:, 3:4], st.h[:, 2:3], AF.Exp, r=[st.b], w=[st.b], scale=-0.5)
        S.ts("dve", xs.h[:, :], xs.h[:, :], st.h[:, 3:4], None, ALU.mult, r=[xs.b, st.b], w=[xs.b])
        import os
        if os.environ.get("SKIPT"):
            continue
        for c0 in range(0, DC, 4):
            ps = K.ps.get()
            n = min(4, DC - c0)
            for j in range(n):
                S.transpose(ps.h[:, j * 128:(j + 1) * 128], xs.h[:, (c0 + j) * 128:(c0 + j + 1) * 128], R.ident.h[:, :],
                            r=[xs.b, R.ident.b], w=[ps.b] if j == 0 else [], wa=[] if j == 0 else [ps.b])
            eng = S.ev()
            for j in range(n):
                c = c0 + j
                o = hT.h[:, c, s * 128:(s + 1) * 128]
                i = ps.h[:, j * 128:(j + 1) * 128]
                g = gt.h[:, goff + c:goff + c + 1]
                if eng == "act":
                    S.act(o, i, AF.Identity, r=[gt.b], w=[ps.b], wa=[hT.b], scale=g)
                else:
                    S.ts("dve", o, i, g, None, ALU.mult, r=[gt.b], w=[ps.b], wa=[hT.b])


def norm_res(K, es, pfx, nxs=3, junk=None):
    R = Ctx()
    nc, C = K.nc, K.C
    R.xs = sbrot(es, nc, pfx + "xs", [128, C.D], F32, nxs)
    R.junk = junk if junk is not None else sbt(es, nc, pfx + "junk", [128, C.D], BF16)
    R.st = sbrot(es, nc, pfx + "st", [128, 4], F32, 4)
    R.ident = load_const(K, es, pfx + "ident", "ident")
    return R


def precast_list(K):
    out = []
    for nm in ("w_a", "w_b", "w_o", "w_fi", "w_fo", "w_pg"):
        src, dst = K.d[nm], K.d["bf_" + nm]
        for g in range(src.shape[0]):
            out.append((dst[g], src[g], K.b["bf_" + nm]))
    return out


def precast_step(K, n=1):
    for _ in range(n):
        if K.pc:
            dst, src, buf = K.pc.pop(0)
            K.S.dma("pool", dst, src, wa=[buf])


def load_w(K, wt, w4, g, c0, cn, width, kstep=8, rbuf=None):
    if rbuf is not None:
        kstep = 32
    S = K.S
    v = w4[g]
    first = True
    for k0 in range(0, cn, kstep):
        kn = min(kstep, cn - k0)
        S.dma("pool", wt.h[:, k0:k0 + kn, 0:width], v[:, c0 + k0:c0 + k0 + kn, 0:width], r=[rbuf] if rbuf is not None else [],
              w=[wt.b] if first else [], wa=[] if first else [wt.b])
        first = False


def z_store(K, col, w, zt, t0, nt):
    S, C = K.S, K.C
    regions = [("zR", 0, C.RC, 0), ("zQ", C.o_q, C.o_dk, C.OWN0), ("zKV", C.o_dk, C.o_ga, 0), ("zG", C.o_ga, C.IC, C.OWN0)]
    for (nm, c0, c1, tk0) in regions:
        a, b = max(col, c0), min(col + w, c1)
        if a >= b:
            continue
        ta = max(t0, tk0)
        if ta >= t0 + nt:
            continue
        S.dma("sp", K.d[nm][a - c0:b - c0, ta - tk0:t0 + nt - tk0], zt.h[a - col:b - col, ta - t0:nt], r=[zt.b], wa=[K.b[nm]])


def phase_A(K):
    S, C, nc = K.S, K.C, K.nc
    DC = C.D // 128
    with contextlib.ExitStack() as es:
        K.ps = psrot(es, nc)
        R = norm_res(K, es, "A")
        gt, gm = load_cols(K, es, "Ag", ["g_mix"])
        hTs = [sbt(es, nc, "AhT%d" % i, [128, DC, C.NT], BF16) for i in range(2)]
        wpool = sbrot(es, nc, "Aw", [128, DC, 512], BF16, 2)
        zpool = sbrot(es, nc, "Az", [128, C.NT], F32, 3)
        ntile = C.CTX // C.NT
        first_full = C.OWN0 // C.NT
        NGA = (C.IC + 511) // 512
        build_hT(K, R, K.d["xc"], 0, C.NT, gt, gm["g_mix"], hTs[0])
        for tt in range(ntile):
            t0 = tt * C.NT
            hT = hTs[tt % 2]
            if tt >= first_full:
                groups = list(range(NGA))
            else:
                groups = [g for g in range(NGA) if (g * 512 < C.RC) or (g * 512 + 512 > C.o_dk and g * 512 < C.o_ga)]
            for gi_, g in enumerate(groups):
                if gi_ == len(groups) // 2 and tt + 1 < ntile:
                    build_hT(K, R, K.d["xc"], t0 + C.NT, C.NT, gt, gm["g_mix"], hTs[(tt + 1) % 2])
                col0 = g * 512
                gw = min(512, C.IC - col0)
                wt = wpool.get()
                load_w(K, wt, K.d["w_in"], g, 0, DC, 512)
                for (j0, wj) in chunks(gw, 128):
                    ps = K.ps.get()
                    items = [(ps.h[0:wj, 0:C.NT], wt.h[:, k, j0:j0 + wj], hT.h[:, k, 0:C.NT], k == 0, k == DC - 1)
                             for k in range(DC)]
                    S.mms(items, r=[wt.b, hT.b], w=[ps.b])
                    zt = zpool.get()
                    S.copy(S.ev(), zt.h[0:wj, :], ps.h[0:wj, 0:C.NT], w=[zt.b, ps.b])
                    z_store(K, col0 + j0, wj, zt, t0, C.NT)
        K.rem.append(nc.sbuf_bytes_remaining)
        S.emit()


def phase_B(K):
    S, C, nc = K.S, K.C, K.nc
    NHC = C.NHC
    TB = min(256, C.HALF)
    NCH = TB // 64
    HG = min(4, NHC)
    NG = NHC // HG
    GW = HG * 128
    GLC = (C.GL + 127) // 128
    zT = K.d["zR"]
    with contextlib.ExitStack() as es:
        K.ps = psrot(es, nc)
        ident = load_const(K, es, "Bident", "ident")
        blk = load_const(K, es, "Bblk", "blk")
        mUs = load_const(K, es, "BmUs", "mUs", rows=64)
        mUi = load_const(K, es, "BmUi", "mUi", rows=64)
        mLs = load_const(K, es, "BmLs", "mLs", rows=64)
        scanm = load_const(K, es, "Bscan", "scan")
        identb = load_const(K, es, "Bidentb", "ident", dt=BF16)
        names = ["mu_r", "mu_k", "mu_v", "mu_w", "mu_a", "mu_g", "w0", "a0", "k_k", "k_a", "r_k", "ln_w", "ln_b"]
        ct, co = load_cols(K, es, "Bcols", names)
        nmu = 3 * NHC + 2 + GLC
        omu = sbt(es, nc, "Bomu", [128, nmu], F32)
        S.ts("dve", omu.h[:, :], ct.h[:, 0:nmu], -1.0, 1.0, ALU.mult, ALU.add, r=[ct.b], w=[omu.b])
        nw0 = sbt(es, nc, "Bnw0", [128, NHC], F32)
        S.ts("dve", nw0.h[:, :], ct.h[:, co["w0"]:co["w0"] + NHC], -1.0, None, ALU.mult, r=[ct.b], w=[nw0.b])
        na0 = sbt(es, nc, "Bna0", [128, NHC], F32)
        S.ts("dve", na0.h[:, :], ct.h[:, co["a0"]:co["a0"] + NHC], -1.0, None, ALU.mult, r=[ct.b], w=[na0.b])
        omka = sbt(es, nc, "Bomka", [128, NHC], F32)
        S.ts("dve", omka.h[:, :], ct.h[:, co["k_a"]:co["k_a"] + NHC], -1.0, 1.0, ALU.mult, ALU.add, r=[ct.b], w=[omka.b])
        cm05 = sbt(es, nc, "Bcm05", [128, 1], F32)
        S.memset("pool", cm05.h[:, :], -0.5, w=[cm05.b])
        c1 = sbt(es, nc, "Bc1", [128, 1], F32)
        S.memset("pool", c1.h[:, :], 1.0, w=[c1.b])
        lw = sbrot(es, nc, "Blw", [128, 2 + GLC, 128], F32, 2)
        wl = sbt(es, nc, "Bwl", [128, TB + 1], F32)
        al = sbt(es, nc, "Bal", [128, TB + 1], F32)
        gl = sbt(es, nc, "Bgl", [128, GLC, TB + 1], F32)
        tw = sbt(es, nc, "Btw", [128, TB], F32)
        als = sbt(es, nc, "Bals", [128, TB], F32)
        sg = sbt(es, nc, "Bsg", [128, GLC, TB], F32)
        tp = sbrot(es, nc, "Btp", [128, TB + 1], F32, 28)
        Rt, KKt, Bt, Kt, Vt, Bct, Kct = [sbt(es, nc, "B" + n, [128, NHC, TB], BF16) for n in ("Rt", "KKt", "Bt", "Kt", "Vt", "Bct", "Kct")]
        gC = sbt(es, nc, "BgC", [128, NHC, NCH], F32)
        OT = sbt(es, nc, "BOT", [128, NHC, TB], F32)
        H = sbt(es, nc, "BH", [128, NHC, 128], F32)
        Hb = sbt(es, nc, "BHb", [128, NHC, 128], BF16)
        bdp = [[sbt(es, nc, "Bbd%d_%d" % (i, j), [128, NHC, 128], BF16) for j in range(3)] for i in range(1)]
        for i in range(1):
            for j in range(3):
                S.memset("pool", bdp[i][j].h[:, :, :], 0.0, w=[bdp[i][j].b])
        Hbufs = [Buf("H%d" % g) for g in range(NG)]
        S.memset("pool", H.h[:, :, :], 0.0, w=[H.b] + Hbufs)
        S.memset("pool", Hb.h[:, :, :], 0.0, w=[Hb.b], wa=Hbufs)
        SLN = ("A0", "A1", "B0", "B1", "X0", "X1", "nM3", "M2", "M4", "Vm", "Bcm", "Kcm")
        slots = [{n: sbt(es, nc, "Bs%d%s" % (g, n), [64, GW], BF16) for n in SLN} for g in range(NG)]
        fp = sbrot(es, nc, "Bfp", [64, GW], F32, 2)
        oabp = sbrot(es, nc, "Boab", [128, TB], BF16, 2)

        def bc64(t):
            return t.h[0:64, 0:64].unsqueeze(1).broadcast_to([64, 2 * HG, 64])

        def v3(ap):
            return ap.rearrange("p (h t) -> p h t", t=64)

        def load_shift(dst_ap_fn, rows, row0, t0, buf, q="sp"):
            if t0 == 0:
                S.memset("pool", dst_ap_fn(0, 1), 0.0, w=[buf])
                S.dma(q, dst_ap_fn(1, TB + 1), zT[row0:row0 + rows, 0:TB], r=[K.b["zR"]], wa=[buf])
            else:
                S.dma(q, dst_ap_fn(0, TB + 1), zT[row0:row0 + rows, t0 - 1:t0 + TB], r=[K.b["zR"]], w=[buf])

        def lerp(eng, out_ap, zt_prev, zt_cur, mu_ap, omu_ap, rbufs, wbuf):
            t = tp.get()
            n = out_ap.shape[0]
            if eng == "act":
                S.act(t.h[0:n, 0:TB], zt_prev, AF.Identity, r=rbufs, w=[t.b], scale=mu_ap)
            else:
                S.ts(eng, t.h[0:n, 0:TB], zt_prev, mu_ap, None, ALU.mult, r=rbufs, w=[t.b])
            S.stt("dve", out_ap, zt_cur, omu_ap, t.h[0:n, 0:TB], ALU.mult, ALU.add, r=rbufs + [t.b], w=[wbuf])

        ntile = C.CTX // TB
        for tt in range(ntile):
            t0 = tt * TB
            need3 = (t0 + TB > C.OWN0)
            lo = max(C.OWN0 - t0, 0)
            load_shift(lambda a, b: wl.h[0:C.DL, a:b], C.DL, C.o_wl, t0, wl.b)
            load_shift(lambda a, b: al.h[0:C.AL, a:b], C.AL, C.o_al, t0, al.b)
            for c in range(GLC):
                n = min(128, C.GL - c * 128)
                load_shift(lambda a, b, c=c, n=n: gl.h[0:n, c, a:b], n, C.o_gl + c * 128, t0, gl.b)
            cw, ca, cg = co["mu_w"], co["mu_a"], co["mu_g"]
            lerp("dve", tw.h[0:C.DL, :], wl.h[0:C.DL, 0:TB], wl.h[0:C.DL, 1:TB + 1], ct.h[0:C.DL, cw:cw + 1], omu.h[0:C.DL, cw:cw + 1], [wl.b, ct.b, omu.b], tw.b)
            S.act(tw.h[0:C.DL, :], tw.h[0:C.DL, :], AF.Tanh, w=[tw.b])
            lerp("dve", als.h[0:C.AL, :], al.h[0:C.AL, 0:TB], al.h[0:C.AL, 1:TB + 1], ct.h[0:C.AL, ca:ca + 1], omu.h[0:C.AL, ca:ca + 1], [al.b, ct.b, omu.b], als.b)
            for c in range(GLC):
                n = min(128, C.GL - c * 128)
                lerp("dve", sg.h[0:n, c, :], gl.h[0:n, c, 0:TB], gl.h[0:n, c, 1:TB + 1], ct.h[0:n, cg + c:cg + c + 1], omu.h[0:n, cg + c:cg + c + 1], [gl.b, ct.b, omu.b], sg.b)
                S.act(sg.h[0:n, c, :], sg.h[0:n, c, :], AF.Sigmoid, w=[sg.b])
            for hc in range(NHC):
                cs = slice(hc * 128, (hc + 1) * 128)
                col = lambda nm: ct.h[:, co[nm] + hc:co[nm] + hc + 1]
                zs = []
                for (o_, nm) in ((C.o_r, "mu_r"), (C.o_k, "mu_k"), (C.o_v, "mu_v")):
                    zt_ = tp.get()
                    load_shift(lambda a, b, zt_=zt_: zt_.h[:, a:b], 128, o_ + hc * 128, t0, zt_.b)
                    out = tp.get()
                    mo = co[nm] + hc
                    lerp("act", out.h[:, 0:TB], zt_.h[:, 0:TB], zt_.h[:, 1:TB + 1], ct.h[:, mo:mo + 1], omu.h[:, mo:mo + 1], [zt_.b, ct.b, omu.b], out.b)
                    zs.append(out)
                r_s, k_s, v_s = zs
                X = slice(0, TB)
                lwt = lw.get()
                S.dma("sp", lwt.h[0:C.DL, 0, :], K.d["w2"][:, cs], w=[lwt.b])
                S.dma("sp", lwt.h[0:C.AL, 1, :], K.d["a2"][:, cs], wa=[lwt.b])
                for c in range(GLC):
                    n = min(128, C.GL - c * 128)
                    S.dma("sp", lwt.h[0:n, 2 + c, :], K.d["g2"][c * 128:c * 128 + n, cs], wa=[lwt.b])
                ps = K.ps.get()
                S.mms([(ps.h[:, 0:TB], lwt.h[0:C.DL, 0, :], tw.h[0:C.DL, :], True, True)], r=[lwt.b, tw.b], w=[ps.b])
                e1 = tp.get()
                S.act(e1.h[:, X], ps.h[:, 0:TB], AF.Exp, r=[nw0.b], w=[e1.b, ps.b], scale=-1.0, bias=nw0.h[:, hc:hc + 1])
                S.act(e1.h[:, X], e1.h[:, X], AF.Ln, r=[c1.b], w=[e1.b], bias=c1.h[:, 0:1])
                elw = tp.get()
                S.act(elw.h[:, X], e1.h[:, X], AF.Exp, r=[e1.b, cm05.b], w=[elw.b], scale=-1.0, bias=cm05.h[:, 0:1])
                ps = K.ps.get()
                S.mms([(ps.h[:, 0:TB], lwt.h[0:C.AL, 1, :], als.h[0:C.AL, :], True, True)], r=[lwt.b, als.b], w=[ps.b])
                a_ = tp.get()
                S.act(a_.h[:, X], ps.h[:, 0:TB], AF.Exp, r=[na0.b], w=[a_.b, ps.b], scale=-1.0, bias=na0.h[:, hc:hc + 1])
                S.ts("dve", a_.h[:, X], a_.h[:, X], 1.0, None, ALU.add, w=[a_.b])
                S.recip(a_.h[:, X], a_.h[:, X], w=[a_.b])
                if need3:
                    ps = K.ps.get()
                    items = []
                    for c in range(GLC):
                        n = min(128, C.GL - c * 128)
                        items.append((ps.h[:, 0:TB], lwt.h[0:n, 2 + c, :], sg.h[0:n, c, :], c == 0, c == GLC - 1))
                    S.mms(items, r=[lwt.b, sg.b], w=[ps.b])
                    gt_ = tp.get()
                    S.copy("act", gt_.h[:, X], ps.h[:, 0:TB], w=[gt_.b, ps.b])
                    S.dma("sp", K.d["gT"][cs, t0:t0 + TB], gt_.h[:, X], r=[gt_.b], wa=[K.b["gT"]])
                kk = tp.get()
                S.act(kk.h[:, X], k_s.h[:, X], AF.Identity, r=[k_s.b, ct.b], w=[kk.b], scale=col("k_k"))
                kk2 = tp.get()
                S.act(kk2.h[:, X], kk.h[:, X], AF.Square, r=[kk.b], w=[kk2.b])
                ps = K.ps.get()
                S.mms([(ps.h[:, 0:TB], blk.h[:, :], kk2.h[:, X], True, True)], r=[blk.b, kk2.b], w=[ps.b])
                rn = tp.get()
                S.ts("dve", rn.h[:, X], ps.h[:, 0:TB], 1e-24, None, ALU.max, w=[rn.b, ps.b])
                S.act(rn.h[:, X], rn.h[:, X], AF.Ln, w=[rn.b])
                S.act(rn.h[:, X], rn.h[:, X], AF.Exp, w=[rn.b], scale=-0.5)
                kkn = tp.get()
                S.tt("dve", kkn.h[:, X], kk.h[:, X], rn.h[:, X], ALU.mult, r=[kk.b, rn.b], w=[kkn.b])
                t1 = tp.get()
                S.ts("dve", t1.h[:, X], a_.h[:, X], col("k_a"), omka.h[:, hc:hc + 1], ALU.mult, ALU.add, r=[a_.b, ct.b, omka.b], w=[t1.b])
                kmod = tp.get()
                S.tt("pool", kmod.h[:, X], k_s.h[:, X], t1.h[:, X], ALU.mult, r=[k_s.b, t1.b], w=[kmod.b])
                b_ = tp.get()
                S.tt("pool", b_.h[:, X], kkn.h[:, X], a_.h[:, X], ALU.mult, r=[kkn.b, a_.b], w=[b_.b])
                if need3:
                    rkr = tp.get()
                    S.stt("dve", rkr.h[:, X], r_s.h[:, X], col("r_k"), kmod.h[:, X], ALU.mult, ALU.mult, r=[r_s.b, ct.b, kmod.b], w=[rkr.b])
                    ps = K.ps.get()
                    S.mms([(ps.h[:, 0:TB], blk.h[:, :], rkr.h[:, X], True, True)], r=[blk.b, rkr.b], w=[ps.b])
                    bon = tp.get()
                    S.tt("dve", bon.h[:, X], ps.h[:, 0:TB], v_s.h[:, X], ALU.mult, r=[v_s.b], w=[bon.b, ps.b])
                    S.dma("sp", K.d["bonT"][cs, t0:t0 + TB], bon.h[:, X], r=[bon.b], wa=[K.b["bonT"]])
                cum = tp.get()
                S.scan(cum.h[:, X], scanm.h[:, 0:TB], elw.h[:, X], 0.0, ALU.mult, ALU.add, r=[scanm.b, elw.b], w=[cum.b])
                gi = tp.get()
                S.act(gi.h[:, X], cum.h[:, X], AF.Exp, r=[cum.b], w=[gi.b], scale=-1.0)
                ge = tp.get()
                S.act(ge.h[:, X], cum.h[:, X], AF.Exp, r=[cum.b], w=[ge.b])
                gx = tp.get()
                S.tt("pool", gx.h[:, X], cum.h[:, X], elw.h[:, X], ALU.subtract, r=[cum.b, elw.b], w=[gx.b])
                S.act(gx.h[:, X], gx.h[:, X], AF.Exp, w=[gx.b], scale=-1.0)
                S.tt("dve", Rt.h[:, hc, :], r_s.h[:, X], gi.h[:, X], ALU.mult, r=[r_s.b, gi.b], wa=[Rt.b])
                S.tt("pool", KKt.h[:, hc, :], kkn.h[:, X], gx.h[:, X], ALU.mult, r=[kkn.b, gx.b], wa=[KKt.b])
                tb_ = tp.get()
                S.tt("dve", tb_.h[:, X], b_.h[:, X], ge.h[:, X], ALU.mult, r=[b_.b, ge.b], w=[tb_.b])
                tk_ = tp.get()
                S.tt("pool", tk_.h[:, X], kmod.h[:, X], ge.h[:, X], ALU.mult, r=[kmod.b, ge.b], w=[tk_.b])
                S.copy("act", Bt.h[:, hc, :], tb_.h[:, X], r=[tb_.b], wa=[Bt.b])
                S.copy("act", Kt.h[:, hc, :], tk_.h[:, X], r=[tk_.b], wa=[Kt.b])
                S.copy("act", Vt.h[:, hc, :], v_s.h[:, X], r=[v_s.b], wa=[Vt.b])
                S.copy("dve", gC.h[:, hc, :], gi.h[:, X].rearrange("p (c t) -> p c t", t=64)[:, :, 63], r=[gi.b], wa=[gC.b])
                gcb = gC.h[:, hc, :].unsqueeze(2).broadcast_to([128, NCH, 64])
                S.stt("dve", Bct.h[:, hc, :].rearrange("p (c t) -> p c t", t=64), tb_.h[:, X].rearrange("p (c t) -> p c t", t=64), -1.0, gcb,
                      ALU.mult, ALU.mult, r=[tb_.b, gC.b], wa=[Bct.b])
                S.tt("pool", Kct.h[:, hc, :].rearrange("p (c t) -> p c t", t=64), tk_.h[:, X].rearrange("p (c t) -> p c t", t=64), gcb,
                     ALU.mult, r=[tk_.b, gC.b], wa=[Kct.b])
            import os
            for ci in range(NCH if not os.environ.get("BSKIP2") else 0):
                cc = slice(ci * 64, (ci + 1) * 64)
                hd = lambda g, hh: (g * HG + hh // 2, (hh % 2) * 64)
                NHG = 2 * HG
                st = [dict() for _ in range(NG)]
                bd = bdp[0]
                for j, src in enumerate((KKt, Rt, Bt)):
                    eng = ("pool", "act", "pool")[j]
                    S.copy(eng, bd[j].h[0:64, :, 0:64], src.h[0:64, :, cc], r=[src.b], w=[bd[j].b])
                    S.copy(eng, bd[j].h[64:128, :, 64:128], src.h[64:128, :, cc], r=[src.b], wa=[bd[j].b])
                KKbd, Rbd, Bbd = bd
                for g in range(NG):
                    d = st[g]
                    def prod(lT, rbd):
                        ps = K.ps.get()
                        items = []
                        for hl in range(HG):
                            hc = g * HG + hl
                            items.append((ps.h[0:64, hl * 128:(hl + 1) * 128], lT.h[:, hc, cc], rbd.h[:, hc, :], True, True))
                        S.mms(items, r=[lT.b, rbd.b], w=[ps.b])
                        return ps
                    ps = prod(Bt, KKbd)
                    sl_ = slots[g]
                    d["A"] = sl_["A0"]
                    S.tt("dve", v3(d["A"].h[:, :]), v3(ps.h[0:64, 0:GW]), bc64(mUs), ALU.mult, r=[mUs.b], w=[d["A"].b, ps.b])
                    bprep = int(os.environ.get("BPREP", "99"))
                    if bprep <= 1:
                        continue
                    d["X"] = sl_["X0"]
                    S.tt("pool", v3(d["X"].h[:, :]), bc64(ident), v3(d["A"].h[:, :]), ALU.subtract, r=[ident.b, d["A"].b], w=[d["X"].b])
                    if bprep <= 2:
                        continue
                    ps = prod(KKt, Bbd)
                    d["Bq"] = sl_["B0"]
                    S.tt("dve", v3(d["Bq"].h[:, :]), v3(ps.h[0:64, 0:GW]), bc64(mLs), ALU.mult, r=[mLs.b], w=[d["Bq"].b, ps.b])
                    ps = prod(Bt, Rbd)
                    d["nM3"] = sl_["nM3"]
                    S.stt("dve", v3(d["nM3"].h[:, :]), v3(ps.h[0:64, 0:GW]), -1.0, bc64(mUi), ALU.mult, ALU.mult, r=[mUi.b], w=[d["nM3"].b, ps.b])
                    ps = prod(Kt, KKbd)
                    d["M2"] = sl_["M2"]
                    S.tt("dve", v3(d["M2"].h[:, :]), v3(ps.h[0:64, 0:GW]), bc64(mUs), ALU.mult, r=[mUs.b], w=[d["M2"].b, ps.b])
                    ps = prod(Kt, Rbd)
                    d["M4"] = sl_["M4"]
                    S.tt("dve", v3(d["M4"].h[:, :]), v3(ps.h[0:64, 0:GW]), bc64(mUi), ALU.mult, r=[mUi.b], w=[d["M4"].b, ps.b])
                    if bprep <= 3:
                        continue
                    for nm, src in (("Vm", Vt), ("Bcm", Bct), ("Kcm", Kct)):
                        ps = K.ps.get()
                        items = []
                        for hl in range(HG):
                            hc = g * HG + hl
                            items.append((ps.h[0:64, hl * 128:(hl + 1) * 128], src.h[:, hc, cc], identb.h[:, :], True, True))
                        S.mms(items, r=[src.b, identb.b], w=[ps.b])
                        d[nm] = sl_[nm]
                        S.copy("act", d[nm].h[:, :], ps.h[0:64, 0:GW], w=[d[nm].b, ps.b])
                bstop = int(os.environ.get("BSTOP", "99"))
                if bstop <= 1:
                    continue
                for lvl in range(1, 6):
                    for g in range(NG):
                        d = st[g]
                        def hmm(l, r_):
                            ps = K.ps.get()
                            items = [(ps.h[0:64, hh * 64:(hh + 1) * 64], l.h[0:64, hh * 64:(hh + 1) * 64], r_.h[0:64, hh * 64:(hh + 1) * 64], True, True)
                                     for hh in range(NHG)]
                            S.mms(items, r=[l.b, r_.b], w=[ps.b])
                            return ps
                        psB = hmm(d["A"], d["Bq"])
                        nB = slots[g]["B%d" % (lvl % 2)]
                        S.copy("act", nB.h[:, :], psB.h[0:64, 0:GW], w=[nB.b, psB.b])
                        if lvl < 5:
                            psA = hmm(d["Bq"], d["A"])
                            nA = slots[g]["A%d" % (lvl % 2)]
                            S.copy("dve", nA.h[:, :], psA.h[0:64, 0:GW], w=[nA.b, psA.b])
                            d["A"] = nA
                        d["Bq"] = nB
                    for g in range(NG):
                        d = st[g]
                        ps = K.ps.get()
                        items = [(ps.h[0:64, hh * 64:(hh + 1) * 64], d["Bq"].h[0:64, hh * 64:(hh + 1) * 64], d["X"].h[0:64, hh * 64:(hh + 1) * 64], True, True)
                                 for hh in range(NHG)]
                        S.mms(items, r=[d["Bq"].b, d["X"].b], w=[ps.b])
                        nX = slots[g]["X%d" % (lvl % 2)]
                        S.tt("dve", nX.h[:, :], ps.h[0:64, 0:GW], d["X"].h[:, :], ALU.add, r=[d["X"].b], w=[nX.b, ps.b])
                        d["X"] = nX
                if bstop <= 2:
                    continue
                for g in range(NG):
                    d = st[g]
                    ps = K.ps.get()
                    items = []
                    for hl in range(HG):
                        hc = g * HG + hl
                        items.append((ps.h[0:64, hl * 128:(hl + 1) * 128], KKt.h[:, hc, cc], Hb.h[:, hc, :], True, False))
                        for e2 in range(2):
                            sl = slice(hl * 128 + e2 * 64, hl * 128 + e2 * 64 + 64)
                            items.append((ps.h[0:64, sl], d["M2"].h[0:64, sl], d["Vm"].h[0:64, sl], False, e2 == 1))
                    S.mms(items, r=[KKt.b, Hbufs[g], d["M2"].b, d["Vm"].b], w=[ps.b])
                    d["W"] = slots[g]["A1"]
                    S.copy("act", d["W"].h[:, :], ps.h[0:64, 0:GW], w=[d["W"].b, ps.b])
                for g in range(NG):
                    d = st[g]
                    ps = K.ps.get()
                    items = [(ps.h[0:64, hh * 64:(hh + 1) * 64], d["X"].h[0:64, hh * 64:(hh + 1) * 64], d["W"].h[0:64, hh * 64:(hh + 1) * 64], True, True)
                             for hh in range(NHG)]
                    S.mms(items, r=[d["X"].b, d["W"].b], w=[ps.b])
                    d["U"] = slots[g]["B0"]
                    S.copy("dve", d["U"].h[:, :], ps.h[0:64, 0:GW], w=[d["U"].b, ps.b])
                if bstop <= 3:
                    continue
                for g in range(NG):
                    d = st[g]
                    ps = K.ps.get()
                    items = []
                    for hl in range(HG):
                        hc = g * HG + hl
                        items.append((ps.h[0:64, hl * 128:(hl + 1) * 128], Rt.h[:, hc, cc], Hb.h[:, hc, :], True, False))
                        for e2 in range(2):
                            sl = slice(hl * 128 + e2 * 64, hl * 128 + e2 * 64 + 64)
                            items.append((ps.h[0:64, sl], d["nM3"].h[0:64, sl], d["U"].h[0:64, sl], False, False))
                            items.append((ps.h[0:64, sl], d["M4"].h[0:64, sl], d["Vm"].h[0:64, sl], False, e2 == 1))
                    S.mms(items, r=[Rt.b, Hbufs[g], d["nM3"].b, d["U"].b, d["M4"].b, d["Vm"].b], w=[ps.b])
                    if need3:
                        otm = fp.get()
                        S.copy("act", otm.h[:, :], ps.h[0:64, 0:GW], w=[otm.b, ps.b])
                        ps2 = K.ps.get()
                        for hl in range(HG):
                            S.transpose(ps2.h[:, hl * 64:(hl + 1) * 64], otm.h[0:64, hl * 128:(hl + 1) * 128], ident.h[0:64, 0:64],
                                        r=[otm.b, ident.b], w=[ps2.b] if hl == 0 else [], wa=[] if hl == 0 else [ps2.b])
                        S.copy("dve", OT.h[:, g * HG:(g + 1) * HG, cc], ps2.h[:, 0:HG * 64].rearrange("p (h t) -> p h t", t=64), w=[ps2.b], wa=[OT.b])
                    ps = K.ps.get()
                    items = []
                    for hl in range(HG):
                        sl = slice(hl * 128, (hl + 1) * 128)
                        o = ps.h[:, sl]
                        items.append((o, d["Bcm"].h[0:64, sl], d["U"].h[0:64, sl], True, False))
                        items.append((o, d["Kcm"].h[0:64, sl], d["Vm"].h[0:64, sl], False, True))
                    S.mms(items, r=[d["Bcm"].b, d["U"].b, d["Kcm"].b, d["Vm"].b], w=[ps.b])
                    for hh in range(NHG):
                        hc, pb = hd(g, hh)
                        hl = hh // 2
                        S.stt("dve", H.h[pb:pb + 64, hc, pb:pb + 64], H.h[pb:pb + 64, hc, pb:pb + 64], gC.h[pb:pb + 64, hc, ci:ci + 1],
                              ps.h[pb:pb + 64, hl * 128 + pb: hl * 128 + pb + 64], ALU.mult, ALU.add,
                              r=[gC.b], w=[Hbufs[g], ps.b])
                    S.copy("pool", Hb.h[:, g * HG:(g + 1) * HG, :], H.h[:, g * HG:(g + 1) * HG, :], w=[Hbufs[g]])
            if need3 and not os.environ.get("BSKIP3"):
                for hc in range(NHC):
                    cs = slice(hc * 128, (hc + 1) * 128)
                    col = lambda nm: ct.h[:, co[nm] + hc:co[nm] + hc + 1]
                    X = slice(0, TB)
                    gt_ = tp.get()
                    S.dma("sp", gt_.h[:, X], K.d["gT"][cs, t0:t0 + TB], r=[K.b["gT"]], w=[gt_.b])
                    bon = tp.get()
                    S.dma("sp", bon.h[:, X], K.d["bonT"][cs, t0:t0 + TB], r=[K.b["bonT"]], w=[bon.b])
                    ps = K.ps.get()
                    S.mms([(ps.h[:, 0:TB], blk.h[:, :], OT.h[:, hc, :], True, True)], r=[blk.b, OT.b], w=[ps.b])
                    cen = tp.get()
                    S.stt("dve", cen.h[:, X], ps.h[:, 0:TB], -1.0 / 64, OT.h[:, hc, :], ALU.mult, ALU.add, r=[OT.b], w=[cen.b, ps.b])
                    sq = tp.get()
                    S.act(sq.h[:, X], cen.h[:, X], AF.Square, r=[cen.b], w=[sq.b])
                    ps = K.ps.get()
                    S.mms([(ps.h[:, 0:TB], blk.h[:, :], sq.h[:, X], True, True)], r=[blk.b, sq.b], w=[ps.b])
                    rs = tp.get()
                    S.ts("dve", rs.h[:, X], ps.h[:, 0:TB], 1.0 / 64, C.GN_EPS, ALU.mult, ALU.add, w=[rs.b, ps.b])
                    S.act(rs.h[:, X], rs.h[:, X], AF.Ln, w=[rs.b])
                    S.act(rs.h[:, X], rs.h[:, X], AF.Exp, w=[rs.b], scale=-0.5)
                    y = tp.get()
                    S.tt("pool", y.h[:, X], cen.h[:, X], rs.h[:, X], ALU.mult, r=[cen.b, rs.b], w=[y.b])
                    S.ts("pool", y.h[:, X], y.h[:, X], col("ln_w"), col("ln_b"), ALU.mult, ALU.add, r=[ct.b], w=[y.b])
                    S.tt("pool", y.h[:, X], y.h[:, X], bon.h[:, X], ALU.add, r=[bon.b], w=[y.b])
                    S.tt("dve", y.h[:, X], y.h[:, X], gt_.h[:, X], ALU.mult, r=[gt_.b], w=[y.b])
                    obf = oabp.get()
                    S.copy("act", obf.h[:, :], y.h[:, X], r=[y.b], w=[obf.b])
                    S.dma("sp", K.d["oaT"][cs, t0 + lo - C.OWN0:t0 + TB - C.OWN0], obf.h[:, lo:TB], r=[obf.b], wa=[K.b["oaT"]])
        K.rem.append(nc.sbuf_bytes_remaining)
        S.emit()


def phase_C(K):
    S, C, nc = K.S, K.C, K.nc
    NKB = C.CTX // 128
    with contextlib.ExitStack() as es:
        allps = psrot(es, nc)
        K.ps = Rot(allps.t[0:4])
        acc = Rot(allps.t[4:8])
        ident = load_const(K, es, "Cident", "ident")
        blk = load_const(K, es, "Cblk", "blk")
        ones = load_const(K, es, "Cones", "ones")
        onesb = load_const(K, es, "Conesb", "ones", dt=BF16)
        pvb = load_const(K, es, "Cpvb", "pvalid", dt=BF16)
        ct, co = load_cols(K, es, "Ccols", ["q_g", "k_g", "subln"])
        lt = sbt(es, nc, "Clamb", [128, 256], F32)
        S.dma("sp", lt.h[:, :], K.d["lamb"][:, :], w=[lt.b])
        sc = sbt(es, nc, "Csc", [128, 12], F32)
        S.memset("dve", sc.h[:, :], 0.0, w=[sc.b])
        tmp = sbt(es, nc, "Cltmp", [128, 128], F32)
        S.tt("dve", tmp.h[:, 0:64], lt.h[:, 0:64], lt.h[:, 64:128], ALU.mult, r=[lt.b], w=[tmp.b])
        S.tt("dve", tmp.h[:, 64:128], lt.h[:, 128:192], lt.h[:, 192:256], ALU.mult, r=[lt.b], w=[tmp.b])
        S.act(tmp.h[:, 0:64], tmp.h[:, 0:64], AF.Identity, w=[tmp.b, sc.b], accum_out=sc.h[:, 0:1])
        S.act(tmp.h[:, 64:128], tmp.h[:, 64:128], AF.Identity, w=[tmp.b, sc.b], accum_out=sc.h[:, 1:2])
        S.act(sc.h[:, 2:4], sc.h[:, 0:2], AF.Exp, w=[sc.b])
        S.tt("dve", sc.h[:, 4:5], sc.h[:, 3:4], sc.h[:, 2:3], ALU.subtract, w=[sc.b])
        S.ts("dve", sc.h[:, 5:6], sc.h[:, 4:5], -C.lambda_init, None, ALU.add, w=[sc.b])
        neglam = sc.h[:, 5:6]
        S.ts("dve", sc.h[:, 6:7], ct.h[:, co["q_g"]:co["q_g"] + 1], 0.125, None, ALU.mult, r=[ct.b], w=[sc.b])
        S.ts("dve", sc.h[:, 7:8], ct.h[:, co["subln"]:co["subln"] + 1], 1.0 - C.lambda_init, None, ALU.mult, r=[ct.b], w=[sc.b])
        qg8, slg = sc.h[:, 6:7], sc.h[:, 7:8]
        kg = ct.h[:, co["k_g"]:co["k_g"] + 1]
        zq = sbrot(es, nc, "Czq", [128, C.NOH], F32, 2)
        zk = sbrot(es, nc, "Czk", [128, C.CTX], F32, 2)
        zv = sbrot(es, nc, "Czv", [128, C.CTX], F32, 2)
        qn = sbt(es, nc, "Cqn", [128, C.NOH], BF16)
        kn = [sbt(es, nc, "Ckn%d" % c, [128, C.CTX], BF16) for c in range(2)]
        S.memset("dve", kn[0].h[:, :], 0.0, w=[kn[0].b])
        S.memset("dve", kn[1].h[:, :], 0.0, w=[kn[1].b])
        Vtm = sbt(es, nc, "CVtm", [128, NKB, 128], BF16)
        tq = sbrot(es, nc, "Ctq", [128, 512], F32, 6)
        ptp = sbrot(es, nc, "Cpt", [128, 512], BF16, 4)
        ocp = sbrot(es, nc, "Coc", [128, 512], F32, 4)
        obp = sbrot(es, nc, "Cob", [128, 512], BF16, 2)

        def rmsn(src, n_tot, gcol, outs):
            for (c0, cn) in chunks(n_tot, 512):
                sq = tq.get()
                S.act(sq.h[:, 0:cn], src.h[:, c0:c0 + cn], AF.Square, r=[src.b], w=[sq.b])
                ps = K.ps.get()
                S.mms([(ps.h[:, 0:cn], blk.h[:, :], sq.h[:, 0:cn], True, True)], r=[blk.b, sq.b], w=[ps.b])
                rs = tq.get()
                S.ts("dve", rs.h[:, 0:cn], ps.h[:, 0:cn], 1.0 / 64, C.EPS, ALU.mult, ALU.add, w=[rs.b, ps.b])
                S.act(rs.h[:, 0:cn], rs.h[:, 0:cn], AF.Ln, w=[rs.b])
                S.act(rs.h[:, 0:cn], rs.h[:, 0:cn], AF.Exp, w=[rs.b], scale=-0.5)
                for (r0, r1, dst) in outs:
                    S.stt("dve", dst.h[r0:r1, c0:c0 + cn], src.h[r0:r1, c0:c0 + cn], gcol[r0:r1, :], rs.h[r0:r1, 0:cn], ALU.mult, ALU.mult,
                          r=[src.b, rs.b, sc.b, ct.b], wa=[dst.b])

        precast_step(K, len(K.pc))
        for h in range(C.NDH):
            rows = slice(h * 128, (h + 1) * 128)
            q_ = zq.get()
            S.dma("sp", q_.h[:, :], K.d["zQ"][h * 128:(h + 1) * 128, 0:C.NOH], r=[K.b["zQ"]], w=[q_.b])
            k_ = zk.get()
            S.dma("sp", k_.h[:, :], K.d["zKV"][h * 128:(h + 1) * 128, 0:C.CTX], r=[K.b["zKV"]], w=[k_.b])
            v_ = zv.get()
            S.dma("sp", v_.h[:, :], K.d["zKV"][C.DW + h * 128:C.DW + (h + 1) * 128, 0:C.CTX], r=[K.b["zKV"]], w=[v_.b])
            S.memset("dve", qn.h[:, 0:1], 0.0, w=[qn.b])
            S.memset("dve", kn[0].h[0:64, 0:1], 0.0, w=[kn[0].b])
            S.memset("dve", kn[1].h[64:128, 0:1], 0.0, w=[kn[1].b])
            rmsn(q_, C.NOH, qg8, [(0, 128, qn)])
            rmsn(k_, C.CTX, kg, [(0, 64, kn[0]), (64, 128, kn[1])])
            first = True
            for kb0 in range(0, NKB, 4):
                ps = K.ps.get()
                n = min(4, NKB - kb0)
                for j in range(n):
                    S.transpose(ps.h[:, j * 128:(j + 1) * 128], v_.h[:, (kb0 + j) * 128:(kb0 + j + 1) * 128], ident.h[:, :],
                                r=[v_.b, ident.b], w=[ps.b] if j == 0 else [], wa=[] if j == 0 else [ps.b])
                S.copy(S.ev(), Vtm.h[:, kb0:kb0 + n, :], ps.h[:, 0:n * 128].rearrange("p (k e) -> p k e", e=128),
                       w=[ps.b] + ([Vtm.b] if first else []), wa=[] if first else [Vtm.b])
                first = False
            for (q0, NQ) in C.tiles_oh():
                jq = q0 - C.OWN0
                nkb = (q0 + NQ) // 128
                ocs = []
                for c in range(2):
                    psO = acc.get()
                    psL = acc.get()

                    def smm(kb):
                        j0 = max(kb * 128 - q0, 0)
                        n = NQ - j0
                        ps = K.ps.get()
                        S.mms([(ps.h[:, 0:n], kn[c].h[:, kb * 128:(kb + 1) * 128], qn.h[:, jq + j0:jq + NQ], True, True)],
                              r=[kn[c].b, qn.b], w=[ps.b])
                        return ps, j0, n
                    nxt = smm(0)
                    for kb in range(nkb):
                        ps, j0, n = nxt
                        if kb + 1 < nkb:
                            nxt = smm(kb + 1)
                        pt = ptp.get()
                        S.act(pt.h[:, 0:n], ps.h[:, 0:n], AF.Exp, w=[pt.b, ps.b])
                        if kb * 128 >= q0:
                            S.memset("dve", pt.h[64:128, 0:64], 0.0, w=[pt.b])
                        lv = pvb if kb * 128 < C.HALF else onesb
                        S.mms([(psO.h[:, j0:NQ], Vtm.h[:, kb, :], pt.h[:, 0:n], kb == 0, kb == nkb - 1)], r=[Vtm.b, pt.b],
                              w=[psO.b] if kb == 0 else [], wa=[] if kb == 0 else [psO.b])
                        S.mms([(psL.h[:, j0:NQ], lv.h[:, :], pt.h[:, 0:n], kb == 0, kb == nkb - 1)], r=[lv.b, pt.b],
                              w=[psL.b] if kb == 0 else [], wa=[] if kb == 0 else [psL.b])
                    rl = tq.get()
                    S.ts("dve", rl.h[:, 0:NQ], psL.h[:, 0:NQ], 1e-30, None, ALU.max, w=[rl.b, psL.b])
                    S.recip(rl.h[:, 0:NQ], rl.h[:, 0:NQ], w=[rl.b])
                    oc = ocp.get()
                    S.tt("dve", oc.h[:, 0:NQ], psO.h[:, 0:NQ], rl.h[:, 0:NQ], ALU.mult, r=[rl.b], w=[oc.b, psO.b])
                    ocs.append(oc)
                df = tq.get()
                S.stt("dve", df.h[:, 0:NQ], ocs[1].h[:, 0:NQ], neglam, ocs[0].h[:, 0:NQ], ALU.mult, ALU.add, r=[ocs[0].b, ocs[1].b, sc.b], w=[df.b])
                sq = tq.get()
                S.act(sq.h[:, 0:NQ], df.h[:, 0:NQ], AF.Square, r=[df.b], w=[sq.b])
                ps = K.ps.get()
                S.mms([(ps.h[:, 0:NQ], ones.h[:, :], sq.h[:, 0:NQ], True, True)], r=[ones.b, sq.b], w=[ps.b])
                rs = tq.get()
                S.ts("dve", rs.h[:, 0:NQ], ps.h[:, 0:NQ], 1.0 / 128, C.EPS, ALU.mult, ALU.add, w=[rs.b, ps.b])
                S.act(rs.h[:, 0:NQ], rs.h[:, 0:NQ], AF.Ln, w=[rs.b])
                S.act(rs.h[:, 0:NQ], rs.h[:, 0:NQ], AF.Exp, w=[rs.b], scale=-0.5)
                ob = obp.get()
                S.stt("dve", ob.h[:, 0:NQ], df.h[:, 0:NQ], slg, rs.h[:, 0:NQ], ALU.mult, ALU.mult, r=[df.b, rs.b, sc.b], w=[ob.b])
                S.dma("sp", K.d["obT"][rows, jq:jq + NQ], ob.h[:, 0:NQ], r=[ob.b], wa=[K.b["obT"]])
        K.rem.append(nc.sbuf_bytes_remaining)
        S.emit()


def phase_DE(K):
    S, C, nc = K.S, K.C, K.nc
    DC = C.D // 128
    RCH = C.RW // 128
    with contextlib.ExitStack() as es:
        K.ps = psrot(es, nc)
        oat = sbt(es, nc, "Doat", [128, RCH, C.NT], BF16)
        obt = sbt(es, nc, "Dobt", [128, RCH, C.NT], BF16)
        mT = sbt(es, nc, "DmT", [128, DC, C.NT], BF16)
        wap = sbrot(es, nc, "Dwa", [128, RCH, 256], BF16, 2)
        wbp = sbrot(es, nc, "Dwb", [128, RCH, 256], BF16, 2)
        wop = sbrot(es, nc, "Dwo", [128, DC, 512], BF16, 2)
        gp = sbrot(es, nc, "Dg", [128, C.NT], F32, 4)
        mp_ = sbrot(es, nc, "Dm", [128, C.NT], F32, 4)
        xp = sbrot(es, nc, "Dx", [128, 512], F32, 4)
        oav = K.d["oaT"].rearrange("(c p) t -> p c t", p=128)
        obv = K.d["obT"].rearrange("(c p) t -> p c t", p=128)
        for (t0, nt) in C.tiles_oh():
            jq = t0 - C.OWN0
            S.dma("sp", oat.h[:, :, 0:nt], oav[:, :, jq:jq + nt], r=[K.b["oaT"]], w=[oat.b])
            S.dma("sp", obt.h[:, :, 0:nt], obv[:, :, jq:jq + nt], r=[K.b["obT"]], w=[obt.b])
            firstm = True
            for (g0, gw) in chunks(C.D, 256):
                wa_ = wap.get()
                load_w(K, wa_, K.d["bf_w_a"], g0 // 256, 0, RCH, gw, rbuf=K.b["bf_w_a"])
                wb_ = wbp.get()
                load_w(K, wb_, K.d["bf_w_b"], g0 // 256, 0, RCH, gw, rbuf=K.b["bf_w_b"])
                for (j0, wj) in chunks(gw, 128):
                    col = g0 + j0
                    psA = K.ps.get()
                    S.mms([(psA.h[0:wj, 0:nt], wa_.h[:, k, j0:j0 + wj], oat.h[:, k, 0:nt], k == 0, k == RCH - 1) for k in range(RCH)],
                          r=[wa_.b, oat.b], w=[psA.b])
                    psB = K.ps.get()
                    S.mms([(psB.h[0:wj, 0:nt], wb_.h[:, k, j0:j0 + wj], obt.h[:, k, 0:nt], k == 0, k == RCH - 1) for k in range(RCH)],
                          r=[wb_.b, obt.b], w=[psB.b])
                    ga = gp.get()
                    S.dma("sp", ga.h[:, 0:nt], K.d["zG"][col:col + 128, jq:jq + nt], r=[K.b["zG"]], w=[ga.b])
                    gb = gp.get()
                    S.dma("sp", gb.h[:, 0:nt], K.d["zG"][C.D + col:C.D + col + 128, jq:jq + nt], r=[K.b["zG"]], w=[gb.b])
                    S.act(ga.h[:, 0:nt], ga.h[:, 0:nt], AF.Sigmoid, w=[ga.b])
                    S.act(gb.h[:, 0:nt], gb.h[:, 0:nt], AF.Sigmoid, w=[gb.b])
                    m1 = mp_.get()
                    S.tt("dve", m1.h[:, 0:nt], psA.h[:, 0:nt], ga.h[:, 0:nt], ALU.mult, r=[ga.b], w=[m1.b, psA.b])
                    m2 = mp_.get()
                    S.tt("dve", m2.h[:, 0:nt], psB.h[:, 0:nt], gb.h[:, 0:nt], ALU.mult, r=[gb.b], w=[m2.b, psB.b])
                    S.tt("pool", mT.h[:, col // 128, 0:nt], m1.h[:, 0:nt], m2.h[:, 0:nt], ALU.add, r=[m1.b, m2.b],
                         w=[mT.b] if firstm else [], wa=[] if firstm else [mT.b])
                    firstm = False
            for (g0, gw) in chunks(C.D, 512):
                wo_ = wop.get()
                load_w(K, wo_, K.d["bf_w_o"], g0 // 512, 0, DC, gw, rbuf=K.b["bf_w_o"])
                for s_ in range(nt // 128):
                    ps = K.ps.get()
                    S.mms([(ps.h[:, 0:gw], mT.h[:, k, s_ * 128:(s_ + 1) * 128], wo_.h[:, k, 0:gw], k == 0, k == DC - 1) for k in range(DC)],
                          r=[mT.b, wo_.b], w=[ps.b])
                    xt = xp.get()
                    S.dma("sp", xt.h[:, 0:gw], K.d["xc"][t0 + s_ * 128:t0 + (s_ + 1) * 128, g0:g0 + gw], w=[xt.b])
                    S.tt("dve", xt.h[:, 0:gw], ps.h[:, 0:gw], xt.h[:, 0:gw], ALU.add, w=[xt.b, ps.b])
                    S.dma("sp", K.d["x1"][jq + s_ * 128:jq + (s_ + 1) * 128, g0:g0 + gw], xt.h[:, 0:gw], r=[xt.b], wa=[K.b["x1"]])
        K.rem.append(nc.sbuf_bytes_remaining)
        S.emit()


def phase_F(K):
    S, C, nc = K.S, K.C, K.nc
    DC = C.D // 128
    FB = C.DFF // 128
    KH = FB // 2
    SEG = chunks(KH, 22)
    with contextlib.ExitStack() as es:
        allps = psrot(es, nc)
        K.ps = Rot(allps.t[0:4])
        acc = allps.t[4:8]
        aT = sbt(es, nc, "FaT", [128, KH, C.NT], BF16)
        junk = Ctx()
        junk.h = aT.h[:, 0:C.D // C.NT, :].rearrange("p a b -> p (a b)")
        junk.b = aT.b
        R = norm_res(K, es, "F", nxs=2, junk=junk)
        ct, co = load_cols(K, es, "Fcols", ["g_ffn", "cw0", "cw1", "cw2", "cb"])
        pv = load_const(K, es, "Fpv", "pvalid")
        hT = sbt(es, nc, "FhT", [128, DC, C.NT], BF16)
        wgp = sbrot(es, nc, "Fwg", [128, DC, 256], BF16, 2)
        wfp = sbrot(es, nc, "Fwf", [128, max(n for _, n in SEG), 256], BF16, 2)
        halo = sbt(es, nc, "Fhalo", [128, 2 * FB, 2], F32)
        S.memset("pool", halo.h[:, :, :], 0.0, w=[halo.b])
        ubp = sbrot(es, nc, "Fub", [128, C.NT + 2], F32, 4)
        cvp = sbrot(es, nc, "Fcv", [128, C.NT], F32, 4)
        xp = sbrot(es, nc, "Fx", [128, 256], F32, 4)
        for ti, (t0, nt) in enumerate(C.tiles_oh()):
            jq = t0 - C.OWN0
            is_halo = (ti == 0)
            row0 = t0 - C.HALF
            build_hT(K, R, K.d["x1"], jq, nt, ct, co["g_ffn"], hT)
            for half2 in range(2):
                firsta = True
                for jb in range(half2 * KH, (half2 + 1) * KH):
                    wg_ = wgp.get()
                    load_w(K, wg_, K.d["bf_w_fi"], jb, 0, DC, 256, rbuf=K.b["bf_w_fi"])
                    cvs = []
                    for half in range(2):
                        blk_i = jb + half * FB
                        ps = K.ps.get()
                        S.mms([(ps.h[:, 0:nt], wg_.h[:, k, half * 128:(half + 1) * 128], hT.h[:, k, 0:nt], k == 0, k == DC - 1) for k in range(DC)],
                              r=[wg_.b, hT.b], w=[ps.b])
                        ub = ubp.get()
                        S.copy("act", ub.h[:, 2:nt + 2], ps.h[:, 0:nt], w=[ub.b, ps.b])
                        S.copy("dve", ub.h[:, 0:2], halo.h[:, blk_i, :], r=[halo.b], wa=[ub.b])
                        if is_halo:
                            S.ts("dve", halo.h[:, blk_i, :], ub.h[:, nt:nt + 2], pv.h[:, 0:1], None, ALU.mult, r=[ub.b, pv.b], w=[halo.b])
                            continue
                        S.copy("dve", halo.h[:, blk_i, :], ub.h[:, nt:nt + 2], r=[ub.b], w=[halo.b])
                        cv = cvp.get()
                        cc = lambda nm: ct.h[:, co[nm] + blk_i:co[nm] + blk_i + 1]
                        S.act(cv.h[:, 0:nt], ub.h[:, 2:nt + 2], AF.Identity, r=[ub.b, ct.b], w=[cv.b], scale=cc("cw2"), bias=cc("cb"))
                        S.stt("dve", cv.h[:, 0:nt], ub.h[:, 1:nt + 1], cc("cw1"), cv.h[:, 0:nt], ALU.mult, ALU.add, r=[ub.b, ct.b], w=[cv.b])
                        S.stt("dve", cv.h[:, 0:nt], ub.h[:, 0:nt], cc("cw0"), cv.h[:, 0:nt], ALU.mult, ALU.add, r=[ub.b, ct.b], w=[cv.b])
                        cvs.append(cv)
                    if is_halo:
                        continue
                    sgt = cvp.get()
                    S.act(sgt.h[:, 0:nt], cvs[0].h[:, 0:nt], AF.Silu, r=[cvs[0].b], w=[sgt.b])
                    S.tt("dve", aT.h[:, jb - half2 * KH, 0:nt], sgt.h[:, 0:nt], cvs[1].h[:, 0:nt], ALU.mult, r=[sgt.b, cvs[1].b],
                         w=[aT.b] if firsta else [], wa=[] if firsta else [aT.b])
                    firsta = False
                if is_halo:
                    continue
                nsub = nt // 128
                for (g0, gw) in chunks(C.D, 256):
                    for si, (k0, kn_) in enumerate(SEG):
                        wf_ = wfp.get()
                        load_w(K, wf_, K.d["bf_w_fo"], g0 // 256, half2 * KH + k0, kn_, gw, rbuf=K.b["bf_w_fo"])
                        for s_ in range(nsub):
                            ps = acc[s_]
                            items = [(ps.h[:, 0:gw], aT.h[:, k0 + k, s_ * 128:(s_ + 1) * 128], wf_.h[:, k, 0:gw],
                                      si == 0 and k == 0, si == len(SEG) - 1 and k == kn_ - 1) for k in range(kn_)]
                            S.mms(items, r=[aT.b, wf_.b], w=[ps.b] if si == 0 else [], wa=[] if si == 0 else [ps.b])
                    for s_ in range(nsub):
                        ps = acc[s_]
                        xt = xp.get()
                        if half2 == 0:
                            S.dma("sp", xt.h[:, 0:gw], K.d["x1"][jq + s_ * 128:jq + (s_ + 1) * 128, g0:g0 + gw], r=[K.b["x1"]], w=[xt.b])
                        else:
                            S.dma("sp", xt.h[:, 0:gw], K.d["x2"][row0 + s_ * 128:row0 + (s_ + 1) * 128, g0:g0 + gw], r=[K.b["x2"]], w=[xt.b])
                        S.tt("dve", xt.h[:, 0:gw], ps.h[:, 0:gw], xt.h[:, 0:gw], ALU.add, w=[xt.b, ps.b])
                        S.dma("sp", K.d["x2"][row0 + s_ * 128:row0 + (s_ + 1) * 128, g0:g0 + gw], xt.h[:, 0:gw], r=[xt.b], wa=[K.b["x2"]])
        K.rem.append(nc.sbuf_bytes_remaining)
        S.emit()


def phase_G(K):
    S, C, nc = K.S, K.C, K.nc
    DC = C.D // 128
    PC = C.PLE // 128
    with contextlib.ExitStack() as es:
        K.ps = psrot(es, nc)
        R = norm_res(K, es, "G")
        ct, co = load_cols(K, es, "Gcols", ["g_ple"])
        hT = sbt(es, nc, "GhT", [128, DC, C.NT], BF16)
        pT = sbt(es, nc, "GpT", [128, PC, C.NT], BF16)
        pp = sbrot(es, nc, "Gp", [128, C.PLE], F32, 2)
        wgp = sbrot(es, nc, "Gwg", [128, DC, 512], BF16, 2)
        wpp = sbrot(es, nc, "Gwp", [128, PC, 512], BF16, 2)
        sgp = sbrot(es, nc, "Gsg", [128, 512], F32, 3)
        xp = sbrot(es, nc, "Gx", [128, 512], F32, 3)
        for (t0, nt) in C.tiles_oh()[1:]:
            row0 = t0 - C.HALF
            build_hT(K, R, K.d["x2"], row0, nt, ct, co["g_ple"], hT)
            firstp = True
            for s_ in range(nt // 128):
                pt = pp.get()
                S.dma("sp", pt.h[:, :], K.d["pc"][row0 + s_ * 128:row0 + (s_ + 1) * 128, :], w=[pt.b])
                ps = K.ps.get()
                for c in range(PC):
                    S.transpose(ps.h[:, c * 128:(c + 1) * 128], pt.h[:, c * 128:(c + 1) * 128], R.ident.h[:, :], r=[pt.b, R.ident.b],
                                w=[ps.b] if c == 0 else [], wa=[] if c == 0 else [ps.b])
                S.copy("act", pT.h[:, :, s_ * 128:(s_ + 1) * 128], ps.h[:, 0:PC * 128].rearrange("p (c t) -> p c t", t=128),
                       w=[ps.b] + ([pT.b] if firstp else []), wa=[] if firstp else [pT.b])
                firstp = False
            for (g0, gw) in chunks(C.D, 512):
                wg_ = wgp.get()
                load_w(K, wg_, K.d["bf_w_pg"], g0 // 512, 0, DC, gw, rbuf=K.b["bf_w_pg"])
                wp_ = wpp.get()
                load_w(K, wp_, K.d["w_pp"], g0 // 512, 0, PC, gw)
                for s_ in range(nt // 128):
                    psG = K.ps.get()
                    S.mms([(psG.h[:, 0:gw], hT.h[:, k, s_ * 128:(s_ + 1) * 128], wg_.h[:, k, 0:gw], k == 0, k == DC - 1) for k in range(DC)],
                          r=[hT.b, wg_.b], w=[psG.b])
                    psP = K.ps.get()
                    S.mms([(psP.h[:, 0:gw], pT.h[:, k, s_ * 128:(s_ + 1) * 128], wp_.h[:, k, 0:gw], k == 0, k == PC - 1) for k in range(PC)],
                          r=[pT.b, wp_.b], w=[psP.b])
                    sg = sgp.get()
                    S.act(sg.h[:, 0:gw], psG.h[:, 0:gw], AF.Sigmoid, w=[sg.b, psG.b])
                    S.tt("dve", sg.h[:, 0:gw], psP.h[:, 0:gw], sg.h[:, 0:gw], ALU.mult, w=[sg.b, psP.b])
                    xt = xp.get()
                    S.dma("sp", xt.h[:, 0:gw], K.d["x2"][row0 + s_ * 128:row0 + (s_ + 1) * 128, g0:g0 + gw], r=[K.b["x2"]], w=[xt.b])
                    S.tt("pool", xt.h[:, 0:gw], xt.h[:, 0:gw], sg.h[:, 0:gw], ALU.add, r=[sg.b], w=[xt.b])
                    S.dma("sp", K.d["out"][row0 + s_ * 128:row0 + (s_ + 1) * 128, g0:g0 + gw], xt.h[:, 0:gw], r=[xt.b], wa=[K.b["out"]])
        K.rem.append(nc.sbuf_bytes_remaining)
        S.emit()

def build_program(C, upto="A", debug=()):
    nc = bass.Bass("TRN2", target_bir_lowering=False)
    K = Ctx()
    K.nc, K.C = nc, C
    K.d, K.b = {}, {}
    K.rem = []

    def din(name, shape):
        K.d[name] = nc.dram_tensor(name, list(shape), F32, kind="ExternalInput").ap()

    def dscr(name, shape, dt=F32, out=False):
        kind = "ExternalOutput" if out else "Internal"
        K.d[name] = nc.dram_tensor(name, list(shape), dt, kind=kind).ap()
        K.b[name] = Buf(name)

    din("xc", [C.CTX, C.D])
    din("pc", [C.HALF, C.PLE])
    din("cols", [128, C.NCOLS])
    din("consts", [128, C.NCONST])
    din("lamb", [128, 256])
    din("w_in", [(C.IC + 511) // 512, 128, C.D // 128, 512])
    din("w2", [C.DL, C.RW])
    din("a2", [C.AL, C.RW])
    din("g2", [C.GL, C.RW])
    din("w_a", [C.D // 256, 128, C.RW // 128, 256])
    din("w_b", [C.D // 256, 128, C.DW // 128, 256])
    din("w_o", [C.D // 512, 128, C.D // 128, 512])
    din("w_fi", [C.DFF // 128, 128, C.D // 128, 256])
    din("w_fo", [C.D // 256, 128, C.DFF // 128, 256])
    din("w_pg", [C.D // 512, 128, C.D // 128, 512])
    din("w_pp", [C.D // 512, 128, C.PLE // 128, 512])
    dscr("out", [C.HALF, C.D], out=True)
    dscr("zR", [C.RC, C.CTX], out=("zT" in debug))
    dscr("zQ", [C.DW, C.NOH], out=("zT" in debug))
    dscr("zKV", [2 * C.DW, C.CTX], out=("zT" in debug))
    dscr("zG", [2 * C.D, C.NOH], out=("zT" in debug))
    for nm_ in ("w_a", "w_b", "w_o", "w_fi", "w_fo", "w_pg"):
        dscr("bf_" + nm_, list(K.d[nm_].shape), BF16)
    dscr("gT", [C.RW, C.CTX], out=("gT" in debug))
    dscr("bonT", [C.RW, C.CTX], out=("gT" in debug))
    dscr("oaT", [C.RW, C.NOH], BF16, out=("oaT" in debug))
    dscr("obT", [C.DW, C.NOH], BF16, out=("obT" in debug))
    dscr("x1", [C.NOH, C.D], out=("x1" in debug))
    dscr("x2", [C.HALF, C.D], out=("x2" in debug))
    with contextlib.ExitStack() as es0:
        K.S = Sched(nc, es0)
        phase_A(K)
        if upto == "A":
            return nc, K
        K.pc = precast_list(K)
        phase_B(K)
        if upto == "B":
            return nc, K
        phase_C(K)
        if upto == "C":
            return nc, K
        phase_DE(K)
        if upto == "DE":
            return nc, K
        phase_F(K)
        if upto == "F":
            return nc, K
        phase_G(K)
    return nc, K


def colpack(v):
    v = np.asarray(v, np.float32).reshape(-1)
    n = (v.size + 127) // 128
    p = np.zeros(n * 128, np.float32)
    p[:v.size] = v
    return np.ascontiguousarray(p.reshape(n, 128).T)


def tile_w(w, W):
    w = np.asarray(w, np.float32)
    Kd, N = w.shape
    NG = (N + W - 1) // W
    if NG * W != N:
        w = np.concatenate([w, np.zeros((Kd, NG * W - N), np.float32)], axis=1)
    return np.ascontiguousarray(w.reshape(Kd // 128, 128, NG, W).transpose(2, 1, 0, 3))


def host_inputs(C, inp):
    g = lambda k: np.asarray(inp[k], np.float32)[0]
    mu = g("rwkv_mu")
    cw = g("ffn_conv_w")
    parts = {
        "g_mix": g("norm_mix_g"), "g_ffn": g("norm_ffn_g"), "g_ple": g("norm_ple_g"),
        "mu_r": mu[C.o_r:C.o_r + C.RW], "mu_k": mu[C.o_k:C.o_k + C.RW], "mu_v": mu[C.o_v:C.o_v + C.RW],
        "mu_w": mu[C.o_wl:C.o_wl + C.DL], "mu_a": mu[C.o_al:C.o_al + C.AL], "mu_g": mu[C.o_gl:C.o_gl + C.GL],
        "w0": g("rwkv_w0"), "a0": g("rwkv_a0"), "k_k": g("rwkv_k_k"), "k_a": g("rwkv_k_a"),
        "r_k": g("rwkv_r_k").reshape(-1), "ln_w": g("rwkv_ln_w"), "ln_b": g("rwkv_ln_b"),
        "q_g": np.tile(g("q_norm_g"), 2), "k_g": np.tile(g("k_norm_g"), 2), "subln": g("subln_g"),
        "cw0": cw[0], "cw1": cw[1], "cw2": cw[2], "cb": g("ffn_conv_b"),
    }
    cols = np.zeros((128, C.NCOLS), np.float32)
    for name, (off, n) in C.colmap.items():
        cp = colpack(parts[name])
        assert cp.shape[1] == n, (name, cp.shape, n)
        cols[:, off:off + n] = cp
    consts = np.zeros((128, C.NCONST), np.float32)
    consts[:, 0:128] = np.eye(128, dtype=np.float32)
    consts[0:64, 128:192] = 1.0
    consts[64:128, 192:256] = 1.0
    consts[:, 256:384] = 1.0
    s = np.arange(64)[:, None]
    t = np.arange(64)[None, :]
    consts[0:64, 384:448] = (s < t)
    consts[0:64, 448:512] = (s <= t)
    consts[0:64, 512:576] = (s > t)
    sm = np.ones(512, np.float32)
    sm[0::64] = 0.0
    consts[:, 576:1088] = sm[None, :]
    lamb = np.concatenate([g("lam_q1"), g("lam_k1"), g("lam_q2"), g("lam_k2")])[None, :].repeat(128, 0)
    x = np.asarray(inp["x"], np.float32)
    p = np.asarray(inp["p"], np.float32)[0]
    shared = {
        "cols": cols, "lamb": np.ascontiguousarray(lamb),
        "w_in": tile_w(g("w_in"), 512), "w2": g("rwkv_w2"), "a2": g("rwkv_a2"), "g2": g("rwkv_g2"),
        "w_a": tile_w(g("w_branch_a"), 256), "w_b": tile_w(g("w_branch_b"), 256), "w_o": tile_w(g("w_out"), 512),
        "w_fi": np.ascontiguousarray(np.concatenate([tile_w(g("w_ffn_in")[:, :C.DFF], 128), tile_w(g("w_ffn_in")[:, C.DFF:], 128)], axis=3)),
        "w_fo": tile_w(g("w_ffn_out"), 256), "w_pg": tile_w(g("w_ple_gate"), 512), "w_pp": tile_w(g("w_ple_proj"), 512),
    }
    maps = []
    for core in range(2 * C.B):
        b, hf = core // 2, core % 2
        m = dict(shared)
        if hf == 1:
            m["xc"] = np.ascontiguousarray(x[b])
        else:
            xc = np.zeros((C.CTX, C.D), np.float32)
            xc[C.HALF:] = x[b, 0:C.HALF]
            m["xc"] = xc
        m["pc"] = np.ascontiguousarray(p[b, hf * C.HALF:(hf + 1) * C.HALF])
        cc = consts.copy()
        cc[:, 1088:1216] = float(hf)
        m["consts"] = cc
        maps.append(m)
    return maps


_PROG = {}


def kernel(**inputs):
    C = Cfg()
    if "full" not in _PROG:
        _PROG["full"] = build_program(C, upto="ALL")
    nc, K = _PROG["full"]
    maps = host_inputs(C, inputs)
    res = run_bass_kernel_spmd(nc, maps, core_ids=list(range(2 * C.B)))
    out = np.zeros((C.B, C.SEQ, C.D), np.float32)
    for core in range(2 * C.B):
        b, hf = core // 2, core % 2
        out[b, hf * C.HALF:(hf + 1) * C.HALF] = res.results[core]["out"]
    return out
```

```python
import contextlib
import math
import numpy as np
import concourse.bass as bass
import concourse.mybir as mybir
from concourse.bass_utils import run_bass_kernel_spmd

F32 = mybir.dt.float32
BF16 = mybir.dt.bfloat16
AF = mybir.ActivationFunctionType
ALU = mybir.AluOpType


class Buf:
    __slots__ = ("name", "w", "r", "g")

    def __init__(self, name):
        self.name = name
        self.w = {}
        self.r = {}
        self.g = None


class Tl:
    def __init__(self, h, name):
        self.h = h
        self.b = Buf(name)


class Rot:
    def __init__(self, tiles):
        self.t = tiles
        self.i = 0

    def get(self):
        t = self.t[self.i % len(self.t)]
        self.i += 1
        return t


class Sched:
    ENGS = ("pe", "act", "dve", "pool", "sp")
    import os
    NDS = int(os.environ.get('NDS', 6))

    def __init__(self, nc, es):
        self.nc = nc
        self.streams = {e: [] for e in self.ENGS}
        self.cnt = {e: 0 for e in self.ENGS}
        self.sems = {}
        for e in self.ENGS:
            self.sems[e] = es.enter_context(nc.semaphore("s_" + e))
        self.dq = {}
        self.dlast = {}
        for q in ("sp", "act", "pool"):
            self.dq[q] = 0
            for i in range(self.NDS):
                self.sems[("d", q, i)] = es.enter_context(nc.semaphore("d_%s_%d" % (q, i)))
        self.known = {}
        self.final = []
        self.rr = 0
        self.log = []

    def _wait(self, e, key, val):
        if key == "pe" and e == "pe":
            return
        if self.known.get((e, key), 0) >= val:
            return
        self.known[(e, key)] = val
        self.log.append((e, "wait", key, val))
        sem = self.sems[key]
        self.streams[e].append(lambda eng, sem=sem, val=val: eng.wait_ge(sem, val))

    def _deps(self, e, r, w, wa):
        for b in r:
            for k, v in b.w.items():
                self._wait(e, k, v)
        for b in w:
            for k, v in b.w.items():
                self._wait(e, k, v)
            for k, v in b.r.items():
                self._wait(e, k, v)
        for b in wa:
            if b.g is not None:
                self._wait(e, b.g[0], b.g[1])
            for k, v in b.r.items():
                self._wait(e, k, v)

    def _mark(self, tok, r, w, wa):
        k, v = tok
        for b in r:
            if b.r.get(k, 0) < v:
                b.r[k] = v
        for b in w:
            b.w = {k: v}
            b.r = {}
            b.g = tok
        for b in wa:
            if b.w.get(k, 0) < v:
                b.w[k] = v

    def op(self, e, fn, r=(), w=(), wa=(), inc=True):
        self._deps(e, r, w, wa)
        sem = self.sems[e]
        if inc:
            self.cnt[e] += 1
            tok = (e, self.cnt[e])
            self.streams[e].append(lambda eng, fn=fn, sem=sem: fn(eng).then_inc(sem, 1))
        else:
            tok = (e, self.cnt[e] + 1)
            self.streams[e].append(lambda eng, fn=fn: fn(eng))
        self._mark(tok, r, w, wa)
        self.log.append((e, "op", tok, inc))
        return tok

    def dma(self, q, out, in_, r=(), w=(), wa=(), final=False):
        j = self.dq[q]
        self.dq[q] += 1
        slot = j % self.NDS
        key = ("d", q, slot)
        val = 16 * (j // self.NDS + 1)
        if j >= self.NDS:
            self._wait(q, key, val - 16)
        self._deps(q, r, w, wa)
        sem = self.sems[key]
        self.streams[q].append(
            lambda eng, out=out, in_=in_, sem=sem: eng.dma_start(out=out, in_=in_).then_inc(sem, 16))
        tok = (key, val)
        self.log.append((q, "dma", tok))
        self.dlast[key] = val
        self._mark(tok, r, w, wa)
        if final:
            self.final.append(tok)
        return tok

    def act(self, out, in_, func, r=(), w=(), wa=(), **kw):
        return self.op("act", lambda e: e.activation(out=out, in_=in_, func=func, **kw), r, w, wa)

    def ts(self, eng, out, in0, s1, s2, op0, op1=None, r=(), w=(), wa=()):
        if op1 is None:
            return self.op(eng, lambda e: e.tensor_scalar(out=out, in0=in0, scalar1=s1, scalar2=None, op0=op0), r, w, wa)
        return self.op(eng, lambda e: e.tensor_scalar(out=out, in0=in0, scalar1=s1, scalar2=s2, op0=op0, op1=op1), r, w, wa)

    def tt(self, eng, out, in0, in1, op, r=(), w=(), wa=()):
        return self.op(eng, lambda e: e.tensor_tensor(out=out, in0=in0, in1=in1, op=op), r, w, wa)

    def stt(self, eng, out, in0, scalar, in1, op0, op1, r=(), w=(), wa=()):
        return self.op(eng, lambda e: e.scalar_tensor_tensor(out=out, in0=in0, scalar=scalar, in1=in1, op0=op0, op1=op1), r, w, wa)

    def copy(self, eng, out, in_, r=(), w=(), wa=()):
        if eng == "act":
            return self.act(out, in_, AF.Identity, r, w, wa)
        return self.op(eng, lambda e: e.tensor_copy(out=out, in_=in_), r, w, wa)

    def memset(self, eng, ap, val, w=(), wa=()):
        return self.op(eng, lambda e: e.memset(ap, val), (), w, wa)

    def recip(self, out, in_, r=(), w=(), wa=()):
        return self.op("dve", lambda e: e.reciprocal(out=out, in_=in_), r, w, wa)

    def scan(self, out, d0, d1, init, op0, op1, r=(), w=(), wa=()):
        return self.op("dve", lambda e: e.tensor_tensor_scan(out=out, data0=d0, data1=d1, initial=init, op0=op0, op1=op1), r, w, wa)

    def transpose(self, out, in_, ident, r=(), w=(), wa=()):
        return self.op("pe", lambda e: e.transpose(out, in_, ident), r, w, wa)

    def mms(self, items, r=(), w=(), wa=()):
        self._deps("pe", r, w, wa)
        n = len(items)
        sem = self.sems["pe"]
        self.cnt["pe"] += 1
        tok = ("pe", self.cnt["pe"])
        for i, (o, l, rh, st, sp) in enumerate(items):
            if i == n - 1:
                self.streams["pe"].append(
                    lambda eng, o=o, l=l, rh=rh, st=st, sp=sp, sem=sem: eng.matmul(o, l, rh, start=st, stop=sp).then_inc(sem, 1))
            else:
                self.streams["pe"].append(
                    lambda eng, o=o, l=l, rh=rh, st=st, sp=sp: eng.matmul(o, l, rh, start=st, stop=sp))
        self._mark(tok, r, w, wa)
        return tok

    def ev(self):
        self.rr += 1
        return "act" if self.rr % 2 else "dve"

    def emit(self, last=False):
        nc = self.nc
        for e in self.ENGS:
            for f in self.ENGS:
                if f != e and self.cnt[f] > 0:
                    self._wait(e, f, self.cnt[f])
            for key, val in self.dlast.items():
                self._wait(e, key, val)
        self.final = []
        streams = self.streams
        self.streams = {e: [] for e in self.ENGS}
        with nc.Block() as block:
            @block.tensor
            def _(eng):
                for f in streams["pe"]:
                    f(eng)

            @block.scalar
            def _(eng):
                for f in streams["act"]:
                    f(eng)

            @block.vector
            def _(eng):
                for f in streams["dve"]:
                    f(eng)

            @block.gpsimd
            def _(eng):
                for f in streams["pool"]:
                    f(eng)

            @block.sync
            def _(eng):
                for f in streams["sp"]:
                    f(eng)


class Cfg:
    def __init__(self, D=4096, SEQ=4096, B=4, PLE=256):
        self.D, self.SEQ, self.B, self.PLE = D, SEQ, B, PLE
        self.EPS = 1e-6
        self.RW = D // 2
        self.NH = self.RW // 64
        self.NHC = self.RW // 128
        self.DL = max(32, int(round(1.8 * self.RW ** 0.5 / 32)) * 32)
        self.AL = max(32, int(round(2.5 * self.RW ** 0.5 / 32)) * 32)
        self.GL = max(32, int(round(0.6 * self.RW ** 0.8 / 32)) * 32)
        self.GN_EPS = 64e-5
        self.RC = 3 * self.RW + self.DL + self.AL + self.GL
        self.DW = D // 2
        self.NDH = self.DW // 128
        self.DCOL = 3 * self.DW
        self.IC = self.RC + self.DCOL + 2 * D
        self.DFF = int(round(8 * D / 3 / 256)) * 256
        self.HALF = SEQ // 2
        self.CTX = SEQ
        self.HALO = 128
        self.OWN0 = self.HALF - self.HALO
        self.NT = min(512, self.HALF)
        self.NOH = self.HALF + self.HALO
        self.o_r, self.o_k, self.o_v = 0, self.RW, 2 * self.RW
        self.o_wl = 3 * self.RW
        self.o_al = self.o_wl + self.DL
        self.o_gl = self.o_al + self.AL
        self.o_q = self.RC
        self.o_dk = self.RC + self.DW
        self.o_dv = self.RC + 2 * self.DW
        self.o_ga = self.RC + self.DCOL
        self.o_gb = self.o_ga + D
        self.lambda_init = 0.8 - 0.6 * math.exp(-0.3 * 0)
        self.colmap = {}
        off = 0
        dc = D // 128
        hc = self.RW // 128
        fc = 2 * self.DFF // 128
        for name, n in [("g_mix", dc), ("g_ffn", dc), ("g_ple", dc),
                        ("mu_r", hc), ("mu_k", hc), ("mu_v", hc), ("mu_w", 1), ("mu_a", 1), ("mu_g", (self.GL + 127) // 128),
                        ("w0", hc), ("a0", hc), ("k_k", hc), ("k_a", hc), ("r_k", hc), ("ln_w", hc), ("ln_b", hc),
                        ("q_g", 1), ("k_g", 1), ("subln", 1),
                        ("cw0", fc), ("cw1", fc), ("cw2", fc), ("cb", fc)]:
            self.colmap[name] = (off, n)
            off += n
        self.NCOLS = off
        self.cm = {"ident": (0, 128), "blk": (128, 128), "ones": (256, 128), "mUs": (384, 64), "mUi": (448, 64),
                   "mLs": (512, 64), "scan": (576, 512), "pvalid": (1088, 128)}
        self.NCONST = 1216

    def tiles_oh(self):
        t = [(self.OWN0, self.HALO)]
        for i in range(self.HALF // self.NT):
            t.append((self.HALF + i * self.NT, self.NT))
        return t


class Ctx:
    pass


def sbt(es, nc, name, shape, dt):
    return Tl(es.enter_context(nc.sbuf_tensor(name, list(shape), dt)), name)


def sbrot(es, nc, name, shape, dt, n):
    return Rot([sbt(es, nc, "%s%d" % (name, i), shape, dt) for i in range(n)])


_PSN = [0]


def psrot(es, nc, n=8):
    _PSN[0] += 1
    return Rot([Tl(es.enter_context(nc.psum_tensor("ps%d_%d" % (_PSN[0], i), [128, 512], F32)), "ps%d" % i) for i in range(n)])


def chunks(total, size):
    return [(i, min(size, total - i)) for i in range(0, total, size)]


def load_cols(K, es, name, names):
    S, C, nc = K.S, K.C, K.nc
    lo = min(C.colmap[x][0] for x in names)
    hi = max(C.colmap[x][0] + C.colmap[x][1] for x in names)
    t = sbt(es, nc, name, [128, hi - lo], F32)
    if hi - lo == 1:
        with nc.allow_non_contiguous_dma(reason="single column"):
            pass
    S.dma("sp", t.h[:, :], K.d["cols"][:, lo:hi], w=[t.b])
    return t, {x: C.colmap[x][0] - lo for x in names}


def load_const(K, es, name, key, rows=128, dt=F32):
    S, C, nc = K.S, K.C, K.nc
    o, n = C.cm[key]
    t = sbt(es, nc, name, [rows, n], dt)
    q = "sp" if dt == F32 else "pool"
    S.dma(q, t.h[:, :], K.d["consts"][0:rows, o:o + n], w=[t.b])
    return t


def build_hT(K, R, src, t0, nt, gt, goff, hT):
    S, C = K.S, K.C
    DC = C.D // 128
    import os
    for s in range(min(nt // 128, int(os.environ.get("KSTOP", "99")))):
        xs = R.xs.get()
        S.dma("sp", xs.h[:, :], src[t0 + s * 128: t0 + (s + 1) * 128, :], w=[xs.b])
        st = R.st.get()
        S.memset("pool", st.h[:, 0:1], 0.0, w=[st.b])
        S.act(R.junk.h[:, :], xs.h[:, :], AF.Square, r=[xs.b], w=[R.junk.b, st.b], accum_out=st.h[:, 0:1])
        S.ts("dve", st.h[:, 1:2], st.h[:, 0:1], 1.0 / C.D, C.EPS, ALU.mult, ALU.add, r=[st.b], w=[st.b])
        S.act(st.h[:, 2:3], st.h[:, 1:2], AF.Ln, r=[st.b], w=[st.b])
        S.act(st.h[:, 3:4], st.h[:, 2:3], AF.Exp, r=[st.b], w=[st.b], scale=-0.5)
        S.ts("dve", xs.h[:, :], xs.h[:, :], st.h[:, 3:4], None, ALU.mult, r=[xs.b, st.b], w=[xs.b])
        import os
        if os.environ.get("SKIPT"):
            continue
        for c0 in range(0, DC, 4):
            ps = K.ps.get()
            n = min(4, DC - c0)
            for j in range(n):
                S.transpose(ps.h[:, j * 128:(j + 1) * 128], xs.h[:, (c0 + j) * 128:(c0 + j + 1) * 128], R.ident.h[:, :],
                            r=[xs.b, R.ident.b], w=[ps.b] if j == 0 else [], wa=[] if j == 0 else [ps.b])
            eng = S.ev()
            for j in range(n):
                c = c0 + j
                o = hT.h[:, c, s * 128:(s + 1) * 128]
                i = ps.h[:, j * 128:(j + 1) * 128]
                g = gt.h[:, goff + c:goff + c + 1]
                if eng == "act":
                    S.act(o, i, AF.Identity, r=[gt.b], w=[ps.b], wa=[hT.b], scale=g)
                else:
                    S.ts("dve", o, i, g, None, ALU.mult, r=[gt.b], w=[ps.b], wa=[hT.b])


def norm_res(K, es, pfx, nxs=3, junk=None):
    R = Ctx()
    nc, C = K.nc, K.C
    R.xs = sbrot(es, nc, pfx + "xs", [128, C.D], F32, nxs)
    R.junk = junk if junk is not None else sbt(es, nc, pfx + "junk", [128, C.D], BF16)
    R.st = sbrot(es, nc, pfx + "st", [128, 4], F32, 4)
    R.ident = load_const(K, es, pfx + "ident", "ident")
    return R


def precast_list(K):
    out = []
    for nm in ("w_a", "w_b", "w_o", "w_fi", "w_fo", "w_pg"):
        src, dst = K.d[nm], K.d["bf_" + nm]
        for g in range(src.shape[0]):
            out.append((dst[g], src[g], K.b["bf_" + nm]))
    return out


def precast_step(K, n=1):
    for _ in range(n):
        if K.pc:
            dst, src, buf = K.pc.pop(0)
            K.S.dma("pool", dst, src, wa=[buf])


def load_w(K, wt, w4, g, c0, cn, width, kstep=8, rbuf=None):
    if rbuf is not None:
        kstep = 32
    S = K.S
    v = w4[g]
    first = True
    for k0 in range(0, cn, kstep):
        kn = min(kstep, cn - k0)
        S.dma("pool", wt.h[:, k0:k0 + kn, 0:width], v[:, c0 + k0:c0 + k0 + kn, 0:width], r=[rbuf] if rbuf is not None else [],
              w=[wt.b] if first else [], wa=[] if first else [wt.b])
        first = False


def z_store(K, col, w, zt, t0, nt):
    S, C = K.S, K.C
    regions = [("zR", 0, C.RC, 0), ("zQ", C.o_q, C.o_dk, C.OWN0), ("zKV", C.o_dk, C.o_ga, 0), ("zG", C.o_ga, C.IC, C.OWN0)]
    for (nm, c0, c1, tk0) in regions:
        a, b = max(col, c0), min(col + w, c1)
        if a >= b:
            continue
        ta = max(t0, tk0)
        if ta >= t0 + nt:
            continue
        S.dma("sp", K.d[nm][a - c0:b - c0, ta - tk0:t0 + nt - tk0], zt.h[a - col:b - col, ta - t0:nt], r=[zt.b], wa=[K.b[nm]])


def phase_A(K):
    S, C, nc = K.S, K.C, K.nc
    DC = C.D // 128
    with contextlib.ExitStack() as es:
        K.ps = psrot(es, nc)
        R = norm_res(K, es, "A")
        gt, gm = load_cols(K, es, "Ag", ["g_mix"])
        hTs = [sbt(es, nc, "AhT%d" % i, [128, DC, C.NT], BF16) for i in range(2)]
        wpool = sbrot(es, nc, "Aw", [128, DC, 512], BF16, 2)
        zpool = sbrot(es, nc, "Az", [128, C.NT], F32, 3)
        ntile = C.CTX // C.NT
        first_full = C.OWN0 // C.NT
        NGA = (C.IC + 511) // 512
        build_hT(K, R, K.d["xc"], 0, C.NT, gt, gm["g_mix"], hTs[0])
        seen = {}
        for tt in range(ntile):
            t0 = tt * C.NT
            hT = hTs[tt % 2]
            if tt >= first_full:
                groups = list(range(NGA))
            else:
                groups = [g for g in range(NGA) if (g * 512 < C.RC) or (g * 512 + 512 > C.o_dk and g * 512 < C.o_ga)]
            for gi_, g in enumerate(groups):
                if gi_ == len(groups) // 2 and tt + 1 < ntile:
                    build_hT(K, R, K.d["xc"], t0 + C.NT, C.NT, gt, gm["g_mix"], hTs[(tt + 1) % 2])
                col0 = g * 512
                gw = min(512, C.IC - col0)
                wt = wpool.get()
                if g not in seen:
                    seen[g] = Buf("bfwin%d" % g)
                    load_w(K, wt, K.d["w_in"], g, 0, DC, 512)
                    if tt + 1 < ntile:
                        S.dma("sp", K.d["bf_w_in"][g], wt.h[:, :, :], r=[wt.b], w=[seen[g]])
                else:
                    load_w(K, wt, K.d["bf_w_in"], g, 0, DC, 512, rbuf=seen[g])
                for (j0, wj) in chunks(gw, 128):
                    ps = K.ps.get()
                    items = [(ps.h[0:wj, 0:C.NT], wt.h[:, k, j0:j0 + wj], hT.h[:, k, 0:C.NT], k == 0, k == DC - 1)
                             for k in range(DC)]
                    S.mms(items, r=[wt.b, hT.b], w=[ps.b])
                    zt = zpool.get()
                    S.copy(S.ev(), zt.h[0:wj, :], ps.h[0:wj, 0:C.NT], w=[zt.b, ps.b])
                    z_store(K, col0 + j0, wj, zt, t0, C.NT)
        K.rem.append(nc.sbuf_bytes_remaining)
        S.emit()


def phase_B(K):
    S, C, nc = K.S, K.C, K.nc
    NHC = C.NHC
    TB = min(256, C.HALF)
    NCH = TB // 64
    HG = min(4, NHC)
    NG = NHC // HG
    GW = HG * 128
    GLC = (C.GL + 127) // 128
    zT = K.d["zR"]
    with contextlib.ExitStack() as es:
        K.ps = psrot(es, nc)
        ident = load_const(K, es, "Bident", "ident")
        blk = load_const(K, es, "Bblk", "blk")
        mUs = load_const(K, es, "BmUs", "mUs", rows=64)
        mUi = load_const(K, es, "BmUi", "mUi", rows=64)
        mLs = load_const(K, es, "BmLs", "mLs", rows=64)
        scanm = load_const(K, es, "Bscan", "scan")
        identb = load_const(K, es, "Bidentb", "ident", dt=BF16)
        names = ["mu_r", "mu_k", "mu_v", "mu_w", "mu_a", "mu_g", "w0", "a0", "k_k", "k_a", "r_k", "ln_w", "ln_b"]
        ct, co = load_cols(K, es, "Bcols", names)
        nmu = 3 * NHC + 2 + GLC
        omu = sbt(es, nc, "Bomu", [128, nmu], F32)
        S.ts("dve", omu.h[:, :], ct.h[:, 0:nmu], -1.0, 1.0, ALU.mult, ALU.add, r=[ct.b], w=[omu.b])
        nw0 = sbt(es, nc, "Bnw0", [128, NHC], F32)
        S.ts("dve", nw0.h[:, :], ct.h[:, co["w0"]:co["w0"] + NHC], -1.0, None, ALU.mult, r=[ct.b], w=[nw0.b])
        na0 = sbt(es, nc, "Bna0", [128, NHC], F32)
        S.ts("dve", na0.h[:, :], ct.h[:, co["a0"]:co["a0"] + NHC], -1.0, None, ALU.mult, r=[ct.b], w=[na0.b])
        omka = sbt(es, nc, "Bomka", [128, NHC], F32)
        S.ts("dve", omka.h[:, :], ct.h[:, co["k_a"]:co["k_a"] + NHC], -1.0, 1.0, ALU.mult, ALU.add, r=[ct.b], w=[omka.b])
        cm05 = sbt(es, nc, "Bcm05", [128, 1], F32)
        S.memset("pool", cm05.h[:, :], -0.5, w=[cm05.b])
        c1 = sbt(es, nc, "Bc1", [128, 1], F32)
        S.memset("pool", c1.h[:, :], 1.0, w=[c1.b])
        lw = sbrot(es, nc, "Blw", [128, 2 + GLC, 128], F32, 2)
        wl = sbt(es, nc, "Bwl", [128, TB + 1], F32)
        al = sbt(es, nc, "Bal", [128, TB + 1], F32)
        gl = sbt(es, nc, "Bgl", [128, GLC, TB + 1], F32)
        tw = sbt(es, nc, "Btw", [128, TB], F32)
        als = sbt(es, nc, "Bals", [128, TB], F32)
        sg = sbt(es, nc, "Bsg", [128, GLC, TB], F32)
        tp = sbrot(es, nc, "Btp", [128, TB + 1], F32, 28)
        Rt, KKt, Bt, Kt, Vt, Bct, Kct = [sbt(es, nc, "B" + n, [128, NHC, TB], BF16) for n in ("Rt", "KKt", "Bt", "Kt", "Vt", "Bct", "Kct")]
        gC = sbt(es, nc, "BgC", [128, NHC, NCH], F32)
        OT = sbt(es, nc, "BOT", [128, NHC, TB], F32)
        H = sbt(es, nc, "BH", [128, NHC, 128], F32)
        Hb = sbt(es, nc, "BHb", [128, NHC, 128], BF16)
        bdp = [[sbt(es, nc, "Bbd%d_%d" % (i, j), [128, NHC, 128], BF16) for j in range(3)] for i in range(1)]
        for i in range(1):
            for j in range(3):
                S.memset("pool", bdp[i][j].h[:, :, :], 0.0, w=[bdp[i][j].b])
        Hbufs = [Buf("H%d" % g) for g in range(NG)]
        S.memset("pool", H.h[:, :, :], 0.0, w=[H.b] + Hbufs)
        S.memset("pool", Hb.h[:, :, :], 0.0, w=[Hb.b], wa=Hbufs)
        SLN = ("A0", "A1", "B0", "B1", "X0", "X1", "nM3", "M2", "M4", "Vm", "Bcm", "Kcm")
        slots = [{n: sbt(es, nc, "Bs%d%s" % (g, n), [64, GW], BF16) for n in SLN} for g in range(NG)]
        fp = sbrot(es, nc, "Bfp", [64, GW], F32, 2)
        oabp = sbrot(es, nc, "Boab", [128, TB], BF16, 2)

        def bc64(t):
            return t.h[0:64, 0:64].unsqueeze(1).broadcast_to([64, 2 * HG, 64])

        def v3(ap):
            return ap.rearrange("p (h t) -> p h t", t=64)

        def load_shift(dst_ap_fn, rows, row0, t0, buf, q="sp"):
            if t0 == 0:
                S.memset("pool", dst_ap_fn(0, 1), 0.0, w=[buf])
                S.dma(q, dst_ap_fn(1, TB + 1), zT[row0:row0 + rows, 0:TB], r=[K.b["zR"]], wa=[buf])
            else:
                S.dma(q, dst_ap_fn(0, TB + 1), zT[row0:row0 + rows, t0 - 1:t0 + TB], r=[K.b["zR"]], w=[buf])

        def lerp(eng, out_ap, zt_prev, zt_cur, mu_ap, omu_ap, rbufs, wbuf):
            t = tp.get()
            n = out_ap.shape[0]
            if eng == "act":
                S.act(t.h[0:n, 0:TB], zt_prev, AF.Identity, r=rbufs, w=[t.b], scale=mu_ap)
            else:
                S.ts(eng, t.h[0:n, 0:TB], zt_prev, mu_ap, None, ALU.mult, r=rbufs, w=[t.b])
            S.stt("dve", out_ap, zt_cur, omu_ap, t.h[0:n, 0:TB], ALU.mult, ALU.add, r=rbufs + [t.b], w=[wbuf])

        ntile = C.CTX // TB
        for tt in range(ntile):
            t0 = tt * TB
            need3 = (t0 + TB > C.OWN0)
            lo = max(C.OWN0 - t0, 0)
            load_shift(lambda a, b: wl.h[0:C.DL, a:b], C.DL, C.o_wl, t0, wl.b)
            load_shift(lambda a, b: al.h[0:C.AL, a:b], C.AL, C.o_al, t0, al.b)
            for c in range(GLC):
                n = min(128, C.GL - c * 128)
                load_shift(lambda a, b, c=c, n=n: gl.h[0:n, c, a:b], n, C.o_gl + c * 128, t0, gl.b)
            cw, ca, cg = co["mu_w"], co["mu_a"], co["mu_g"]
            lerp("dve", tw.h[0:C.DL, :], wl.h[0:C.DL, 0:TB], wl.h[0:C.DL, 1:TB + 1], ct.h[0:C.DL, cw:cw + 1], omu.h[0:C.DL, cw:cw + 1], [wl.b, ct.b, omu.b], tw.b)
            S.act(tw.h[0:C.DL, :], tw.h[0:C.DL, :], AF.Tanh, w=[tw.b])
            lerp("dve", als.h[0:C.AL, :], al.h[0:C.AL, 0:TB], al.h[0:C.AL, 1:TB + 1], ct.h[0:C.AL, ca:ca + 1], omu.h[0:C.AL, ca:ca + 1], [al.b, ct.b, omu.b], als.b)
            for c in range(GLC):
                n = min(128, C.GL - c * 128)
                lerp("dve", sg.h[0:n, c, :], gl.h[0:n, c, 0:TB], gl.h[0:n, c, 1:TB + 1], ct.h[0:n, cg + c:cg + c + 1], omu.h[0:n, cg + c:cg + c + 1], [gl.b, ct.b, omu.b], sg.b)
                S.act(sg.h[0:n, c, :], sg.h[0:n, c, :], AF.Sigmoid, w=[sg.b])
            for hc in range(NHC):
                cs = slice(hc * 128, (hc + 1) * 128)
                col = lambda nm: ct.h[:, co[nm] + hc:co[nm] + hc + 1]
                zs = []
                for (o_, nm) in ((C.o_r, "mu_r"), (C.o_k, "mu_k"), (C.o_v, "mu_v")):
                    zt_ = tp.get()
                    load_shift(lambda a, b, zt_=zt_: zt_.h[:, a:b], 128, o_ + hc * 128, t0, zt_.b)
                    out = tp.get()
                    mo = co[nm] + hc
                    lerp("act", out.h[:, 0:TB], zt_.h[:, 0:TB], zt_.h[:, 1:TB + 1], ct.h[:, mo:mo + 1], omu.h[:, mo:mo + 1], [zt_.b, ct.b, omu.b], out.b)
                    zs.append(out)
                r_s, k_s, v_s = zs
                X = slice(0, TB)
                lwt = lw.get()
                S.dma("sp", lwt.h[0:C.DL, 0, :], K.d["w2"][:, cs], w=[lwt.b])
                S.dma("sp", lwt.h[0:C.AL, 1, :], K.d["a2"][:, cs], wa=[lwt.b])
                for c in range(GLC):
                    n = min(128, C.GL - c * 128)
                    S.dma("sp", lwt.h[0:n, 2 + c, :], K.d["g2"][c * 128:c * 128 + n, cs], wa=[lwt.b])
                ps = K.ps.get()
                S.mms([(ps.h[:, 0:TB], lwt.h[0:C.DL, 0, :], tw.h[0:C.DL, :], True, True)], r=[lwt.b, tw.b], w=[ps.b])
                e1 = tp.get()
                S.act(e1.h[:, X], ps.h[:, 0:TB], AF.Exp, r=[nw0.b], w=[e1.b, ps.b], scale=-1.0, bias=nw0.h[:, hc:hc + 1])
                S.act(e1.h[:, X], e1.h[:, X], AF.Ln, r=[c1.b], w=[e1.b], bias=c1.h[:, 0:1])
                elw = tp.get()
                S.act(elw.h[:, X], e1.h[:, X], AF.Exp, r=[e1.b, cm05.b], w=[elw.b], scale=-1.0, bias=cm05.h[:, 0:1])
                ps = K.ps.get()
                S.mms([(ps.h[:, 0:TB], lwt.h[0:C.AL, 1, :], als.h[0:C.AL, :], True, True)], r=[lwt.b, als.b], w=[ps.b])
                a_ = tp.get()
                S.act(a_.h[:, X], ps.h[:, 0:TB], AF.Exp, r=[na0.b], w=[a_.b, ps.b], scale=-1.0, bias=na0.h[:, hc:hc + 1])
                S.ts("dve", a_.h[:, X], a_.h[:, X], 1.0, None, ALU.add, w=[a_.b])
                S.recip(a_.h[:, X], a_.h[:, X], w=[a_.b])
                if need3:
                    ps = K.ps.get()
                    items = []
                    for c in range(GLC):
                        n = min(128, C.GL - c * 128)
                        items.append((ps.h[:, 0:TB], lwt.h[0:n, 2 + c, :], sg.h[0:n, c, :], c == 0, c == GLC - 1))
                    S.mms(items, r=[lwt.b, sg.b], w=[ps.b])
                    gt_ = tp.get()
                    S.copy("act", gt_.h[:, X], ps.h[:, 0:TB], w=[gt_.b, ps.b])
                    S.dma("sp", K.d["gT"][cs, t0:t0 + TB], gt_.h[:, X], r=[gt_.b], wa=[K.b["gT"]])
                kk = tp.get()
                S.act(kk.h[:, X], k_s.h[:, X], AF.Identity, r=[k_s.b, ct.b], w=[kk.b], scale=col("k_k"))
                kk2 = tp.get()
                S.act(kk2.h[:, X], kk.h[:, X], AF.Square, r=[kk.b], w=[kk2.b])
                ps = K.ps.get()
                S.mms([(ps.h[:, 0:TB], blk.h[:, :], kk2.h[:, X], True, True)], r=[blk.b, kk2.b], w=[ps.b])
                rn = tp.get()
                S.ts("dve", rn.h[:, X], ps.h[:, 0:TB], 1e-24, None, ALU.max, w=[rn.b, ps.b])
                S.act(rn.h[:, X], rn.h[:, X], AF.Ln, w=[rn.b])
                S.act(rn.h[:, X], rn.h[:, X], AF.Exp, w=[rn.b], scale=-0.5)
                kkn = tp.get()
                S.tt("dve", kkn.h[:, X], kk.h[:, X], rn.h[:, X], ALU.mult, r=[kk.b, rn.b], w=[kkn.b])
                t1 = tp.get()
                S.ts("dve", t1.h[:, X], a_.h[:, X], col("k_a"), omka.h[:, hc:hc + 1], ALU.mult, ALU.add, r=[a_.b, ct.b, omka.b], w=[t1.b])
                kmod = tp.get()
                S.tt("pool", kmod.h[:, X], k_s.h[:, X], t1.h[:, X], ALU.mult, r=[k_s.b, t1.b], w=[kmod.b])
                b_ = tp.get()
                S.tt("pool", b_.h[:, X], kkn.h[:, X], a_.h[:, X], ALU.mult, r=[kkn.b, a_.b], w=[b_.b])
                if need3:
                    rkr = tp.get()
                    S.stt("dve", rkr.h[:, X], r_s.h[:, X], col("r_k"), kmod.h[:, X], ALU.mult, ALU.mult, r=[r_s.b, ct.b, kmod.b], w=[rkr.b])
                    ps = K.ps.get()
                    S.mms([(ps.h[:, 0:TB], blk.h[:, :], rkr.h[:, X], True, True)], r=[blk.b, rkr.b], w=[ps.b])
                    bon = tp.get()
                    S.tt("dve", bon.h[:, X], ps.h[:, 0:TB], v_s.h[:, X], ALU.mult, r=[v_s.b], w=[bon.b, ps.b])
                    S.dma("sp", K.d["bonT"][cs, t0:t0 + TB], bon.h[:, X], r=[bon.b], wa=[K.b["bonT"]])
                cum = tp.get()
                S.scan(cum.h[:, X], scanm.h[:, 0:TB], elw.h[:, X], 0.0, ALU.mult, ALU.add, r=[scanm.b, elw.b], w=[cum.b])
                gi = tp.get()
                S.act(gi.h[:, X], cum.h[:, X], AF.Exp, r=[cum.b], w=[gi.b], scale=-1.0)
                ge = tp.get()
                S.act(ge.h[:, X], cum.h[:, X], AF.Exp, r=[cum.b], w=[ge.b])
                gx = tp.get()
                S.tt("pool", gx.h[:, X], cum.h[:, X], elw.h[:, X], ALU.subtract, r=[cum.b, elw.b], w=[gx.b])
                S.act(gx.h[:, X], gx.h[:, X], AF.Exp, w=[gx.b], scale=-1.0)
                S.tt("dve", Rt.h[:, hc, :], r_s.h[:, X], gi.h[:, X], ALU.mult, r=[r_s.b, gi.b], wa=[Rt.b])
                S.tt("pool", KKt.h[:, hc, :], kkn.h[:, X], gx.h[:, X], ALU.mult, r=[kkn.b, gx.b], wa=[KKt.b])
                tb_ = tp.get()
                S.tt("dve", tb_.h[:, X], b_.h[:, X], ge.h[:, X], ALU.mult, r=[b_.b, ge.b], w=[tb_.b])
                tk_ = tp.get()
                S.tt("pool", tk_.h[:, X], kmod.h[:, X], ge.h[:, X], ALU.mult, r=[kmod.b, ge.b], w=[tk_.b])
                S.copy("act", Bt.h[:, hc, :], tb_.h[:, X], r=[tb_.b], wa=[Bt.b])
                S.copy("act", Kt.h[:, hc, :], tk_.h[:, X], r=[tk_.b], wa=[Kt.b])
                S.copy("act", Vt.h[:, hc, :], v_s.h[:, X], r=[v_s.b], wa=[Vt.b])
                S.copy("dve", gC.h[:, hc, :], gi.h[:, X].rearrange("p (c t) -> p c t", t=64)[:, :, 63], r=[gi.b], wa=[gC.b])
                gcb = gC.h[:, hc, :].unsqueeze(2).broadcast_to([128, NCH, 64])
                S.stt("dve", Bct.h[:, hc, :].rearrange("p (c t) -> p c t", t=64), tb_.h[:, X].rearrange("p (c t) -> p c t", t=64), -1.0, gcb,
                      ALU.mult, ALU.mult, r=[tb_.b, gC.b], wa=[Bct.b])
                S.tt("pool", Kct.h[:, hc, :].rearrange("p (c t) -> p c t", t=64), tk_.h[:, X].rearrange("p (c t) -> p c t", t=64), gcb,
                     ALU.mult, r=[tk_.b, gC.b], wa=[Kct.b])
            import os
            for ci in range(NCH if not os.environ.get("BSKIP2") else 0):
                cc = slice(ci * 64, (ci + 1) * 64)
                hd = lambda g, hh: (g * HG + hh // 2, (hh % 2) * 64)
                NHG = 2 * HG
                st = [dict() for _ in range(NG)]
                bd = bdp[0]
                for j, src in enumerate((KKt, Rt, Bt)):
                    eng = ("pool", "act", "pool")[j]
                    S.copy(eng, bd[j].h[0:64, :, 0:64], src.h[0:64, :, cc], r=[src.b], w=[bd[j].b])
                    S.copy(eng, bd[j].h[64:128, :, 64:128], src.h[64:128, :, cc], r=[src.b], wa=[bd[j].b])
                KKbd, Rbd, Bbd = bd
                for g in range(NG):
                    d = st[g]
                    def prod(lT, rbd):
                        ps = K.ps.get()
                        items = []
                        for hl in range(HG):
                            hc = g * HG + hl
                            items.append((ps.h[0:64, hl * 128:(hl + 1) * 128], lT.h[:, hc, cc], rbd.h[:, hc, :], True, True))
                        S.mms(items, r=[lT.b, rbd.b], w=[ps.b])
                        return ps
                    ps = prod(Bt, KKbd)
                    sl_ = slots[g]
                    d["A"] = sl_["A0"]
                    S.tt("dve", v3(d["A"].h[:, :]), v3(ps.h[0:64, 0:GW]), bc64(mUs), ALU.mult, r=[mUs.b], w=[d["A"].b, ps.b])
                    bprep = int(os.environ.get("BPREP", "99"))
                    if bprep <= 1:
                        continue
                    d["X"] = sl_["X0"]
                    S.tt("pool", v3(d["X"].h[:, :]), bc64(ident), v3(d["A"].h[:, :]), ALU.subtract, r=[ident.b, d["A"].b], w=[d["X"].b])
                    if bprep <= 2:
                        continue
                    ps = prod(KKt, Bbd)
                    d["Bq"] = sl_["B0"]
                    S.tt("dve", v3(d["Bq"].h[:, :]), v3(ps.h[0:64, 0:GW]), bc64(mLs), ALU.mult, r=[mLs.b], w=[d["Bq"].b, ps.b])
                    ps = prod(Bt, Rbd)
                    d["nM3"] = sl_["nM3"]
                    S.stt("dve", v3(d["nM3"].h[:, :]), v3(ps.h[0:64, 0:GW]), -1.0, bc64(mUi), ALU.mult, ALU.mult, r=[mUi.b], w=[d["nM3"].b, ps.b])
                    ps = prod(Kt, KKbd)
                    d["M2"] = sl_["M2"]
                    S.tt("dve", v3(d["M2"].h[:, :]), v3(ps.h[0:64, 0:GW]), bc64(mUs), ALU.mult, r=[mUs.b], w=[d["M2"].b, ps.b])
                    ps = prod(Kt, Rbd)
                    d["M4"] = sl_["M4"]
                    S.tt("dve", v3(d["M4"].h[:, :]), v3(ps.h[0:64, 0:GW]), bc64(mUi), ALU.mult, r=[mUi.b], w=[d["M4"].b, ps.b])
                    if bprep <= 3:
                        continue
                    for nm, src in (("Vm", Vt), ("Bcm", Bct), ("Kcm", Kct)):
                        ps = K.ps.get()
                        items = []
                        for hl in range(HG):
                            hc = g * HG + hl
                            items.append((ps.h[0:64, hl * 128:(hl + 1) * 128], src.h[:, hc, cc], identb.h[:, :], True, True))
                        S.mms(items, r=[src.b, identb.b], w=[ps.b])
                        d[nm] = sl_[nm]
                        S.copy("act", d[nm].h[:, :], ps.h[0:64, 0:GW], w=[d[nm].b, ps.b])
                bstop = int(os.environ.get("BSTOP", "99"))
                if bstop <= 1:
                    continue
                for lvl in range(1, 6):
                    for g in range(NG):
                        d = st[g]
                        def hmm(l, r_):
                            ps = K.ps.get()
                            items = [(ps.h[0:64, hh * 64:(hh + 1) * 64], l.h[0:64, hh * 64:(hh + 1) * 64], r_.h[0:64, hh * 64:(hh + 1) * 64], True, True)
                                     for hh in range(NHG)]
                            S.mms(items, r=[l.b, r_.b], w=[ps.b])
                            return ps
                        psB = hmm(d["A"], d["Bq"])
                        nB = slots[g]["B%d" % (lvl % 2)]
                        S.copy("act", nB.h[:, :], psB.h[0:64, 0:GW], w=[nB.b, psB.b])
                        if lvl < 5:
                            psA = hmm(d["Bq"], d["A"])
                            nA = slots[g]["A%d" % (lvl % 2)]
                            S.copy("dve", nA.h[:, :], psA.h[0:64, 0:GW], w=[nA.b, psA.b])
                            d["A"] = nA
                        d["Bq"] = nB
                    for g in range(NG):
                        d = st[g]
                        ps = K.ps.get()
                        items = [(ps.h[0:64, hh * 64:(hh + 1) * 64], d["Bq"].h[0:64, hh * 64:(hh + 1) * 64], d["X"].h[0:64, hh * 64:(hh + 1) * 64], True, True)
                                 for hh in range(NHG)]
                        S.mms(items, r=[d["Bq"].b, d["X"].b], w=[ps.b])
                        nX = slots[g]["X%d" % (lvl % 2)]
                        S.tt("dve", nX.h[:, :], ps.h[0:64, 0:GW], d["X"].h[:, :], ALU.add, r=[d["X"].b], w=[nX.b, ps.b])
                        d["X"] = nX
                if bstop <= 2:
                    continue
                for g in range(NG):
                    d = st[g]
                    ps = K.ps.get()
                    items = []
                    for hl in range(HG):
                        hc = g * HG + hl
                        items.append((ps.h[0:64, hl * 128:(hl + 1) * 128], KKt.h[:, hc, cc], Hb.h[:, hc, :], True, False))
                        for e2 in range(2):
                            sl = slice(hl * 128 + e2 * 64, hl * 128 + e2 * 64 + 64)
                            items.append((ps.h[0:64, sl], d["M2"].h[0:64, sl], d["Vm"].h[0:64, sl], False, e2 == 1))
                    S.mms(items, r=[KKt.b, Hbufs[g], d["M2"].b, d["Vm"].b], w=[ps.b])
                    d["W"] = slots[g]["A1"]
                    S.copy("act", d["W"].h[:, :], ps.h[0:64, 0:GW], w=[d["W"].b, ps.b])
                for g in range(NG):
                    d = st[g]
                    ps = K.ps.get()
                    items = [(ps.h[0:64, hh * 64:(hh + 1) * 64], d["X"].h[0:64, hh * 64:(hh + 1) * 64], d["W"].h[0:64, hh * 64:(hh + 1) * 64], True, True)
                             for hh in range(NHG)]
                    S.mms(items, r=[d["X"].b, d["W"].b], w=[ps.b])
                    d["U"] = slots[g]["B0"]
                    S.copy("dve", d["U"].h[:, :], ps.h[0:64, 0:GW], w=[d["U"].b, ps.b])
                if bstop <= 3:
                    continue
                for g in range(NG):
                    d = st[g]
                    ps = K.ps.get()
                    items = []
                    for hl in range(HG):
                        hc = g * HG + hl
                        items.append((ps.h[0:64, hl * 128:(hl + 1) * 128], Rt.h[:, hc, cc], Hb.h[:, hc, :], True, False))
                        for e2 in range(2):
                            sl = slice(hl * 128 + e2 * 64, hl * 128 + e2 * 64 + 64)
                            items.append((ps.h[0:64, sl], d["nM3"].h[0:64, sl], d["U"].h[0:64, sl], False, False))
                            items.append((ps.h[0:64, sl], d["M4"].h[0:64, sl], d["Vm"].h[0:64, sl], False, e2 == 1))
                    S.mms(items, r=[Rt.b, Hbufs[g], d["nM3"].b, d["U"].b, d["M4"].b, d["Vm"].b], w=[ps.b])
                    if need3:
                        otm = fp.get()
                        S.copy("act", otm.h[:, :], ps.h[0:64, 0:GW], w=[otm.b, ps.b])
                        ps2 = K.ps.get()
                        for hl in range(HG):
                            S.transpose(ps2.h[:, hl * 64:(hl + 1) * 64], otm.h[0:64, hl * 128:(hl + 1) * 128], ident.h[0:64, 0:64],
                                        r=[otm.b, ident.b], w=[ps2.b] if hl == 0 else [], wa=[] if hl == 0 else [ps2.b])
                        S.copy("dve", OT.h[:, g * HG:(g + 1) * HG, cc], ps2.h[:, 0:HG * 64].rearrange("p (h t) -> p h t", t=64), w=[ps2.b], wa=[OT.b])
                    ps = K.ps.get()
                    items = []
                    for hl in range(HG):
                        sl = slice(hl * 128, (hl + 1) * 128)
                        o = ps.h[:, sl]
                        items.append((o, d["Bcm"].h[0:64, sl], d["U"].h[0:64, sl], True, False))
                        items.append((o, d["Kcm"].h[0:64, sl], d["Vm"].h[0:64, sl], False, True))
                    S.mms(items, r=[d["Bcm"].b, d["U"].b, d["Kcm"].b, d["Vm"].b], w=[ps.b])
                    for hh in range(NHG):
                        hc, pb = hd(g, hh)
                        hl = hh // 2
                        S.stt("dve", H.h[pb:pb + 64, hc, pb:pb + 64], H.h[pb:pb + 64, hc, pb:pb + 64], gC.h[pb:pb + 64, hc, ci:ci + 1],
                              ps.h[pb:pb + 64, hl * 128 + pb: hl * 128 + pb + 64], ALU.mult, ALU.add,
                              r=[gC.b], w=[Hbufs[g], ps.b])
                    S.copy("pool", Hb.h[:, g * HG:(g + 1) * HG, :], H.h[:, g * HG:(g + 1) * HG, :], w=[Hbufs[g]])
            if need3 and not os.environ.get("BSKIP3"):
                for hc in range(NHC):
                    cs = slice(hc * 128, (hc + 1) * 128)
                    col = lambda nm: ct.h[:, co[nm] + hc:co[nm] + hc + 1]
                    X = slice(0, TB)
                    gt_ = tp.get()
                    S.dma("sp", gt_.h[:, X], K.d["gT"][cs, t0:t0 + TB], r=[K.b["gT"]], w=[gt_.b])
                    bon = tp.get()
                    S.dma("sp", bon.h[:, X], K.d["bonT"][cs, t0:t0 + TB], r=[K.b["bonT"]], w=[bon.b])
                    ps = K.ps.get()
                    S.mms([(ps.h[:, 0:TB], blk.h[:, :], OT.h[:, hc, :], True, True)], r=[blk.b, OT.b], w=[ps.b])
                    cen = tp.get()
                    S.stt("dve", cen.h[:, X], ps.h[:, 0:TB], -1.0 / 64, OT.h[:, hc, :], ALU.mult, ALU.add, r=[OT.b], w=[cen.b, ps.b])
                    sq = tp.get()
                    S.act(sq.h[:, X], cen.h[:, X], AF.Square, r=[cen.b], w=[sq.b])
                    ps = K.ps.get()
                    S.mms([(ps.h[:, 0:TB], blk.h[:, :], sq.h[:, X], True, True)], r=[blk.b, sq.b], w=[ps.b])
                    rs = tp.get()
                    S.ts("dve", rs.h[:, X], ps.h[:, 0:TB], 1.0 / 64, C.GN_EPS, ALU.mult, ALU.add, w=[rs.b, ps.b])
                    S.act(rs.h[:, X], rs.h[:, X], AF.Ln, w=[rs.b])
                    S.act(rs.h[:, X], rs.h[:, X], AF.Exp, w=[rs.b], scale=-0.5)
                    y = tp.get()
                    S.tt("pool", y.h[:, X], cen.h[:, X], rs.h[:, X], ALU.mult, r=[cen.b, rs.b], w=[y.b])
                    S.ts("pool", y.h[:, X], y.h[:, X], col("ln_w"), col("ln_b"), ALU.mult, ALU.add, r=[ct.b], w=[y.b])
                    S.tt("pool", y.h[:, X], y.h[:, X], bon.h[:, X], ALU.add, r=[bon.b], w=[y.b])
                    S.tt("dve", y.h[:, X], y.h[:, X], gt_.h[:, X], ALU.mult, r=[gt_.b], w=[y.b])
                    obf = oabp.get()
                    S.copy("act", obf.h[:, :], y.h[:, X], r=[y.b], w=[obf.b])
                    S.dma("sp", K.d["oaT"][cs, t0 + lo - C.OWN0:t0 + TB - C.OWN0], obf.h[:, lo:TB], r=[obf.b], wa=[K.b["oaT"]])
        K.rem.append(nc.sbuf_bytes_remaining)
        S.emit()


def phase_C(K):
    S, C, nc = K.S, K.C, K.nc
    NKB = C.CTX // 128
    with contextlib.ExitStack() as es:
        allps = psrot(es, nc)
        K.ps = Rot(allps.t[0:4])
        acc = Rot(allps.t[4:8])
        ident = load_const(K, es, "Cident", "ident")
        blk = load_const(K, es, "Cblk", "blk")
        ones = load_const(K, es, "Cones", "ones")
        onesb = load_const(K, es, "Conesb", "ones", dt=BF16)
        pvb = load_const(K, es, "Cpvb", "pvalid", dt=BF16)
        ct, co = load_cols(K, es, "Ccols", ["q_g", "k_g", "subln"])
        lt = sbt(es, nc, "Clamb", [128, 256], F32)
        S.dma("sp", lt.h[:, :], K.d["lamb"][:, :], w=[lt.b])
        sc = sbt(es, nc, "Csc", [128, 12], F32)
        S.memset("dve", sc.h[:, :], 0.0, w=[sc.b])
        tmp = sbt(es, nc, "Cltmp", [128, 128], F32)
        S.tt("dve", tmp.h[:, 0:64], lt.h[:, 0:64], lt.h[:, 64:128], ALU.mult, r=[lt.b], w=[tmp.b])
        S.tt("dve", tmp.h[:, 64:128], lt.h[:, 128:192], lt.h[:, 192:256], ALU.mult, r=[lt.b], w=[tmp.b])
        S.act(tmp.h[:, 0:64], tmp.h[:, 0:64], AF.Identity, w=[tmp.b, sc.b], accum_out=sc.h[:, 0:1])
        S.act(tmp.h[:, 64:128], tmp.h[:, 64:128], AF.Identity, w=[tmp.b, sc.b], accum_out=sc.h[:, 1:2])
        S.act(sc.h[:, 2:4], sc.h[:, 0:2], AF.Exp, w=[sc.b])
        S.tt("dve", sc.h[:, 4:5], sc.h[:, 3:4], sc.h[:, 2:3], ALU.subtract, w=[sc.b])
        S.ts("dve", sc.h[:, 5:6], sc.h[:, 4:5], -C.lambda_init, None, ALU.add, w=[sc.b])
        neglam = sc.h[:, 5:6]
        S.ts("dve", sc.h[:, 6:7], ct.h[:, co["q_g"]:co["q_g"] + 1], 0.125, None, ALU.mult, r=[ct.b], w=[sc.b])
        S.ts("dve", sc.h[:, 7:8], ct.h[:, co["subln"]:co["subln"] + 1], 1.0 - C.lambda_init, None, ALU.mult, r=[ct.b], w=[sc.b])
        qg8, slg = sc.h[:, 6:7], sc.h[:, 7:8]
        kg = ct.h[:, co["k_g"]:co["k_g"] + 1]
        zq = sbrot(es, nc, "Czq", [128, C.NOH], F32, 2)
        zk = sbrot(es, nc, "Czk", [128, C.CTX], F32, 2)
        zv = sbrot(es, nc, "Czv", [128, C.CTX], F32, 2)
        qns = [sbt(es, nc, "Cqn%d" % i, [128, C.NOH], BF16) for i in range(2)]
        kns = [[sbt(es, nc, "Ckn%d_%d" % (i, c), [128, C.CTX], BF16) for c in range(2)] for i in range(2)]
        for i in range(2):
            for c in range(2):
                S.memset("dve", kns[i][c].h[:, :], 0.0, w=[kns[i][c].b])
        Vtms = [sbt(es, nc, "CVtm%d" % i, [128, NKB, 128], BF16) for i in range(2)]
        tq = sbrot(es, nc, "Ctq", [128, 512], F32, 6)
        ptp = sbrot(es, nc, "Cpt", [128, 512], BF16, 4)
        ocp = sbrot(es, nc, "Coc", [128, 512], F32, 4)
        obp = sbrot(es, nc, "Cob", [128, 512], BF16, 2)

        def rmsn(src, n_tot, gcol, outs):
            for (c0, cn) in chunks(n_tot, 512):
                sq = tq.get()
                S.act(sq.h[:, 0:cn], src.h[:, c0:c0 + cn], AF.Square, r=[src.b], w=[sq.b])
                ps = K.ps.get()
                S.mms([(ps.h[:, 0:cn], blk.h[:, :], sq.h[:, 0:cn], True, True)], r=[blk.b, sq.b], w=[ps.b])
                rs = tq.get()
                S.ts("dve", rs.h[:, 0:cn], ps.h[:, 0:cn], 1.0 / 64, C.EPS, ALU.mult, ALU.add, w=[rs.b, ps.b])
                S.act(rs.h[:, 0:cn], rs.h[:, 0:cn], AF.Ln, w=[rs.b])
                S.act(rs.h[:, 0:cn], rs.h[:, 0:cn], AF.Exp, w=[rs.b], scale=-0.5)
                for (r0, r1, dst) in outs:
                    S.stt("dve", dst.h[r0:r1, c0:c0 + cn], src.h[r0:r1, c0:c0 + cn], gcol[r0:r1, :], rs.h[r0:r1, 0:cn], ALU.mult, ALU.mult,
                          r=[src.b, rs.b, sc.b, ct.b], wa=[dst.b])

        precast_step(K, len(K.pc))
        def prep_head(h):
            qn, kn, Vtm = qns[h % 2], kns[h % 2], Vtms[h % 2]
            q_ = zq.get()
            S.dma("sp", q_.h[:, :], K.d["zQ"][h * 128:(h + 1) * 128, 0:C.NOH], r=[K.b["zQ"]], w=[q_.b])
            k_ = zk.get()
            S.dma("sp", k_.h[:, :], K.d["zKV"][h * 128:(h + 1) * 128, 0:C.CTX], r=[K.b["zKV"]], w=[k_.b])
            v_ = zv.get()
            S.dma("sp", v_.h[:, :], K.d["zKV"][C.DW + h * 128:C.DW + (h + 1) * 128, 0:C.CTX], r=[K.b["zKV"]], w=[v_.b])
            S.memset("dve", qn.h[:, 0:1], 0.0, w=[qn.b])
            S.memset("dve", kn[0].h[0:64, 0:1], 0.0, w=[kn[0].b])
            S.memset("dve", kn[1].h[64:128, 0:1], 0.0, w=[kn[1].b])
            rmsn(q_, C.NOH, qg8, [(0, 128, qn)])
            rmsn(k_, C.CTX, kg, [(0, 64, kn[0]), (64, 128, kn[1])])
            first = True
            for kb0 in range(0, NKB, 4):
                ps = K.ps.get()
                n = min(4, NKB - kb0)
                for j in range(n):
                    S.transpose(ps.h[:, j * 128:(j + 1) * 128], v_.h[:, (kb0 + j) * 128:(kb0 + j + 1) * 128], ident.h[:, :],
                                r=[v_.b, ident.b], w=[ps.b] if j == 0 else [], wa=[] if j == 0 else [ps.b])
                S.copy(S.ev(), Vtm.h[:, kb0:kb0 + n, :], ps.h[:, 0:n * 128].rearrange("p (k e) -> p k e", e=128),
                       w=[ps.b] + ([Vtm.b] if first else []), wa=[] if first else [Vtm.b])
                first = False

        precast_step(K, len(K.pc))
        prep_head(0)
        for h in range(C.NDH):
            rows = slice(h * 128, (h + 1) * 128)
            qn, kn, Vtm = qns[h % 2], kns[h % 2], Vtms[h % 2]
            for qi_, (q0, NQ) in enumerate(C.tiles_oh()):
                if qi_ == min(2, len(C.tiles_oh()) - 1) and h + 1 < C.NDH:
                    prep_head(h + 1)
                jq = q0 - C.OWN0
                nkb = (q0 + NQ) // 128
                ocs = []
                for c in range(2):
                    psO = acc.get()
                    psL = acc.get()

                    def smm(kb):
                        j0 = max(kb * 128 - q0, 0)
                        n = NQ - j0
                        ps = K.ps.get()
                        S.mms([(ps.h[:, 0:n], kn[c].h[:, kb * 128:(kb + 1) * 128], qn.h[:, jq + j0:jq + NQ], True, True)],
                              r=[kn[c].b, qn.b], w=[ps.b])
                        return ps, j0, n
                    nxt = smm(0)
                    for kb in range(nkb):
                        ps, j0, n = nxt
                        if kb + 1 < nkb:
                            nxt = smm(kb + 1)
                        pt = ptp.get()
                        S.act(pt.h[:, 0:n], ps.h[:, 0:n], AF.Exp, w=[pt.b, ps.b])
                        if kb * 128 >= q0:
                            S.memset("dve", pt.h[64:128, 0:64], 0.0, w=[pt.b])
                        lv = pvb if kb * 128 < C.HALF else onesb
                        S.mms([(psO.h[:, j0:NQ], Vtm.h[:, kb, :], pt.h[:, 0:n], kb == 0, kb == nkb - 1)], r=[Vtm.b, pt.b],
                              w=[psO.b] if kb == 0 else [], wa=[] if kb == 0 else [psO.b])
                        S.mms([(psL.h[:, j0:NQ], lv.h[:, :], pt.h[:, 0:n], kb == 0, kb == nkb - 1)], r=[lv.b, pt.b],
                              w=[psL.b] if kb == 0 else [], wa=[] if kb == 0 else [psL.b])
                    rl = tq.get()
                    S.ts("dve", rl.h[:, 0:NQ], psL.h[:, 0:NQ], 1e-30, None, ALU.max, w=[rl.b, psL.b])
                    S.recip(rl.h[:, 0:NQ], rl.h[:, 0:NQ], w=[rl.b])
                    oc = ocp.get()
                    S.tt("dve", oc.h[:, 0:NQ], psO.h[:, 0:NQ], rl.h[:, 0:NQ], ALU.mult, r=[rl.b], w=[oc.b, psO.b])
                    ocs.append(oc)
                df = tq.get()
                S.stt("dve", df.h[:, 0:NQ], ocs[1].h[:, 0:NQ], neglam, ocs[0].h[:, 0:NQ], ALU.mult, ALU.add, r=[ocs[0].b, ocs[1].b, sc.b], w=[df.b])
                sq = tq.get()
                S.act(sq.h[:, 0:NQ], df.h[:, 0:NQ], AF.Square, r=[df.b], w=[sq.b])
                ps = K.ps.get()
                S.mms([(ps.h[:, 0:NQ], ones.h[:, :], sq.h[:, 0:NQ], True, True)], r=[ones.b, sq.b], w=[ps.b])
                rs = tq.get()
                S.ts("dve", rs.h[:, 0:NQ], ps.h[:, 0:NQ], 1.0 / 128, C.EPS, ALU.mult, ALU.add, w=[rs.b, ps.b])
                S.act(rs.h[:, 0:NQ], rs.h[:, 0:NQ], AF.Ln, w=[rs.b])
                S.act(rs.h[:, 0:NQ], rs.h[:, 0:NQ], AF.Exp, w=[rs.b], scale=-0.5)
                ob = obp.get()
                S.stt("dve", ob.h[:, 0:NQ], df.h[:, 0:NQ], slg, rs.h[:, 0:NQ], ALU.mult, ALU.mult, r=[df.b, rs.b, sc.b], w=[ob.b])
                S.dma("sp", K.d["obT"][rows, jq:jq + NQ], ob.h[:, 0:NQ], r=[ob.b], wa=[K.b["obT"]])
        K.rem.append(nc.sbuf_bytes_remaining)
        S.emit()


def phase_DE(K):
    S, C, nc = K.S, K.C, K.nc
    DC = C.D // 128
    RCH = C.RW // 128
    with contextlib.ExitStack() as es:
        K.ps = psrot(es, nc)
        oat = sbt(es, nc, "Doat", [128, RCH, C.NT], BF16)
        obt = sbt(es, nc, "Dobt", [128, RCH, C.NT], BF16)
        mT = sbt(es, nc, "DmT", [128, DC, C.NT], BF16)
        wap = sbrot(es, nc, "Dwa", [128, RCH, 256], BF16, 2)
        wbp = sbrot(es, nc, "Dwb", [128, RCH, 256], BF16, 2)
        wop = sbrot(es, nc, "Dwo", [128, DC, 512], BF16, 2)
        gp = sbrot(es, nc, "Dg", [128, C.NT], F32, 4)
        mp_ = sbrot(es, nc, "Dm", [128, C.NT], F32, 4)
        xp = sbrot(es, nc, "Dx", [128, 512], F32, 4)
        oav = K.d["oaT"].rearrange("(c p) t -> p c t", p=128)
        obv = K.d["obT"].rearrange("(c p) t -> p c t", p=128)
        for (t0, nt) in C.tiles_oh():
            jq = t0 - C.OWN0
            S.dma("sp", oat.h[:, :, 0:nt], oav[:, :, jq:jq + nt], r=[K.b["oaT"]], w=[oat.b])
            S.dma("sp", obt.h[:, :, 0:nt], obv[:, :, jq:jq + nt], r=[K.b["obT"]], w=[obt.b])
            firstm = True
            for (g0, gw) in chunks(C.D, 256):
                wa_ = wap.get()
                load_w(K, wa_, K.d["bf_w_a"], g0 // 256, 0, RCH, gw, rbuf=K.b["bf_w_a"])
                wb_ = wbp.get()
                load_w(K, wb_, K.d["bf_w_b"], g0 // 256, 0, RCH, gw, rbuf=K.b["bf_w_b"])
                for (j0, wj) in chunks(gw, 128):
                    col = g0 + j0
                    psA = K.ps.get()
                    S.mms([(psA.h[0:wj, 0:nt], wa_.h[:, k, j0:j0 + wj], oat.h[:, k, 0:nt], k == 0, k == RCH - 1) for k in range(RCH)],
                          r=[wa_.b, oat.b], w=[psA.b])
                    psB = K.ps.get()
                    S.mms([(psB.h[0:wj, 0:nt], wb_.h[:, k, j0:j0 + wj], obt.h[:, k, 0:nt], k == 0, k == RCH - 1) for k in range(RCH)],
                          r=[wb_.b, obt.b], w=[psB.b])
                    ga = gp.get()
                    S.dma("sp", ga.h[:, 0:nt], K.d["zG"][col:col + 128, jq:jq + nt], r=[K.b["zG"]], w=[ga.b])
                    gb = gp.get()
                    S.dma("sp", gb.h[:, 0:nt], K.d["zG"][C.D + col:C.D + col + 128, jq:jq + nt], r=[K.b["zG"]], w=[gb.b])
                    S.act(ga.h[:, 0:nt], ga.h[:, 0:nt], AF.Sigmoid, w=[ga.b])
                    S.act(gb.h[:, 0:nt], gb.h[:, 0:nt], AF.Sigmoid, w=[gb.b])
                    m1 = mp_.get()
                    S.tt("dve", m1.h[:, 0:nt], psA.h[:, 0:nt], ga.h[:, 0:nt], ALU.mult, r=[ga.b], w=[m1.b, psA.b])
                    m2 = mp_.get()
                    S.tt("dve", m2.h[:, 0:nt], psB.h[:, 0:nt], gb.h[:, 0:nt], ALU.mult, r=[gb.b], w=[m2.b, psB.b])
                    S.tt("pool", mT.h[:, col // 128, 0:nt], m1.h[:, 0:nt], m2.h[:, 0:nt], ALU.add, r=[m1.b, m2.b],
                         w=[mT.b] if firstm else [], wa=[] if firstm else [mT.b])
                    firstm = False
            for (g0, gw) in chunks(C.D, 512):
                wo_ = wop.get()
                load_w(K, wo_, K.d["bf_w_o"], g0 // 512, 0, DC, gw, rbuf=K.b["bf_w_o"])
                for s_ in range(nt // 128):
                    ps = K.ps.get()
                    S.mms([(ps.h[:, 0:gw], mT.h[:, k, s_ * 128:(s_ + 1) * 128], wo_.h[:, k, 0:gw], k == 0, k == DC - 1) for k in range(DC)],
                          r=[mT.b, wo_.b], w=[ps.b])
                    xt = xp.get()
                    S.dma("sp", xt.h[:, 0:gw], K.d["xc"][t0 + s_ * 128:t0 + (s_ + 1) * 128, g0:g0 + gw], w=[xt.b])
                    S.tt("dve", xt.h[:, 0:gw], ps.h[:, 0:gw], xt.h[:, 0:gw], ALU.add, w=[xt.b, ps.b])
                    S.dma("sp", K.d["x1"][jq + s_ * 128:jq + (s_ + 1) * 128, g0:g0 + gw], xt.h[:, 0:gw], r=[xt.b], wa=[K.b["x1"]])
        K.rem.append(nc.sbuf_bytes_remaining)
        S.emit()


def phase_F(K):
    S, C, nc = K.S, K.C, K.nc
    DC = C.D // 128
    FB = C.DFF // 128
    KH = FB // 2
    SEG = chunks(KH, 22)
    with contextlib.ExitStack() as es:
        allps = psrot(es, nc)
        K.ps = Rot(allps.t[0:4])
        acc = allps.t[4:8]
        aT = sbt(es, nc, "FaT", [128, KH, C.NT], BF16)
        junk = Ctx()
        junk.h = aT.h[:, 0:C.D // C.NT, :].rearrange("p a b -> p (a b)")
        junk.b = aT.b
        R = norm_res(K, es, "F", nxs=2, junk=junk)
        ct, co = load_cols(K, es, "Fcols", ["g_ffn", "cw0", "cw1", "cw2", "cb"])
        pv = load_const(K, es, "Fpv", "pvalid")
        hT = sbt(es, nc, "FhT", [128, DC, C.NT], BF16)
        wgp = sbrot(es, nc, "Fwg", [128, DC, 256], BF16, 2)
        wfp = sbrot(es, nc, "Fwf", [128, max(n for _, n in SEG), 256], BF16, 2)
        halo = sbt(es, nc, "Fhalo", [128, 2 * FB, 2], F32)
        S.memset("pool", halo.h[:, :, :], 0.0, w=[halo.b])
        ubp = sbrot(es, nc, "Fub", [128, C.NT + 2], F32, 4)
        cvp = sbrot(es, nc, "Fcv", [128, C.NT], F32, 4)
        xp = sbrot(es, nc, "Fx", [128, 256], F32, 4)
        for ti, (t0, nt) in enumerate(C.tiles_oh()):
            jq = t0 - C.OWN0
            is_halo = (ti == 0)
            row0 = t0 - C.HALF
            build_hT(K, R, K.d["x1"], jq, nt, ct, co["g_ffn"], hT)
            for half2 in range(2):
                firsta = True
                for jb in range(half2 * KH, (half2 + 1) * KH):
                    wg_ = wgp.get()
                    load_w(K, wg_, K.d["bf_w_fi"], jb, 0, DC, 256, rbuf=K.b["bf_w_fi"])
                    cvs = []
                    for half in range(2):
                        blk_i = jb + half * FB
                        ps = K.ps.get()
                        S.mms([(ps.h[:, 0:nt], wg_.h[:, k, half * 128:(half + 1) * 128], hT.h[:, k, 0:nt], k == 0, k == DC - 1) for k in range(DC)],
                              r=[wg_.b, hT.b], w=[ps.b])
                        ub = ubp.get()
                        S.copy("act", ub.h[:, 2:nt + 2], ps.h[:, 0:nt], w=[ub.b, ps.b])
                        S.copy("dve", ub.h[:, 0:2], halo.h[:, blk_i, :], r=[halo.b], wa=[ub.b])
                        if is_halo:
                            S.ts("dve", halo.h[:, blk_i, :], ub.h[:, nt:nt + 2], pv.h[:, 0:1], None, ALU.mult, r=[ub.b, pv.b], w=[halo.b])
                            continue
                        S.copy("dve", halo.h[:, blk_i, :], ub.h[:, nt:nt + 2], r=[ub.b], w=[halo.b])
                        cv = cvp.get()
                        cc = lambda nm: ct.h[:, co[nm] + blk_i:co[nm] + blk_i + 1]
                        S.act(cv.h[:, 0:nt], ub.h[:, 2:nt + 2], AF.Identity, r=[ub.b, ct.b], w=[cv.b], scale=cc("cw2"), bias=cc("cb"))
                        S.stt("dve", cv.h[:, 0:nt], ub.h[:, 1:nt + 1], cc("cw1"), cv.h[:, 0:nt], ALU.mult, ALU.add, r=[ub.b, ct.b], w=[cv.b])
                        S.stt("dve", cv.h[:, 0:nt], ub.h[:, 0:nt], cc("cw0"), cv.h[:, 0:nt], ALU.mult, ALU.add, r=[ub.b, ct.b], w=[cv.b])
                        cvs.append(cv)
                    if is_halo:
                        continue
                    sgt = cvp.get()
                    S.act(sgt.h[:, 0:nt], cvs[0].h[:, 0:nt], AF.Silu, r=[cvs[0].b], w=[sgt.b])
                    S.tt("dve", aT.h[:, jb - half2 * KH, 0:nt], sgt.h[:, 0:nt], cvs[1].h[:, 0:nt], ALU.mult, r=[sgt.b, cvs[1].b],
                         w=[aT.b] if firsta else [], wa=[] if firsta else [aT.b])
                    firsta = False
                if is_halo:
                    continue
                nsub = nt // 128
                for (g0, gw) in chunks(C.D, 256):
                    for si, (k0, kn_) in enumerate(SEG):
                        wf_ = wfp.get()
                        load_w(K, wf_, K.d["bf_w_fo"], g0 // 256, half2 * KH + k0, kn_, gw, rbuf=K.b["bf_w_fo"])
                        for s_ in range(nsub):
                            ps = acc[s_]
                            items = [(ps.h[:, 0:gw], aT.h[:, k0 + k, s_ * 128:(s_ + 1) * 128], wf_.h[:, k, 0:gw],
                                      si == 0 and k == 0, si == len(SEG) - 1 and k == kn_ - 1) for k in range(kn_)]
                            S.mms(items, r=[aT.b, wf_.b], w=[ps.b] if si == 0 else [], wa=[] if si == 0 else [ps.b])
                    for s_ in range(nsub):
                        ps = acc[s_]
                        xt = xp.get()
                        if half2 == 0:
                            S.dma("sp", xt.h[:, 0:gw], K.d["x1"][jq + s_ * 128:jq + (s_ + 1) * 128, g0:g0 + gw], r=[K.b["x1"]], w=[xt.b])
                        else:
                            S.dma("sp", xt.h[:, 0:gw], K.d["x2"][row0 + s_ * 128:row0 + (s_ + 1) * 128, g0:g0 + gw], r=[K.b["x2"]], w=[xt.b])
                        S.tt("dve", xt.h[:, 0:gw], ps.h[:, 0:gw], xt.h[:, 0:gw], ALU.add, w=[xt.b, ps.b])
                        S.dma("sp", K.d["x2"][row0 + s_ * 128:row0 + (s_ + 1) * 128, g0:g0 + gw], xt.h[:, 0:gw], r=[xt.b], wa=[K.b["x2"]])
        K.rem.append(nc.sbuf_bytes_remaining)
        S.emit()


def phase_G(K):
    S, C, nc = K.S, K.C, K.nc
    DC = C.D // 128
    PC = C.PLE // 128
    with contextlib.ExitStack() as es:
        K.ps = psrot(es, nc)
        R = norm_res(K, es, "G")
        ct, co = load_cols(K, es, "Gcols", ["g_ple"])
        hT = sbt(es, nc, "GhT", [128, DC, C.NT], BF16)
        pT = sbt(es, nc, "GpT", [128, PC, C.NT], BF16)
        pp = sbrot(es, nc, "Gp", [128, C.PLE], F32, 2)
        wgp = sbrot(es, nc, "Gwg", [128, DC, 512], BF16, 2)
        wpp = sbrot(es, nc, "Gwp", [128, PC, 512], BF16, 2)
        sgp = sbrot(es, nc, "Gsg", [128, 512], F32, 3)
        xp = sbrot(es, nc, "Gx", [128, 512], F32, 3)
        for (t0, nt) in C.tiles_oh()[1:]:
            row0 = t0 - C.HALF
            build_hT(K, R, K.d["x2"], row0, nt, ct, co["g_ple"], hT)
            firstp = True
            for s_ in range(nt // 128):
                pt = pp.get()
                S.dma("sp", pt.h[:, :], K.d["pc"][row0 + s_ * 128:row0 + (s_ + 1) * 128, :], w=[pt.b])
                ps = K.ps.get()
                for c in range(PC):
                    S.transpose(ps.h[:, c * 128:(c + 1) * 128], pt.h[:, c * 128:(c + 1) * 128], R.ident.h[:, :], r=[pt.b, R.ident.b],
                                w=[ps.b] if c == 0 else [], wa=[] if c == 0 else [ps.b])
                S.copy("act", pT.h[:, :, s_ * 128:(s_ + 1) * 128], ps.h[:, 0:PC * 128].rearrange("p (c t) -> p c t", t=128),
                       w=[ps.b] + ([pT.b] if firstp else []), wa=[] if firstp else [pT.b])
                firstp = False
            for (g0, gw) in chunks(C.D, 512):
                wg_ = wgp.get()
                load_w(K, wg_, K.d["bf_w_pg"], g0 // 512, 0, DC, gw, rbuf=K.b["bf_w_pg"])
                wp_ = wpp.get()
                load_w(K, wp_, K.d["w_pp"], g0 // 512, 0, PC, gw)
                for s_ in range(nt // 128):
                    psG = K.ps.get()
                    S.mms([(psG.h[:, 0:gw], hT.h[:, k, s_ * 128:(s_ + 1) * 128], wg_.h[:, k, 0:gw], k == 0, k == DC - 1) for k in range(DC)],
                          r=[hT.b, wg_.b], w=[psG.b])
                    psP = K.ps.get()
                    S.mms([(psP.h[:, 0:gw], pT.h[:, k, s_ * 128:(s_ + 1) * 128], wp_.h[:, k, 0:gw], k == 0, k == PC - 1) for k in range(PC)],
                          r=[pT.b, wp_.b], w=[psP.b])
                    sg = sgp.get()
                    S.act(sg.h[:, 0:gw], psG.h[:, 0:gw], AF.Sigmoid, w=[sg.b, psG.b])
                    S.tt("dve", sg.h[:, 0:gw], psP.h[:, 0:gw], sg.h[:, 0:gw], ALU.mult, w=[sg.b, psP.b])
                    xt = xp.get()
                    S.dma("sp", xt.h[:, 0:gw], K.d["x2"][row0 + s_ * 128:row0 + (s_ + 1) * 128, g0:g0 + gw], r=[K.b["x2"]], w=[xt.b])
                    S.tt("pool", xt.h[:, 0:gw], xt.h[:, 0:gw], sg.h[:, 0:gw], ALU.add, r=[sg.b], w=[xt.b])
                    S.dma("sp", K.d["out"][row0 + s_ * 128:row0 + (s_ + 1) * 128, g0:g0 + gw], xt.h[:, 0:gw], r=[xt.b], wa=[K.b["out"]])
        K.rem.append(nc.sbuf_bytes_remaining)
        S.emit()

def build_program(C, upto="A", debug=()):
    nc = bass.Bass("TRN2", target_bir_lowering=False)
    K = Ctx()
    K.nc, K.C = nc, C
    K.d, K.b = {}, {}
    K.rem = []

    def din(name, shape):
        K.d[name] = nc.dram_tensor(name, list(shape), F32, kind="ExternalInput").ap()

    def dscr(name, shape, dt=F32, out=False):
        kind = "ExternalOutput" if out else "Internal"
        K.d[name] = nc.dram_tensor(name, list(shape), dt, kind=kind).ap()
        K.b[name] = Buf(name)

    din("xc", [C.CTX, C.D])
    din("pc", [C.HALF, C.PLE])
    din("cols", [128, C.NCOLS])
    din("consts", [128, C.NCONST])
    din("lamb", [128, 256])
    din("w_in", [(C.IC + 511) // 512, 128, C.D // 128, 512])
    din("w2", [C.DL, C.RW])
    din("a2", [C.AL, C.RW])
    din("g2", [C.GL, C.RW])
    din("w_a", [C.D // 256, 128, C.RW // 128, 256])
    din("w_b", [C.D // 256, 128, C.DW // 128, 256])
    din("w_o", [C.D // 512, 128, C.D // 128, 512])
    din("w_fi", [C.DFF // 128, 128, C.D // 128, 256])
    din("w_fo", [C.D // 256, 128, C.DFF // 128, 256])
    din("w_pg", [C.D // 512, 128, C.D // 128, 512])
    din("w_pp", [C.D // 512, 128, C.PLE // 128, 512])
    dscr("out", [C.HALF, C.D], out=True)
    dscr("zR", [C.RC, C.CTX], out=("zT" in debug))
    dscr("zQ", [C.DW, C.NOH], out=("zT" in debug))
    dscr("zKV", [2 * C.DW, C.CTX], out=("zT" in debug))
    dscr("zG", [2 * C.D, C.NOH], out=("zT" in debug))
    for nm_ in ("w_a", "w_b", "w_o", "w_fi", "w_fo", "w_pg", "w_in"):
        dscr("bf_" + nm_, list(K.d[nm_].shape), BF16)
    dscr("gT", [C.RW, C.CTX], out=("gT" in debug))
    dscr("bonT", [C.RW, C.CTX], out=("gT" in debug))
    dscr("oaT", [C.RW, C.NOH], BF16, out=("oaT" in debug))
    dscr("obT", [C.DW, C.NOH], BF16, out=("obT" in debug))
    dscr("x1", [C.NOH, C.D], out=("x1" in debug))
    dscr("x2", [C.HALF, C.D], out=("x2" in debug))
    with contextlib.ExitStack() as es0:
        K.S = Sched(nc, es0)
        phase_A(K)
        if upto == "A":
            return nc, K
        K.pc = precast_list(K)
        phase_B(K)
        if upto == "B":
            return nc, K
        phase_C(K)
        if upto == "C":
            return nc, K
        phase_DE(K)
        if upto == "DE":
            return nc, K
        phase_F(K)
        if upto == "F":
            return nc, K
        phase_G(K)
    return nc, K


def colpack(v):
    v = np.asarray(v, np.float32).reshape(-1)
    n = (v.size + 127) // 128
    p = np.zeros(n * 128, np.float32)
    p[:v.size] = v
    return np.ascontiguousarray(p.reshape(n, 128).T)


def tile_w(w, W):
    w = np.asarray(w, np.float32)
    Kd, N = w.shape
    NG = (N + W - 1) // W
    if NG * W != N:
        w = np.concatenate([w, np.zeros((Kd, NG * W - N), np.float32)], axis=1)
    return np.ascontiguousarray(w.reshape(Kd // 128, 128, NG, W).transpose(2, 1, 0, 3))


def host_inputs(C, inp):
    g = lambda k: np.asarray(inp[k], np.float32)[0]
    mu = g("rwkv_mu")
    cw = g("ffn_conv_w")
    parts = {
        "g_mix": g("norm_mix_g"), "g_ffn": g("norm_ffn_g"), "g_ple": g("norm_ple_g"),
        "mu_r": mu[C.o_r:C.o_r + C.RW], "mu_k": mu[C.o_k:C.o_k + C.RW], "mu_v": mu[C.o_v:C.o_v + C.RW],
        "mu_w": mu[C.o_wl:C.o_wl + C.DL], "mu_a": mu[C.o_al:C.o_al + C.AL], "mu_g": mu[C.o_gl:C.o_gl + C.GL],
        "w0": g("rwkv_w0"), "a0": g("rwkv_a0"), "k_k": g("rwkv_k_k"), "k_a": g("rwkv_k_a"),
        "r_k": g("rwkv_r_k").reshape(-1), "ln_w": g("rwkv_ln_w"), "ln_b": g("rwkv_ln_b"),
        "q_g": np.tile(g("q_norm_g"), 2), "k_g": np.tile(g("k_norm_g"), 2), "subln": g("subln_g"),
        "cw0": cw[0], "cw1": cw[1], "cw2": cw[2], "cb": g("ffn_conv_b"),
    }
    cols = np.zeros((128, C.NCOLS), np.float32)
    for name, (off, n) in C.colmap.items():
        cp = colpack(parts[name])
        assert cp.shape[1] == n, (name, cp.shape, n)
        cols[:, off:off + n] = cp
    consts = np.zeros((128, C.NCONST), np.float32)
    consts[:, 0:128] = np.eye(128, dtype=np.float32)
    consts[0:64, 128:192] = 1.0
    consts[64:128, 192:256] = 1.0
    consts[:, 256:384] = 1.0
    s = np.arange(64)[:, None]
    t = np.arange(64)[None, :]
    consts[0:64, 384:448] = (s < t)
    consts[0:64, 448:512] = (s <= t)
    consts[0:64, 512:576] = (s > t)
    sm = np.ones(512, np.float32)
    sm[0::64] = 0.0
    consts[:, 576:1088] = sm[None, :]
    lamb = np.concatenate([g("lam_q1"), g("lam_k1"), g("lam_q2"), g("lam_k2")])[None, :].repeat(128, 0)
    x = np.asarray(inp["x"], np.float32)
    p = np.asarray(inp["p"], np.float32)[0]
    shared = {
        "cols": cols, "lamb": np.ascontiguousarray(lamb),
        "w_in": tile_w(g("w_in"), 512), "w2": g("rwkv_w2"), "a2": g("rwkv_a2"), "g2": g("rwkv_g2"),
        "w_a": tile_w(g("w_branch_a"), 256), "w_b": tile_w(g("w_branch_b"), 256), "w_o": tile_w(g("w_out"), 512),
        "w_fi": np.ascontiguousarray(np.concatenate([tile_w(g("w_ffn_in")[:, :C.DFF], 128), tile_w(g("w_ffn_in")[:, C.DFF:], 128)], axis=3)),
        "w_fo": tile_w(g("w_ffn_out"), 256), "w_pg": tile_w(g("w_ple_gate"), 512), "w_pp": tile_w(g("w_ple_proj"), 512),
    }
    maps = []
    for core in range(2 * C.B):
        b, hf = core // 2, core % 2
        m = dict(shared)
        if hf == 1:
            m["xc"] = np.ascontiguousarray(x[b])
        else:
            xc = np.zeros((C.CTX, C.D), np.float32)
            xc[C.HALF:] = x[b, 0:C.HALF]
            m["xc"] = xc
        m["pc"] = np.ascontiguousarray(p[b, hf * C.HALF:(hf + 1) * C.HALF])
        cc = consts.copy()
        cc[:, 1088:1216] = float(hf)
        m["consts"] = cc
        maps.append(m)
    return maps


_PROG = {}


def kernel(**inputs):
    C = Cfg()
    if "full" not in _PROG:
        _PROG["full"] = build_program(C, upto="ALL")
    nc, K = _PROG["full"]
    maps = host_inputs(C, inputs)
    res = run_bass_kernel_spmd(nc, maps, core_ids=list(range(2 * C.B)))
    out = np.zeros((C.B, C.SEQ, C.D), np.float32)
    for core in range(2 * C.B):
        b, hf = core // 2, core % 2
        out[b, hf * C.HALF:(hf + 1) * C.HALF] = res.results[core]["out"]
    return out
```

```python
import contextlib
import math
import numpy as np
import concourse.bass as bass
import concourse.mybir as mybir
from concourse.bass_utils import run_bass_kernel_spmd

F32 = mybir.dt.float32
BF16 = mybir.dt.bfloat16
AF = mybir.ActivationFunctionType
ALU = mybir.AluOpType


class Buf:
    __slots__ = ("name", "w", "r", "g")

    def __init__(self, name):
        self.name = name
        self.w = {}
        self.r = {}
        self.g = None


class Tl:
    def __init__(self, h, name):
        self.h = h
        self.b = Buf(name)


class Rot:
    def __init__(self, tiles):
        self.t = tiles
        self.i = 0

    def get(self):
        t = self.t[self.i % len(self.t)]
        self.i += 1
        return t


class Sched:
    ENGS = ("pe", "act", "dve", "pool", "sp")
    import os
    NDS = int(os.environ.get('NDS', 6))

    def __init__(self, nc, es):
        self.nc = nc
        self.streams = {e: [] for e in self.ENGS}
        self.cnt = {e: 0 for e in self.ENGS}
        self.sems = {}
        for e in self.ENGS:
            self.sems[e] = es.enter_context(nc.semaphore("s_" + e))
        self.dq = {}
        self.dlast = {}
        for q in ("sp", "act", "pool"):
            self.dq[q] = 0
            for i in range(self.NDS):
                self.sems[("d", q, i)] = es.enter_context(nc.semaphore("d_%s_%d" % (q, i)))
        self.known = {}
        self.final = []
        self.rr = 0
        self.log = []

    def _wait(self, e, key, val):
        if key == "pe" and e == "pe":
            return
        if self.known.get((e, key), 0) >= val:
            return
        self.known[(e, key)] = val
        self.log.append((e, "wait", key, val))
        sem = self.sems[key]
        self.streams[e].append(lambda eng, sem=sem, val=val: eng.wait_ge(sem, val))

    def _deps(self, e, r, w, wa):
        for b in r:
            for k, v in b.w.items():
                self._wait(e, k, v)
        for b in w:
            for k, v in b.w.items():
                self._wait(e, k, v)
            for k, v in b.r.items():
                self._wait(e, k, v)
        for b in wa:
            if b.g is not None:
                self._wait(e, b.g[0], b.g[1])
            for k, v in b.r.items():
                self._wait(e, k, v)

    def _mark(self, tok, r, w, wa):
        k, v = tok
        for b in r:
            if b.r.get(k, 0) < v:
                b.r[k] = v
        for b in w:
            b.w = {k: v}
            b.r = {}
            b.g = tok
        for b in wa:
            if b.w.get(k, 0) < v:
                b.w[k] = v

    def op(self, e, fn, r=(), w=(), wa=(), inc=True):
        self._deps(e, r, w, wa)
        sem = self.sems[e]
        if inc:
            self.cnt[e] += 1
            tok = (e, self.cnt[e])
            self.streams[e].append(lambda eng, fn=fn, sem=sem: fn(eng).then_inc(sem, 1))
        else:
            tok = (e, self.cnt[e] + 1)
            self.streams[e].append(lambda eng, fn=fn: fn(eng))
        self._mark(tok, r, w, wa)
        self.log.append((e, "op", tok, inc))
        return tok

    def dma(self, q, out, in_, r=(), w=(), wa=(), final=False):
        j = self.dq[q]
        self.dq[q] += 1
        slot = j % self.NDS
        key = ("d", q, slot)
        val = 16 * (j // self.NDS + 1)
        if j >= self.NDS:
            self._wait(q, key, val - 16)
        self._deps(q, r, w, wa)
        sem = self.sems[key]
        self.streams[q].append(
            lambda eng, out=out, in_=in_, sem=sem: eng.dma_start(out=out, in_=in_).then_inc(sem, 16))
        tok = (key, val)
        self.log.append((q, "dma", tok))
        self.dlast[key] = val
        self._mark(tok, r, w, wa)
        if final:
            self.final.append(tok)
        return tok

    def act(self, out, in_, func, r=(), w=(), wa=(), **kw):
        return self.op("act", lambda e: e.activation(out=out, in_=in_, func=func, **kw), r, w, wa)

    def ts(self, eng, out, in0, s1, s2, op0, op1=None, r=(), w=(), wa=()):
        if op1 is None:
            return self.op(eng, lambda e: e.tensor_scalar(out=out, in0=in0, scalar1=s1, scalar2=None, op0=op0), r, w, wa)
        return self.op(eng, lambda e: e.tensor_scalar(out=out, in0=in0, scalar1=s1, scalar2=s2, op0=op0, op1=op1), r, w, wa)

    def tt(self, eng, out, in0, in1, op, r=(), w=(), wa=()):
        return self.op(eng, lambda e: e.tensor_tensor(out=out, in0=in0, in1=in1, op=op), r, w, wa)

    def stt(self, eng, out, in0, scalar, in1, op0, op1, r=(), w=(), wa=()):
        return self.op(eng, lambda e: e.scalar_tensor_tensor(out=out, in0=in0, scalar=scalar, in1=in1, op0=op0, op1=op1), r, w, wa)

    def copy(self, eng, out, in_, r=(), w=(), wa=()):
        if eng == "act":
            return self.act(out, in_, AF.Identity, r, w, wa)
        return self.op(eng, lambda e: e.tensor_copy(out=out, in_=in_), r, w, wa)

    def memset(self, eng, ap, val, w=(), wa=()):
        return self.op(eng, lambda e: e.memset(ap, val), (), w, wa)

    def recip(self, out, in_, r=(), w=(), wa=()):
        return self.op("dve", lambda e: e.reciprocal(out=out, in_=in_), r, w, wa)

    def scan(self, out, d0, d1, init, op0, op1, r=(), w=(), wa=()):
        return self.op("dve", lambda e: e.tensor_tensor_scan(out=out, data0=d0, data1=d1, initial=init, op0=op0, op1=op1), r, w, wa)

    def transpose(self, out, in_, ident, r=(), w=(), wa=()):
        return self.op("pe", lambda e: e.transpose(out, in_, ident), r, w, wa)

    def mms(self, items, r=(), w=(), wa=()):
        self._deps("pe", r, w, wa)
        n = len(items)
        sem = self.sems["pe"]
        self.cnt["pe"] += 1
        tok = ("pe", self.cnt["pe"])
        for i, (o, l, rh, st, sp) in enumerate(items):
            if i == n - 1:
                self.streams["pe"].append(
                    lambda eng, o=o, l=l, rh=rh, st=st, sp=sp, sem=sem: eng.matmul(o, l, rh, start=st, stop=sp).then_inc(sem, 1))
            else:
                self.streams["pe"].append(
                    lambda eng, o=o, l=l, rh=rh, st=st, sp=sp: eng.matmul(o, l, rh, start=st, stop=sp))
        self._mark(tok, r, w, wa)
        return tok

    def ev(self):
        self.rr += 1
        return "act" if self.rr % 2 else "dve"

    def emit(self, last=False):
        nc = self.nc
        for e in self.ENGS:
            for f in self.ENGS:
                if f != e and self.cnt[f] > 0:
                    self._wait(e, f, self.cnt[f])
            for key, val in self.dlast.items():
                self._wait(e, key, val)
        self.final = []
        streams = self.streams
        self.streams = {e: [] for e in self.ENGS}
        with nc.Block() as block:
            @block.tensor
            def _(eng):
                for f in streams["pe"]:
                    f(eng)

            @block.scalar
            def _(eng):
                for f in streams["act"]:
                    f(eng)

            @block.vector
            def _(eng):
                for f in streams["dve"]:
                    f(eng)

            @block.gpsimd
            def _(eng):
                for f in streams["pool"]:
                    f(eng)

            @block.sync
            def _(eng):
                for f in streams["sp"]:
                    f(eng)


class Cfg:
    def __init__(self, D=4096, SEQ=4096, B=4, PLE=256):
        self.D, self.SEQ, self.B, self.PLE = D, SEQ, B, PLE
        self.EPS = 1e-6
        self.RW = D // 2
        self.NH = self.RW // 64
        self.NHC = self.RW // 128
        self.DL = max(32, int(round(1.8 * self.RW ** 0.5 / 32)) * 32)
        self.AL = max(32, int(round(2.5 * self.RW ** 0.5 / 32)) * 32)
        self.GL = max(32, int(round(0.6 * self.RW ** 0.8 / 32)) * 32)
        self.GN_EPS = 64e-5
        self.RC = 3 * self.RW + self.DL + self.AL + self.GL
        self.DW = D // 2
        self.NDH = self.DW // 128
        self.DCOL = 3 * self.DW
        self.IC = self.RC + self.DCOL + 2 * D
        self.DFF = int(round(8 * D / 3 / 256)) * 256
        self.HALF = SEQ // 2
        self.CTX = SEQ
        self.HALO = 128
        self.OWN0 = self.HALF - self.HALO
        self.NT = min(512, self.HALF)
        self.NOH = self.HALF + self.HALO
        self.o_r, self.o_k, self.o_v = 0, self.RW, 2 * self.RW
        self.o_wl = 3 * self.RW
        self.o_al = self.o_wl + self.DL
        self.o_gl = self.o_al + self.AL
        self.o_q = self.RC
        self.o_dk = self.RC + self.DW
        self.o_dv = self.RC + 2 * self.DW
        self.o_ga = self.RC + self.DCOL
        self.o_gb = self.o_ga + D
        self.lambda_init = 0.8 - 0.6 * math.exp(-0.3 * 0)
        self.colmap = {}
        off = 0
        dc = D // 128
        hc = self.RW // 128
        fc = 2 * self.DFF // 128
        for name, n in [("g_mix", dc), ("g_ffn", dc), ("g_ple", dc),
                        ("mu_r", hc), ("mu_k", hc), ("mu_v", hc), ("mu_w", 1), ("mu_a", 1), ("mu_g", (self.GL + 127) // 128),
                        ("w0", hc), ("a0", hc), ("k_k", hc), ("k_a", hc), ("r_k", hc), ("ln_w", hc), ("ln_b", hc),
                        ("q_g", 1), ("k_g", 1), ("subln", 1),
                        ("cw0", fc), ("cw1", fc), ("cw2", fc), ("cb", fc)]:
            self.colmap[name] = (off, n)
            off += n
        self.NCOLS = off
        self.cm = {"ident": (0, 128), "blk": (128, 128), "ones": (256, 128), "mUs": (384, 64), "mUi": (448, 64),
                   "mLs": (512, 64), "scan": (576, 512), "pvalid": (1088, 128)}
        self.NCONST = 1216

    def tiles_oh(self):
        t = [(self.OWN0, self.HALO)]
        for i in range(self.HALF // self.NT):
            t.append((self.HALF + i * self.NT, self.NT))
        return t


class Ctx:
    pass


def sbt(es, nc, name, shape, dt):
    return Tl(es.enter_context(nc.sbuf_tensor(name, list(shape), dt)), name)


def sbrot(es, nc, name, shape, dt, n):
    return Rot([sbt(es, nc, "%s%d" % (name, i), shape, dt) for i in range(n)])


_PSN = [0]


def psrot(es, nc, n=8):
    _PSN[0] += 1
    return Rot([Tl(es.enter_context(nc.psum_tensor("ps%d_%d" % (_PSN[0], i), [128, 512], F32)), "ps%d" % i) for i in range(n)])


def chunks(total, size):
    return [(i, min(size, total - i)) for i in range(0, total, size)]


def load_cols(K, es, name, names):
    S, C, nc = K.S, K.C, K.nc
    lo = min(C.colmap[x][0] for x in names)
    hi = max(C.colmap[x][0] + C.colmap[x][1] for x in names)
    t = sbt(es, nc, name, [128, hi - lo], F32)
    if hi - lo == 1:
        with nc.allow_non_contiguous_dma(reason="single column"):
            pass
    S.dma("sp", t.h[:, :], K.d["cols"][:, lo:hi], w=[t.b])
    return t, {x: C.colmap[x][0] - lo for x in names}


def load_const(K, es, name, key, rows=128, dt=F32):
    S, C, nc = K.S, K.C, K.nc
    o, n = C.cm[key]
    t = sbt(es, nc, name, [rows, n], dt)
    q = "sp" if dt == F32 else "pool"
    S.dma(q, t.h[:, :], K.d["consts"][0:rows, o:o + n], w=[t.b])
    return t


def build_hT(K, R, src, t0, nt, gt, goff, hT):
    S, C = K.S, K.C
    DC = C.D // 128
    import os
    for s in range(min(nt // 128, int(os.environ.get("KSTOP", "99")))):
        xs = R.xs.get()
        S.dma("sp", xs.h[:, :], src[t0 + s * 128: t0 + (s + 1) * 128, :], w=[xs.b])
        st = R.st.get()
        S.memset("pool", st.h[:, 0:1], 0.0, w=[st.b])
        S.act(R.junk.h[:, :], xs.h[:, :], AF.Square, r=[xs.b], w=[R.junk.b, st.b], accum_out=st.h[:, 0:1])
        S.ts("dve", st.h[:, 1:2], st.h[:, 0:1], 1.0 / C.D, C.EPS, ALU.mult, ALU.add, r=[st.b], w=[st.b])
        S.act(st.h[:, 2:3], st.h[:, 1:2], AF.Ln, r=[st.b], w=[st.b])
        S.act(st.h[:, 3:4], st.h[:, 2:3], AF.Exp, r=[st.b], w=[st.b], scale=-0.5)
        S.ts("dve", xs.h[:, :], xs.h[:, :], st.h[:, 3:4], None, ALU.mult, r=[xs.b, st.b], w=[xs.b])
        import os
        if os.environ.get("SKIPT"):
            continue
        for c0 in range(0, DC, 4):
            ps = K.ps.get()
            n = min(4, DC - c0)
            for j in range(n):
                S.transpose(ps.h[:, j * 128:(j + 1) * 128], xs.h[:, (c0 + j) * 128:(c0 + j + 1) * 128], R.ident.h[:, :],
                            r=[xs.b, R.ident.b], w=[ps.b] if j == 0 else [], wa=[] if j == 0 else [ps.b])
            eng = S.ev()
            for j in range(n):
                c = c0 + j
                o = hT.h[:, c, s * 128:(s + 1) * 128]
                i = ps.h[:, j * 128:(j + 1) * 128]
                g = gt.h[:, goff + c:goff + c + 1]
                if eng == "act":
                    S.act(o, i, AF.Identity, r=[gt.b], w=[ps.b], wa=[hT.b], scale=g)
                else:
                    S.ts("dve", o, i, g, None, ALU.mult, r=[gt.b], w=[ps.b], wa=[hT.b])


def norm_res(K, es, pfx, nxs=3, junk=None):
    R = Ctx()
    nc, C = K.nc, K.C
    R.xs = sbrot(es, nc, pfx + "xs", [128, C.D], F32, nxs)
    R.junk = junk if junk is not None else sbt(es, nc, pfx + "junk", [128, C.D], BF16)
    R.st = sbrot(es, nc, pfx + "st", [128, 4], F32, 4)
    R.ident = load_const(K, es, pfx + "ident", "ident")
    return R


def precast_list(K):
    out = []
    for nm in ("w_a", "w_b", "w_o", "w_fi", "w_fo", "w_pg"):
        src, dst = K.d[nm], K.d["bf_" + nm]
        for g in range(src.shape[0]):
            out.append((dst[g], src[g], K.b["bf_" + nm]))
    return out


def precast_step(K, n=1):
    for _ in range(n):
        if K.pc:
            dst, src, buf = K.pc.pop(0)
            K.S.dma("pool", dst, src, wa=[buf])


def load_w(K, wt, w4, g, c0, cn, width, kstep=8, rbuf=None):
    if rbuf is not None:
        kstep = 32
    S = K.S
    v = w4[g]
    first = True
    for k0 in range(0, cn, kstep):
        kn = min(kstep, cn - k0)
        S.dma("pool", wt.h[:, k0:k0 + kn, 0:width], v[:, c0 + k0:c0 + k0 + kn, 0:width], r=[rbuf] if rbuf is not None else [],
              w=[wt.b] if first else [], wa=[] if first else [wt.b])
        first = False


def z_store(K, col, w, zt, t0, nt):
    S, C = K.S, K.C
    regions = [("zR", 0, C.RC, 0), ("zQ", C.o_q, C.o_dk, C.OWN0), ("zKV", C.o_dk, C.o_ga, 0), ("zG", C.o_ga, C.IC, C.OWN0)]
    for (nm, c0, c1, tk0) in regions:
        a, b = max(col, c0), min(col + w, c1)
        if a >= b:
            continue
        ta = max(t0, tk0)
        if ta >= t0 + nt:
            continue
        S.dma("sp", K.d[nm][a - c0:b - c0, ta - tk0:t0 + nt - tk0], zt.h[a - col:b - col, ta - t0:nt], r=[zt.b], wa=[K.b[nm]])


def phase_A(K):
    S, C, nc = K.S, K.C, K.nc
    DC = C.D // 128
    with contextlib.ExitStack() as es:
        K.ps = psrot(es, nc)
        R = norm_res(K, es, "A")
        gt, gm = load_cols(K, es, "Ag", ["g_mix"])
        hTs = [sbt(es, nc, "AhT%d" % i, [128, DC, C.NT], BF16) for i in range(2)]
        wpool = sbrot(es, nc, "Aw", [128, DC, 512], BF16, 2)
        zpool = sbrot(es, nc, "Az", [128, C.NT], F32, 3)
        ntile = C.CTX // C.NT
        first_full = C.OWN0 // C.NT
        NGA = (C.IC + 511) // 512
        build_hT(K, R, K.d["xc"], 0, C.NT, gt, gm["g_mix"], hTs[0])
        seen = {}
        for tt in range(ntile):
            t0 = tt * C.NT
            hT = hTs[tt % 2]
            if tt >= first_full:
                groups = list(range(NGA))
            else:
                groups = [g for g in range(NGA) if (g * 512 < C.RC) or (g * 512 + 512 > C.o_dk and g * 512 < C.o_ga)]
            for gi_, g in enumerate(groups):
                if gi_ == len(groups) // 2 and tt + 1 < ntile:
                    build_hT(K, R, K.d["xc"], t0 + C.NT, C.NT, gt, gm["g_mix"], hTs[(tt + 1) % 2])
                col0 = g * 512
                gw = min(512, C.IC - col0)
                wt = wpool.get()
                if g not in seen:
                    seen[g] = Buf("bfwin%d" % g)
                    load_w(K, wt, K.d["w_in"], g, 0, DC, 512)
                    if tt + 1 < ntile:
                        S.dma("sp", K.d["bf_w_in"][g], wt.h[:, :, :], r=[wt.b], w=[seen[g]])
                else:
                    load_w(K, wt, K.d["bf_w_in"], g, 0, DC, 512, rbuf=seen[g])
                for (j0, wj) in chunks(gw, 128):
                    ps = K.ps.get()
                    items = [(ps.h[0:wj, 0:C.NT], wt.h[:, k, j0:j0 + wj], hT.h[:, k, 0:C.NT], k == 0, k == DC - 1)
                             for k in range(DC)]
                    S.mms(items, r=[wt.b, hT.b], w=[ps.b])
                    zt = zpool.get()
                    S.copy(S.ev(), zt.h[0:wj, :], ps.h[0:wj, 0:C.NT], w=[zt.b, ps.b])
                    z_store(K, col0 + j0, wj, zt, t0, C.NT)
        K.rem.append(nc.sbuf_bytes_remaining)
        S.emit()


def phase_B(K):
    S, C, nc = K.S, K.C, K.nc
    NHC = C.NHC
    TB = min(256, C.HALF)
    NCH = TB // 64
    HG = min(4, NHC)
    NG = NHC // HG
    GW = HG * 128
    GLC = (C.GL + 127) // 128
    zT = K.d["zR"]
    with contextlib.ExitStack() as es:
        K.ps = psrot(es, nc)
        ident = load_const(K, es, "Bident", "ident")
        blk = load_const(K, es, "Bblk", "blk")
        mUs = load_const(K, es, "BmUs", "mUs", rows=64)
        mUi = load_const(K, es, "BmUi", "mUi", rows=64)
        mLs = load_const(K, es, "BmLs", "mLs", rows=64)
        scanm = load_const(K, es, "Bscan", "scan")
        identb = load_const(K, es, "Bidentb", "ident", dt=BF16)
        names = ["mu_r", "mu_k", "mu_v", "mu_w", "mu_a", "mu_g", "w0", "a0", "k_k", "k_a", "r_k", "ln_w", "ln_b"]
        ct, co = load_cols(K, es, "Bcols", names)
        nmu = 3 * NHC + 2 + GLC
        omu = sbt(es, nc, "Bomu", [128, nmu], F32)
        S.ts("dve", omu.h[:, :], ct.h[:, 0:nmu], -1.0, 1.0, ALU.mult, ALU.add, r=[ct.b], w=[omu.b])
        nw0 = sbt(es, nc, "Bnw0", [128, NHC], F32)
        S.ts("dve", nw0.h[:, :], ct.h[:, co["w0"]:co["w0"] + NHC], -1.0, None, ALU.mult, r=[ct.b], w=[nw0.b])
        na0 = sbt(es, nc, "Bna0", [128, NHC], F32)
        S.ts("dve", na0.h[:, :], ct.h[:, co["a0"]:co["a0"] + NHC], -1.0, None, ALU.mult, r=[ct.b], w=[na0.b])
        omka = sbt(es, nc, "Bomka", [128, NHC], F32)
        S.ts("dve", omka.h[:, :], ct.h[:, co["k_a"]:co["k_a"] + NHC], -1.0, 1.0, ALU.mult, ALU.add, r=[ct.b], w=[omka.b])
        cm05 = sbt(es, nc, "Bcm05", [128, 1], F32)
        S.memset("pool", cm05.h[:, :], -0.5, w=[cm05.b])
        c1 = sbt(es, nc, "Bc1", [128, 1], F32)
        S.memset("pool", c1.h[:, :], 1.0, w=[c1.b])
        lw = sbrot(es, nc, "Blw", [128, 2 + GLC, 128], F32, 2)
        wl = sbt(es, nc, "Bwl", [128, TB + 1], F32)
        al = sbt(es, nc, "Bal", [128, TB + 1], F32)
        gl = sbt(es, nc, "Bgl", [128, GLC, TB + 1], F32)
        tw = sbt(es, nc, "Btw", [128, TB], F32)
        als = sbt(es, nc, "Bals", [128, TB], F32)
        sg = sbt(es, nc, "Bsg", [128, GLC, TB], F32)
        tp = sbrot(es, nc, "Btp", [128, TB + 1], F32, 28)
        Rt, KKt, Bt, Kt, Vt, Bct, Kct = [sbt(es, nc, "B" + n, [128, NHC, TB], BF16) for n in ("Rt", "KKt", "Bt", "Kt", "Vt", "Bct", "Kct")]
        gC = sbt(es, nc, "BgC", [128, NHC, NCH], F32)
        OT = sbt(es, nc, "BOT", [128, NHC, TB], F32)
        H = sbt(es, nc, "BH", [128, NHC, 128], F32)
        Hb = sbt(es, nc, "BHb", [128, NHC, 128], BF16)
        bdp = [[sbt(es, nc, "Bbd%d_%d" % (i, j), [128, NHC, 128], BF16) for j in range(3)] for i in range(1)]
        for i in range(1):
            for j in range(3):
                S.memset("pool", bdp[i][j].h[:, :, :], 0.0, w=[bdp[i][j].b])
        Hbufs = [Buf("H%d" % g) for g in range(NG)]
        S.memset("pool", H.h[:, :, :], 0.0, w=[H.b] + Hbufs)
        S.memset("pool", Hb.h[:, :, :], 0.0, w=[Hb.b], wa=Hbufs)
        SLN = ("A0", "A1", "B0", "B1", "X0", "X1", "nM3", "M2", "M4", "Vm", "Bcm", "Kcm")
        slots = [{n: sbt(es, nc, "Bs%d%s" % (g, n), [64, GW], BF16) for n in SLN} for g in range(NG)]
        fp = sbrot(es, nc, "Bfp", [64, GW], F32, 2)
        oabp = sbrot(es, nc, "Boab", [128, TB], BF16, 2)

        def bc64(t):
            return t.h[0:64, 0:64].unsqueeze(1).broadcast_to([64, 2 * HG, 64])

        def v3(ap):
            return ap.rearrange("p (h t) -> p h t", t=64)

        def load_shift(dst_ap_fn, rows, row0, t0, buf, q="sp"):
            if t0 == 0:
                S.memset("pool", dst_ap_fn(0, 1), 0.0, w=[buf])
                S.dma(q, dst_ap_fn(1, TB + 1), zT[row0:row0 + rows, 0:TB], r=[K.b["zR"]], wa=[buf])
            else:
                S.dma(q, dst_ap_fn(0, TB + 1), zT[row0:row0 + rows, t0 - 1:t0 + TB], r=[K.b["zR"]], w=[buf])

        def lerp(eng, out_ap, zt_prev, zt_cur, mu_ap, omu_ap, rbufs, wbuf):
            t = tp.get()
            n = out_ap.shape[0]
            if eng == "act":
                S.act(t.h[0:n, 0:TB], zt_prev, AF.Identity, r=rbufs, w=[t.b], scale=mu_ap)
            else:
                S.ts(eng, t.h[0:n, 0:TB], zt_prev, mu_ap, None, ALU.mult, r=rbufs, w=[t.b])
            S.stt("dve", out_ap, zt_cur, omu_ap, t.h[0:n, 0:TB], ALU.mult, ALU.add, r=rbufs + [t.b], w=[wbuf])

        ntile = C.CTX // TB
        for tt in range(ntile):
            t0 = tt * TB
            need3 = (t0 + TB > C.OWN0)
            lo = max(C.OWN0 - t0, 0)
            load_shift(lambda a, b: wl.h[0:C.DL, a:b], C.DL, C.o_wl, t0, wl.b)
            load_shift(lambda a, b: al.h[0:C.AL, a:b], C.AL, C.o_al, t0, al.b)
            for c in range(GLC):
                n = min(128, C.GL - c * 128)
                load_shift(lambda a, b, c=c, n=n: gl.h[0:n, c, a:b], n, C.o_gl + c * 128, t0, gl.b)
            cw, ca, cg = co["mu_w"], co["mu_a"], co["mu_g"]
            lerp("dve", tw.h[0:C.DL, :], wl.h[0:C.DL, 0:TB], wl.h[0:C.DL, 1:TB + 1], ct.h[0:C.DL, cw:cw + 1], omu.h[0:C.DL, cw:cw + 1], [wl.b, ct.b, omu.b], tw.b)
            S.act(tw.h[0:C.DL, :], tw.h[0:C.DL, :], AF.Tanh, w=[tw.b])
            lerp("dve", als.h[0:C.AL, :], al.h[0:C.AL, 0:TB], al.h[0:C.AL, 1:TB + 1], ct.h[0:C.AL, ca:ca + 1], omu.h[0:C.AL, ca:ca + 1], [al.b, ct.b, omu.b], als.b)
            for c in range(GLC):
                n = min(128, C.GL - c * 128)
                lerp("dve", sg.h[0:n, c, :], gl.h[0:n, c, 0:TB], gl.h[0:n, c, 1:TB + 1], ct.h[0:n, cg + c:cg + c + 1], omu.h[0:n, cg + c:cg + c + 1], [gl.b, ct.b, omu.b], sg.b)
                S.act(sg.h[0:n, c, :], sg.h[0:n, c, :], AF.Sigmoid, w=[sg.b])
            for hc in range(NHC):
                cs = slice(hc * 128, (hc + 1) * 128)
                col = lambda nm: ct.h[:, co[nm] + hc:co[nm] + hc + 1]
                zs = []
                for (o_, nm) in ((C.o_r, "mu_r"), (C.o_k, "mu_k"), (C.o_v, "mu_v")):
                    zt_ = tp.get()
                    load_shift(lambda a, b, zt_=zt_: zt_.h[:, a:b], 128, o_ + hc * 128, t0, zt_.b)
                    out = tp.get()
                    mo = co[nm] + hc
                    lerp("act", out.h[:, 0:TB], zt_.h[:, 0:TB], zt_.h[:, 1:TB + 1], ct.h[:, mo:mo + 1], omu.h[:, mo:mo + 1], [zt_.b, ct.b, omu.b], out.b)
                    zs.append(out)
                r_s, k_s, v_s = zs
                X = slice(0, TB)
                lwt = lw.get()
                S.dma("sp", lwt.h[0:C.DL, 0, :], K.d["w2"][:, cs], w=[lwt.b])
                S.dma("sp", lwt.h[0:C.AL, 1, :], K.d["a2"][:, cs], wa=[lwt.b])
                for c in range(GLC):
                    n = min(128, C.GL - c * 128)
                    S.dma("sp", lwt.h[0:n, 2 + c, :], K.d["g2"][c * 128:c * 128 + n, cs], wa=[lwt.b])
                ps = K.ps.get()
                S.mms([(ps.h[:, 0:TB], lwt.h[0:C.DL, 0, :], tw.h[0:C.DL, :], True, True)], r=[lwt.b, tw.b], w=[ps.b])
                e1 = tp.get()
                S.act(e1.h[:, X], ps.h[:, 0:TB], AF.Exp, r=[nw0.b], w=[e1.b, ps.b], scale=-1.0, bias=nw0.h[:, hc:hc + 1])
                S.act(e1.h[:, X], e1.h[:, X], AF.Ln, r=[c1.b], w=[e1.b], bias=c1.h[:, 0:1])
                elw = tp.get()
                S.act(elw.h[:, X], e1.h[:, X], AF.Exp, r=[e1.b, cm05.b], w=[elw.b], scale=-1.0, bias=cm05.h[:, 0:1])
                ps = K.ps.get()
                S.mms([(ps.h[:, 0:TB], lwt.h[0:C.AL, 1, :], als.h[0:C.AL, :], True, True)], r=[lwt.b, als.b], w=[ps.b])
                a_ = tp.get()
                S.act(a_.h[:, X], ps.h[:, 0:TB], AF.Exp, r=[na0.b], w=[a_.b, ps.b], scale=-1.0, bias=na0.h[:, hc:hc + 1])
                S.ts("dve", a_.h[:, X], a_.h[:, X], 1.0, None, ALU.add, w=[a_.b])
                S.recip(a_.h[:, X], a_.h[:, X], w=[a_.b])
                if need3:
                    ps = K.ps.get()
                    items = []
                    for c in range(GLC):
                        n = min(128, C.GL - c * 128)
                        items.append((ps.h[:, 0:TB], lwt.h[0:n, 2 + c, :], sg.h[0:n, c, :], c == 0, c == GLC - 1))
                    S.mms(items, r=[lwt.b, sg.b], w=[ps.b])
                    gt_ = tp.get()
                    S.copy("act", gt_.h[:, X], ps.h[:, 0:TB], w=[gt_.b, ps.b])
                    S.dma("sp", K.d["gT"][cs, t0:t0 + TB], gt_.h[:, X], r=[gt_.b], wa=[K.b["gT"]])
                kk = tp.get()
                S.act(kk.h[:, X], k_s.h[:, X], AF.Identity, r=[k_s.b, ct.b], w=[kk.b], scale=col("k_k"))
                kk2 = tp.get()
                S.act(kk2.h[:, X], kk.h[:, X], AF.Square, r=[kk.b], w=[kk2.b])
                ps = K.ps.get()
                S.mms([(ps.h[:, 0:TB], blk.h[:, :], kk2.h[:, X], True, True)], r=[blk.b, kk2.b], w=[ps.b])
                rn = tp.get()
                S.ts("dve", rn.h[:, X], ps.h[:, 0:TB], 1e-24, None, ALU.max, w=[rn.b, ps.b])
                S.act(rn.h[:, X], rn.h[:, X], AF.Ln, w=[rn.b])
                S.act(rn.h[:, X], rn.h[:, X], AF.Exp, w=[rn.b], scale=-0.5)
                kkn = tp.get()
                S.tt("dve", kkn.h[:, X], kk.h[:, X], rn.h[:, X], ALU.mult, r=[kk.b, rn.b], w=[kkn.b])
                t1 = tp.get()
                S.ts("dve", t1.h[:, X], a_.h[:, X], col("k_a"), omka.h[:, hc:hc + 1], ALU.mult, ALU.add, r=[a_.b, ct.b, omka.b], w=[t1.b])
                kmod = tp.get()
                S.tt("pool", kmod.h[:, X], k_s.h[:, X], t1.h[:, X], ALU.mult, r=[k_s.b, t1.b], w=[kmod.b])
                b_ = tp.get()
                S.tt("pool", b_.h[:, X], kkn.h[:, X], a_.h[:, X], ALU.mult, r=[kkn.b, a_.b], w=[b_.b])
                if need3:
                    rkr = tp.get()
                    S.stt("dve", rkr.h[:, X], r_s.h[:, X], col("r_k"), kmod.h[:, X], ALU.mult, ALU.mult, r=[r_s.b, ct.b, kmod.b], w=[rkr.b])
                    ps = K.ps.get()
                    S.mms([(ps.h[:, 0:TB], blk.h[:, :], rkr.h[:, X], True, True)], r=[blk.b, rkr.b], w=[ps.b])
                    bon = tp.get()
                    S.tt("dve", bon.h[:, X], ps.h[:, 0:TB], v_s.h[:, X], ALU.mult, r=[v_s.b], w=[bon.b, ps.b])
                    S.dma("sp", K.d["bonT"][cs, t0:t0 + TB], bon.h[:, X], r=[bon.b], wa=[K.b["bonT"]])
                cum = tp.get()
                S.scan(cum.h[:, X], scanm.h[:, 0:TB], elw.h[:, X], 0.0, ALU.mult, ALU.add, r=[scanm.b, elw.b], w=[cum.b])
                gi = tp.get()
                S.act(gi.h[:, X], cum.h[:, X], AF.Exp, r=[cum.b], w=[gi.b], scale=-1.0)
                ge = tp.get()
                S.act(ge.h[:, X], cum.h[:, X], AF.Exp, r=[cum.b], w=[ge.b])
                gx = tp.get()
                S.tt("pool", gx.h[:, X], cum.h[:, X], elw.h[:, X], ALU.subtract, r=[cum.b, elw.b], w=[gx.b])
                S.act(gx.h[:, X], gx.h[:, X], AF.Exp, w=[gx.b], scale=-1.0)
                S.tt("dve", Rt.h[:, hc, :], r_s.h[:, X], gi.h[:, X], ALU.mult, r=[r_s.b, gi.b], wa=[Rt.b])
                S.tt("pool", KKt.h[:, hc, :], kkn.h[:, X], gx.h[:, X], ALU.mult, r=[kkn.b, gx.b], wa=[KKt.b])
                tb_ = tp.get()
                S.tt("dve", tb_.h[:, X], b_.h[:, X], ge.h[:, X], ALU.mult, r=[b_.b, ge.b], w=[tb_.b])
                tk_ = tp.get()
                S.tt("pool", tk_.h[:, X], kmod.h[:, X], ge.h[:, X], ALU.mult, r=[kmod.b, ge.b], w=[tk_.b])
                S.copy("act", Bt.h[:, hc, :], tb_.h[:, X], r=[tb_.b], wa=[Bt.b])
                S.copy("act", Kt.h[:, hc, :], tk_.h[:, X], r=[tk_.b], wa=[Kt.b])
                S.copy("act", Vt.h[:, hc, :], v_s.h[:, X], r=[v_s.b], wa=[Vt.b])
                S.copy("dve", gC.h[:, hc, :], gi.h[:, X].rearrange("p (c t) -> p c t", t=64)[:, :, 63], r=[gi.b], wa=[gC.b])
                gcb = gC.h[:, hc, :].unsqueeze(2).broadcast_to([128, NCH, 64])
                S.stt("dve", Bct.h[:, hc, :].rearrange("p (c t) -> p c t", t=64), tb_.h[:, X].rearrange("p (c t) -> p c t", t=64), -1.0, gcb,
                      ALU.mult, ALU.mult, r=[tb_.b, gC.b], wa=[Bct.b])
                S.tt("pool", Kct.h[:, hc, :].rearrange("p (c t) -> p c t", t=64), tk_.h[:, X].rearrange("p (c t) -> p c t", t=64), gcb,
                     ALU.mult, r=[tk_.b, gC.b], wa=[Kct.b])
            import os
            for ci in range(NCH if not os.environ.get("BSKIP2") else 0):
                cc = slice(ci * 64, (ci + 1) * 64)
                hd = lambda g, hh: (g * HG + hh // 2, (hh % 2) * 64)
                NHG = 2 * HG
                st = [dict() for _ in range(NG)]
                bd = bdp[0]
                for j, src in enumerate((KKt, Rt, Bt)):
                    eng = ("act", "dve", "act")[j]
                    S.copy(eng, bd[j].h[0:64, :, 0:64], src.h[0:64, :, cc], r=[src.b], w=[bd[j].b])
                    S.copy(eng, bd[j].h[64:128, :, 64:128], src.h[64:128, :, cc], r=[src.b], wa=[bd[j].b])
                KKbd, Rbd, Bbd = bd
                for g in range(NG):
                    d = st[g]
                    def prod(lT, rbd):
                        ps = K.ps.get()
                        items = []
                        for hl in range(HG):
                            hc = g * HG + hl
                            items.append((ps.h[0:64, hl * 128:(hl + 1) * 128], lT.h[:, hc, cc], rbd.h[:, hc, :], True, True))
                        S.mms(items, r=[lT.b, rbd.b], w=[ps.b])
                        return ps
                    ps = prod(Bt, KKbd)
                    sl_ = slots[g]
                    d["A"] = sl_["A0"]
                    S.tt("dve", v3(d["A"].h[:, :]), v3(ps.h[0:64, 0:GW]), bc64(mUs), ALU.mult, r=[mUs.b], w=[d["A"].b, ps.b])
                    bprep = int(os.environ.get("BPREP", "99"))
                    if bprep <= 1:
                        continue
                    d["X"] = sl_["X0"]
                    S.tt("pool", v3(d["X"].h[:, :]), bc64(ident), v3(d["A"].h[:, :]), ALU.subtract, r=[ident.b, d["A"].b], w=[d["X"].b])
                    if bprep <= 2:
                        continue
                    ps = prod(KKt, Bbd)
                    d["Bq"] = sl_["B0"]
                    S.tt("dve", v3(d["Bq"].h[:, :]), v3(ps.h[0:64, 0:GW]), bc64(mLs), ALU.mult, r=[mLs.b], w=[d["Bq"].b, ps.b])
                    ps = prod(Bt, Rbd)
                    d["nM3"] = sl_["nM3"]
                    S.stt("dve", v3(d["nM3"].h[:, :]), v3(ps.h[0:64, 0:GW]), -1.0, bc64(mUi), ALU.mult, ALU.mult, r=[mUi.b], w=[d["nM3"].b, ps.b])
                    ps = prod(Kt, KKbd)
                    d["M2"] = sl_["M2"]
                    S.tt("dve", v3(d["M2"].h[:, :]), v3(ps.h[0:64, 0:GW]), bc64(mUs), ALU.mult, r=[mUs.b], w=[d["M2"].b, ps.b])
                    ps = prod(Kt, Rbd)
                    d["M4"] = sl_["M4"]
                    S.tt("dve", v3(d["M4"].h[:, :]), v3(ps.h[0:64, 0:GW]), bc64(mUi), ALU.mult, r=[mUi.b], w=[d["M4"].b, ps.b])
                    if bprep <= 3:
                        continue
                    for nm, src in (("Vm", Vt), ("Bcm", Bct), ("Kcm", Kct)):
                        ps = K.ps.get()
                        items = []
                        for hl in range(HG):
                            hc = g * HG + hl
                            items.append((ps.h[0:64, hl * 128:(hl + 1) * 128], src.h[:, hc, cc], identb.h[:, :], True, True))
                        S.mms(items, r=[src.b, identb.b], w=[ps.b])
                        d[nm] = sl_[nm]
                        S.copy("act", d[nm].h[:, :], ps.h[0:64, 0:GW], w=[d[nm].b, ps.b])
                bstop = int(os.environ.get("BSTOP", "99"))
                if bstop <= 1:
                    continue
                for lvl in range(1, 6):
                    for g in range(NG):
                        d = st[g]
                        def hmm(l, r_):
                            ps = K.ps.get()
                            items = [(ps.h[0:64, hh * 64:(hh + 1) * 64], l.h[0:64, hh * 64:(hh + 1) * 64], r_.h[0:64, hh * 64:(hh + 1) * 64], True, True)
                                     for hh in range(NHG)]
                            S.mms(items, r=[l.b, r_.b], w=[ps.b])
                            return ps
                        psB = hmm(d["A"], d["Bq"])
                        nB = slots[g]["B%d" % (lvl % 2)]
                        S.copy("act", nB.h[:, :], psB.h[0:64, 0:GW], w=[nB.b, psB.b])
                        if lvl < 5:
                            psA = hmm(d["Bq"], d["A"])
                            nA = slots[g]["A%d" % (lvl % 2)]
                            S.copy("dve", nA.h[:, :], psA.h[0:64, 0:GW], w=[nA.b, psA.b])
                            d["A"] = nA
                        d["Bq"] = nB
                    for g in range(NG):
                        d = st[g]
                        ps = K.ps.get()
                        items = [(ps.h[0:64, hh * 64:(hh + 1) * 64], d["Bq"].h[0:64, hh * 64:(hh + 1) * 64], d["X"].h[0:64, hh * 64:(hh + 1) * 64], True, True)
                                 for hh in range(NHG)]
                        S.mms(items, r=[d["Bq"].b, d["X"].b], w=[ps.b])
                        nX = slots[g]["X%d" % (lvl % 2)]
                        S.tt("dve", nX.h[:, :], ps.h[0:64, 0:GW], d["X"].h[:, :], ALU.add, r=[d["X"].b], w=[nX.b, ps.b])
                        d["X"] = nX
                if bstop <= 2:
                    continue
                for g in range(NG):
                    d = st[g]
                    ps = K.ps.get()
                    items = []
                    for hl in range(HG):
                        hc = g * HG + hl
                        items.append((ps.h[0:64, hl * 128:(hl + 1) * 128], KKt.h[:, hc, cc], Hb.h[:, hc, :], True, False))
                        for e2 in range(2):
                            sl = slice(hl * 128 + e2 * 64, hl * 128 + e2 * 64 + 64)
                            items.append((ps.h[0:64, sl], d["M2"].h[0:64, sl], d["Vm"].h[0:64, sl], False, e2 == 1))
                    S.mms(items, r=[KKt.b, Hbufs[g], d["M2"].b, d["Vm"].b], w=[ps.b])
                    d["W"] = slots[g]["A1"]
                    S.copy("act", d["W"].h[:, :], ps.h[0:64, 0:GW], w=[d["W"].b, ps.b])
                for g in range(NG):
                    d = st[g]
                    ps = K.ps.get()
                    items = [(ps.h[0:64, hh * 64:(hh + 1) * 64], d["X"].h[0:64, hh * 64:(hh + 1) * 64], d["W"].h[0:64, hh * 64:(hh + 1) * 64], True, True)
                             for hh in range(NHG)]
                    S.mms(items, r=[d["X"].b, d["W"].b], w=[ps.b])
                    d["U"] = slots[g]["B0"]
                    S.copy("dve", d["U"].h[:, :], ps.h[0:64, 0:GW], w=[d["U"].b, ps.b])
                if bstop <= 3:
                    continue
                for g in range(NG):
                    d = st[g]
                    ps = K.ps.get()
                    items = []
                    for hl in range(HG):
                        hc = g * HG + hl
                        items.append((ps.h[0:64, hl * 128:(hl + 1) * 128], Rt.h[:, hc, cc], Hb.h[:, hc, :], True, False))
                        for e2 in range(2):
                            sl = slice(hl * 128 + e2 * 64, hl * 128 + e2 * 64 + 64)
                            items.append((ps.h[0:64, sl], d["nM3"].h[0:64, sl], d["U"].h[0:64, sl], False, False))
                            items.append((ps.h[0:64, sl], d["M4"].h[0:64, sl], d["Vm"].h[0:64, sl], False, e2 == 1))
                    S.mms(items, r=[Rt.b, Hbufs[g], d["nM3"].b, d["U"].b, d["M4"].b, d["Vm"].b], w=[ps.b])
                    if need3:
                        otm = fp.get()
                        S.copy("act", otm.h[:, :], ps.h[0:64, 0:GW], w=[otm.b, ps.b])
                        ps2 = K.ps.get()
                        for hl in range(HG):
                            S.transpose(ps2.h[:, hl * 64:(hl + 1) * 64], otm.h[0:64, hl * 128:(hl + 1) * 128], ident.h[0:64, 0:64],
                                        r=[otm.b, ident.b], w=[ps2.b] if hl == 0 else [], wa=[] if hl == 0 else [ps2.b])
                        S.copy("dve", OT.h[:, g * HG:(g + 1) * HG, cc], ps2.h[:, 0:HG * 64].rearrange("p (h t) -> p h t", t=64), w=[ps2.b], wa=[OT.b])
                    ps = K.ps.get()
                    items = []
                    for hl in range(HG):
                        sl = slice(hl * 128, (hl + 1) * 128)
                        o = ps.h[:, sl]
                        items.append((o, d["Bcm"].h[0:64, sl], d["U"].h[0:64, sl], True, False))
                        items.append((o, d["Kcm"].h[0:64, sl], d["Vm"].h[0:64, sl], False, True))
                    S.mms(items, r=[d["Bcm"].b, d["U"].b, d["Kcm"].b, d["Vm"].b], w=[ps.b])
                    for hh in range(NHG):
                        hc, pb = hd(g, hh)
                        hl = hh // 2
                        S.stt("dve", H.h[pb:pb + 64, hc, pb:pb + 64], H.h[pb:pb + 64, hc, pb:pb + 64], gC.h[pb:pb + 64, hc, ci:ci + 1],
                              ps.h[pb:pb + 64, hl * 128 + pb: hl * 128 + pb + 64], ALU.mult, ALU.add,
                              r=[gC.b], w=[Hbufs[g], ps.b])
                    S.copy("act", Hb.h[:, g * HG:(g + 1) * HG, :], H.h[:, g * HG:(g + 1) * HG, :], w=[Hbufs[g]])
            if need3 and not os.environ.get("BSKIP3"):
                for hc in range(NHC):
                    cs = slice(hc * 128, (hc + 1) * 128)
                    col = lambda nm: ct.h[:, co[nm] + hc:co[nm] + hc + 1]
                    X = slice(0, TB)
                    gt_ = tp.get()
                    S.dma("sp", gt_.h[:, X], K.d["gT"][cs, t0:t0 + TB], r=[K.b["gT"]], w=[gt_.b])
                    bon = tp.get()
                    S.dma("sp", bon.h[:, X], K.d["bonT"][cs, t0:t0 + TB], r=[K.b["bonT"]], w=[bon.b])
                    ps = K.ps.get()
                    S.mms([(ps.h[:, 0:TB], blk.h[:, :], OT.h[:, hc, :], True, True)], r=[blk.b, OT.b], w=[ps.b])
                    cen = tp.get()
                    S.stt("dve", cen.h[:, X], ps.h[:, 0:TB], -1.0 / 64, OT.h[:, hc, :], ALU.mult, ALU.add, r=[OT.b], w=[cen.b, ps.b])
                    sq = tp.get()
                    S.act(sq.h[:, X], cen.h[:, X], AF.Square, r=[cen.b], w=[sq.b])
                    ps = K.ps.get()
                    S.mms([(ps.h[:, 0:TB], blk.h[:, :], sq.h[:, X], True, True)], r=[blk.b, sq.b], w=[ps.b])
                    rs = tp.get()
                    S.ts("dve", rs.h[:, X], ps.h[:, 0:TB], 1.0 / 64, C.GN_EPS, ALU.mult, ALU.add, w=[rs.b, ps.b])
                    S.act(rs.h[:, X], rs.h[:, X], AF.Ln, w=[rs.b])
                    S.act(rs.h[:, X], rs.h[:, X], AF.Exp, w=[rs.b], scale=-0.5)
                    y = tp.get()
                    S.tt("pool", y.h[:, X], cen.h[:, X], rs.h[:, X], ALU.mult, r=[cen.b, rs.b], w=[y.b])
                    S.ts("pool", y.h[:, X], y.h[:, X], col("ln_w"), col("ln_b"), ALU.mult, ALU.add, r=[ct.b], w=[y.b])
                    S.tt("pool", y.h[:, X], y.h[:, X], bon.h[:, X], ALU.add, r=[bon.b], w=[y.b])
                    S.tt("dve", y.h[:, X], y.h[:, X], gt_.h[:, X], ALU.mult, r=[gt_.b], w=[y.b])
                    obf = oabp.get()
                    S.copy("act", obf.h[:, :], y.h[:, X], r=[y.b], w=[obf.b])
                    S.dma("sp", K.d["oaT"][cs, t0 + lo - C.OWN0:t0 + TB - C.OWN0], obf.h[:, lo:TB], r=[obf.b], wa=[K.b["oaT"]])
        K.rem.append(nc.sbuf_bytes_remaining)
        S.emit()


def phase_C(K):
    S, C, nc = K.S, K.C, K.nc
    NKB = C.CTX // 128
    with contextlib.ExitStack() as es:
        allps = psrot(es, nc)
        K.ps = Rot(allps.t[0:4])
        acc = Rot(allps.t[4:8])
        ident = load_const(K, es, "Cident", "ident")
        blk = load_const(K, es, "Cblk", "blk")
        ones = load_const(K, es, "Cones", "ones")
        onesb = load_const(K, es, "Conesb", "ones", dt=BF16)
        pvb = load_const(K, es, "Cpvb", "pvalid", dt=BF16)
        ct, co = load_cols(K, es, "Ccols", ["q_g", "k_g", "subln"])
        lt = sbt(es, nc, "Clamb", [128, 256], F32)
        S.dma("sp", lt.h[:, :], K.d["lamb"][:, :], w=[lt.b])
        sc = sbt(es, nc, "Csc", [128, 12], F32)
        S.memset("dve", sc.h[:, :], 0.0, w=[sc.b])
        tmp = sbt(es, nc, "Cltmp", [128, 128], F32)
        S.tt("dve", tmp.h[:, 0:64], lt.h[:, 0:64], lt.h[:, 64:128], ALU.mult, r=[lt.b], w=[tmp.b])
        S.tt("dve", tmp.h[:, 64:128], lt.h[:, 128:192], lt.h[:, 192:256], ALU.mult, r=[lt.b], w=[tmp.b])
        S.act(tmp.h[:, 0:64], tmp.h[:, 0:64], AF.Identity, w=[tmp.b, sc.b], accum_out=sc.h[:, 0:1])
        S.act(tmp.h[:, 64:128], tmp.h[:, 64:128], AF.Identity, w=[tmp.b, sc.b], accum_out=sc.h[:, 1:2])
        S.act(sc.h[:, 2:4], sc.h[:, 0:2], AF.Exp, w=[sc.b])
        S.tt("dve", sc.h[:, 4:5], sc.h[:, 3:4], sc.h[:, 2:3], ALU.subtract, w=[sc.b])
        S.ts("dve", sc.h[:, 5:6], sc.h[:, 4:5], -C.lambda_init, None, ALU.add, w=[sc.b])
        neglam = sc.h[:, 5:6]
        S.ts("dve", sc.h[:, 6:7], ct.h[:, co["q_g"]:co["q_g"] + 1], 0.125, None, ALU.mult, r=[ct.b], w=[sc.b])
        S.ts("dve", sc.h[:, 7:8], ct.h[:, co["subln"]:co["subln"] + 1], 1.0 - C.lambda_init, None, ALU.mult, r=[ct.b], w=[sc.b])
        qg8, slg = sc.h[:, 6:7], sc.h[:, 7:8]
        kg = ct.h[:, co["k_g"]:co["k_g"] + 1]
        zq = sbrot(es, nc, "Czq", [128, C.NOH], F32, 2)
        zk = sbrot(es, nc, "Czk", [128, C.CTX], F32, 2)
        zv = sbrot(es, nc, "Czv", [128, C.CTX], F32, 2)
        qns = [sbt(es, nc, "Cqn%d" % i, [128, C.NOH], BF16) for i in range(2)]
        kns = [[sbt(es, nc, "Ckn%d_%d" % (i, c), [128, C.CTX], BF16) for c in range(2)] for i in range(2)]
        for i in range(2):
            for c in range(2):
                S.memset("dve", kns[i][c].h[:, :], 0.0, w=[kns[i][c].b])
        Vtms = [sbt(es, nc, "CVtm%d" % i, [128, NKB, 128], BF16) for i in range(2)]
        tq = sbrot(es, nc, "Ctq", [128, 512], F32, 6)
        ptp = sbrot(es, nc, "Cpt", [128, 512], BF16, 4)
        ocp = sbrot(es, nc, "Coc", [128, 512], F32, 4)
        obp = sbrot(es, nc, "Cob", [128, 512], BF16, 2)

        def rmsn(src, n_tot, gcol, outs):
            for (c0, cn) in chunks(n_tot, 512):
                sq = tq.get()
                S.act(sq.h[:, 0:cn], src.h[:, c0:c0 + cn], AF.Square, r=[src.b], w=[sq.b])
                ps = K.ps.get()
                S.mms([(ps.h[:, 0:cn], blk.h[:, :], sq.h[:, 0:cn], True, True)], r=[blk.b, sq.b], w=[ps.b])
                rs = tq.get()
                S.ts("dve", rs.h[:, 0:cn], ps.h[:, 0:cn], 1.0 / 64, C.EPS, ALU.mult, ALU.add, w=[rs.b, ps.b])
                S.act(rs.h[:, 0:cn], rs.h[:, 0:cn], AF.Ln, w=[rs.b])
                S.act(rs.h[:, 0:cn], rs.h[:, 0:cn], AF.Exp, w=[rs.b], scale=-0.5)
                for (r0, r1, dst) in outs:
                    S.stt("dve", dst.h[r0:r1, c0:c0 + cn], src.h[r0:r1, c0:c0 + cn], gcol[r0:r1, :], rs.h[r0:r1, 0:cn], ALU.mult, ALU.mult,
                          r=[src.b, rs.b, sc.b, ct.b], wa=[dst.b])

        precast_step(K, len(K.pc))
        def prep_head(h):
            qn, kn, Vtm = qns[h % 2], kns[h % 2], Vtms[h % 2]
            q_ = zq.get()
            S.dma("sp", q_.h[:, :], K.d["zQ"][h * 128:(h + 1) * 128, 0:C.NOH], r=[K.b["zQ"]], w=[q_.b])
            k_ = zk.get()
            S.dma("sp", k_.h[:, :], K.d["zKV"][h * 128:(h + 1) * 128, 0:C.CTX], r=[K.b["zKV"]], w=[k_.b])
            v_ = zv.get()
            S.dma("sp", v_.h[:, :], K.d["zKV"][C.DW + h * 128:C.DW + (h + 1) * 128, 0:C.CTX], r=[K.b["zKV"]], w=[v_.b])
            S.memset("dve", qn.h[:, 0:1], 0.0, w=[qn.b])
            S.memset("dve", kn[0].h[0:64, 0:1], 0.0, w=[kn[0].b])
            S.memset("dve", kn[1].h[64:128, 0:1], 0.0, w=[kn[1].b])
            rmsn(q_, C.NOH, qg8, [(0, 128, qn)])
            rmsn(k_, C.CTX, kg, [(0, 64, kn[0]), (64, 128, kn[1])])
            first = True
            for kb0 in range(0, NKB, 4):
                ps = K.ps.get()
                n = min(4, NKB - kb0)
                for j in range(n):
                    S.transpose(ps.h[:, j * 128:(j + 1) * 128], v_.h[:, (kb0 + j) * 128:(kb0 + j + 1) * 128], ident.h[:, :],
                                r=[v_.b, ident.b], w=[ps.b] if j == 0 else [], wa=[] if j == 0 else [ps.b])
                S.copy(S.ev(), Vtm.h[:, kb0:kb0 + n, :], ps.h[:, 0:n * 128].rearrange("p (k e) -> p k e", e=128),
                       w=[ps.b] + ([Vtm.b] if first else []), wa=[] if first else [Vtm.b])
                first = False

        precast_step(K, len(K.pc))
        prep_head(0)
        for h in range(C.NDH):
            rows = slice(h * 128, (h + 1) * 128)
            qn, kn, Vtm = qns[h % 2], kns[h % 2], Vtms[h % 2]
            for qi_, (q0, NQ) in enumerate(C.tiles_oh()):
                if qi_ == min(2, len(C.tiles_oh()) - 1) and h + 1 < C.NDH:
                    prep_head(h + 1)
                jq = q0 - C.OWN0
                nkb = (q0 + NQ) // 128
                ocs = []
                for c in range(2):
                    psO = acc.get()
                    psL = acc.get()

                    def smm(kb):
                        j0 = max(kb * 128 - q0, 0)
                        n = NQ - j0
                        ps = K.ps.get()
                        S.mms([(ps.h[:, 0:n], kn[c].h[:, kb * 128:(kb + 1) * 128], qn.h[:, jq + j0:jq + NQ], True, True)],
                              r=[kn[c].b, qn.b], w=[ps.b])
                        return ps, j0, n
                    pend = [smm(kb_) for kb_ in range(min(2, nkb))]
                    for kb in range(nkb):
                        ps, j0, n = pend.pop(0)
                        if kb + 2 < nkb:
                            pend.append(smm(kb + 2))
                        pt = ptp.get()
                        S.act(pt.h[:, 0:n], ps.h[:, 0:n], AF.Exp, w=[pt.b, ps.b])
                        if kb * 128 >= q0:
                            S.memset("dve", pt.h[64:128, 0:64], 0.0, w=[pt.b])
                        lv = pvb if kb * 128 < C.HALF else onesb
                        S.mms([(psO.h[:, j0:NQ], Vtm.h[:, kb, :], pt.h[:, 0:n], kb == 0, kb == nkb - 1)], r=[Vtm.b, pt.b],
                              w=[psO.b] if kb == 0 else [], wa=[] if kb == 0 else [psO.b])
                        S.mms([(psL.h[:, j0:NQ], lv.h[:, :], pt.h[:, 0:n], kb == 0, kb == nkb - 1)], r=[lv.b, pt.b],
                              w=[psL.b] if kb == 0 else [], wa=[] if kb == 0 else [psL.b])
                    rl = tq.get()
                    S.ts("dve", rl.h[:, 0:NQ], psL.h[:, 0:NQ], 1e-30, None, ALU.max, w=[rl.b, psL.b])
                    S.recip(rl.h[:, 0:NQ], rl.h[:, 0:NQ], w=[rl.b])
                    oc = ocp.get()
                    S.tt("dve", oc.h[:, 0:NQ], psO.h[:, 0:NQ], rl.h[:, 0:NQ], ALU.mult, r=[rl.b], w=[oc.b, psO.b])
                    ocs.append(oc)
                df = tq.get()
                S.stt("dve", df.h[:, 0:NQ], ocs[1].h[:, 0:NQ], neglam, ocs[0].h[:, 0:NQ], ALU.mult, ALU.add, r=[ocs[0].b, ocs[1].b, sc.b], w=[df.b])
                sq = tq.get()
                S.act(sq.h[:, 0:NQ], df.h[:, 0:NQ], AF.Square, r=[df.b], w=[sq.b])
                ps = K.ps.get()
                S.mms([(ps.h[:, 0:NQ], ones.h[:, :], sq.h[:, 0:NQ], True, True)], r=[ones.b, sq.b], w=[ps.b])
                rs = tq.get()
                S.ts("dve", rs.h[:, 0:NQ], ps.h[:, 0:NQ], 1.0 / 128, C.EPS, ALU.mult, ALU.add, w=[rs.b, ps.b])
                S.act(rs.h[:, 0:NQ], rs.h[:, 0:NQ], AF.Ln, w=[rs.b])
                S.act(rs.h[:, 0:NQ], rs.h[:, 0:NQ], AF.Exp, w=[rs.b], scale=-0.5)
                ob = obp.get()
                S.stt("dve", ob.h[:, 0:NQ], df.h[:, 0:NQ], slg, rs.h[:, 0:NQ], ALU.mult, ALU.mult, r=[df.b, rs.b, sc.b], w=[ob.b])
                S.dma("sp", K.d["obT"][rows, jq:jq + NQ], ob.h[:, 0:NQ], r=[ob.b], wa=[K.b["obT"]])
        K.rem.append(nc.sbuf_bytes_remaining)
        S.emit()


def phase_DE(K):
    S, C, nc = K.S, K.C, K.nc
    DC = C.D // 128
    RCH = C.RW // 128
    with contextlib.ExitStack() as es:
        K.ps = psrot(es, nc)
        oat = sbt(es, nc, "Doat", [128, RCH, C.NT], BF16)
        obt = sbt(es, nc, "Dobt", [128, RCH, C.NT], BF16)
        mT = sbt(es, nc, "DmT", [128, DC, C.NT], BF16)
        wap = sbrot(es, nc, "Dwa", [128, RCH, 256], BF16, 2)
        wbp = sbrot(es, nc, "Dwb", [128, RCH, 256], BF16, 2)
        wop = sbrot(es, nc, "Dwo", [128, DC, 512], BF16, 2)
        gp = sbrot(es, nc, "Dg", [128, C.NT], F32, 4)
        mp_ = sbrot(es, nc, "Dm", [128, C.NT], F32, 4)
        xp = sbrot(es, nc, "Dx", [128, 512], F32, 4)
        oav = K.d["oaT"].rearrange("(c p) t -> p c t", p=128)
        obv = K.d["obT"].rearrange("(c p) t -> p c t", p=128)
        for (t0, nt) in C.tiles_oh():
            jq = t0 - C.OWN0
            S.dma("sp", oat.h[:, :, 0:nt], oav[:, :, jq:jq + nt], r=[K.b["oaT"]], w=[oat.b])
            S.dma("sp", obt.h[:, :, 0:nt], obv[:, :, jq:jq + nt], r=[K.b["obT"]], w=[obt.b])
            firstm = True
            for (g0, gw) in chunks(C.D, 256):
                wa_ = wap.get()
                load_w(K, wa_, K.d["bf_w_a"], g0 // 256, 0, RCH, gw, rbuf=K.b["bf_w_a"])
                wb_ = wbp.get()
                load_w(K, wb_, K.d["bf_w_b"], g0 // 256, 0, RCH, gw, rbuf=K.b["bf_w_b"])
                for (j0, wj) in chunks(gw, 128):
                    col = g0 + j0
                    psA = K.ps.get()
                    S.mms([(psA.h[0:wj, 0:nt], wa_.h[:, k, j0:j0 + wj], oat.h[:, k, 0:nt], k == 0, k == RCH - 1) for k in range(RCH)],
                          r=[wa_.b, oat.b], w=[psA.b])
                    psB = K.ps.get()
                    S.mms([(psB.h[0:wj, 0:nt], wb_.h[:, k, j0:j0 + wj], obt.h[:, k, 0:nt], k == 0, k == RCH - 1) for k in range(RCH)],
                          r=[wb_.b, obt.b], w=[psB.b])
                    ga = gp.get()
                    S.dma("sp", ga.h[:, 0:nt], K.d["zG"][col:col + 128, jq:jq + nt], r=[K.b["zG"]], w=[ga.b])
                    gb = gp.get()
                    S.dma("sp", gb.h[:, 0:nt], K.d["zG"][C.D + col:C.D + col + 128, jq:jq + nt], r=[K.b["zG"]], w=[gb.b])
                    S.act(ga.h[:, 0:nt], ga.h[:, 0:nt], AF.Sigmoid, w=[ga.b])
                    S.act(gb.h[:, 0:nt], gb.h[:, 0:nt], AF.Sigmoid, w=[gb.b])
                    m1 = mp_.get()
                    S.tt("dve", m1.h[:, 0:nt], psA.h[:, 0:nt], ga.h[:, 0:nt], ALU.mult, r=[ga.b], w=[m1.b, psA.b])
                    m2 = mp_.get()
                    S.tt("dve", m2.h[:, 0:nt], psB.h[:, 0:nt], gb.h[:, 0:nt], ALU.mult, r=[gb.b], w=[m2.b, psB.b])
                    S.tt("pool", mT.h[:, col // 128, 0:nt], m1.h[:, 0:nt], m2.h[:, 0:nt], ALU.add, r=[m1.b, m2.b],
                         w=[mT.b] if firstm else [], wa=[] if firstm else [mT.b])
                    firstm = False
            for (g0, gw) in chunks(C.D, 512):
                wo_ = wop.get()
                load_w(K, wo_, K.d["bf_w_o"], g0 // 512, 0, DC, gw, rbuf=K.b["bf_w_o"])
                for s_ in range(nt // 128):
                    ps = K.ps.get()
                    S.mms([(ps.h[:, 0:gw], mT.h[:, k, s_ * 128:(s_ + 1) * 128], wo_.h[:, k, 0:gw], k == 0, k == DC - 1) for k in range(DC)],
                          r=[mT.b, wo_.b], w=[ps.b])
                    xt = xp.get()
                    S.dma("sp", xt.h[:, 0:gw], K.d["xc"][t0 + s_ * 128:t0 + (s_ + 1) * 128, g0:g0 + gw], w=[xt.b])
                    S.tt("dve", xt.h[:, 0:gw], ps.h[:, 0:gw], xt.h[:, 0:gw], ALU.add, w=[xt.b, ps.b])
                    S.dma("sp", K.d["x1"][jq + s_ * 128:jq + (s_ + 1) * 128, g0:g0 + gw], xt.h[:, 0:gw], r=[xt.b], wa=[K.b["x1"]])
        K.rem.append(nc.sbuf_bytes_remaining)
        S.emit()


def phase_F(K):
    S, C, nc = K.S, K.C, K.nc
    DC = C.D // 128
    FB = C.DFF // 128
    KH = FB // 2
    SEG = chunks(KH, 22)
    with contextlib.ExitStack() as es:
        allps = psrot(es, nc)
        K.ps = Rot(allps.t[0:4])
        acc = allps.t[4:8]
        aT = sbt(es, nc, "FaT", [128, KH, C.NT], BF16)
        junk = Ctx()
        junk.h = aT.h[:, 0:C.D // C.NT, :].rearrange("p a b -> p (a b)")
        junk.b = aT.b
        R = norm_res(K, es, "F", nxs=2, junk=junk)
        ct, co = load_cols(K, es, "Fcols", ["g_ffn", "cw0", "cw1", "cw2", "cb"])
        pv = load_const(K, es, "Fpv", "pvalid")
        hT = sbt(es, nc, "FhT", [128, DC, C.NT], BF16)
        wgp = sbrot(es, nc, "Fwg", [128, DC, 256], BF16, 2)
        wfp = sbrot(es, nc, "Fwf", [128, max(n for _, n in SEG), 256], BF16, 2)
        halo = sbt(es, nc, "Fhalo", [128, 2 * FB, 2], F32)
        S.memset("pool", halo.h[:, :, :], 0.0, w=[halo.b])
        ubp = sbrot(es, nc, "Fub", [128, C.NT + 2], F32, 4)
        cvp = sbrot(es, nc, "Fcv", [128, C.NT], F32, 4)
        xp = sbrot(es, nc, "Fx", [128, 256], F32, 4)
        for ti, (t0, nt) in enumerate(C.tiles_oh()):
            jq = t0 - C.OWN0
            is_halo = (ti == 0)
            row0 = t0 - C.HALF
            build_hT(K, R, K.d["x1"], jq, nt, ct, co["g_ffn"], hT)
            for half2 in range(2):
                firsta = True
                for jb in range(half2 * KH, (half2 + 1) * KH):
                    wg_ = wgp.get()
                    load_w(K, wg_, K.d["bf_w_fi"], jb, 0, DC, 256, rbuf=K.b["bf_w_fi"])
                    cvs = []
                    for half in range(2):
                        blk_i = jb + half * FB
                        ps = K.ps.get()
                        S.mms([(ps.h[:, 0:nt], wg_.h[:, k, half * 128:(half + 1) * 128], hT.h[:, k, 0:nt], k == 0, k == DC - 1) for k in range(DC)],
                              r=[wg_.b, hT.b], w=[ps.b])
                        ub = ubp.get()
                        S.copy("act", ub.h[:, 2:nt + 2], ps.h[:, 0:nt], w=[ub.b, ps.b])
                        S.copy("dve", ub.h[:, 0:2], halo.h[:, blk_i, :], r=[halo.b], wa=[ub.b])
                        if is_halo:
                            S.ts("dve", halo.h[:, blk_i, :], ub.h[:, nt:nt + 2], pv.h[:, 0:1], None, ALU.mult, r=[ub.b, pv.b], w=[halo.b])
                            continue
                        S.copy("dve", halo.h[:, blk_i, :], ub.h[:, nt:nt + 2], r=[ub.b], w=[halo.b])
                        cv = cvp.get()
                        cc = lambda nm: ct.h[:, co[nm] + blk_i:co[nm] + blk_i + 1]
                        S.act(cv.h[:, 0:nt], ub.h[:, 2:nt + 2], AF.Identity, r=[ub.b, ct.b], w=[cv.b], scale=cc("cw2"), bias=cc("cb"))
                        S.stt("dve", cv.h[:, 0:nt], ub.h[:, 1:nt + 1], cc("cw1"), cv.h[:, 0:nt], ALU.mult, ALU.add, r=[ub.b, ct.b], w=[cv.b])
                        S.stt("dve", cv.h[:, 0:nt], ub.h[:, 0:nt], cc("cw0"), cv.h[:, 0:nt], ALU.mult, ALU.add, r=[ub.b, ct.b], w=[cv.b])
                        cvs.append(cv)
                    if is_halo:
                        continue
                    sgt = cvp.get()
                    S.act(sgt.h[:, 0:nt], cvs[0].h[:, 0:nt], AF.Silu, r=[cvs[0].b], w=[sgt.b])
                    S.tt("dve", aT.h[:, jb - half2 * KH, 0:nt], sgt.h[:, 0:nt], cvs[1].h[:, 0:nt], ALU.mult, r=[sgt.b, cvs[1].b],
                         w=[aT.b] if firsta else [], wa=[] if firsta else [aT.b])
                    firsta = False
                if is_halo:
                    continue
                nsub = nt // 128
                for (g0, gw) in chunks(C.D, 256):
                    for si, (k0, kn_) in enumerate(SEG):
                        wf_ = wfp.get()
                        load_w(K, wf_, K.d["bf_w_fo"], g0 // 256, half2 * KH + k0, kn_, gw, rbuf=K.b["bf_w_fo"])
                        for s_ in range(nsub):
                            ps = acc[s_]
                            items = [(ps.h[:, 0:gw], aT.h[:, k0 + k, s_ * 128:(s_ + 1) * 128], wf_.h[:, k, 0:gw],
                                      si == 0 and k == 0, si == len(SEG) - 1 and k == kn_ - 1) for k in range(kn_)]
                            S.mms(items, r=[aT.b, wf_.b], w=[ps.b] if si == 0 else [], wa=[] if si == 0 else [ps.b])
                    for s_ in range(nsub):
                        ps = acc[s_]
                        xt = xp.get()
                        if half2 == 0:
                            S.dma("sp", xt.h[:, 0:gw], K.d["x1"][jq + s_ * 128:jq + (s_ + 1) * 128, g0:g0 + gw], r=[K.b["x1"]], w=[xt.b])
                        else:
                            S.dma("sp", xt.h[:, 0:gw], K.d["x2"][row0 + s_ * 128:row0 + (s_ + 1) * 128, g0:g0 + gw], r=[K.b["x2"]], w=[xt.b])
                        S.tt("dve", xt.h[:, 0:gw], ps.h[:, 0:gw], xt.h[:, 0:gw], ALU.add, w=[xt.b, ps.b])
                        S.dma("sp", K.d["x2"][row0 + s_ * 128:row0 + (s_ + 1) * 128, g0:g0 + gw], xt.h[:, 0:gw], r=[xt.b], wa=[K.b["x2"]])
        K.rem.append(nc.sbuf_bytes_remaining)
        S.emit()


def phase_G(K):
    S, C, nc = K.S, K.C, K.nc
    DC = C.D // 128
    PC = C.PLE // 128
    with contextlib.ExitStack() as es:
        K.ps = psrot(es, nc)
        R = norm_res(K, es, "G")
        ct, co = load_cols(K, es, "Gcols", ["g_ple"])
        hT = sbt(es, nc, "GhT", [128, DC, C.NT], BF16)
        pT = sbt(es, nc, "GpT", [128, PC, C.NT], BF16)
        pp = sbrot(es, nc, "Gp", [128, C.PLE], F32, 2)
        wgp = sbrot(es, nc, "Gwg", [128, DC, 512], BF16, 2)
        wpp = sbrot(es, nc, "Gwp", [128, PC, 512], BF16, 2)
        sgp = sbrot(es, nc, "Gsg", [128, 512], F32, 3)
        xp = sbrot(es, nc, "Gx", [128, 512], F32, 3)
        for (t0, nt) in C.tiles_oh()[1:]:
            row0 = t0 - C.HALF
            build_hT(K, R, K.d["x2"], row0, nt, ct, co["g_ple"], hT)
            firstp = True
            for s_ in range(nt // 128):
                pt = pp.get()
                S.dma("sp", pt.h[:, :], K.d["pc"][row0 + s_ * 128:row0 + (s_ + 1) * 128, :], w=[pt.b])
                ps = K.ps.get()
                for c in range(PC):
                    S.transpose(ps.h[:, c * 128:(c + 1) * 128], pt.h[:, c * 128:(c + 1) * 128], R.ident.h[:, :], r=[pt.b, R.ident.b],
                                w=[ps.b] if c == 0 else [], wa=[] if c == 0 else [ps.b])
                S.copy("act", pT.h[:, :, s_ * 128:(s_ + 1) * 128], ps.h[:, 0:PC * 128].rearrange("p (c t) -> p c t", t=128),
                       w=[ps.b] + ([pT.b] if firstp else []), wa=[] if firstp else [pT.b])
                firstp = False
            for (g0, gw) in chunks(C.D, 512):
                wg_ = wgp.get()
                load_w(K, wg_, K.d["bf_w_pg"], g0 // 512, 0, DC, gw, rbuf=K.b["bf_w_pg"])
                wp_ = wpp.get()
                load_w(K, wp_, K.d["w_pp"], g0 // 512, 0, PC, gw)
                for s_ in range(nt // 128):
                    psG = K.ps.get()
                    S.mms([(psG.h[:, 0:gw], hT.h[:, k, s_ * 128:(s_ + 1) * 128], wg_.h[:, k, 0:gw], k == 0, k == DC - 1) for k in range(DC)],
                          r=[hT.b, wg_.b], w=[psG.b])
                    psP = K.ps.get()
                    S.mms([(psP.h[:, 0:gw], pT.h[:, k, s_ * 128:(s_ + 1) * 128], wp_.h[:, k, 0:gw], k == 0, k == PC - 1) for k in range(PC)],
                          r=[pT.b, wp_.b], w=[psP.b])
                    sg = sgp.get()
                    S.act(sg.h[:, 0:gw], psG.h[:, 0:gw], AF.Sigmoid, w=[sg.b, psG.b])
                    S.tt("dve", sg.h[:, 0:gw], psP.h[:, 0:gw], sg.h[:, 0:gw], ALU.mult, w=[sg.b, psP.b])
                    xt = xp.get()
                    S.dma("sp", xt.h[:, 0:gw], K.d["x2"][row0 + s_ * 128:row0 + (s_ + 1) * 128, g0:g0 + gw], r=[K.b["x2"]], w=[xt.b])
                    S.tt("pool", xt.h[:, 0:gw], xt.h[:, 0:gw], sg.h[:, 0:gw], ALU.add, r=[sg.b], w=[xt.b])
                    S.dma("sp", K.d["out"][row0 + s_ * 128:row0 + (s_ + 1) * 128, g0:g0 + gw], xt.h[:, 0:gw], r=[xt.b], wa=[K.b["out"]])
        K.rem.append(nc.sbuf_bytes_remaining)
        S.emit()

def build_program(C, upto="A", debug=()):
    nc = bass.Bass("TRN2", target_bir_lowering=False)
    K = Ctx()
    K.nc, K.C = nc, C
    K.d, K.b = {}, {}
    K.rem = []

    def din(name, shape):
        K.d[name] = nc.dram_tensor(name, list(shape), F32, kind="ExternalInput").ap()

    def dscr(name, shape, dt=F32, out=False):
        kind = "ExternalOutput" if out else "Internal"
        K.d[name] = nc.dram_tensor(name, list(shape), dt, kind=kind).ap()
        K.b[name] = Buf(name)

    din("xc", [C.CTX, C.D])
    din("pc", [C.HALF, C.PLE])
    din("cols", [128, C.NCOLS])
    din("consts", [128, C.NCONST])
    din("lamb", [128, 256])
    din("w_in", [(C.IC + 511) // 512, 128, C.D // 128, 512])
    din("w2", [C.DL, C.RW])
    din("a2", [C.AL, C.RW])
    din("g2", [C.GL, C.RW])
    din("w_a", [C.D // 256, 128, C.RW // 128, 256])
    din("w_b", [C.D // 256, 128, C.DW // 128, 256])
    din("w_o", [C.D // 512, 128, C.D // 128, 512])
    din("w_fi", [C.DFF // 128, 128, C.D // 128, 256])
    din("w_fo", [C.D // 256, 128, C.DFF // 128, 256])
    din("w_pg", [C.D // 512, 128, C.D // 128, 512])
    din("w_pp", [C.D // 512, 128, C.PLE // 128, 512])
    dscr("out", [C.HALF, C.D], out=True)
    dscr("zR", [C.RC, C.CTX], out=("zT" in debug))
    dscr("zQ", [C.DW, C.NOH], out=("zT" in debug))
    dscr("zKV", [2 * C.DW, C.CTX], out=("zT" in debug))
    dscr("zG", [2 * C.D, C.NOH], out=("zT" in debug))
    for nm_ in ("w_a", "w_b", "w_o", "w_fi", "w_fo", "w_pg", "w_in"):
        dscr("bf_" + nm_, list(K.d[nm_].shape), BF16)
    dscr("gT", [C.RW, C.CTX], out=("gT" in debug))
    dscr("bonT", [C.RW, C.CTX], out=("gT" in debug))
    dscr("oaT", [C.RW, C.NOH], BF16, out=("oaT" in debug))
    dscr("obT", [C.DW, C.NOH], BF16, out=("obT" in debug))
    dscr("x1", [C.NOH, C.D], out=("x1" in debug))
    dscr("x2", [C.HALF, C.D], out=("x2" in debug))
    with contextlib.ExitStack() as es0:
        K.S = Sched(nc, es0)
        phase_A(K)
        if upto == "A":
            return nc, K
        K.pc = precast_list(K)
        phase_B(K)
        if upto == "B":
            return nc, K
        phase_C(K)
        if upto == "C":
            return nc, K
        phase_DE(K)
        if upto == "DE":
            return nc, K
        phase_F(K)
        if upto == "F":
            return nc, K
        phase_G(K)
    return nc, K


def colpack(v):
    v = np.asarray(v, np.float32).reshape(-1)
    n = (v.size + 127) // 128
    p = np.zeros(n * 128, np.float32)
    p[:v.size] = v
    return np.ascontiguousarray(p.reshape(n, 128).T)


def tile_w(w, W):
    w = np.asarray(w, np.float32)
    Kd, N = w.shape
    NG = (N + W - 1) // W
    if NG * W != N:
        w = np.concatenate([w, np.zeros((Kd, NG * W - N), np.float32)], axis=1)
    return np.ascontiguousarray(w.reshape(Kd // 128, 128, NG, W).transpose(2, 1, 0, 3))


def host_inputs(C, inp):
    g = lambda k: np.asarray(inp[k], np.float32)[0]
    mu = g("rwkv_mu")
    cw = g("ffn_conv_w")
    parts = {
        "g_mix": g("norm_mix_g"), "g_ffn": g("norm_ffn_g"), "g_ple": g("norm_ple_g"),
        "mu_r": mu[C.o_r:C.o_r + C.RW], "mu_k": mu[C.o_k:C.o_k + C.RW], "mu_v": mu[C.o_v:C.o_v + C.RW],
        "mu_w": mu[C.o_wl:C.o_wl + C.DL], "mu_a": mu[C.o_al:C.o_al + C.AL], "mu_g": mu[C.o_gl:C.o_gl + C.GL],
        "w0": g("rwkv_w0"), "a0": g("rwkv_a0"), "k_k": g("rwkv_k_k"), "k_a": g("rwkv_k_a"),
        "r_k": g("rwkv_r_k").reshape(-1), "ln_w": g("rwkv_ln_w"), "ln_b": g("rwkv_ln_b"),
        "q_g": np.tile(g("q_norm_g"), 2), "k_g": np.tile(g("k_norm_g"), 2), "subln": g("subln_g"),
        "cw0": cw[0], "cw1": cw[1], "cw2": cw[2], "cb": g("ffn_conv_b"),
    }
    cols = np.zeros((128, C.NCOLS), np.float32)
    for name, (off, n) in C.colmap.items():
        cp = colpack(parts[name])
        assert cp.shape[1] == n, (name, cp.shape, n)
        cols[:, off:off + n] = cp
    consts = np.zeros((128, C.NCONST), np.float32)
    consts[:, 0:128] = np.eye(128, dtype=np.float32)
    consts[0:64, 128:192] = 1.0
    consts[64:128, 192:256] = 1.0
    consts[:, 256:384] = 1.0
    s = np.arange(64)[:, None]
    t = np.arange(64)[None, :]
    consts[0:64, 384:448] = (s < t)
    consts[0:64, 448:512] = (s <= t)
    consts[0:64, 512:576] = (s > t)
    sm = np.ones(512, np.float32)
    sm[0::64] = 0.0
    consts[:, 576:1088] = sm[None, :]
    lamb = np.concatenate([g("lam_q1"), g("lam_k1"), g("lam_q2"), g("lam_k2")])[None, :].repeat(128, 0)
    x = np.asarray(inp["x"], np.float32)
    p = np.asarray(inp["p"], np.float32)[0]
    shared = {
        "cols": cols, "lamb": np.ascontiguousarray(lamb),
        "w_in": tile_w(g("w_in"), 512), "w2": g("rwkv_w2"), "a2": g("rwkv_a2"), "g2": g("rwkv_g2"),
        "w_a": tile_w(g("w_branch_a"), 256), "w_b": tile_w(g("w_branch_b"), 256), "w_o": tile_w(g("w_out"), 512),
        "w_fi": np.ascontiguousarray(np.concatenate([tile_w(g("w_ffn_in")[:, :C.DFF], 128), tile_w(g("w_ffn_in")[:, C.DFF:], 128)], axis=3)),
        "w_fo": tile_w(g("w_ffn_out"), 256), "w_pg": tile_w(g("w_ple_gate"), 512), "w_pp": tile_w(g("w_ple_proj"), 512),
    }
    maps = []
    for core in range(2 * C.B):
        b, hf = core // 2, core % 2
        m = dict(shared)
        if hf == 1:
            m["xc"] = np.ascontiguousarray(x[b])
        else:
            xc = np.zeros((C.CTX, C.D), np.float32)
            xc[C.HALF:] = x[b, 0:C.HALF]
            m["xc"] = xc
        m["pc"] = np.ascontiguousarray(p[b, hf * C.HALF:(hf + 1) * C.HALF])
        cc = consts.copy()
        cc[:, 1088:1216] = float(hf)
        m["consts"] = cc
        maps.append(m)
    return maps


_PROG = {}


def kernel(**inputs):
    C = Cfg()
    if "full" not in _PROG:
        _PROG["full"] = build_program(C, upto="ALL")
    nc, K = _PROG["full"]
    maps = host_inputs(C, inputs)
    res = run_bass_kernel_spmd(nc, maps, core_ids=list(range(2 * C.B)))
    out = np.zeros((C.B, C.SEQ, C.D), np.float32)
    for core in range(2 * C.B):
        b, hf = core // 2, core % 2
        out[b, hf * C.HALF:(hf + 1) * C.HALF] = res.results[core]["out"]
    return out
```
